# Optimizing a Trainium2 kernel written in Bass

```python
import math
import jax
import jax.numpy as jnp
from jax import lax
import numpy as np

D_MODEL = 1024
BATCH = 2
SEQ = 8192
DEPTH = 2

CTX_LEN = 256
GRID_W = 64
N_ADA = 9
D_FF = 2816
MACARON_W = 0.5
NORM_EPS = 1e-6
MIX_W = 512
CONV_K = 3

M_HEADS = 4
M_DQK = 64
M_DV = MIX_W // M_HEADS
M_QK_W = M_HEADS * M_DQK
M_CHUNK = 64

R_DH = 64
R_HEADS = MIX_W // R_DH
DECAY_LORA = 64
AAA_LORA = 64
GATE_LORA = 128
R_GN_EPS = 64e-5

A_HEADS = 4
A_DV = MIX_W // A_HEADS
A_DH = A_DV // 2
Q_BLOCK = 128
ROPE_BASE = 10000.0

IN_SPLITS = (
    ('m_q', M_QK_W), ('m_k', M_QK_W), ('m_v', MIX_W), ('m_o', MIX_W),
    ('m_if', M_HEADS), ('m_ff', M_HEADS), ('m_ib', M_HEADS), ('m_fb', M_HEADS),
    ('r_r', MIX_W), ('r_k', MIX_W), ('r_v', MIX_W),
    ('r_wf', DECAY_LORA), ('r_wb', DECAY_LORA), ('r_af', AAA_LORA), ('r_ab', AAA_LORA), ('r_g', GATE_LORA),
    ('a_q', 2 * A_HEADS * A_DH), ('a_k', 2 * A_HEADS * A_DH), ('a_v', MIX_W),
    ('g_m', D_MODEL), ('g_r', D_MODEL), ('g_a', D_MODEL),
)
D_IN = sum(w for _, w in IN_SPLITS)

kernel_name = 'hybrid_mlstm_rwkv7_diffattn_macaron_prefix'


def _rmsnorm(x, g):
    x32 = x.astype(jnp.float32)
    y = x32 * lax.rsqrt(jnp.mean(x32 * x32, axis=-1, keepdims=True) + NORM_EPS)
    return (y * g.astype(jnp.float32)).astype(x.dtype)


def _modulate(h, shift, scale):
    return h * (1 + scale) + shift


def _ada(cvec, w_ada, b_ada):
    return jnp.split(jax.nn.silu(cvec) @ w_ada + b_ada, N_ADA, axis=-1)


def _split_cols(z):
    cols, off = {}, 0
    for name, width in IN_SPLITS:
        cols[name] = z[..., off:off + width]
        off += width
    return cols


def _dwconv_centred(x, w):
    pad = CONV_K // 2
    return lax.conv_general_dilated(
        x, w[:, None, :].astype(x.dtype), window_strides=(1,), padding=[(pad, pad)],
        dimension_numbers=('NWC', 'WIO', 'NWC'), feature_group_count=x.shape[-1])


def _swiglu_half_step(z, mods, g, w_gu, w_down):
    shift, scale, gate = mods
    h = _modulate(_rmsnorm(z, g), shift, scale)
    a, b = jnp.split(h @ w_gu, 2, axis=-1)
    return z + MACARON_W * gate * ((jax.nn.silu(a) * b) @ w_down)


def _rope2d_tables(n):
    rows = n // GRID_W
    row = jnp.repeat(jnp.arange(rows, dtype=jnp.float32), GRID_W)
    col = jnp.tile(jnp.arange(GRID_W, dtype=jnp.float32), rows)
    half = A_DH // 2
    inv = ROPE_BASE ** (-jnp.arange(0, half, 2, dtype=jnp.float32) / half)
    ang = jnp.concatenate([row[:, None] * inv, col[:, None] * inv], axis=-1)
    return jnp.cos(ang), jnp.sin(ang)


def _rope2d(x, cos, sin):
    x1, x2 = x[..., 0::2], x[..., 1::2]
    c = cos[None, :, None, None, :].astype(x.dtype)
    s = sin[None, :, None, None, :].astype(x.dtype)
    return jnp.stack([x1 * c - x2 * s, x1 * s + x2 * c], axis=-1).reshape(x.shape)


def _mlstm_chunkwise(q, k, v, ig, lf, state):
    bsz, nh, n, _ = q.shape
    nc = n // M_CHUNK
    to_chunks = lambda a: jnp.moveaxis(a.reshape(bsz, nh, nc, M_CHUNK, *a.shape[3:]), 2, 0)
    tril = jnp.tril(jnp.ones((M_CHUNK, M_CHUNK), dtype=bool))

    def step(carry, inp):
        c_st, n_st, m_st = carry
        qb, kb, vb, ib, fb = inp
        bcum = jnp.cumsum(fb, axis=-1)
        dlog = jnp.where(tril, bcum[..., :, None] - bcum[..., None, :] + ib[..., None, :], -jnp.inf)
        inter = bcum + m_st[..., None]
        mt = jnp.maximum(inter, jnp.max(dlog, axis=-1))
        s = jnp.einsum('bhtd,bhsd->bhts', qb, kb) * jnp.exp(dlog - mt[..., None])
        iw = jnp.exp(inter - mt)
        num = jnp.einsum('bhts,bhsv->bhtv', s, vb) + iw[..., None] * jnp.einsum('bhvd,bhtd->bhtv', c_st, qb)
        den = jnp.sum(s, axis=-1) + iw * jnp.einsum('bhd,bhtd->bht', n_st, qb)
        h = num / jnp.maximum(jnp.abs(den), jnp.exp(-mt))[..., None]
        btot = bcum[..., -1]
        wlog = btot[..., None] - bcum + ib
        m_new = jnp.maximum(btot + m_st, jnp.max(wlog, axis=-1))
        ws = jnp.exp(wlog - m_new[..., None])
        dec = jnp.exp(btot + m_st - m_new)
        c_new = dec[..., None, None] * c_st + jnp.einsum('bhs,bhsv,bhsd->bhvd', ws, vb, kb)
        n_new = dec[..., None] * n_st + jnp.einsum('bhs,bhsd->bhd', ws, kb)
        return (c_new, n_new, m_new), h

    state, hs = lax.scan(step, state, (to_chunks(q), to_chunks(k), to_chunks(v), to_chunks(ig), to_chunks(lf)))
    return jnp.moveaxis(hs, 0, 2).reshape(bsz, nh, n, -1), state


def _mlstm_mixer(pc, pl, conv_w, gate_bias, out_norm):
    f32 = jnp.float32

    def prep(p):
        qk = jax.nn.silu(_dwconv_centred(jnp.concatenate([p['m_q'], p['m_k']], axis=-1), conv_w)).astype(f32)
        bsz, n, _ = qk.shape
        heads = lambda a, d: a.reshape(bsz, n, M_HEADS, d).transpose(0, 2, 1, 3)
        q = heads(qk[..., :M_QK_W], M_DQK)
        k = heads(qk[..., M_QK_W:], M_DQK) * (M_DQK ** -0.5)
        v = heads(p['m_v'].astype(f32), M_DV)
        gates = [p[name].astype(f32).transpose(0, 2, 1) + gate_bias[j].astype(f32)[:, None]
                 for j, name in enumerate(('m_if', 'm_ff', 'm_ib', 'm_fb'))]
        return q, k, v, gates

    qc, kc, vc, gc = prep(pc)
    ql, kl, vl, gl = prep(pl)
    bsz = ql.shape[0]
    outs = []
    for d in range(2):
        flip = (lambda a: jnp.flip(a, axis=2)) if d == 1 else (lambda a: a)
        state = (jnp.zeros((bsz, M_HEADS, M_DV, M_DQK), f32), jnp.zeros((bsz, M_HEADS, M_DQK), f32),
                 jnp.zeros((bsz, M_HEADS), f32))
        hc, state = _mlstm_chunkwise(flip(qc), flip(kc), flip(vc), flip(gc[2 * d]),
                                     flip(jax.nn.log_sigmoid(gc[2 * d + 1])), state)
        hl, _ = _mlstm_chunkwise(flip(ql), flip(kl), flip(vl), flip(gl[2 * d]),
                                 flip(jax.nn.log_sigmoid(gl[2 * d + 1])), state)
        outs.append((flip(hc), flip(hl)))

    def post(h, p):
        b, _, n, _ = h.shape
        h = _rmsnorm(h.transpose(0, 2, 1, 3), out_norm.reshape(M_HEADS, M_DV)).reshape(b, n, MIX_W)
        return (h * jax.nn.sigmoid(p['m_o'].astype(f32))).astype(p['m_o'].dtype)

    return post(outs[0][0] + outs[1][0], pc), post(outs[0][1] + outs[1][1], pl)


def _rwkv7_scan(r, w, k, v, kk, a, state, reverse):
    tm = lambda t: jnp.moveaxis(t, 1, 0)

    def step(s, inp):
        rt, wt, kt, vt, kkt, at = inp
        sa = jnp.einsum('bhvk,bhk->bhv', s, -kkt)
        s = s * wt[:, :, None, :] + sa[..., None] * (kkt * at)[:, :, None, :] + vt[..., None] * kt[:, :, None, :]
        return s, jnp.einsum('bhvk,bhk->bhv', s, rt)

    state, ys = lax.scan(step, state, (tm(r), tm(w), tm(k), tm(v), tm(kk), tm(a)), reverse=reverse)
    return jnp.moveaxis(ys, 0, 1), state


def _rwkv7_mixer(pc, pl, conv_w, w0, w2, a0, a2, g2, k_k, k_a, r_k, ln_w, ln_b):
    f32 = jnp.float32

    def prep(p):
        rkv = _dwconv_centred(jnp.concatenate([p['r_r'], p['r_k'], p['r_v']], axis=-1), conv_w).astype(f32)
        bsz, n, _ = rkv.shape
        heads = lambda t: t.reshape(bsz, n, R_HEADS, R_DH)
        r, k, v = rkv[..., :MIX_W], rkv[..., MIX_W:2 * MIX_W], rkv[..., 2 * MIX_W:]
        kk = heads(k * k_k.astype(f32))
        kk = kk / jnp.maximum(jnp.sqrt(jnp.sum(kk * kk, axis=-1, keepdims=True)), 1e-12)
        per_dir = []
        for d, (wl, al) in enumerate((('r_wf', 'r_af'), ('r_wb', 'r_ab'))):
            w_raw = w0[d].astype(f32) + jnp.tanh(p[wl].astype(f32)) @ w2[d].astype(f32)
            decay = jnp.exp(-jnp.exp(-jax.nn.softplus(-w_raw) - 0.5))
            a = jax.nn.sigmoid(a0[d].astype(f32) + p[al].astype(f32) @ a2[d].astype(f32))
            k_dir = k * (1 + (a - 1) * k_a.astype(f32))
            per_dir.append((heads(decay), heads(k_dir), heads(a)))
        return heads(r), heads(k), heads(v), kk, per_dir

    rc, kc, vc, kkc, dc = prep(pc)
    rl, kl, vl, kkl, dl = prep(pl)
    bsz = rl.shape[0]
    y_ctx, y_lat = [], []
    for d in range(2):
        s0 = jnp.zeros((bsz, R_HEADS, R_DH, R_DH), f32)
        wc, kdc, ac = dc[d]
        wlt, kdl, alt = dl[d]
        yc_d, s_ctx = _rwkv7_scan(rc, wc, kdc, vc, kkc, ac, s0, reverse=(d == 1))
        yl_d, _ = _rwkv7_scan(rl, wlt, kdl, vl, kkl, alt, s_ctx, reverse=(d == 1))
        y_ctx.append(yc_d)
        y_lat.append(yl_d)

    def post(y, r, k, v, p):
        b, n = y.shape[:2]
        mu = jnp.mean(y, axis=-1, keepdims=True)
        var = jnp.mean(jnp.square(y - mu), axis=-1, keepdims=True)
        yn = ((y - mu) * lax.rsqrt(var + R_GN_EPS)).reshape(b, n, MIX_W) * ln_w.astype(f32) + ln_b.astype(f32)
        bonus = jnp.sum(r * k * r_k.astype(f32).reshape(R_HEADS, R_DH), axis=-1, keepdims=True) * v
        g = jax.nn.sigmoid(p['r_g'].astype(f32)) @ g2.astype(f32)
        return ((yn + bonus.reshape(b, n, MIX_W)) * g).astype(p['r_g'].dtype)

    return post(y_ctx[0] + y_ctx[1], rc, kc, vc, pc), post(y_lat[0] + y_lat[1], rl, kl, vl, pl)


def _lambda_init(layer):
    return 0.8 - 0.6 * math.exp(-0.3 * layer)


def _diff_attention(pc, pl, qk_norm, lam_p, subln, lam_init, cos, sin):
    f32 = jnp.float32

    def heads(p):
        b, n, _ = p['a_q'].shape
        q = _rmsnorm(p['a_q'].reshape(b, n, A_HEADS, 2, A_DH), qk_norm[0]) * (A_DH ** -0.5)
        k = _rmsnorm(p['a_k'].reshape(b, n, A_HEADS, 2, A_DH), qk_norm[1])
        return q, k, p['a_v'].reshape(b, n, A_HEADS, A_DV)

    qc, kc, vc = heads(pc)
    ql, kl, vl = heads(pl)
    ql = _rope2d(ql, cos, sin)
    kl = _rope2d(kl, cos, sin)
    lp = lam_p.astype(f32)
    lam = jnp.exp(jnp.sum(lp[0] * lp[1])) - jnp.exp(jnp.sum(lp[2] * lp[3])) + lam_init

    def attend(q, k, v):
        s = jnp.einsum('bqhmd,bkhmd->bhmqk', q, k).astype(f32)
        pr = jax.nn.softmax(s, axis=-1)
        amap = pr[:, :, 0] - lam * pr[:, :, 1]
        return jnp.einsum('bhqk,bkhv->bqhv', amap.astype(v.dtype), v)

    yc = attend(qc, kc, vc)
    k_all = jnp.concatenate([kc, kl], axis=1)
    v_all = jnp.concatenate([vc, vl], axis=1)
    b, n = ql.shape[:2]
    nb = n // Q_BLOCK
    qb = jnp.moveaxis(ql.reshape(b, nb, Q_BLOCK, A_HEADS, 2, A_DH), 1, 0)
    yl = lax.map(lambda q: attend(q, k_all, v_all), qb)
    yl = jnp.moveaxis(yl, 0, 1).reshape(b, n, A_HEADS, A_DV)
    post = lambda y: (_rmsnorm(y, subln) * (1 - lam_init)).reshape(y.shape[0], y.shape[1], MIX_W)
    return post(yc), post(yl)


def _merge(p, ym, yr, ya, w_branch, w_o):
    z = (jax.nn.sigmoid(p['g_m']) * (ym @ w_branch[0])
         + jax.nn.sigmoid(p['g_r']) * (yr @ w_branch[1])
         + jax.nn.sigmoid(p['g_a']) * (ya @ w_branch[2]))
    return z @ w_o


def setup_inputs(seed: int = 0) -> dict:
    key = jax.random.key(seed)
    ks = jax.random.split(key, 34)
    L, D = DEPTH, D_MODEL

    def nrm(i, shape, scale):
        return jax.random.normal(ks[i], shape, jnp.float32) * scale

    def uni(i, shape):
        return jax.random.uniform(ks[i], shape, jnp.float32)

    m_gate_bias = jnp.stack([nrm(13, (L, M_HEADS), 0.1), 3.0 + 3.0 * uni(14, (L, M_HEADS)),
                             nrm(15, (L, M_HEADS), 0.1), 3.0 + 3.0 * uni(16, (L, M_HEADS))], axis=1)
    return {
        'x': nrm(0, (BATCH, SEQ, D), 1.0),
        'c': nrm(1, (BATCH, D), 1.0),
        'ctx': nrm(2, (BATCH, CTX_LEN, D), 1.0),
        'c_ctx': nrm(3, (D,), 1.0),
        'w_ada': nrm(4, (L, D, N_ADA * D), 0.5 * D ** -0.5),
        'b_ada': nrm(5, (L, N_ADA * D), 0.01),
        'norm_g': 1.0 + nrm(6, (L, 3, D), 0.02),
        'ffn1_w_gu': nrm(7, (L, D, 2 * D_FF), D ** -0.5),
        'ffn1_w_down': nrm(8, (L, D_FF, D), D_FF ** -0.5),
        'ffn2_w_gu': nrm(9, (L, D, 2 * D_FF), D ** -0.5),
        'ffn2_w_down': nrm(10, (L, D_FF, D), D_FF ** -0.5),
        'w_in': nrm(11, (L, D, D_IN), D ** -0.5),
        'm_conv': nrm(12, (L, CONV_K, 2 * M_QK_W), CONV_K ** -0.5),
        'm_gate_bias': m_gate_bias,
        'm_out_norm': 1.0 + nrm(17, (L, MIX_W), 0.02),
        'r_conv': nrm(18, (L, CONV_K, 3 * MIX_W), CONV_K ** -0.5),
        'r_w0': -6.0 + 5.0 * uni(19, (L, 2, MIX_W)),
        'r_w2': nrm(20, (L, 2, DECAY_LORA, MIX_W), 0.5 * DECAY_LORA ** -0.5),
        'r_a0': nrm(21, (L, 2, MIX_W), 0.1),
        'r_a2': nrm(22, (L, 2, AAA_LORA, MIX_W), 0.5 * AAA_LORA ** -0.5),
        'r_g2': nrm(23, (L, GATE_LORA, MIX_W), GATE_LORA ** -0.5),
        'r_kk': 0.85 + nrm(24, (L, MIX_W), 0.02),
        'r_ka': 1.0 + nrm(25, (L, MIX_W), 0.02),
        'r_rk': nrm(26, (L, MIX_W), 0.1),
        'r_ln_w': 1.0 + nrm(27, (L, MIX_W), 0.02),
        'r_ln_b': nrm(28, (L, MIX_W), 0.01),
        'a_qk_norm': 1.0 + nrm(29, (L, 2, A_DH), 0.02),
        'a_lambda': nrm(30, (L, 4, A_DH), 0.1),
        'a_subln': 1.0 + nrm(31, (L, A_DV), 0.02),
        'w_branch': nrm(32, (L, 3, MIX_W, D), MIX_W ** -0.5),
        'w_o': nrm(33, (L, D, D), D ** -0.5),
    }


def reference(x, c, ctx, c_ctx, w_ada, b_ada, norm_g, ffn1_w_gu, ffn1_w_down, ffn2_w_gu, ffn2_w_down,
              w_in, m_conv, m_gate_bias, m_out_norm, r_conv, r_w0, r_w2, r_a0, r_a2, r_g2, r_kk, r_ka,
              r_rk, r_ln_w, r_ln_b, a_qk_norm, a_lambda, a_subln, w_branch, w_o):
    cos, sin = _rope2d_tables(x.shape[1])
    xc = ctx
    for li in range(DEPTH):
        last = li == DEPTH - 1
        mod_l = [m[:, None, :] for m in _ada(c, w_ada[li], b_ada[li])]
        mod_c = _ada(c_ctx, w_ada[li], b_ada[li])
        x = _swiglu_half_step(x, mod_l[0:3], norm_g[li, 0], ffn1_w_gu[li], ffn1_w_down[li])
        xc = _swiglu_half_step(xc, mod_c[0:3], norm_g[li, 0], ffn1_w_gu[li], ffn1_w_down[li])
        hl = _modulate(_rmsnorm(x, norm_g[li, 1]), mod_l[3], mod_l[4])
        hc = _modulate(_rmsnorm(xc, norm_g[li, 1]), mod_c[3], mod_c[4])
        pl = _split_cols(hl @ w_in[li])
        pc = _split_cols(hc @ w_in[li])
        ym_c, ym_l = _mlstm_mixer(pc, pl, m_conv[li], m_gate_bias[li], m_out_norm[li])
        yr_c, yr_l = _rwkv7_mixer(pc, pl, r_conv[li], r_w0[li], r_w2[li], r_a0[li], r_a2[li], r_g2[li],
                                  r_kk[li], r_ka[li], r_rk[li], r_ln_w[li], r_ln_b[li])
        ya_c, ya_l = _diff_attention(pc, pl, a_qk_norm[li], a_lambda[li], a_subln[li], _lambda_init(li), cos, sin)
        x = x + mod_l[5] * _merge(pl, ym_l, yr_l, ya_l, w_branch[li], w_o[li])
        x = _swiglu_half_step(x, mod_l[6:9], norm_g[li, 2], ffn2_w_gu[li], ffn2_w_down[li])
        if not last:
            xc = xc + mod_c[5] * _merge(pc, ym_c, yr_c, ya_c, w_branch[li], w_o[li])
            xc = _swiglu_half_step(xc, mod_c[6:9], norm_g[li, 2], ffn2_w_gu[li], ffn2_w_down[li])
    return x
```

```python
import contextlib
import numpy as np
import ml_dtypes
import concourse.bass as bass
import concourse.mybir as mybir
from concourse.bass_utils import run_bass_kernel_spmd

F32 = mybir.dt.float32
BF16 = mybir.dt.bfloat16
AF = mybir.ActivationFunctionType
ALU = mybir.AluOpType
AX = mybir.AxisListType
NPBF = ml_dtypes.bfloat16

ENGS = ("pe", "dve", "act", "pool", "sp")


class Buf:
    __slots__ = ("t", "w", "r", "name", "ds", "psum")

    def __init__(self, t=None, name=""):
        self.t = t
        self.ds = None
        self.psum = False
        self.w = None
        self.r = []
        self.name = name

    def __getitem__(self, idx):
        return self.t[idx]


class Sem:
    __slots__ = ("h", "count")

    def __init__(self, h):
        self.h = h
        self.count = 0


class Prog:
    def __init__(self, name="k"):
        self.nc = bass.Bass("TRN2", target_bir_lowering=False)
        self.es = contextlib.ExitStack()
        self.q = {e: [] for e in ENGS}
        self.esem = {}
        for e in ENGS:
            self.esem[e] = Sem(self.es.enter_context(self.nc.semaphore(f"s_{e}")))
        self.seen = {e: {} for e in ENGS}
        self.dsems = []
        self.nbuf = 0

    def dram(self, name, shape, dt, kind):
        return self.nc.dram_tensor(name, list(shape), dt, kind=kind).ap()

    def sb(self, shape, dt=F32, name=None):
        self.nbuf += 1
        name = name or f"sb{self.nbuf}"
        t = self.es.enter_context(self.nc.sbuf_tensor(name, list(shape), dt))
        return Buf(t, name)

    def ps(self, shape, dt=F32, name=None):
        self.nbuf += 1
        name = name or f"ps{self.nbuf}"
        t = self.es.enter_context(self.nc.psum_tensor(name, list(shape), dt))
        b = Buf(t, name)
        b.psum = True
        return b

    def dsem(self):
        s = Sem(self.es.enter_context(self.nc.semaphore(f"d{len(self.dsems)}")))
        self.dsems.append(s)
        return s

    def _deps(self, eng, reads, writes):
        deps = []
        for b in reads:
            if b.w is not None:
                deps.append(b.w)
            if b.psum:
                deps.extend(ev for ev in b.r if ev[2] != eng)
        for b in writes:
            if b.w is not None:
                deps.append(b.w)
            deps.extend(b.r)
        best = {}
        for (s, v, src) in deps:
            if src == "pe" and eng == "pe":
                continue
            if v > best.get(s, (0, None))[0]:
                best[s] = (v, src)
        for s, (v, src) in best.items():
            if self.seen[eng].get(s, 0) >= v:
                continue
            self.seen[eng][s] = v
            self.q[eng].append(("w", s, v))

    def op(self, eng, fn, reads=(), writes=()):
        self._deps(eng, reads, writes)
        s = self.esem[eng]
        s.count += 1
        ev = (s, s.count, eng)
        self.q[eng].append(("i", fn, s, 1))
        for b in reads:
            b.r.append(ev)
        for b in writes:
            b.w = ev
            b.r = []
        return ev

    def dma(self, eng, out, in_, sem=None, reads=(), writes=(), **kw):
        b0 = (list(writes) + list(reads))[0]
        if b0.ds is None:
            b0.ds = self.dsem()
        sem = b0.ds
        self._deps(eng, reads, writes)
        sem.count += 16
        ev = (sem, sem.count, "dma")
        self.q[eng].append(("i", lambda e: e.dma_start(out=out, in_=in_, **kw), sem, 16))
        for b in reads:
            b.r.append(ev)
        for b in writes:
            b.w = ev
            b.r = []
        return ev

    def finish(self):
        for s in self.dsems + [self.esem[e] for e in ENGS if e != "sp"]:
            if s.count > 0:
                self.q["sp"].append(("w", s, s.count))
        q = self.q

        def run(eng_obj, items):
            for it in items:
                if it[0] == "w":
                    eng_obj.wait_ge(it[1].h, it[2])
                else:
                    it[1](eng_obj).then_inc(it[2].h, it[3])

        with self.nc.Block() as block:
            @block.tensor
            def _(e):
                run(e, q["pe"])

            @block.vector
            def _(e):
                run(e, q["dve"])

            @block.scalar
            def _(e):
                run(e, q["act"])

            @block.gpsimd
            def _(e):
                run(e, q["pool"])

            @block.sync
            def _(e):
                run(e, q["sp"])
        self.es.close()
        return self.nc

    def mm(self, out, lhsT, rhs, start, stop, reads, writes):
        return self.op("pe", lambda e: e.matmul(out, lhsT, rhs, start=start, stop=stop), reads, writes)

    def tr(self, out, in_, ident, reads, writes):
        return self.op("pe", lambda e: e.transpose(out, in_, ident), reads, writes)

    def act(self, out, in_, func, reads, writes, bias=None, scale=None, accum_out=None):
        kw = {}
        if bias is not None:
            kw["bias"] = bias
        if scale is not None:
            kw["scale"] = scale
        if accum_out is not None:
            kw["accum_out"] = accum_out
        return self.op("act", lambda e: e.activation(out, in_, func, **kw), reads, writes)

    def tt(self, out, in0, in1, op, reads, writes, eng="dve"):
        return self.op(eng, lambda e: e.tensor_tensor(out, in0, in1, op), reads, writes)

    def ts(self, out, in0, s1, s2, op0, op1, reads, writes, eng="dve", accum_out=None):
        if op1 is None:
            return self.op(eng, lambda e: e.tensor_scalar(out, in0, s1, None, op0), reads, writes)
        if accum_out is not None:
            return self.op(eng, lambda e: e.tensor_scalar(out, in0, s1, s2, op0, op1, accum_out=accum_out), reads, writes)
        return self.op(eng, lambda e: e.tensor_scalar(out, in0, s1, s2, op0, op1), reads, writes)

    def stt(self, out, in0, scalar, in1, op0, op1, reads, writes):
        return self.op("dve", lambda e: e.scalar_tensor_tensor(out, in0, scalar, in1, op0, op1), reads, writes)

    def cp(self, out, in_, reads, writes, eng="dve"):
        if eng == "act":
            return self.op("act", lambda e: e.copy(out, in_), reads, writes)
        return self.op(eng, lambda e: e.tensor_copy(out, in_), reads, writes)

    def memset(self, ap, val, writes, eng="dve"):
        return self.op(eng, lambda e: e.memset(ap, val), (), writes)


D = 1024
DFF = 2816
NT = 17
TOK = NT * 128
CTXT = 2
SEQT = 66
EPS = 1e-6
BLOCKS = [(0, 2), (2, 3), (5, 3), (8, 3), (11, 3), (14, 3)]


def make_ident(P, n=128, dt=F32):
    ident = P.sb([n, n], dt)
    P.memset(ident[:], 1.0, [ident], eng="pool")
    P.op("pool", lambda e: e.affine_select(ident[:], ident[:], [[-1, n]], ALU.is_equal, 0.0, base=0,
                                           channel_multiplier=1), [ident], [ident])
    return ident


def load_cast(P, dst, dst_ap, src_ap, stage, sem, i, shape_cols):
    st = stage[i % len(stage)]
    P.dma("sp", st[:, 0:shape_cols], src_ap, writes=[st])
    eng = "dve" if i % 2 == 0 else "pool"
    P.cp(dst_ap, st[:, 0:shape_cols], [st], [dst], eng=eng)


def build_ffn(emit_h):
    P = Prog()
    x = P.dram("x", [NT, 128, D], F32, "ExternalInput")
    wgu = P.dram("wgu", [D, 2 * DFF], F32, "ExternalInput")
    wd = P.dram("wd", [DFF, D], F32, "ExternalInput")
    pp = P.dram("pp", [128, 2 * 48], F32, "ExternalInput")
    gateb = P.dram("gateb", [2, 128, D], F32, "ExternalInput")
    xo = P.dram("xo", [NT, 128, D], F32, "ExternalOutput")
    if emit_h:
        ho = P.dram("ho", [8, 128, TOK], BF16, "ExternalOutput")

    ident = make_ident(P)
    Wgu = P.sb([128, 8, 2 * DFF], BF16, "Wgu")
    Wd = P.sb([128, 22, D], BF16, "Wd")
    stage = [P.sb([128, 704], F32, f"stg{i}") for i in range(2)]
    ssem = [P.dsem() for _ in range(2)]
    pps = P.sb([128, 96], F32, "pps")
    Gt = P.sb([128, 32], F32, "Gt")
    gates = P.sb([128, 2, D], F32, "gates")
    msem = P.dsem()
    P.dma("act", pps[:], pp[:, :], msem, writes=[pps])
    for s in range(2):
        P.dma("act", gates[:, s, :], gateb[s], msem, writes=[gates])
    for s in range(2):
        for j in range(2):
            sc = pps[:, s * 48 + j * 24 + 8: s * 48 + j * 24 + 16]
            g = pps[:, s * 48 + j * 24 + 16: s * 48 + j * 24 + 24]
            P.stt(Gt[:, s * 16 + j * 8: s * 16 + j * 8 + 8], sc, 1.0, g, ALU.add, ALU.mult, [pps], [Gt])
    P.ts(gates[:], gates[:], 0.5, None, ALU.mult, None, [gates], [gates], eng="pool")

    n = 0
    for kc in range(8):
        for q in range(8):
            load_cast(P, Wgu, Wgu[:, kc, q * 704:(q + 1) * 704], wgu[kc * 128:(kc + 1) * 128, q * 704:(q + 1) * 704],
                      stage, ssem, n, 704)
            n += 1
    for fc in range(22):
        for q in range(2):
            load_cast(P, Wd, Wd[:, fc, q * 512:(q + 1) * 512], wd[fc * 128:(fc + 1) * 128, q * 512:(q + 1) * 512], stage, ssem, n, 512)
            n += 1

    xb = [P.sb([128, 3, D], F32, "xb0")] * 2
    xsem = [P.dsem() for _ in range(2)]
    osem = [P.dsem() for _ in range(2)]
    scr = {"xn": P.sb([128, 3, D], F32, "xn"), "sq": P.sb([128, D], BF16, "sq")}
    small = {"ss": P.sb([128, 4], F32, "ss")}
    hT = P.sb([128, 8, 384], BF16, "hT")
    hT2 = hT
    hsem = P.dsem()
    uT = P.sb([128, 22, 384], BF16, "uT")
    sa = [P.sb([128, 384], F32, f"sa{i}") for i in range(2)]
    pst = [P.ps([128, 512], F32, f"pst{i}") for i in range(2)]
    pa = [P.ps([128, 512], F32, f"pa{i}") for i in range(2)]
    pb = [P.ps([128, 512], F32, f"pb{i}") for i in range(2)]
    py = [P.ps([128, 512], F32, f"py{i}") for i in range(2)]
    tmp = [P.sb([128, 512], F32, f"tmp{i}") for i in range(2)]

    for bi, (t0, nb) in enumerate(BLOCKS):
        s = 0 if bi == 0 else 1
        N = nb * 128
        X = xb[bi % 2]
        for i in range(nb):
            P.dma("sp", X[:, i, :], x[t0 + i], xsem[bi % 2], writes=[X])
        _norm(P, X, nb, Gt, s * 16, pps, s * 48, hT, ident, pst, scr, small)
        for fc in range(22):
            A, B = pa[fc % 2], pb[fc % 2]
            for kc in range(8):
                P.mm(A[:, 0:N], Wgu[:, kc, fc * 128:(fc + 1) * 128], hT[:, kc, 0:N], kc == 0, kc == 7, [Wgu, hT], [A])
            for kc in range(8):
                P.mm(B[:, 0:N], Wgu[:, kc, DFF + fc * 128:DFF + (fc + 1) * 128], hT[:, kc, 0:N], kc == 0, kc == 7,
                     [Wgu, hT], [B])
            S = sa[fc % 2]
            P.act(S[:, 0:N], A[:, 0:N], AF.Silu, [A], [S])
            P.tt(uT[:, fc, 0:N], S[:, 0:N], B[:, 0:N], ALU.mult, [S, B], [uT])
        for i in range(nb):
            for h in range(2):
                Y = py[(i * 2 + h) % 2]
                for fc in range(22):
                    P.mm(Y[:, :], uT[:, fc, i * 128:(i + 1) * 128], Wd[:, fc, h * 512:(h + 1) * 512], fc == 0, fc == 21,
                         [uT, Wd], [Y])
                T = tmp[(i * 2 + h) % 2]
                P.tt(T[:], Y[:], gates[:, s, h * 512:(h + 1) * 512], ALU.mult, [Y, gates], [T])
                P.tt(X[:, i, h * 512:(h + 1) * 512], X[:, i, h * 512:(h + 1) * 512], T[:], ALU.add, [X, T], [X], eng="pool")
            P.dma("sp", xo[t0 + i], X[:, i, :], osem[bi % 2], reads=[X])
        if emit_h:
            _norm(P, X, nb, Gt, s * 16 + 8, pps, s * 48 + 24, hT2, ident, pst, scr, small)
            for c in range(8):
                P.dma("act", ho[c, :, t0 * 128:t0 * 128 + N], hT2[:, c, 0:N], hsem, reads=[hT2])
    return P.finish()


def _norm(P, xb, nb, Gt, gcol, SHt, scol, hT, ident, pst, scr, small):
    xn = scr["xn"]
    ss = small["ss"]
    for i in range(nb):
        P.act(scr["sq"][:], xb[:, i, :], AF.Square, [xb], [scr["sq"], ss], accum_out=ss[:, 0:1])
        P.ts(ss[:, 1:2], ss[:, 0:1], 1.0 / D, EPS, ALU.mult, ALU.add, [ss], [ss])
        P.act(ss[:, 2:3], ss[:, 1:2], AF.Sqrt, [ss], [ss])
        P.op("dve", lambda e: e.reciprocal(ss[:, 3:4], ss[:, 2:3]), [ss], [ss])
        P.ts(xn[:, i, :], xb[:, i, :], ss[:, 3:4], None, ALU.mult, None, [xb, ss], [xn])
    for c in range(8):
        pt = pst[c % len(pst)]
        for i in range(nb):
            P.tr(pt[:, i * 128:(i + 1) * 128], xn[:, i, c * 128:(c + 1) * 128], ident[:], [xn, ident], [pt])
        P.act(hT[:, c, 0:nb * 128], pt[:, 0:nb * 128], AF.Identity, [pt, Gt, SHt], [hT],
              bias=SHt[:, scol + c:scol + c + 1], scale=Gt[:, gcol + c:gcol + c + 1])


def build_mod():
    P = Prog()
    cT = P.dram("cT", [128, 8, 3], F32, "ExternalInput")
    wa = P.dram("wa", [2, D, 1152], F32, "ExternalInput")
    ba = P.dram("ba", [2, 3, 1152], F32, "ExternalInput")
    mo = P.dram("mo", [2, 3, 1152], F32, "ExternalOutput")
    cs = P.sb([128, 8, 3], F32)
    sg = P.sb([128, 8, 3], F32)
    ds = P.dsem()
    P.dma("sp", cs[:], cT[:, :, :], ds, writes=[cs])
    P.act(sg[:], cs[:], AF.Sigmoid, [cs], [sg])
    P.tt(cs[:], cs[:], sg[:], ALU.mult, [cs, sg], [cs])
    W = [P.sb([128, 8, 1152], F32, f"W{l}") for l in range(2)]
    wsem = P.dsem()
    bsb = P.sb([3, 2, 1152], F32)
    osb = P.sb([3, 2, 1152], F32)
    for l in range(2):
        P.dma("act", bsb[:, l, :], ba[l], ds, writes=[bsb])
        for kc in range(8):
            P.dma("sp" if kc % 2 == 0 else "act", W[l][:, kc, :], wa[l, kc * 128:(kc + 1) * 128, :], wsem, writes=[W[l]])
    pp = [P.ps([3, 512], F32, f"pp{i}") for i in range(2)]
    n = 0
    for l in range(2):
        for (c0, cw) in ((0, 512), (512, 512), (1024, 128)):
            ps = pp[n % 2]
            n += 1
            for kc in range(8):
                P.mm(ps[:, 0:cw], cs[:, kc, :], W[l][:, kc, c0:c0 + cw], kc == 0, kc == 7, [cs, W[l]], [ps])
            P.tt(osb[:, l, c0:c0 + cw], ps[:, 0:cw], bsb[:, l, c0:c0 + cw], ALU.add, [ps, bsb], [osb])
    osem = P.dsem()
    for l in range(2):
        P.dma("sp", mo[l], osb[:, l, :], osem, reads=[osb])
    return P.finish()


def run_mod(c, c_ctx, w_ada, b_ada):
    cv = np.stack([c[0], c[1], c_ctx], 0)
    cT = np.ascontiguousarray(cv.reshape(3, 8, 128).transpose(2, 1, 0))
    nc = build_mod()
    ins = []
    for i in range(8):
        cols = slice(i * 1152, (i + 1) * 1152)
        ins.append({"cT": cT, "wa": np.ascontiguousarray(w_ada[:, :, cols]),
                    "ba": np.ascontiguousarray(np.broadcast_to(b_ada[:, None, cols], (2, 3, 1152)))})
    res = run_bass_kernel_spmd(nc, ins, core_ids=list(range(8)))
    return np.concatenate([r["mo"] for r in res.results], axis=2)


def pvec(v):
    return np.ascontiguousarray(v.reshape(8, 128).T)


def tok_shard(xfull_b):
    pad = np.zeros((4 * TOK, xfull_b.shape[1]), np.float32)
    pad[:xfull_b.shape[0]] = xfull_b
    return [np.ascontiguousarray(pad[i * TOK:(i + 1) * TOK].reshape(NT, 128, -1)) for i in range(4)]


def ffn_inputs(xs, mods_l, li, which, norm_g, wgu, wd, emit_h):
    m = mods_l.reshape(3, 9, D)
    o = 0 if which == 1 else 6
    ins = []
    for b in range(2):
        shards = tok_shard(xs[b])
        for i in range(4):
            sets = []
            for s in range(2):
                r = 2 if (s == 0 and i == 0) else b
                vecs = [m[r, o + 0], m[r, o + 1], norm_g[li, 0 if which == 1 else 2]]
                if emit_h:
                    vecs += [m[r, 3], m[r, 4], norm_g[li, 1]]
                else:
                    vecs += [m[r, 3] * 0, m[r, 3] * 0, m[r, 3] * 0]
                sets.append(np.concatenate([pvec(v) for v in vecs], axis=1))
            pp = np.ascontiguousarray(np.concatenate(sets, axis=1).astype(np.float32))
            gb = np.stack([np.broadcast_to(m[2 if i == 0 else b, o + 2], (128, D)),
                           np.broadcast_to(m[b, o + 2], (128, D))], 0)
            ins.append({"x": shards[i], "wgu": wgu, "wd": wd, "pp": pp, "gateb": np.ascontiguousarray(gb)})
    return ins


def tok_unshard(res, key):
    out = []
    for b in range(2):
        full = np.concatenate([res[b * 4 + i][key].reshape(TOK, -1) for i in range(4)], axis=0)
        out.append(full[:SEQT * 128])
    return out


def hT_unshard(res, key):
    out = []
    for b in range(2):
        full = np.concatenate([res[b * 4 + i][key].reshape(D, TOK) for i in range(4)], axis=1)
        out.append(np.ascontiguousarray(full[:, :SEQT * 128]))
    return out


SEQ_ALL = SEQT * 128
NCTX = 256


def load_hblk(P, hT, hb, sem, t0, N, eng="sp"):
    for kc in range(8):
        P.dma(eng, hb[:, kc, 0:N], hT[kc, :, t0:t0 + N], sem, writes=[hb])


def build_attn(debug=False):
    P = Prog()
    hT = P.dram("hT", [8, 128, SEQ_ALL], BF16, "ExternalInput")
    wqkv = P.dram("wqkv", [3, D, 128], F32, "ExternalInput")
    gqk = P.dram("gqk", [128, 2], F32, "ExternalInput")
    cs = P.dram("cs", [2, 128, 8192], F32, "ExternalInput")
    cmat = P.dram("cmat", [2, 128, 128], F32, "ExternalInput")
    lamp = P.dram("lamp", [128, 258], F32, "ExternalInput")
    subg = P.dram("subg", [128, 128], F32, "ExternalInput")
    ya = P.dram("ya", [SEQ_ALL, 128], F32, "ExternalOutput")

    banks = [P.ps([128, 512], F32, f"bank{i}") for i in range(8)]
    csem = P.dsem()
    W = P.sb([128, 3, 8, 128], BF16, "W")
    wst = P.sb([128, 3, 8, 128], F32, "wst")
    for j in range(3):
        for kc in range(8):
            P.dma("sp", wst[:, j, kc, :], wqkv[j, kc * 128:(kc + 1) * 128, :], csem, writes=[wst])
    P.cp(W[:], wst[:], [wst], [W])
    gq = P.sb([128, 2], F32, "gq")
    P.dma("act", gq[:], gqk[:, :], csem, writes=[gq])
    P.ts(gq[:, 0:1], gq[:, 0:1], 0.125, None, ALU.mult, None, [gq], [gq])
    Bm = P.sb([128, 128], F32, "Bm")
    Rm = P.sb([128, 128], F32, "Rm")
    P.dma("act", Bm[:], cmat[0], csem, writes=[Bm])
    P.dma("act", Rm[:], cmat[1], csem, writes=[Rm])
    lp = P.sb([128, 258], F32, "lp")
    P.dma("act", lp[:], lamp[:, :], csem, writes=[lp])
    sg = P.sb([128, 128], F32, "sg")
    P.dma("act", sg[:], subg[:, :], csem, writes=[sg])
    P.ts(sg[:], sg[:], lp[:, 257:258], None, ALU.mult, None, [sg, lp], [sg])
    lt = P.sb([128, 128], F32, "lt")
    lam = P.sb([128, 4], F32, "lam")
    P.tt(lt[:, 0:64], lp[:, 0:64], lp[:, 64:128], ALU.mult, [lp], [lt])
    P.tt(lt[:, 64:128], lp[:, 128:192], lp[:, 192:256], ALU.mult, [lp], [lt])
    P.op("dve", lambda e: e.tensor_reduce(lam[:, 0:1], lt[:, 0:64], AX.X, ALU.add), [lt], [lam])
    P.op("dve", lambda e: e.tensor_reduce(lam[:, 1:2], lt[:, 64:128], AX.X, ALU.add), [lt], [lam])
    P.act(lam[:, 0:2], lam[:, 0:2], AF.Exp, [lam], [lam])
    P.tt(lam[:, 2:3], lam[:, 1:2], lam[:, 0:1], ALU.subtract, [lam], [lam])
    P.tt(lam[:, 3:4], lam[:, 2:3], lp[:, 256:257], ALU.subtract, [lam, lp], [lam])
    epsb = P.sb([128, 1], F32, "epsb")
    P.memset(epsb[:], EPS, [epsb])

    QK = [P.sb([128, SEQ_ALL], BF16, "QT"), P.sb([128, SEQ_ALL], BF16, "KT")]
    V = P.sb([128, SEQT, 129], BF16, "V")
    P.memset(V[:, :, 128:129], 1.0, [V], eng="pool")
    hb = [P.sb([128, 8, 512], BF16, f"hb{i}") for i in range(2)]
    hsem = [P.dsem() for _ in range(2)]
    cst = [P.sb([128, 2, 512], F32, f"cst{i}") for i in range(2)]
    cssem = [P.dsem() for _ in range(2)]
    sq = P.sb([128, 512], F32, "sq")
    rs = P.sb([128, 512], F32, "rs")
    xn = P.sb([128, 512], F32, "xn")
    t1 = P.sb([128, 512], F32, "t1")
    t2 = P.sb([128, 512], F32, "t2")

    blocks = [(0, 256)] + [(256 + 512 * j, 512) for j in range(16)]
    for bi, (t0, N) in enumerate(blocks):
        H = hb[bi % 2]
        load_hblk(P, hT, H, hsem[bi % 2], t0, N)
        C = cst[bi % 2]
        if bi > 0:
            for j in range(2):
                P.dma("act", C[:, j, :], cs[j, :, t0 - 256:t0 - 256 + N], cssem[bi % 2], writes=[C])
        for j in range(2):
            pq, pms, prot = banks[0 + j], banks[2 + j], banks[4 + j]
            for kc in range(8):
                P.mm(pq[:, 0:N], W[:, j, kc, :], H[:, kc, 0:N], kc == 0, kc == 7, [W, H], [pq])
            P.act(sq[:, 0:N], pq[:, 0:N], AF.Square, [pq], [sq])
            P.mm(pms[:, 0:N], Bm[:], sq[:, 0:N], True, True, [Bm, sq], [pms])
            P.act(rs[:, 0:N], pms[:, 0:N], AF.Sqrt, [pms, epsb], [rs], bias=epsb[:, 0:1])
            P.op("dve", lambda e, N=N: e.reciprocal(rs[:, 0:N], rs[:, 0:N]), [rs], [rs])
            P.stt(xn[:, 0:N], pq[:, 0:N], gq[:, j:j + 1], rs[:, 0:N], ALU.mult, ALU.mult, [pq, gq, rs], [xn])
            if bi == 0:
                P.cp(QK[j][:, t0:t0 + N], xn[:, 0:N], [xn], [QK[j]], eng="pool")
            else:
                P.mm(prot[:, 0:N], Rm[:], xn[:, 0:N], True, True, [Rm, xn], [prot])
                P.tt(t1[:, 0:N], xn[:, 0:N], C[:, 0, 0:N], ALU.mult, [xn, C], [t1], eng="pool")
                P.tt(t2[:, 0:N], prot[:, 0:N], C[:, 1, 0:N], ALU.mult, [prot, C], [t2])
                P.tt(QK[j][:, t0:t0 + N], t1[:, 0:N], t2[:, 0:N], ALU.add, [t1, t2], [QK[j]])
        for i in range(N // 128):
            pv = banks[6 + i % 2]
            for kc in range(8):
                P.mm(pv[:, 0:128], H[:, kc, i * 128:(i + 1) * 128], W[:, 2, kc, :], kc == 0, kc == 7, [H, W], [pv])
            P.cp(V[:, t0 // 128 + i, 0:128], pv[:, 0:128], [pv], [V], eng="act")

    if debug:
        dq = P.dram("dq", [2, 128, SEQ_ALL], BF16, "ExternalOutput")
        dv = P.dram("dv", [128, SEQT, 129], BF16, "ExternalOutput")
        dl = P.dram("dl", [128, 4], F32, "ExternalOutput")
        dsm = P.dsem()
        P.dma("sp", dq[0], QK[0][:], dsm, reads=[QK[0]])
        P.dma("sp", dq[1], QK[1][:], dsm, reads=[QK[1]])
        P.dma("sp", dv[:, :, :], V[:], dsm, reads=[V])
        P.dma("sp", dl[:, :], lam[:], dsm, reads=[lam])
    Pm = [P.sb([128, 512], BF16, f"Pm{i}") for i in range(3)]
    Sb = [banks[0], banks[1]]
    accb = [banks[2], banks[3], banks[4]]
    yo = [P.sb([128, 128], F32, f"yo{i}") for i in range(2)]
    ysq = P.sb([128, 128], F32, "ysq")
    st = P.sb([128, 8], F32, "st")
    osem = [P.dsem() for _ in range(2)]

    def acc(m, qs):
        a = m * 4 + qs
        return accb[a // 3], (a % 3) * 129

    qblocks = [(0, 256, 0, CTXT)] + [(256 + 512 * j, 512, 0, SEQT) for j in range(16)]
    n = 0
    no = 0
    for (q0, N, k0, k1) in qblocks:
        nq = N // 128
        for b in accb:
            P.memset(b[:], 0.0, [b])
        for kt in range(k0, k1):
            for m in range(2):
                S = Sb[n % 2]
                pm = Pm[n % 3]
                n += 1
                P.mm(S[:, 0:N], QK[1][m * 64:(m + 1) * 64, kt * 128:(kt + 1) * 128], QK[0][m * 64:(m + 1) * 64, q0:q0 + N],
                     True, True, [QK[0], QK[1]], [S])
                P.act(pm[:, 0:N], S[:, 0:N], AF.Exp, [S], [pm])
                for qs in range(nq):
                    ab, c0 = acc(m, qs)
                    P.mm(ab[:, c0:c0 + 129], pm[:, qs * 128:(qs + 1) * 128], V[:, kt, :], False, False, [pm, V], [ab])
        for qs in range(nq):
            a0, c0 = acc(0, qs)
            a1, c1 = acc(1, qs)
            Y = yo[no % 2]
            P.op("dve", lambda e, a0=a0, c0=c0: e.reciprocal(st[:, 0:1], a0[:, c0 + 128:c0 + 129]), [a0], [st])
            P.op("dve", lambda e, a1=a1, c1=c1: e.reciprocal(st[:, 1:2], a1[:, c1 + 128:c1 + 129]), [a1], [st])
            P.tt(st[:, 1:2], st[:, 1:2], lam[:, 3:4], ALU.mult, [st, lam], [st])
            P.ts(Y[:], a0[:, c0:c0 + 128], st[:, 0:1], None, ALU.mult, None, [a0, st], [Y])
            P.stt(Y[:], a1[:, c1:c1 + 128], st[:, 1:2], Y[:], ALU.mult, ALU.add, [a1, st, Y], [Y])
            P.act(ysq[:], Y[:], AF.Square, [Y], [ysq, st], accum_out=st[:, 2:3])
            P.ts(st[:, 3:4], st[:, 2:3], 1.0 / 128, EPS, ALU.mult, ALU.add, [st], [st])
            P.act(st[:, 4:5], st[:, 3:4], AF.Sqrt, [st], [st])
            P.op("dve", lambda e: e.reciprocal(st[:, 5:6], st[:, 4:5]), [st], [st])
            P.stt(Y[:], Y[:], st[:, 5:6], sg[:], ALU.mult, ALU.mult, [Y, st, sg], [Y])
            P.dma("sp", ya[q0 + qs * 128:q0 + (qs + 1) * 128, :], Y[:], osem[no % 2], reads=[Y])
            no += 1
    return P.finish()


def rope_tables():
    n = 8192
    rows = n // 64
    row = np.repeat(np.arange(rows, dtype=np.float32), 64)
    col = np.tile(np.arange(64, dtype=np.float32), rows)
    half = 32
    inv = (np.float32(10000.0) ** (-np.arange(0, half, 2, dtype=np.float32) / np.float32(half))).astype(np.float32)
    ang = np.concatenate([row[:, None] * inv, col[:, None] * inv], axis=-1).astype(np.float32)
    cos, sin = np.cos(ang).astype(np.float32), np.sin(ang).astype(np.float32)
    idx = (np.arange(128) % 64) // 2
    return np.ascontiguousarray(np.stack([cos[:, idx].T, sin[:, idx].T], 0))


def attn_consts():
    Bm = np.zeros((128, 128), np.float32)
    Bm[:64, :64] = 1.0 / 64
    Bm[64:, 64:] = 1.0 / 64
    Rm = np.zeros((128, 128), np.float32)
    for i in range(64):
        Rm[2 * i + 1, 2 * i] = -1.0
        Rm[2 * i, 2 * i + 1] = 1.0
    return np.stack([Bm, Rm], 0)


def lambda_init(li):
    import math
    return 0.8 - 0.6 * math.exp(-0.3 * li)


def attn_inputs(hTs, li, w_in, a_qk_norm, a_lambda, a_subln):
    off_q = 5008 - 512 - 512 - 512
    off = {}
    o = 0
    for name, wdt in IN_SPLITS:
        off[name] = o
        o += wdt
    cs = rope_tables()
    cm = attn_consts()
    ins = []
    for b in range(2):
        hT = np.ascontiguousarray(hTs[b].reshape(8, 128, SEQ_ALL))
        for h in range(4):
            wq = w_in[li][:, off['a_q'] + 128 * h: off['a_q'] + 128 * (h + 1)]
            wk = w_in[li][:, off['a_k'] + 128 * h: off['a_k'] + 128 * (h + 1)]
            wv = w_in[li][:, off['a_v'] + 128 * h: off['a_v'] + 128 * (h + 1)]
            gqk = np.stack([np.tile(a_qk_norm[li, 0], 2), np.tile(a_qk_norm[li, 1], 2)], 1).astype(np.float32)
            li_ = np.float32(lambda_init(li))
            lamp = np.concatenate([np.broadcast_to(a_lambda[li].reshape(1, 256), (128, 256)),
                                   np.full((128, 1), li_, np.float32), np.full((128, 1), np.float32(1.0) - li_, np.float32)], 1)
            ins.append({"hT": hT, "wqkv": np.ascontiguousarray(np.stack([wq, wk, wv], 0)), "gqk": np.ascontiguousarray(gqk),
                        "cs": cs, "cmat": cm, "lamp": np.ascontiguousarray(lamp.astype(np.float32)),
                        "subg": np.ascontiguousarray(np.broadcast_to(a_subln[li][None, :], (128, 128)))})
    return ins


IN_SPLITS = (
    ('m_q', 256), ('m_k', 256), ('m_v', 512), ('m_o', 512),
    ('m_if', 4), ('m_ff', 4), ('m_ib', 4), ('m_fb', 4),
    ('r_r', 512), ('r_k', 512), ('r_v', 512),
    ('r_wf', 64), ('r_wb', 64), ('r_af', 64), ('r_ab', 64), ('r_g', 128),
    ('a_q', 512), ('a_k', 512), ('a_v', 512),
    ('g_m', 1024), ('g_r', 1024), ('g_a', 1024),
)


def tri_mask(P, upper, neg=False):
    m = P.sb([128, 128], F32)
    P.memset(m[:], 0.0 if neg else 1.0, [m], eng="pool")
    pat, cm = ([[1, 128]], -1) if upper else ([[-1, 128]], 1)
    P.op("pool", lambda e: e.affine_select(m[:], m[:], pat, ALU.is_ge, -1.0e4 if neg else 0.0, base=0,
                                           channel_multiplier=cm), [m], [m])
    return m


def load_hblk_halo(P, hT, hb, t0, N, lo, hi, eng="sp"):
    a = t0 - 1 if t0 - 1 >= lo else t0
    b = t0 + N + 1 if t0 + N + 1 <= hi else t0 + N
    for kc in range(8):
        P.dma(eng, hb[:, kc, a - (t0 - 1):b - (t0 - 1)], hT[kc, :, a:b], writes=[hb])
    if a == t0:
        P.memset(hb[:, :, 0:1], 0.0, [hb], eng="pool")
    if b == t0 + N:
        P.memset(hb[:, :, N + 1:N + 2], 0.0, [hb], eng="pool")


def chunk_orders():
    f = list(range(SEQT))
    b = [1, 0] + list(range(SEQT - 1, 1, -1))
    return f, b


def build_mlstm(stop=99):
    P = Prog()
    hT = P.dram("hT", [8, 128, SEQ_ALL], BF16, "ExternalInput")
    wqk = P.dram("wqk", [2, D, 64], F32, "ExternalInput")
    wvo = P.dram("wvo", [D, 256], F32, "ExternalInput")
    wg = P.dram("wg", [D, 4], F32, "ExternalInput")
    cw = P.dram("cw", [2, 128, 3, 64], F32, "ExternalInput")
    gb = P.dram("gb", [128, 4], F32, "ExternalInput")
    og = P.dram("og", [128, 128], F32, "ExternalInput")
    ym = P.dram("ym", [SEQ_ALL, 128], F32, "ExternalOutput")

    banks = [P.ps([128, 512], F32, f"bank{i}") for i in range(8)]
    ident = make_ident(P)
    ones = P.sb([128, 128], F32, "ones")
    P.memset(ones[:], 1.0, [ones])
    one1 = P.sb([128, 1], F32, "one1")
    P.memset(one1[:], 1.0, [one1])
    triU = tri_mask(P, True)
    triL = tri_mask(P, False)
    negU = tri_mask(P, True, True)
    negL = tri_mask(P, False, True)

    wst = P.sb([128, 8, 388], F32, "wst")
    for kc in range(8):
        P.dma("sp", wst[:, kc, 0:64], wqk[0, kc * 128:(kc + 1) * 128, :], writes=[wst])
        P.dma("sp", wst[:, kc, 64:128], wqk[1, kc * 128:(kc + 1) * 128, :], writes=[wst])
        P.dma("sp", wst[:, kc, 128:384], wvo[kc * 128:(kc + 1) * 128, :], writes=[wst])
        P.dma("sp", wst[:, kc, 384:388], wg[kc * 128:(kc + 1) * 128, :], writes=[wst])
    cws = P.sb([128, 2, 3, 64], F32, "cws")
    P.dma("act", cws[:, 0], cw[0], writes=[cws])
    P.dma("act", cws[:, 1], cw[1], writes=[cws])
    gbs = P.sb([128, 4], F32, "gbs")
    P.dma("act", gbs[:], gb[:, :], writes=[gbs])
    ogs = P.sb([128, 128], F32, "ogs")
    P.dma("act", ogs[:], og[:, :], writes=[ogs])
    Wqk = P.sb([128, 2, 3, 8, 64], BF16, "Wqk")
    Wvo = P.sb([128, 8, 260], BF16, "Wvo")
    P.cp(Wvo[:], wst[:, :, 128:388], [wst], [Wvo])
    for j in range(2):
        for tap in range(3):
            for kc in range(8):
                P.tt(Wqk[:, j, tap, kc, :], wst[:, kc, j * 64:(j + 1) * 64], cws[:, j, tap, :], ALU.mult, [wst, cws], [Wqk],
                     eng="pool" if kc % 2 else "dve")

    QT = P.sb([64, SEQ_ALL], F32, "QT")
    KT = P.sb([64, SEQ_ALL], F32, "KT")
    VE = P.sb([128, SEQT, 129], F32, "VE")
    P.memset(VE[:, :, 128:129], 1.0, [VE], eng="pool")
    OG = P.sb([128, SEQT, 128], BF16, "OG")
    G = P.sb([128, SEQT, 4], F32, "G")
    hb = [P.sb([128, 8, 514], BF16, f"hb{i}") for i in range(2)]

    blocks = [(0, 256, 0, 256)] + [(256 + 512 * j, 512, 256, SEQ_ALL) for j in range(16)]
    for bi, (t0, N, lo, hi) in enumerate(blocks):
        H = hb[bi % 2]
        load_hblk_halo(P, hT, H, t0, N, lo, hi)
        for j, dst in enumerate((QT, KT)):
            pq = banks[j]
            n = 0
            for tap in range(3):
                for kc in range(8):
                    P.mm(pq[0:64, 0:N], Wqk[:, j, tap, kc, :], H[:, kc, tap:tap + N], n == 0, n == 23, [Wqk, H], [pq])
                    n += 1
            P.act(dst[:, t0:t0 + N], pq[0:64, 0:N], AF.Silu, [pq], [dst], scale=1.0)
        for i in range(N // 128):
            pv = banks[2 + i % 2]
            for kc in range(8):
                P.mm(pv[:, 0:260], H[:, kc, 1 + i * 128:1 + (i + 1) * 128], Wvo[:, kc, :], kc == 0, kc == 7, [H, Wvo], [pv])
            tl = t0 // 128 + i
            P.cp(VE[:, tl, 0:128], pv[:, 0:128], [pv], [VE])
            P.act(OG[:, tl, :], pv[:, 128:256], AF.Sigmoid, [pv], [OG])
            P.tt(G[:, tl, :], pv[:, 256:260], gbs[:], ALU.add, [pv, gbs], [G])
    if stop == 1:
        return P.finish()
    P.ts(KT[:], KT[:], 0.125, None, ALU.mult, None, [KT], [KT], eng="pool")

    ge = P.sb([128, SEQT, 4], F32, "ge")
    P.act(ge[:], G[:], AF.Exp, [G], [ge], scale=-1.0)
    P.act(ge[:], ge[:], AF.Ln, [ge, one1], [ge], bias=one1[:, 0:1])
    LF = P.sb([128, 2, SEQT], F32, "LF")
    IG = P.sb([128, 2, SEQT], F32, "IG")
    for d in range(2):
        P.ts(LF[:, d, :], ge[:, :, 2 * d + 1], -1.0, None, ALU.mult, None, [ge], [LF])
        P.cp(IG[:, d, :], G[:, :, 2 * d], [G], [IG])
    BC = P.sb([128, 2, SEQT], F32, "BC")
    BT = P.sb([128, 2, SEQT], F32, "BT")
    for d in range(2):
        pb = banks[4 + d]
        P.mm(pb[:, 0:SEQT], (triU if d == 0 else triL)[:], LF[:, d, :], True, True, [triU, triL, LF], [pb])
        P.cp(BC[:, d, :], pb[:, 0:SEQT], [pb], [BC])
        pb2 = banks[6 + d]
        P.mm(pb2[:, 0:SEQT], ones[:], LF[:, d, :], True, True, [ones, LF], [pb2])
        P.cp(BT[:, d, :], pb2[:, 0:SEQT], [pb2], [BT])
    BIAS = P.sb([128, 2, SEQT], F32, "BIAS")
    WS = P.sb([128, 2, SEQT], F32, "WS")
    EB = P.sb([128, 2, SEQT], F32, "EB")
    DEC = P.sb([128, 2, SEQT], F32, "DEC")
    P.tt(BIAS[:], IG[:], BC[:], ALU.subtract, [IG, BC], [BIAS])
    P.tt(WS[:], BIAS[:], BT[:], ALU.add, [BIAS, BT], [WS])
    P.act(WS[:], WS[:], AF.Exp, [WS], [WS])
    P.act(EB[:], BC[:], AF.Exp, [BC], [EB])
    P.act(DEC[:], BT[:], AF.Exp, [BT], [DEC])

    if stop == 2:
        return P.finish()
    HS = P.sb([128, SEQT, 128], F32, "HS")
    CT = [[P.sb([64, 129], F32, f"CT{d}{i}") for i in range(2)] for d in range(2)]
    for d in range(2):
        P.memset(CT[d][0][:], 0.0, [CT[d][0]])
    lrep = [P.sb([128, 128], F32, f"lrep{i}") for i in range(2)]
    arg = [P.sb([128, 128], F32, f"arg{i}") for i in range(2)]
    ST = [P.sb([128, 128], F32, f"ST{i}") for i in range(2)]
    KW = [P.sb([128, 64], F32, f"KW{i}") for i in range(2)]
    it = [P.sb([128, 129], F32, f"it{i}") for i in range(2)]
    tot = [P.sb([128, 129], F32, f"tot{i}") for i in range(2)]
    sm = [P.sb([128, 2], F32, f"sm{i}") for i in range(2)]
    orders = chunk_orders()
    done = set()
    for step in range(SEQT):
        for d in range(2):
            c = orders[d][step]
            cs_ = slice(c * 128, (c + 1) * 128)
            Ccur, Cnew = CT[d][step % 2], CT[d][(step + 1) % 2]
            tri, neg = (triU, negU) if d == 0 else (triL, negL)
            p_brd, p_qk, p_n, p_i, p_kt, p_st = (banks[d * 4 + 0], banks[d * 4 + 1], banks[d * 4 + 2], banks[d * 4 + 3],
                                                 banks[d * 4 + 0], banks[d * 4 + 1])
            L = lrep[d]
            P.ts(L[:], ones[:], LF[:, d, c:c + 1], None, ALU.mult, None, [ones, LF], [L], eng="pool")
            P.mm(p_brd[:, 0:128], L[:], tri[:], True, True, [L, tri], [p_brd])
            A = arg[d]
            P.tt(A[:], p_brd[:, 0:128], neg[:], ALU.add, [p_brd, neg], [A])
            P.act(A[:], A[:], AF.Exp, [A, BIAS], [A], bias=BIAS[:, d, c:c + 1])
            P.mm(p_qk[:, 0:128], KT[:, cs_], QT[:, cs_], True, True, [KT, QT], [p_qk])
            S = ST[d]
            P.tt(S[:], p_qk[:, 0:128], A[:], ALU.mult, [p_qk, A], [S])
            P.mm(p_n[:, 0:129], S[:], VE[:, c, :], True, True, [S, VE], [p_n])
            P.mm(p_i[:, 0:129], QT[:, cs_], Ccur[:], True, True, [QT, Ccur], [p_i])
            I = it[d]
            P.act(I[:], p_i[:, 0:129], AF.Identity, [p_i, EB], [I], scale=EB[:, d, c:c + 1])
            T = tot[d]
            P.tt(T[:], p_n[:, 0:129], I[:], ALU.add, [p_n, I], [T])
            s_ = sm[d]
            P.act(s_[:, 0:1], T[:, 128:129], AF.Abs, [T], [s_])
            P.ts(s_[:, 0:1], s_[:, 0:1], 1.0, None, ALU.max, None, [s_], [s_])
            P.op("dve", lambda e, s_=s_: e.reciprocal(s_[:, 1:2], s_[:, 0:1]), [s_], [s_])
            if c in done:
                P.stt(HS[:, c, :], T[:, 0:128], s_[:, 1:2], HS[:, c, :], ALU.mult, ALU.add, [T, s_, HS], [HS])
            else:
                P.ts(HS[:, c, :], T[:, 0:128], s_[:, 1:2], None, ALU.mult, None, [T, s_], [HS])
                done.add(c)
            P.tr(p_kt[:, 0:64], KT[:, cs_], ident[0:64, 0:64], [KT, ident], [p_kt])
            kw = KW[d]
            P.ts(kw[:], p_kt[:, 0:64], WS[:, d, c:c + 1], None, ALU.mult, None, [p_kt, WS], [kw])
            P.mm(p_st[0:64, 0:129], kw[:], VE[:, c, :], True, True, [kw, VE], [p_st])
            P.stt(Cnew[:], Ccur[:], DEC[0:64, d, c:c + 1], p_st[0:64, 0:129], ALU.mult, ALU.add, [Ccur, DEC, p_st], [Cnew])

    if stop == 3:
        return P.finish()
    yo = [P.sb([128, 128], F32, f"yo{i}") for i in range(2)]
    junk = P.sb([128, 128], F32, "junk")
    st = [P.sb([128, 4], F32, f"st{i}") for i in range(2)]
    for c in range(SEQT):
        Y, s_ = yo[c % 2], st[c % 2]
        P.act(junk[:], HS[:, c, :], AF.Square, [HS], [junk, s_], accum_out=s_[:, 0:1])
        P.ts(s_[:, 1:2], s_[:, 0:1], 1.0 / 128, EPS, ALU.mult, ALU.add, [s_], [s_])
        P.act(s_[:, 2:3], s_[:, 1:2], AF.Sqrt, [s_], [s_])
        P.op("dve", lambda e, s_=s_: e.reciprocal(s_[:, 3:4], s_[:, 2:3]), [s_], [s_])
        P.stt(Y[:], HS[:, c, :], s_[:, 3:4], ogs[:], ALU.mult, ALU.mult, [HS, s_, ogs], [Y])
        P.tt(Y[:], Y[:], OG[:, c, :], ALU.mult, [Y, OG], [Y])
        P.dma("sp", ym[c * 128:(c + 1) * 128, :], Y[:], reads=[Y])
    return P.finish()


def col_offsets():
    off, o = {}, 0
    for name, wdt in IN_SPLITS:
        off[name] = o
        o += wdt
    return off


def mlstm_inputs(hTs, li, w_in, m_conv, m_gate_bias, m_out_norm):
    off = col_offsets()
    ins = []
    for b in range(2):
        hT = np.ascontiguousarray(hTs[b].reshape(8, 128, SEQ_ALL))
        for h in range(4):
            W = w_in[li]
            wq = W[:, off['m_q'] + 64 * h: off['m_q'] + 64 * (h + 1)]
            wk = W[:, off['m_k'] + 64 * h: off['m_k'] + 64 * (h + 1)]
            wvo = np.concatenate([W[:, off['m_v'] + 128 * h: off['m_v'] + 128 * (h + 1)],
                                  W[:, off['m_o'] + 128 * h: off['m_o'] + 128 * (h + 1)]], 1)
            wg = np.stack([W[:, off['m_if'] + h], W[:, off['m_ff'] + h], W[:, off['m_ib'] + h], W[:, off['m_fb'] + h]], 1)
            cq = m_conv[li][:, 64 * h:64 * (h + 1)]
            ck = m_conv[li][:, 256 + 64 * h:256 + 64 * (h + 1)]
            cw = np.stack([np.broadcast_to(cq[None], (128, 3, 64)), np.broadcast_to(ck[None], (128, 3, 64))], 0)
            gbv = m_gate_bias[li][:, h]
            ins.append({"hT": hT, "wqk": np.ascontiguousarray(np.stack([wq, wk], 0)), "wvo": np.ascontiguousarray(wvo),
                        "wg": np.ascontiguousarray(wg), "cw": np.ascontiguousarray(cw.astype(np.float32)),
                        "gb": np.ascontiguousarray(np.broadcast_to(gbv[None, :], (128, 4)).astype(np.float32)),
                        "og": np.ascontiguousarray(np.broadcast_to(m_out_norm[li][None, 128 * h:128 * (h + 1)], (128, 128)))})
    return ins


R_GN_EPS = 64e-5
W_SCALE = -0.6065306597126334


def aff_mask(P, pat, cm, op, val=1.0, base=0):
    m = P.sb([128, 128], F32)
    P.memset(m[:], val, [m], eng="pool")
    P.op("pool", lambda e: e.affine_select(m[:], m[:], pat, op, 0.0, base=base, channel_multiplier=cm), [m], [m])
    return m


def build_rwkv():
    P = Prog()
    hT = P.dram("hT", [8, 128, SEQ_ALL], BF16, "ExternalInput")
    wrkv = P.dram("wrkv", [D, 384], F32, "ExternalInput")
    crkv = P.dram("crkv", [128, 3, 384], F32, "ExternalInput")
    wl = P.dram("wl", [D, 384], F32, "ExternalInput")
    w2a2 = P.dram("w2a2", [2, 128, 128], F32, "ExternalInput")
    bias01 = P.dram("bias01", [1, 2, 256], F32, "ExternalInput")
    g2 = P.dram("g2", [128, 128], F32, "ExternalInput")
    vecs = P.dram("vecs", [128, 5, 128], F32, "ExternalInput")
    yr = P.dram("yr", [SEQ_ALL, 128], F32, "ExternalOutput")

    bk = [P.ps([128, 512], F32, f"bank{i}") for i in range(8)]
    ident = make_ident(P)
    ones = P.sb([128, 128], F32, "ones")
    P.memset(ones[:], 1.0, [ones])
    mI = [aff_mask(P, [[1, 128]], -1, ALU.is_ge), aff_mask(P, [[-1, 128]], 1, ALU.is_ge)]
    mS = [aff_mask(P, [[1, 128]], -1, ALU.is_gt), aff_mask(P, [[-1, 128]], 1, ALU.is_gt)]
    cI = [aff_mask(P, [[1, 128]], -1, ALU.is_ge, W_SCALE), aff_mask(P, [[-1, 128]], 1, ALU.is_ge, W_SCALE)]
    cS = [aff_mask(P, [[1, 128]], -1, ALU.is_gt, W_SCALE), aff_mask(P, [[-1, 128]], 1, ALU.is_gt, W_SCALE)]
    mSI = []
    for d in range(2):
        m = P.sb([128, 256], F32)
        P.cp(m[:, 0:128], mS[d][:], [mS[d]], [m])
        P.cp(m[:, 128:256], mI[d][:], [mI[d]], [m])
        mSI.append(m)

    wst = P.sb([128, 8, 768], F32, "wst")
    for kc in range(8):
        P.dma("sp", wst[:, kc, 0:384], wrkv[kc * 128:(kc + 1) * 128, :], writes=[wst])
        P.dma("act", wst[:, kc, 384:768], wl[kc * 128:(kc + 1) * 128, :], writes=[wst])
    cws = P.sb([128, 3, 384], F32, "cws")
    P.dma("sp", cws[:], crkv[:, :, :], writes=[cws])
    Wc = P.sb([128, 3, 8, 384], BF16, "Wc")
    for tap in range(3):
        for kc in range(8):
            P.tt(Wc[:, tap, kc, :], wst[:, kc, 0:384], cws[:, tap, :], ALU.mult, [wst, cws], [Wc])
    Wl = P.sb([128, 8, 384], BF16, "Wl")
    P.cp(Wl[:], wst[:, :, 384:768], [wst], [Wl])
    W2 = P.sb([128, 2, 128], F32, "W2")
    for d in range(2):
        P.dma("act", W2[:, d, :], w2a2[d], writes=[W2])
    B01 = P.sb([1, 2, 256], F32, "B01")
    P.dma("act", B01[:], bias01[:, :, :], writes=[B01])
    G2 = P.sb([128, 128], F32, "G2")
    P.dma("act", G2[:], g2[:, :], writes=[G2])
    VEC = P.sb([128, 5, 128], F32, "VEC")
    P.dma("act", VEC[:], vecs[:, :, :], writes=[VEC])
    epsg = P.sb([128, 1], F32, "epsg")
    P.memset(epsg[:], R_GN_EPS, [epsg])

    YF = P.sb([128, SEQT, 128], F32, "YF")
    hb = [P.sb([128, 8, 130], BF16, f"hb{i}") for i in range(2)]
    STB = P.sb([128, 128], F32, "STB")

    def T(shape, name):
        return P.sb(shape, F32, name)

    rkv = T([128, 384], "rkv")
    pl = T([128, 128], "pl")
    sgg = T([128, 128], "sgg")
    sig = T([128, 128], "sig")
    av = T([128, 128], "av")
    t0_ = T([128, 128], "t0_")
    ss = T([128, 8], "ss")
    kkn = T([128, 128], "kkn")
    bh = T([128, 128], "bh")
    t2 = T([128, 128], "t2")
    key = T([128, 128], "key")
    eG = T([128, 256], "eG")
    enG = T([128, 128], "enG")
    eR = T([128, 128], "eR")
    AR = T([128, 256], "AR")
    BtT = T([128, 128], "BtT")
    KtT = T([128, 128], "KtT")
    Bb = T([128, 128], "Bb")
    Kb = T([128, 128], "Kb")
    Mm = [T([128, 256], f"Mm{i}") for i in range(2)]
    Ak = [T([128, 256], f"Ak{i}") for i in range(2)]
    Lm = [T([128, 128], f"Lm{i}") for i in range(2)]
    TtA = [T([128, 128], f"TtA{i}") for i in range(2)]
    TA = [T([128, 128], f"TA{i}") for i in range(2)]
    Pp = [[T([128, 128], f"Pp{i}{j}") for j in range(2)] for i in range(2)]
    Qq = [[T([128, 128], f"Qq{i}{j}") for j in range(2)] for i in range(2)]
    X = T([128, 128], "X")
    U = T([128, 128], "U")
    yb = T([128, 128], "yb")
    gn = T([128, 16], "gn")
    yn = T([128, 128], "yn")
    rk = T([128, 128], "rk")
    yo = [T([128, 128], f"yo{i}") for i in range(2)]

    orders = chunk_orders()
    for d in range(2):
        P.memset(STB[:], 0.0, [STB])
        for step in range(SEQT):
            c = orders[d][step]
            t0 = c * 128
            lo, hi = (0, NCTX) if c < CTXT else (NCTX, SEQ_ALL)
            H = hb[step % 2]
            load_hblk_halo(P, hT, H, t0, 128, lo, hi)
            n = 0
            for tap in range(3):
                for kc in range(8):
                    P.mm(bk[0][:, 0:384], H[:, kc, tap:tap + 128], Wc[:, tap, kc, :], n == 0, n == 23, [H, Wc], [bk[0]])
                    n += 1
            P.cp(rkv[:], bk[0][:, 0:384], [bk[0]], [rkv], eng="act")
            for kc in range(8):
                P.mm(bk[1][:, 0:128], Wl[:, kc, d * 128:(d + 1) * 128], H[:, kc, 1:129], kc == 0, kc == 7, [Wl, H], [bk[1]])
            if d == 1:
                for kc in range(8):
                    P.mm(bk[1][:, 128:256], Wl[:, kc, 256:384], H[:, kc, 1:129], kc == 0, kc == 7, [Wl, H], [bk[1]])
            P.act(pl[0:64, :], bk[1][0:64, 0:128], AF.Tanh, [bk[1]], [pl])
            P.cp(pl[64:128, :], bk[1][64:128, 0:128], [bk[1]], [pl])
            if d == 1:
                P.act(sgg[:], bk[1][:, 128:256], AF.Sigmoid, [bk[1]], [sgg])
            P.mm(bk[1][:, 256:384], pl[0:64, :], W2[0:64, d, :], True, False, [pl, W2], [bk[1]])
            P.mm(bk[1][:, 256:384], ones[0:1, :], B01[0:1, d, 0:128], False, True, [ones, B01], [bk[1]])
            P.mm(bk[1][:, 384:512], pl[64:128, :], W2[64:128, d, :], True, False, [pl, W2], [bk[1]])
            P.mm(bk[1][:, 384:512], ones[0:1, :], B01[0:1, d, 128:256], False, True, [ones, B01], [bk[1]])
            P.act(sig[:], bk[1][:, 256:384], AF.Sigmoid, [bk[1]], [sig])
            P.act(av[:], bk[1][:, 384:512], AF.Sigmoid, [bk[1]], [av])
            r_, k_, v_ = rkv[:, 0:128], rkv[:, 128:256], rkv[:, 256:384]
            P.tt(t0_[:], k_, VEC[:, 0, :], ALU.mult, [rkv, VEC], [t0_])
            P.tt(t2[:], t0_[:], t0_[:], ALU.mult, [t0_], [t2])
            for hh in range(2):
                P.op("dve", lambda e, hh=hh: e.tensor_reduce(ss[:, hh:hh + 1], t2[:, hh * 64:(hh + 1) * 64], AX.X, ALU.add),
                     [t2], [ss])
            P.act(ss[:, 2:4], ss[:, 0:2], AF.Sqrt, [ss], [ss])
            P.ts(ss[:, 2:4], ss[:, 2:4], 1e-12, None, ALU.max, None, [ss], [ss])
            P.op("dve", lambda e: e.reciprocal(ss[:, 4:6], ss[:, 2:4]), [ss], [ss])
            for hh in range(2):
                hs = slice(hh * 64, (hh + 1) * 64)
                P.ts(kkn[:, hs], t0_[:, hs], ss[:, 4 + hh:5 + hh], -1.0, ALU.mult, ALU.mult, [t0_, ss], [kkn])
            P.stt(bh[:], kkn[:], -1.0, av[:], ALU.mult, ALU.mult, [kkn, av], [bh])
            P.stt(t2[:], av[:], -1.0, VEC[:, 1, :], ALU.add, ALU.mult, [av, VEC], [t2])
            P.stt(key[:], t2[:], 1.0, k_, ALU.add, ALU.mult, [t2, rkv], [key])
            P.tr(bk[2][:, 0:128], r_, ident[:], [rkv, ident], [bk[2]])
            P.tr(bk[2][:, 128:256], kkn[:], ident[:], [kkn, ident], [bk[2]])
            P.tr(bk[2][:, 256:384], bh[:], ident[:], [bh, ident], [bk[2]])
            P.tr(bk[2][:, 384:512], key[:], ident[:], [key, ident], [bk[2]])
            P.mm(bk[3][:, 0:128], sig[:], cS[d][:], True, True, [sig, cS[d]], [bk[3]])
            P.mm(bk[3][:, 128:256], sig[:], cI[d][:], True, True, [sig, cI[d]], [bk[3]])
            P.mm(bk[3][:, 256:384], cS[1 - d][:], sig[:], True, True, [sig, cS[1 - d]], [bk[3]])
            P.act(eG[:], bk[3][:, 0:256], AF.Exp, [bk[3]], [eG])
            P.act(enG[:], bk[3][:, 128:256], AF.Exp, [bk[3]], [enG], scale=-1.0)
            P.act(eR[:], bk[3][:, 256:384], AF.Exp, [bk[3]], [eR])
            P.tt(AR[:, 0:128], bk[2][:, 128:256], eG[:, 0:128], ALU.mult, [bk[2], eG], [AR])
            P.tt(AR[:, 128:256], bk[2][:, 0:128], eG[:, 128:256], ALU.mult, [bk[2], eG], [AR])
            P.tt(BtT[:], bk[2][:, 256:384], enG[:], ALU.mult, [bk[2], enG], [BtT])
            P.tt(KtT[:], bk[2][:, 384:512], enG[:], ALU.mult, [bk[2], enG], [KtT])
            P.tt(Bb[:], bh[:], eR[:], ALU.mult, [bh, eR], [Bb])
            P.tt(Kb[:], key[:], eR[:], ALU.mult, [key, eR], [Kb])
            for hh in range(2):
                hp_ = slice(hh * 64, (hh + 1) * 64)
                P.mm(bk[4][:, 0:256], BtT[hp_, :], AR[hp_, :], True, True, [BtT, AR], [bk[4]])
                P.mm(bk[5][:, 0:256], KtT[hp_, :], AR[hp_, :], True, True, [KtT, AR], [bk[5]])
                P.mm(bk[4][:, 256:384], AR[hp_, 0:128], BtT[hp_, :], True, True, [AR, BtT], [bk[4]])
                P.tt(Mm[hh][:], bk[4][:, 0:256], mSI[d][:], ALU.mult, [bk[4], mSI[d]], [Mm[hh]])
                P.tt(Ak[hh][:], bk[5][:, 0:256], mSI[d][:], ALU.mult, [bk[5], mSI[d]], [Ak[hh]])
                P.tt(Lm[hh][:], bk[4][:, 256:384], mS[1 - d][:], ALU.mult, [bk[4], mS[1 - d]], [Lm[hh]])
                P.tt(TtA[hh][:], Mm[hh][:, 0:128], ident[:], ALU.add, [Mm[hh], ident], [TtA[hh]], eng="pool")
                P.tt(TA[hh][:], Lm[hh][:], ident[:], ALU.add, [Lm[hh], ident], [TA[hh]], eng="pool")
                Pc, Qc = Mm[hh], Lm[hh]
                pc_ap, qc_ap = Mm[hh][:, 0:128], Lm[hh][:]
                for lvl in range(6):
                    last = lvl == 5
                    Pn, Qn = Pp[hh][lvl % 2], Qq[hh][lvl % 2]
                    P.mm(bk[6][:, 0:128], qc_ap, pc_ap, True, True, [Pc, Qc], [bk[6]])
                    if not last:
                        P.mm(bk[6][:, 128:256], pc_ap, qc_ap, True, True, [Pc, Qc], [bk[6]])
                    P.cp(Pn[:], bk[6][:, 0:128], [bk[6]], [Pn])
                    if not last:
                        P.cp(Qn[:], bk[6][:, 128:256], [bk[6]], [Qn], eng="act")
                    P.mm(bk[6][:, 256:384], TA[hh][:], Pn[:], True, True, [TA[hh], Pn], [bk[6]])
                    if not last:
                        P.mm(bk[6][:, 384:512], Pn[:], TA[hh][:], True, True, [TA[hh], Pn], [bk[6]])
                    P.tt(TtA[hh][:], TtA[hh][:], bk[6][:, 256:384], ALU.add, [TtA[hh], bk[6]], [TtA[hh]])
                    if not last:
                        P.tt(TA[hh][:], TA[hh][:], bk[6][:, 384:512], ALU.add, [TA[hh], bk[6]], [TA[hh]])
                    Pc, Qc = Pn, Qn
                    pc_ap, qc_ap = Pn[:], Qn[:]
            P.mm(bk[7][:, 0:128], AR[:, 0:128], STB[:], True, False, [AR, STB], [bk[7]])
            for hh in range(2):
                hs = slice(hh * 64, (hh + 1) * 64)
                P.mm(bk[7][:, hs], Ak[hh][:, 0:128], rkv[:, 256 + hh * 64:256 + (hh + 1) * 64], False, hh == 1,
                     [Ak[hh], rkv], [bk[7]])
            P.cp(X[:], bk[7][:, 0:128], [bk[7]], [X])
            for hh in range(2):
                hs = slice(hh * 64, (hh + 1) * 64)
                P.mm(bk[7][:, 128 + hh * 64:128 + (hh + 1) * 64], TtA[hh][:], X[:, hs], True, True, [TtA[hh], X], [bk[7]])
            P.cp(U[:], bk[7][:, 128:256], [bk[7]], [U])
            P.mm(bk[7][:, 256:384], AR[:, 128:256], STB[:], True, False, [AR, STB], [bk[7]])
            for hh in range(2):
                ys = slice(256 + hh * 64, 256 + (hh + 1) * 64)
                hs = slice(hh * 64, (hh + 1) * 64)
                P.mm(bk[7][:, ys], Mm[hh][:, 128:256], U[:, hs], False, False, [Mm[hh], U], [bk[7]])
                P.mm(bk[7][:, ys], Ak[hh][:, 128:256], rkv[:, 256 + hh * 64:256 + (hh + 1) * 64], False, hh == 1,
                     [Ak[hh], rkv], [bk[7]])
            P.mm(bk[7][:, 384:512], Bb[:], U[:], True, False, [Bb, U], [bk[7]])
            P.mm(bk[7][:, 384:512], Kb[:], v_, False, True, [Kb, rkv], [bk[7]])
            dcol = 255 if d == 0 else 128
            if d == 0:
                P.cp(YF[:, c, :], bk[7][:, 256:384], [bk[7]], [YF], eng="act")
            else:
                P.tt(yb[:], bk[7][:, 256:384], YF[:, c, :], ALU.add, [bk[7], YF], [yb])
            for hh in range(2):
                hs = slice(hh * 64, (hh + 1) * 64)
                P.stt(STB[hs, hs], STB[hs, hs], eG[hs, dcol:dcol + 1], bk[7][hs, 384 + hh * 64:384 + (hh + 1) * 64],
                      ALU.mult, ALU.add, [STB, eG, bk[7]], [STB])
            if d == 1:
                P.mm(bk[0][:, 384:512], sgg[:], G2[:], True, True, [sgg, G2], [bk[0]])
                P.tt(t2[:], yb[:], yb[:], ALU.mult, [yb], [t2])
                for hh in range(2):
                    hs = slice(hh * 64, (hh + 1) * 64)
                    P.op("dve", lambda e, hh=hh, hs=hs: e.tensor_reduce(gn[:, hh:hh + 1], yb[:, hs], AX.X, ALU.add), [yb], [gn])
                    P.op("dve", lambda e, hh=hh, hs=hs: e.tensor_reduce(gn[:, 2 + hh:3 + hh], t2[:, hs], AX.X, ALU.add), [t2], [gn])
                P.ts(gn[:, 4:8], gn[:, 0:4], 1.0 / 64, None, ALU.mult, None, [gn], [gn])
                P.tt(gn[:, 8:10], gn[:, 4:6], gn[:, 4:6], ALU.mult, [gn], [gn])
                P.tt(gn[:, 10:12], gn[:, 6:8], gn[:, 8:10], ALU.subtract, [gn], [gn])
                P.act(gn[:, 12:14], gn[:, 10:12], AF.Sqrt, [gn, epsg], [gn], bias=epsg[:, 0:1])
                P.op("dve", lambda e: e.reciprocal(gn[:, 14:16], gn[:, 12:14]), [gn], [gn])
                for hh in range(2):
                    hs = slice(hh * 64, (hh + 1) * 64)
                    P.ts(yn[:, hs], yb[:, hs], gn[:, 4 + hh:5 + hh], gn[:, 14 + hh:15 + hh], ALU.subtract, ALU.mult,
                         [yb, gn], [yn])
                P.tt(yn[:], yn[:], VEC[:, 3, :], ALU.mult, [yn, VEC], [yn])
                P.tt(yn[:], yn[:], VEC[:, 4, :], ALU.add, [yn, VEC], [yn])
                P.tt(rk[:], r_, k_, ALU.mult, [rkv], [rk])
                P.tt(rk[:], rk[:], VEC[:, 2, :], ALU.mult, [rk, VEC], [rk])
                for hh in range(2):
                    hs = slice(hh * 64, (hh + 1) * 64)
                    P.op("dve", lambda e, hh=hh, hs=hs: e.tensor_reduce(ss[:, 6 + hh:7 + hh], rk[:, hs], AX.X, ALU.add), [rk], [ss])
                    P.stt(yn[:, hs], rkv[:, 256 + hh * 64:256 + (hh + 1) * 64], ss[:, 6 + hh:7 + hh], yn[:, hs], ALU.mult, ALU.add,
                          [rkv, ss, yn], [yn])
                Y = yo[step % 2]
                P.tt(Y[:], yn[:], bk[0][:, 384:512], ALU.mult, [yn, bk[0]], [Y])
                P.dma("sp", yr[t0:t0 + 128, :], Y[:], reads=[Y])
    return P.finish()


def rwkv_inputs(hTs, li, w_in, r_conv, r_w0, r_w2, r_a0, r_a2, r_g2, r_kk, r_ka, r_rk, r_ln_w, r_ln_b):
    off = col_offsets()
    ins = []
    W = w_in[li]
    for b in range(2):
        hT = np.ascontiguousarray(hTs[b].reshape(8, 128, SEQ_ALL))
        for hp in range(4):
            cs_ = slice(128 * hp, 128 * (hp + 1))
            wrkv = np.concatenate([W[:, off[n] + 128 * hp: off[n] + 128 * (hp + 1)] for n in ('r_r', 'r_k', 'r_v')], 1)
            conv = np.concatenate([r_conv[li][:, j * 512 + 128 * hp: j * 512 + 128 * (hp + 1)] for j in range(3)], 1)
            wl = np.concatenate([W[:, off['r_wf']:off['r_wf'] + 64], W[:, off['r_af']:off['r_af'] + 64],
                                 W[:, off['r_wb']:off['r_wb'] + 64], W[:, off['r_ab']:off['r_ab'] + 64],
                                 W[:, off['r_g']:off['r_g'] + 128]], 1)
            w2a2 = np.stack([np.concatenate([r_w2[li, d][:, cs_], r_a2[li, d][:, cs_]], 0) for d in range(2)], 0)
            bias01 = np.stack([np.concatenate([r_w0[li, d][cs_], r_a0[li, d][cs_]], 0) for d in range(2)], 0)[None]
            vecs = np.stack([np.broadcast_to(v[li][None, cs_], (128, 128)) for v in (r_kk, r_ka, r_rk, r_ln_w, r_ln_b)], 1)
            ins.append({"hT": hT, "wrkv": np.ascontiguousarray(wrkv),
                        "crkv": np.ascontiguousarray(np.broadcast_to(conv[None], (128, 3, 384)).astype(np.float32)),
                        "wl": np.ascontiguousarray(wl), "w2a2": np.ascontiguousarray(w2a2.astype(np.float32)),
                        "bias01": np.ascontiguousarray(bias01.astype(np.float32)),
                        "g2": np.ascontiguousarray(r_g2[li][:, cs_]), "vecs": np.ascontiguousarray(vecs.astype(np.float32))})
    return ins


def build_merge():
    P = Prog()
    x = P.dram("x", [NT, 128, D], F32, "ExternalInput")
    hT = P.dram("hT", [8, 128, TOK], BF16, "ExternalInput")
    yT = P.dram("yT", [12, 128, TOK], F32, "ExternalInput")
    wg = P.dram("wg", [D, 3 * D], F32, "ExternalInput")
    wb = P.dram("wb", [3, 512, D], F32, "ExternalInput")
    wo = P.dram("wo", [D, D], F32, "ExternalInput")
    gateb = P.dram("gateb", [2, 128, D], F32, "ExternalInput")
    xo = P.dram("xo", [NT, 128, D], F32, "ExternalOutput")

    Wg = P.sb([128, 8, 3 * D], BF16, "Wg")
    Pb = P.sb([128, 12, D], BF16, "Pb")
    Wo = P.sb([128, 8, D], BF16, "Wo")
    stage = [P.sb([128, 1024], F32, f"stg{i}") for i in range(2)]
    n = 0
    for kc in range(8):
        for q in range(3):
            load_cast(P, Wg, Wg[:, kc, q * 1024:(q + 1) * 1024], wg[kc * 128:(kc + 1) * 128, q * 1024:(q + 1) * 1024],
                      stage, None, n, 1024)
            n += 1
    for br in range(3):
        for c in range(4):
            load_cast(P, Pb, Pb[:, br * 4 + c, :], wb[br, c * 128:(c + 1) * 128, :], stage, None, n, 1024)
            n += 1
    for kc in range(8):
        load_cast(P, Wo, Wo[:, kc, :], wo[kc * 128:(kc + 1) * 128, :], stage, None, n, 1024)
        n += 1
    gates = P.sb([128, 2, D], F32, "gates")
    for s in range(2):
        P.dma("act", gates[:, s, :], gateb[s], writes=[gates])

    X = P.sb([128, 3, D], F32, "X")
    hb = P.sb([128, 8, 384], BF16, "hb")
    yf = P.sb([128, 12, 384], F32, "yf")
    yb = P.sb([128, 12, 384], BF16, "yb")
    zT = P.sb([128, 8, 384], BF16, "zT")
    sg = [P.sb([128, 384], F32, f"sg{i}") for i in range(2)]
    za = P.sb([128, 384], F32, "za")
    tm = [P.sb([128, 384], F32, f"tm{i}") for i in range(2)]
    tmp = [P.sb([128, 512], F32, f"tmp{i}") for i in range(2)]
    pg = [P.ps([128, 512], F32, f"pg{i}") for i in range(2)]
    pp = [P.ps([128, 512], F32, f"pp{i}") for i in range(2)]
    py = [P.ps([128, 512], F32, f"py{i}") for i in range(2)]

    k = 0
    for bi, (t0, nb) in enumerate(BLOCKS):
        s = 0 if bi == 0 else 1
        N = nb * 128
        for i in range(nb):
            P.dma("sp", X[:, i, :], x[t0 + i], writes=[X])
        for kc in range(8):
            P.dma("act", hb[:, kc, 0:N], hT[kc, :, t0 * 128:t0 * 128 + N], writes=[hb])
        for c in range(12):
            P.dma("sp" if c % 2 else "act", yf[:, c, 0:N], yT[c, :, t0 * 128:t0 * 128 + N], writes=[yf])
        P.cp(yb[:, :, 0:N], yf[:, :, 0:N], [yf], [yb])
        for dc in range(8):
            for br in range(3):
                G, Q = pg[k % 2], pp[k % 2]
                S, T_ = sg[k % 2], tm[k % 2]
                k += 1
                for kc in range(8):
                    P.mm(G[:, 0:N], Wg[:, kc, br * D + dc * 128: br * D + (dc + 1) * 128], hb[:, kc, 0:N], kc == 0, kc == 7,
                         [Wg, hb], [G])
                for c in range(4):
                    P.mm(Q[:, 0:N], Pb[:, br * 4 + c, dc * 128:(dc + 1) * 128], yb[:, br * 4 + c, 0:N], c == 0, c == 3,
                         [Pb, yb], [Q])
                P.act(S[:, 0:N], G[:, 0:N], AF.Sigmoid, [G], [S])
                if br == 0:
                    P.tt(za[:, 0:N], S[:, 0:N], Q[:, 0:N], ALU.mult, [S, Q], [za])
                else:
                    P.tt(T_[:, 0:N], S[:, 0:N], Q[:, 0:N], ALU.mult, [S, Q], [T_])
                    if br == 1:
                        P.tt(za[:, 0:N], za[:, 0:N], T_[:, 0:N], ALU.add, [za, T_], [za], eng="pool")
                    else:
                        P.tt(zT[:, dc, 0:N], za[:, 0:N], T_[:, 0:N], ALU.add, [za, T_], [zT])
        for i in range(nb):
            for h in range(2):
                Y = py[(i * 2 + h) % 2]
                for dc in range(8):
                    P.mm(Y[:, :], zT[:, dc, i * 128:(i + 1) * 128], Wo[:, dc, h * 512:(h + 1) * 512], dc == 0, dc == 7,
                         [zT, Wo], [Y])
                T2 = tmp[(i * 2 + h) % 2]
                P.tt(T2[:], Y[:], gates[:, s, h * 512:(h + 1) * 512], ALU.mult, [Y, gates], [T2])
                P.tt(X[:, i, h * 512:(h + 1) * 512], X[:, i, h * 512:(h + 1) * 512], T2[:], ALU.add, [X, T2], [X], eng="pool")
            P.dma("sp", xo[t0 + i], X[:, i, :], reads=[X])
    return P.finish()


def featT_shard(y_b, nchunk):
    pad = np.zeros((4 * TOK, y_b.shape[1]), y_b.dtype)
    pad[:y_b.shape[0]] = y_b
    out = []
    for i in range(4):
        blk = pad[i * TOK:(i + 1) * TOK]
        out.append(np.ascontiguousarray(blk.T.reshape(nchunk, 128, TOK)))
    return out


def merge_inputs(xs, hTs, yms, yrs, yas, mods_l, li, w_in, w_branch, w_o):
    off = col_offsets()
    m = mods_l.reshape(3, 9, D)
    wg = np.ascontiguousarray(w_in[li][:, off['g_m']:off['g_m'] + 3 * D])
    ins = []
    for b in range(2):
        xsh = tok_shard(xs[b])
        hsh = featT_shard(np.ascontiguousarray(hTs[b].T), 8)
        ysh = featT_shard(np.concatenate([yms[b], yrs[b], yas[b]], axis=1), 12)
        for i in range(4):
            gb = np.stack([np.broadcast_to(m[2 if i == 0 else b, 5], (128, D)), np.broadcast_to(m[b, 5], (128, D))], 0)
            ins.append({"x": xsh[i], "hT": hsh[i], "yT": ysh[i], "wg": wg, "wb": w_branch[li], "wo": w_o[li],
                        "gateb": np.ascontiguousarray(gb)})
    return ins


_PROGS = {}


def _prog(name, fn):
    if name not in _PROGS:
        _PROGS[name] = fn()
    return _PROGS[name]


def _run(nc, ins):
    return run_bass_kernel_spmd(nc, ins, core_ids=list(range(8))).results


def kernel(x, c, ctx, c_ctx, w_ada, b_ada, norm_g, ffn1_w_gu, ffn1_w_down, ffn2_w_gu, ffn2_w_down,
           w_in, m_conv, m_gate_bias, m_out_norm, r_conv, r_w0, r_w2, r_a0, r_a2, r_g2, r_kk, r_ka,
           r_rk, r_ln_w, r_ln_b, a_qk_norm, a_lambda, a_subln, w_branch, w_o):
    f = lambda a: np.asarray(a, dtype=np.float32)
    (x, c, ctx, c_ctx, w_ada, b_ada, norm_g, ffn1_w_gu, ffn1_w_down, ffn2_w_gu, ffn2_w_down, w_in, m_conv, m_gate_bias,
     m_out_norm, r_conv, r_w0, r_w2, r_a0, r_a2, r_g2, r_kk, r_ka, r_rk, r_ln_w, r_ln_b, a_qk_norm, a_lambda, a_subln,
     w_branch, w_o) = map(f, (x, c, ctx, c_ctx, w_ada, b_ada, norm_g, ffn1_w_gu, ffn1_w_down, ffn2_w_gu, ffn2_w_down, w_in,
                              m_conv, m_gate_bias, m_out_norm, r_conv, r_w0, r_w2, r_a0, r_a2, r_g2, r_kk, r_ka, r_rk,
                              r_ln_w, r_ln_b, a_qk_norm, a_lambda, a_subln, w_branch, w_o))
    mods = run_mod(c, c_ctx, w_ada, b_ada)
    xs = [np.concatenate([ctx[b], x[b]], 0) for b in range(2)]
    for li in range(2):
        res = _run(_prog("ffn_h", lambda: build_ffn(True)),
                   ffn_inputs(xs, mods[li], li, 1, norm_g, np.ascontiguousarray(ffn1_w_gu[li]), np.ascontiguousarray(ffn1_w_down[li]), True))
        xs = tok_unshard(res, "xo")
        hTs = hT_unshard(res, "ho")
        ra = _run(_prog("attn", build_attn), attn_inputs(hTs, li, w_in, a_qk_norm, a_lambda, a_subln))
        rm = _run(_prog("mlstm", build_mlstm), mlstm_inputs(hTs, li, w_in, m_conv, m_gate_bias, m_out_norm))
        rr = _run(_prog("rwkv", build_rwkv), rwkv_inputs(hTs, li, w_in, r_conv, r_w0, r_w2, r_a0, r_a2, r_g2, r_kk, r_ka,
                                                          r_rk, r_ln_w, r_ln_b))
        yas = [np.concatenate([ra[b * 4 + h]["ya"] for h in range(4)], axis=1) for b in range(2)]
        yms = [np.concatenate([rm[b * 4 + h]["ym"] for h in range(4)], axis=1) for b in range(2)]
        yrs = [np.concatenate([rr[b * 4 + h]["yr"] for h in range(4)], axis=1) for b in range(2)]
        res = _run(_prog("merge", build_merge), merge_inputs(xs, hTs, yms, yrs, yas, mods[li], li, w_in, w_branch, w_o))
        xs = tok_unshard(res, "xo")
        res = _run(_prog("ffn", lambda: build_ffn(False)),
                   ffn_inputs(xs, mods[li], li, 2, norm_g, np.ascontiguousarray(ffn2_w_gu[li]), np.ascontiguousarray(ffn2_w_down[li]), False))
        xs = tok_unshard(res, "xo")
    return np.stack([xs[b][NCTX:] for b in range(2)], 0).astype(np.float32)
```

```python
import contextlib
import numpy as np
import ml_dtypes
import concourse.bass as bass
import concourse.mybir as mybir
from concourse.bass_utils import run_bass_kernel_spmd

F32 = mybir.dt.float32
BF16 = mybir.dt.bfloat16
AF = mybir.ActivationFunctionType
ALU = mybir.AluOpType
AX = mybir.AxisListType
NPBF = ml_dtypes.bfloat16

ENGS = ("pe", "dve", "act", "pool", "sp")


class Buf:
    __slots__ = ("t", "w", "r", "name", "ds", "psum", "gath")

    def __init__(self, t=None, name=""):
        self.t = t
        self.ds = None
        self.psum = False
        self.gath = False
        self.w = None
        self.r = []
        self.name = name

    def __getitem__(self, idx):
        return self.t[idx]


class Sem:
    __slots__ = ("h", "count")

    def __init__(self, h):
        self.h = h
        self.count = 0


class Prog:
    def __init__(self, name="k"):
        self.nc = bass.Bass("TRN2", target_bir_lowering=False)
        self.es = contextlib.ExitStack()
        self.ss = contextlib.ExitStack()
        self.q = {e: [] for e in ENGS}
        self.esem = {}
        for e in ENGS:
            self.esem[e] = Sem(self.es.enter_context(self.nc.semaphore(f"s_{e}")))
        self.seen = {e: {} for e in ENGS}
        self.dsems = []
        self.free_ds = []
        self.stage_ds = []
        self.nbuf = 0

    def dram(self, name, shape, dt, kind):
        return Buf(self.nc.dram_tensor(name, list(shape), dt, kind=kind).ap(), name)

    def io(self, T, name, shape, dt, kind):
        if T is not None:
            return T[name]
        return self.dram(name, shape, dt, kind)

    def sb(self, shape, dt=F32, name=None):
        self.nbuf += 1
        name = f"{name or 'sb'}_{self.nbuf}"
        t = self.ss.enter_context(self.nc.sbuf_tensor(name, list(shape), dt))
        return Buf(t, name)

    def ps(self, shape, dt=F32, name=None):
        self.nbuf += 1
        name = f"{name or 'ps'}_{self.nbuf}"
        t = self.ss.enter_context(self.nc.psum_tensor(name, list(shape), dt))
        b = Buf(t, name)
        b.psum = True
        return b

    def dsem(self):
        if self.free_ds:
            s = self.free_ds.pop()
        else:
            s = Sem(self.es.enter_context(self.nc.semaphore(f"d{len(self.dsems)}")))
            self.dsems.append(s)
        self.stage_ds.append(s)
        return s

    def _deps(self, eng, reads, writes):
        deps = []
        for b in reads:
            if b.w is not None:
                deps.append(b.w)
            if b.psum:
                deps.extend(ev for ev in b.r if ev[2] != eng)
        for b in writes:
            if b.w is not None:
                deps.append(b.w)
            deps.extend(b.r)
        best = {}
        for (s, v, src) in deps:
            if src == "pe" and eng == "pe":
                continue
            if v > best.get(s, (0, None))[0]:
                best[s] = (v, src)
        for s, (v, src) in best.items():
            if self.seen[eng].get(s, 0) >= v:
                continue
            self.seen[eng][s] = v
            self.q[eng].append(("w", s, v))

    def op(self, eng, fn, reads=(), writes=()):
        self._deps(eng, reads, writes)
        s = self.esem[eng]
        s.count += 1
        ev = (s, s.count, eng)
        self.q[eng].append(("i", fn, s, 1))
        for b in reads:
            b.r.append(ev)
        for b in writes:
            b.w = ev
            b.r = []
        return ev

    def dma(self, eng, out, in_, sem=None, reads=(), writes=(), **kw):
        b0 = (list(writes) + list(reads))[0]
        if b0.ds is None:
            b0.ds = self.dsem()
        sem = b0.ds
        self._deps(eng, reads, writes)
        sem.count += 16
        ev = (sem, sem.count, "dma")
        self.q[eng].append(("i", lambda e: e.dma_start(out=out, in_=in_, **kw), sem, 16))
        for b in reads:
            b.r.append(ev)
        for b in writes:
            b.w = ev
            b.r = []
        return ev

    def coll(self, kind, in_ap, out_ap, groups, reads=(), writes=()):
        b0 = list(writes)[0]
        if b0.ds is None:
            b0.ds = self.dsem()
        sem = b0.ds
        self._deps("pool", reads, writes)
        sem.count += 1
        ev = (sem, sem.count, "dma")
        self.q["pool"].append(("i", lambda e: e.collective_compute(kind, ALU.bypass, replica_groups=groups,
                                                                   ins=[in_ap.opt()], outs=[out_ap.opt()]), sem, 1))
        for b in reads:
            b.r.append(ev)
        for b in writes:
            b.w = ev
            b.r = []
        return ev

    def raw(self, eng, fn, reads=()):
        self._deps(eng, reads, ())
        self.q[eng].append(("r", fn))

    def idram(self, shape, dt, name=None, shared=False):
        self.nbuf += 1
        t = self.nc.dram_tensor(name or f"idram{self.nbuf}", list(shape), dt, addr_space="Shared" if shared else "Local")
        return Buf(t.ap(), name or f"idram{self.nbuf}")

    def _emit_block(self):
        q = self.q

        def run(eng_obj, items):
            for it in items:
                if it[0] == "w":
                    eng_obj.wait_ge(it[1].h, it[2])
                elif it[0] == "r":
                    it[1](eng_obj)
                else:
                    it[1](eng_obj).then_inc(it[2].h, it[3])

        with self.nc.Block() as block:
            @block.tensor
            def _(e):
                run(e, q["pe"])

            @block.vector
            def _(e):
                run(e, q["dve"])

            @block.scalar
            def _(e):
                run(e, q["act"])

            @block.gpsimd
            def _(e):
                run(e, q["pool"])

            @block.sync
            def _(e):
                run(e, q["sp"])
        self.q = {e: [] for e in ENGS}

    def end_stage(self):
        sems = [x for x in self.dsems + [self.esem[e] for e in ENGS] if x.count > 0]
        for e in ENGS:
            for x in sems:
                if self.seen[e].get(x, 0) < x.count:
                    self.seen[e][x] = x.count
                    self.q[e].append(("w", x, x.count))
        self._emit_block()
        self.ss.close()
        self.ss = contextlib.ExitStack()
        self.free_ds.extend(self.stage_ds)
        self.stage_ds = []

    def finish(self):
        self.end_stage()
        self.es.close()
        return self.nc

    def mm(self, out, lhsT, rhs, start, stop, reads, writes, skip=False):
        if skip:
            return self.op("pe", lambda e: e.matmul(out, lhsT, rhs, start=start, stop=stop, skip_group_check=True), reads, writes)
        return self.op("pe", lambda e: e.matmul(out, lhsT, rhs, start=start, stop=stop), reads, writes)

    def tr(self, out, in_, ident, reads, writes):
        return self.op("pe", lambda e: e.transpose(out, in_, ident), reads, writes)

    def act(self, out, in_, func, reads, writes, bias=None, scale=None, accum_out=None):
        kw = {}
        if bias is not None:
            kw["bias"] = bias
        if scale is not None:
            kw["scale"] = scale
        if accum_out is not None:
            kw["accum_out"] = accum_out
        return self.op("act", lambda e: e.activation(out, in_, func, **kw), reads, writes)

    def tt(self, out, in0, in1, op, reads, writes, eng="dve"):
        return self.op(eng, lambda e: e.tensor_tensor(out, in0, in1, op), reads, writes)

    def ts(self, out, in0, s1, s2, op0, op1, reads, writes, eng="dve", accum_out=None):
        if op1 is None:
            return self.op(eng, lambda e: e.tensor_scalar(out, in0, s1, None, op0), reads, writes)
        if accum_out is not None:
            return self.op(eng, lambda e: e.tensor_scalar(out, in0, s1, s2, op0, op1, accum_out=accum_out), reads, writes)
        return self.op(eng, lambda e: e.tensor_scalar(out, in0, s1, s2, op0, op1), reads, writes)

    def stt(self, out, in0, scalar, in1, op0, op1, reads, writes):
        return self.op("dve", lambda e: e.scalar_tensor_tensor(out, in0, scalar, in1, op0, op1), reads, writes)

    def cp(self, out, in_, reads, writes, eng="dve"):
        if eng == "act":
            return self.op("act", lambda e: e.copy(out, in_), reads, writes)
        return self.op(eng, lambda e: e.tensor_copy(out, in_), reads, writes)

    def memset(self, ap, val, writes, eng="dve"):
        return self.op(eng, lambda e: e.memset(ap, val), (), writes)


D = 1024
DFF = 2816
NT = 17
TOK = NT * 128
CTXT = 2
SEQT = 66
EPS = 1e-6
BLOCKS = [(0, 2), (2, 3), (5, 3), (8, 3), (11, 3), (14, 3)]


def make_ident(P, n=128, dt=F32):
    ident = P.sb([n, n], dt)
    P.memset(ident[:], 1.0, [ident], eng="pool")
    P.op("pool", lambda e: e.affine_select(ident[:], ident[:], [[-1, n]], ALU.is_equal, 0.0, base=0,
                                           channel_multiplier=1), [ident], [ident])
    return ident


def load_cast(P, dst, dst_ap, src_ap, stage, sem, i, shape_cols):
    st = stage[i % len(stage)]
    P.dma("sp", st[:, 0:shape_cols], src_ap, writes=[st])
    eng = "dve" if i % 2 == 0 else "pool"
    P.cp(dst_ap, st[:, 0:shape_cols], [st], [dst], eng=eng)


def build_ffn(emit_h, P=None, T=None):
    own = P is None
    P = P or Prog()
    x = P.io(T, "x", [NT, 128, D], F32, "ExternalInput")
    wgu = P.io(T, "wgu", [D, 2 * DFF], F32, "ExternalInput")
    wd = P.io(T, "wd", [DFF, D], F32, "ExternalInput")
    pp = P.io(T, "pp", [128, 2 * 48], F32, "ExternalInput")
    gateb = P.io(T, "gateb", [2, 128, D], F32, "ExternalInput")
    xo = P.io(T, "xo", [NT, 128, D], F32, "ExternalOutput")
    if emit_h:
        ho = P.io(T, "ho", [8, 128, TOK], BF16, "ExternalOutput")

    ident = make_ident(P)
    Wgu = P.sb([128, 8, 2 * DFF], BF16, "Wgu")
    Wd = P.sb([128, 22, D], BF16, "Wd")
    stage = [P.sb([128, 704], F32, f"stg{i}") for i in range(2)]
    ssem = [P.dsem() for _ in range(2)]
    pps = P.sb([128, 96], F32, "pps")
    Gt = P.sb([128, 32], F32, "Gt")
    gates = P.sb([128, 2, D], F32, "gates")
    msem = P.dsem()
    P.dma("act", pps[:], pp[:, :], msem, writes=[pps])
    for s in range(2):
        P.dma("act", gates[:, s, :], gateb[s], msem, writes=[gates])
    for s in range(2):
        for j in range(2):
            sc = pps[:, s * 48 + j * 24 + 8: s * 48 + j * 24 + 16]
            g = pps[:, s * 48 + j * 24 + 16: s * 48 + j * 24 + 24]
            P.stt(Gt[:, s * 16 + j * 8: s * 16 + j * 8 + 8], sc, 1.0, g, ALU.add, ALU.mult, [pps], [Gt])
    P.ts(gates[:], gates[:], 0.5, None, ALU.mult, None, [gates], [gates], eng="pool")

    n = 0
    for kc in range(8):
        for q in range(8):
            load_cast(P, Wgu, Wgu[:, kc, q * 704:(q + 1) * 704], wgu[kc * 128:(kc + 1) * 128, q * 704:(q + 1) * 704],
                      stage, ssem, n, 704)
            n += 1
    for fc in range(22):
        for q in range(2):
            load_cast(P, Wd, Wd[:, fc, q * 512:(q + 1) * 512], wd[fc * 128:(fc + 1) * 128, q * 512:(q + 1) * 512], stage, ssem, n, 512)
            n += 1

    xb = [P.sb([128, 3, D], F32, "xb0")] * 2
    xsem = [P.dsem() for _ in range(2)]
    osem = [P.dsem() for _ in range(2)]
    scr = {"xn": P.sb([128, 3, D], F32, "xn"), "sq": P.sb([128, D], BF16, "sq")}
    small = {"ss": P.sb([128, 4], F32, "ss")}
    hT = P.sb([128, 8, 384], BF16, "hT")
    hT2 = hT
    hsem = P.dsem()
    uT = P.sb([128, 22, 384], BF16, "uT")
    sa = [P.sb([128, 384], F32, f"sa{i}") for i in range(2)]
    pst = [P.ps([128, 512], F32, f"pst{i}") for i in range(2)]
    pa = [P.ps([128, 512], F32, f"pa{i}") for i in range(2)]
    pb = [P.ps([128, 512], F32, f"pb{i}") for i in range(2)]
    py = [P.ps([128, 512], F32, f"py{i}") for i in range(2)]
    tmp = [P.sb([128, 512], F32, f"tmp{i}") for i in range(2)]

    for bi, (t0, nb) in enumerate(BLOCKS):
        s = 0 if bi == 0 else 1
        N = nb * 128
        X = xb[bi % 2]
        for i in range(nb):
            P.dma("sp", X[:, i, :], x[t0 + i], xsem[bi % 2], writes=[X])
        _norm(P, X, nb, Gt, s * 16, pps, s * 48, hT, ident, pst, scr, small)
        for fc in range(22):
            A, B = pa[fc % 2], pb[fc % 2]
            for kc in range(8):
                P.mm(A[:, 0:N], Wgu[:, kc, fc * 128:(fc + 1) * 128], hT[:, kc, 0:N], kc == 0, kc == 7, [Wgu, hT], [A])
            for kc in range(8):
                P.mm(B[:, 0:N], Wgu[:, kc, DFF + fc * 128:DFF + (fc + 1) * 128], hT[:, kc, 0:N], kc == 0, kc == 7,
                     [Wgu, hT], [B])
            S = sa[fc % 2]
            P.act(S[:, 0:N], A[:, 0:N], AF.Silu, [A], [S])
            P.tt(uT[:, fc, 0:N], S[:, 0:N], B[:, 0:N], ALU.mult, [S, B], [uT])
        for i in range(nb):
            for h in range(2):
                Y = py[(i * 2 + h) % 2]
                for fc in range(22):
                    P.mm(Y[:, :], uT[:, fc, i * 128:(i + 1) * 128], Wd[:, fc, h * 512:(h + 1) * 512], fc == 0, fc == 21,
                         [uT, Wd], [Y])
                T = tmp[(i * 2 + h) % 2]
                P.tt(T[:], Y[:], gates[:, s, h * 512:(h + 1) * 512], ALU.mult, [Y, gates], [T])
                P.tt(X[:, i, h * 512:(h + 1) * 512], X[:, i, h * 512:(h + 1) * 512], T[:], ALU.add, [X, T], [X], eng="pool")
            P.dma("sp", xo[t0 + i], X[:, i, :], osem[bi % 2], reads=[X])
        if emit_h:
            _norm(P, X, nb, Gt, s * 16 + 8, pps, s * 48 + 24, hT2, ident, pst, scr, small)
            for c in range(8):
                P.dma("act", ho[c, :, t0 * 128:t0 * 128 + N], hT2[:, c, 0:N], hsem, reads=[hT2])
    return P.finish() if own else None


def _norm(P, xb, nb, Gt, gcol, SHt, scol, hT, ident, pst, scr, small):
    xn = scr["xn"]
    ss = small["ss"]
    for i in range(nb):
        P.act(scr["sq"][:], xb[:, i, :], AF.Square, [xb], [scr["sq"], ss], accum_out=ss[:, 0:1])
        P.ts(ss[:, 1:2], ss[:, 0:1], 1.0 / D, EPS, ALU.mult, ALU.add, [ss], [ss])
        P.act(ss[:, 2:3], ss[:, 1:2], AF.Sqrt, [ss], [ss])
        P.op("dve", lambda e: e.reciprocal(ss[:, 3:4], ss[:, 2:3]), [ss], [ss])
        P.ts(xn[:, i, :], xb[:, i, :], ss[:, 3:4], None, ALU.mult, None, [xb, ss], [xn])
    for c in range(8):
        pt = pst[c % len(pst)]
        for i in range(nb):
            P.tr(pt[:, i * 128:(i + 1) * 128], xn[:, i, c * 128:(c + 1) * 128], ident[:], [xn, ident], [pt])
        P.act(hT[:, c, 0:nb * 128], pt[:, 0:nb * 128], AF.Identity, [pt, Gt, SHt], [hT],
              bias=SHt[:, scol + c:scol + c + 1], scale=Gt[:, gcol + c:gcol + c + 1])


def build_mod():
    P = Prog()
    cT = P.dram("cT", [128, 8, 3], F32, "ExternalInput")
    wa = P.dram("wa", [2, D, 1152], F32, "ExternalInput")
    ba = P.dram("ba", [2, 3, 1152], F32, "ExternalInput")
    mo = P.dram("mo", [2, 3, 1152], F32, "ExternalOutput")
    cs = P.sb([128, 8, 3], F32)
    sg = P.sb([128, 8, 3], F32)
    ds = P.dsem()
    P.dma("sp", cs[:], cT[:, :, :], ds, writes=[cs])
    P.act(sg[:], cs[:], AF.Sigmoid, [cs], [sg])
    P.tt(cs[:], cs[:], sg[:], ALU.mult, [cs, sg], [cs])
    W = [P.sb([128, 8, 1152], F32, f"W{l}") for l in range(2)]
    wsem = P.dsem()
    bsb = P.sb([3, 2, 1152], F32)
    osb = P.sb([3, 2, 1152], F32)
    for l in range(2):
        P.dma("act", bsb[:, l, :], ba[l], ds, writes=[bsb])
        for kc in range(8):
            P.dma("sp" if kc % 2 == 0 else "act", W[l][:, kc, :], wa[l, kc * 128:(kc + 1) * 128, :], wsem, writes=[W[l]])
    pp = [P.ps([3, 512], F32, f"pp{i}") for i in range(2)]
    n = 0
    for l in range(2):
        for (c0, cw) in ((0, 512), (512, 512), (1024, 128)):
            ps = pp[n % 2]
            n += 1
            for kc in range(8):
                P.mm(ps[:, 0:cw], cs[:, kc, :], W[l][:, kc, c0:c0 + cw], kc == 0, kc == 7, [cs, W[l]], [ps])
            P.tt(osb[:, l, c0:c0 + cw], ps[:, 0:cw], bsb[:, l, c0:c0 + cw], ALU.add, [ps, bsb], [osb])
    osem = P.dsem()
    for l in range(2):
        P.dma("sp", mo[l], osb[:, l, :], osem, reads=[osb])
    return P.finish()


def run_mod(c, c_ctx, w_ada, b_ada):
    cv = np.stack([c[0], c[1], c_ctx], 0)
    cT = np.ascontiguousarray(cv.reshape(3, 8, 128).transpose(2, 1, 0))
    nc = build_mod()
    ins = []
    for i in range(8):
        cols = slice(i * 1152, (i + 1) * 1152)
        ins.append({"cT": cT, "wa": np.ascontiguousarray(w_ada[:, :, cols]),
                    "ba": np.ascontiguousarray(np.broadcast_to(b_ada[:, None, cols], (2, 3, 1152)))})
    res = run_bass_kernel_spmd(nc, ins, core_ids=list(range(8)))
    return np.concatenate([r["mo"] for r in res.results], axis=2)


def pvec(v):
    return np.ascontiguousarray(v.reshape(8, 128).T)


def tok_shard(xfull_b):
    pad = np.zeros((4 * TOK, xfull_b.shape[1]), np.float32)
    pad[:xfull_b.shape[0]] = xfull_b
    return [np.ascontiguousarray(pad[i * TOK:(i + 1) * TOK].reshape(NT, 128, -1)) for i in range(4)]


def ffn_inputs(xs, mods_l, li, which, norm_g, wgu, wd, emit_h):
    m = mods_l.reshape(3, 9, D)
    o = 0 if which == 1 else 6
    ins = []
    for b in range(2):
        shards = tok_shard(xs[b])
        for i in range(4):
            sets = []
            for s in range(2):
                r = 2 if (s == 0 and i == 0) else b
                vecs = [m[r, o + 0], m[r, o + 1], norm_g[li, 0 if which == 1 else 2]]
                if emit_h:
                    vecs += [m[r, 3], m[r, 4], norm_g[li, 1]]
                else:
                    vecs += [m[r, 3] * 0, m[r, 3] * 0, m[r, 3] * 0]
                sets.append(np.concatenate([pvec(v) for v in vecs], axis=1))
            pp = np.ascontiguousarray(np.concatenate(sets, axis=1).astype(np.float32))
            gb = np.stack([np.broadcast_to(m[2 if i == 0 else b, o + 2], (128, D)),
                           np.broadcast_to(m[b, o + 2], (128, D))], 0)
            ins.append({"x": shards[i], "wgu": wgu, "wd": wd, "pp": pp, "gateb": np.ascontiguousarray(gb)})
    return ins


def tok_unshard(res, key):
    out = []
    for b in range(2):
        full = np.concatenate([res[b * 4 + i][key].reshape(TOK, -1) for i in range(4)], axis=0)
        out.append(full[:SEQT * 128])
    return out


def hT_unshard(res, key):
    out = []
    for b in range(2):
        full = np.concatenate([res[b * 4 + i][key].reshape(D, TOK) for i in range(4)], axis=1)
        out.append(np.ascontiguousarray(full[:, :SEQT * 128]))
    return out


SEQ_ALL = SEQT * 128
NCTX = 256


def h_pieces(hT, kc, a, b):
    if hT.gath:
        out, t = [], a
        while t < b:
            r = t // TOK
            e = min(b, (r + 1) * TOK)
            out.append((t - a, e - t, hT[kc, r * 128:(r + 1) * 128, t - r * TOK:e - r * TOK]))
            t = e
        return out
    return [(0, b - a, hT[kc, :, a:b])]


def ycols(buf, a):
    if buf.gath:
        q = a // TOK
        return buf[q, :, a - q * TOK:a - q * TOK + 128]
    return buf[:, a:a + 128]


def load_hblk(P, hT, hb, sem, t0, N, eng="sp"):
    for kc in range(8):
        for (o, n, ap) in h_pieces(hT, kc, t0, t0 + N):
            P.dma(eng, hb[:, kc, o:o + n], ap, writes=[hb], **({"allow_slow_non_contiguous": True} if n == 1 else {}))


def build_attn(debug=False, P=None, T=None):
    own = P is None
    P = P or Prog()
    hT = P.io(T, "hT", [8, 128, SEQ_ALL], BF16, "ExternalInput")
    wqkv = P.io(T, "wqkv", [3, D, 128], F32, "ExternalInput")
    gqk = P.io(T, "gqk", [128, 2], F32, "ExternalInput")
    cs = P.io(T, "cs", [2, 128, 8192], F32, "ExternalInput")
    cmat = P.io(T, "cmat", [2, 128, 128], F32, "ExternalInput")
    lamp = P.io(T, "lamp", [128, 258], F32, "ExternalInput")
    subg = P.io(T, "subg", [128, 128], F32, "ExternalInput")
    ya = P.io(T, "ya", [128, SEQ_ALL], BF16, "ExternalOutput")

    banks = [P.ps([128, 512], F32, f"bank{i}") for i in range(8)]
    ident = make_ident(P)
    csem = P.dsem()
    W = P.sb([128, 3, 8, 128], BF16, "W")
    wst = P.sb([128, 3, 8, 128], F32, "wst")
    for j in range(3):
        for kc in range(8):
            P.dma("sp", wst[:, j, kc, :], wqkv[j, kc * 128:(kc + 1) * 128, :], csem, writes=[wst])
    P.cp(W[:], wst[:], [wst], [W])
    gq = P.sb([128, 2], F32, "gq")
    P.dma("act", gq[:], gqk[:, :], csem, writes=[gq])
    P.ts(gq[:, 0:1], gq[:, 0:1], 0.125, None, ALU.mult, None, [gq], [gq])
    Bm = P.sb([128, 128], F32, "Bm")
    Rm = P.sb([128, 128], F32, "Rm")
    P.dma("act", Bm[:], cmat[0], csem, writes=[Bm])
    P.dma("act", Rm[:], cmat[1], csem, writes=[Rm])
    lp = P.sb([128, 258], F32, "lp")
    P.dma("act", lp[:], lamp[:, :], csem, writes=[lp])
    sg = P.sb([128, 128], F32, "sg")
    P.dma("act", sg[:], subg[:, :], csem, writes=[sg])
    P.ts(sg[:], sg[:], lp[:, 257:258], None, ALU.mult, None, [sg, lp], [sg])
    lt = P.sb([128, 128], F32, "lt")
    lam = P.sb([128, 4], F32, "lam")
    P.tt(lt[:, 0:64], lp[:, 0:64], lp[:, 64:128], ALU.mult, [lp], [lt])
    P.tt(lt[:, 64:128], lp[:, 128:192], lp[:, 192:256], ALU.mult, [lp], [lt])
    P.op("dve", lambda e: e.tensor_reduce(lam[:, 0:1], lt[:, 0:64], AX.X, ALU.add), [lt], [lam])
    P.op("dve", lambda e: e.tensor_reduce(lam[:, 1:2], lt[:, 64:128], AX.X, ALU.add), [lt], [lam])
    P.act(lam[:, 0:2], lam[:, 0:2], AF.Exp, [lam], [lam])
    P.tt(lam[:, 2:3], lam[:, 1:2], lam[:, 0:1], ALU.subtract, [lam], [lam])
    P.tt(lam[:, 3:4], lam[:, 2:3], lp[:, 256:257], ALU.subtract, [lam, lp], [lam])
    epsb = P.sb([128, 1], F32, "epsb")
    P.memset(epsb[:], EPS, [epsb])

    QK = [P.sb([128, SEQ_ALL], BF16, "QT"), P.sb([128, SEQ_ALL], BF16, "KT")]
    V = P.sb([128, SEQT, 129], BF16, "V")
    P.memset(V[:, :, 128:129], 1.0, [V], eng="pool")
    hb = [P.sb([128, 8, 512], BF16, f"hb{i}") for i in range(2)]
    hsem = [P.dsem() for _ in range(2)]
    cst = [P.sb([128, 2, 512], F32, f"cst{i}") for i in range(2)]
    cssem = [P.dsem() for _ in range(2)]
    sq = P.sb([128, 512], F32, "sq")
    rs = P.sb([128, 512], F32, "rs")
    xn = P.sb([128, 512], F32, "xn")
    t1 = P.sb([128, 512], F32, "t1")
    t2 = P.sb([128, 512], F32, "t2")

    blocks = [(0, 256)] + [(256 + 512 * j, 512) for j in range(16)]
    for bi, (t0, N) in enumerate(blocks):
        H = hb[bi % 2]
        load_hblk(P, hT, H, hsem[bi % 2], t0, N)
        C = cst[bi % 2]
        if bi > 0:
            for j in range(2):
                P.dma("act", C[:, j, :], cs[j, :, t0 - 256:t0 - 256 + N], cssem[bi % 2], writes=[C])
        for j in range(2):
            pq, pms, prot = banks[0 + j], banks[2 + j], banks[4 + j]
            for kc in range(8):
                P.mm(pq[:, 0:N], W[:, j, kc, :], H[:, kc, 0:N], kc == 0, kc == 7, [W, H], [pq])
            P.act(sq[:, 0:N], pq[:, 0:N], AF.Square, [pq], [sq])
            P.mm(pms[:, 0:N], Bm[:], sq[:, 0:N], True, True, [Bm, sq], [pms])
            P.act(rs[:, 0:N], pms[:, 0:N], AF.Sqrt, [pms, epsb], [rs], bias=epsb[:, 0:1])
            P.op("dve", lambda e, N=N: e.reciprocal(rs[:, 0:N], rs[:, 0:N]), [rs], [rs])
            P.stt(xn[:, 0:N], pq[:, 0:N], gq[:, j:j + 1], rs[:, 0:N], ALU.mult, ALU.mult, [pq, gq, rs], [xn])
            if bi == 0:
                P.cp(QK[j][:, t0:t0 + N], xn[:, 0:N], [xn], [QK[j]], eng="pool")
            else:
                P.mm(prot[:, 0:N], Rm[:], xn[:, 0:N], True, True, [Rm, xn], [prot])
                P.tt(t1[:, 0:N], xn[:, 0:N], C[:, 0, 0:N], ALU.mult, [xn, C], [t1], eng="pool")
                P.tt(t2[:, 0:N], prot[:, 0:N], C[:, 1, 0:N], ALU.mult, [prot, C], [t2])
                P.tt(QK[j][:, t0:t0 + N], t1[:, 0:N], t2[:, 0:N], ALU.add, [t1, t2], [QK[j]])
        for i in range(N // 128):
            pv = banks[6 + i % 2]
            for kc in range(8):
                P.mm(pv[:, 0:128], H[:, kc, i * 128:(i + 1) * 128], W[:, 2, kc, :], kc == 0, kc == 7, [H, W], [pv])
            P.cp(V[:, t0 // 128 + i, 0:128], pv[:, 0:128], [pv], [V], eng="act")

    if debug:
        dq = P.io(T, "dq", [2, 128, SEQ_ALL], BF16, "ExternalOutput")
        dv = P.io(T, "dv", [128, SEQT, 129], BF16, "ExternalOutput")
        dl = P.io(T, "dl", [128, 4], F32, "ExternalOutput")
        dsm = P.dsem()
        P.dma("sp", dq[0], QK[0][:], dsm, reads=[QK[0]])
        P.dma("sp", dq[1], QK[1][:], dsm, reads=[QK[1]])
        P.dma("sp", dv[:, :, :], V[:], dsm, reads=[V])
        P.dma("sp", dl[:, :], lam[:], dsm, reads=[lam])
    Pm = [P.sb([128, 512], BF16, f"Pm{i}") for i in range(3)]
    Sb = [banks[0], banks[1]]
    accb = [banks[2], banks[3], banks[4]]
    yo = [P.sb([128, 128], F32, f"yo{i}") for i in range(2)]
    yob = [P.sb([128, 128], BF16, f"yob{i}") for i in range(2)]
    ysq = P.sb([128, 128], F32, "ysq")
    st = P.sb([128, 8], F32, "st")
    osem = [P.dsem() for _ in range(2)]

    def acc(m, qs):
        a = m * 4 + qs
        return accb[a // 3], (a % 3) * 129

    qblocks = [(0, 256, 0, CTXT)] + [(256 + 512 * j, 512, 0, SEQT) for j in range(16)]
    n = 0
    no = 0
    for (q0, N, k0, k1) in qblocks:
        nq = N // 128
        for b in accb:
            P.memset(b[:], 0.0, [b])
        its = [(kt, m) for kt in range(k0, k1) for m in range(2)]

        def issue_s(j):
            kt, m = its[j]
            S, pm = Sb[(n + j) % 2], Pm[(n + j) % 3]
            P.mm(S[:, 0:N], QK[1][m * 64:(m + 1) * 64, kt * 128:(kt + 1) * 128], QK[0][m * 64:(m + 1) * 64, q0:q0 + N],
                 True, True, [QK[0], QK[1]], [S])
            P.act(pm[:, 0:N], S[:, 0:N], AF.Exp, [S], [pm])

        issue_s(0)
        for j, (kt, m) in enumerate(its):
            if j + 1 < len(its):
                issue_s(j + 1)
            pm = Pm[(n + j) % 3]
            for qs in range(nq):
                ab, c0 = acc(m, qs)
                P.mm(ab[:, c0:c0 + 129], pm[:, qs * 128:(qs + 1) * 128], V[:, kt, :], False, False, [pm, V], [ab], skip=True)
        n += len(its)
        for qs in range(nq):
            a0, c0 = acc(0, qs)
            a1, c1 = acc(1, qs)
            Y = yo[no % 2]
            P.op("dve", lambda e, a0=a0, c0=c0: e.reciprocal(st[:, 0:1], a0[:, c0 + 128:c0 + 129]), [a0], [st])
            P.op("dve", lambda e, a1=a1, c1=c1: e.reciprocal(st[:, 1:2], a1[:, c1 + 128:c1 + 129]), [a1], [st])
            P.tt(st[:, 1:2], st[:, 1:2], lam[:, 3:4], ALU.mult, [st, lam], [st])
            P.ts(Y[:], a0[:, c0:c0 + 128], st[:, 0:1], None, ALU.mult, None, [a0, st], [Y])
            P.stt(Y[:], a1[:, c1:c1 + 128], st[:, 1:2], Y[:], ALU.mult, ALU.add, [a1, st, Y], [Y])
            P.act(ysq[:], Y[:], AF.Square, [Y], [ysq, st], accum_out=st[:, 2:3])
            P.ts(st[:, 3:4], st[:, 2:3], 1.0 / 128, EPS, ALU.mult, ALU.add, [st], [st])
            P.act(st[:, 4:5], st[:, 3:4], AF.Sqrt, [st], [st])
            P.op("dve", lambda e: e.reciprocal(st[:, 5:6], st[:, 4:5]), [st], [st])
            P.stt(Y[:], Y[:], st[:, 5:6], sg[:], ALU.mult, ALU.mult, [Y, st, sg], [Y])
            P.tr(banks[5][:, 0:128], Y[:], ident[:], [Y, ident], [banks[5]])
            Yb = yob[no % 2]
            P.cp(Yb[:], banks[5][:, 0:128], [banks[5]], [Yb], eng="act")
            P.dma("sp", ycols(ya, q0 + qs * 128), Yb[:], reads=[Yb])
            no += 1
    return P.finish() if own else None


def rope_tables():
    n = 8192
    rows = n // 64
    row = np.repeat(np.arange(rows, dtype=np.float32), 64)
    col = np.tile(np.arange(64, dtype=np.float32), rows)
    half = 32
    inv = (np.float32(10000.0) ** (-np.arange(0, half, 2, dtype=np.float32) / np.float32(half))).astype(np.float32)
    ang = np.concatenate([row[:, None] * inv, col[:, None] * inv], axis=-1).astype(np.float32)
    cos, sin = np.cos(ang).astype(np.float32), np.sin(ang).astype(np.float32)
    idx = (np.arange(128) % 64) // 2
    return np.ascontiguousarray(np.stack([cos[:, idx].T, sin[:, idx].T], 0))


def attn_consts():
    Bm = np.zeros((128, 128), np.float32)
    Bm[:64, :64] = 1.0 / 64
    Bm[64:, 64:] = 1.0 / 64
    Rm = np.zeros((128, 128), np.float32)
    for i in range(64):
        Rm[2 * i + 1, 2 * i] = -1.0
        Rm[2 * i, 2 * i + 1] = 1.0
    return np.stack([Bm, Rm], 0)


def lambda_init(li):
    import math
    return 0.8 - 0.6 * math.exp(-0.3 * li)


def attn_inputs(hTs, li, w_in, a_qk_norm, a_lambda, a_subln):
    off_q = 5008 - 512 - 512 - 512
    off = {}
    o = 0
    for name, wdt in IN_SPLITS:
        off[name] = o
        o += wdt
    cs = rope_tables()
    cm = attn_consts()
    ins = []
    for b in range(2):
        hT = np.ascontiguousarray(hTs[b].reshape(8, 128, SEQ_ALL))
        for h in range(4):
            wq = w_in[li][:, off['a_q'] + 128 * h: off['a_q'] + 128 * (h + 1)]
            wk = w_in[li][:, off['a_k'] + 128 * h: off['a_k'] + 128 * (h + 1)]
            wv = w_in[li][:, off['a_v'] + 128 * h: off['a_v'] + 128 * (h + 1)]
            gqk = np.stack([np.tile(a_qk_norm[li, 0], 2), np.tile(a_qk_norm[li, 1], 2)], 1).astype(np.float32)
            li_ = np.float32(lambda_init(li))
            lamp = np.concatenate([np.broadcast_to(a_lambda[li].reshape(1, 256), (128, 256)),
                                   np.full((128, 1), li_, np.float32), np.full((128, 1), np.float32(1.0) - li_, np.float32)], 1)
            ins.append({"hT": hT, "wqkv": np.ascontiguousarray(np.stack([wq, wk, wv], 0)), "gqk": np.ascontiguousarray(gqk),
                        "cs": cs, "cmat": cm, "lamp": np.ascontiguousarray(lamp.astype(np.float32)),
                        "subg": np.ascontiguousarray(np.broadcast_to(a_subln[li][None, :], (128, 128)))})
    return ins


IN_SPLITS = (
    ('m_q', 256), ('m_k', 256), ('m_v', 512), ('m_o', 512),
    ('m_if', 4), ('m_ff', 4), ('m_ib', 4), ('m_fb', 4),
    ('r_r', 512), ('r_k', 512), ('r_v', 512),
    ('r_wf', 64), ('r_wb', 64), ('r_af', 64), ('r_ab', 64), ('r_g', 128),
    ('a_q', 512), ('a_k', 512), ('a_v', 512),
    ('g_m', 1024), ('g_r', 1024), ('g_a', 1024),
)


def tri_mask(P, upper, neg=False):
    m = P.sb([128, 128], F32)
    P.memset(m[:], 0.0 if neg else 1.0, [m], eng="pool")
    pat, cm = ([[1, 128]], -1) if upper else ([[-1, 128]], 1)
    P.op("pool", lambda e: e.affine_select(m[:], m[:], pat, ALU.is_ge, -1.0e4 if neg else 0.0, base=0,
                                           channel_multiplier=cm), [m], [m])
    return m


def load_hblk_halo(P, hT, hb, t0, N, lo, hi, eng="sp"):
    a = t0 - 1 if t0 - 1 >= lo else t0
    b = t0 + N + 1 if t0 + N + 1 <= hi else t0 + N
    for kc in range(8):
        for (o, n, ap) in h_pieces(hT, kc, a, b):
            P.dma(eng, hb[:, kc, a - (t0 - 1) + o:a - (t0 - 1) + o + n], ap, writes=[hb],
                  **({"allow_slow_non_contiguous": True} if n == 1 else {}))
    if a == t0:
        P.memset(hb[:, :, 0:1], 0.0, [hb], eng="pool")
    if b == t0 + N:
        P.memset(hb[:, :, N + 1:N + 2], 0.0, [hb], eng="pool")


def chunk_orders():
    f = list(range(SEQT))
    b = [1, 0] + list(range(SEQT - 1, 1, -1))
    return f, b


def build_mlstm(stop=99, P=None, T=None):
    own = P is None
    P = P or Prog()
    hT = P.io(T, "hT", [8, 128, SEQ_ALL], BF16, "ExternalInput")
    wqk = P.io(T, "wqk", [2, D, 64], F32, "ExternalInput")
    wvo = P.io(T, "wvo", [D, 256], F32, "ExternalInput")
    wg = P.io(T, "wg", [D, 4], F32, "ExternalInput")
    cw = P.io(T, "cw", [2, 128, 3, 64], F32, "ExternalInput")
    gb = P.io(T, "gb", [128, 4], F32, "ExternalInput")
    og = P.io(T, "og", [128, 128], F32, "ExternalInput")
    ym = P.io(T, "ym", [128, SEQ_ALL], BF16, "ExternalOutput")

    banks = [P.ps([128, 512], F32, f"bank{i}") for i in range(8)]
    ident = make_ident(P)
    ones = P.sb([128, 128], F32, "ones")
    P.memset(ones[:], 1.0, [ones])
    one1 = P.sb([128, 1], F32, "one1")
    P.memset(one1[:], 1.0, [one1])
    triU = tri_mask(P, True)
    triL = tri_mask(P, False)
    negU = tri_mask(P, True, True)
    negL = tri_mask(P, False, True)

    wst = P.sb([128, 8, 388], F32, "wst")
    for kc in range(8):
        P.dma("sp", wst[:, kc, 0:64], wqk[0, kc * 128:(kc + 1) * 128, :], writes=[wst])
        P.dma("sp", wst[:, kc, 64:128], wqk[1, kc * 128:(kc + 1) * 128, :], writes=[wst])
        P.dma("sp", wst[:, kc, 128:384], wvo[kc * 128:(kc + 1) * 128, :], writes=[wst])
        P.dma("sp", wst[:, kc, 384:388], wg[kc * 128:(kc + 1) * 128, :], writes=[wst])
    cws = P.sb([128, 2, 3, 64], F32, "cws")
    P.dma("act", cws[:, 0], cw[0], writes=[cws])
    P.dma("act", cws[:, 1], cw[1], writes=[cws])
    gbs = P.sb([128, 4], F32, "gbs")
    P.dma("act", gbs[:], gb[:, :], writes=[gbs])
    ogs = P.sb([128, 128], F32, "ogs")
    P.dma("act", ogs[:], og[:, :], writes=[ogs])
    Wqk = P.sb([128, 2, 3, 8, 64], BF16, "Wqk")
    Wvo = P.sb([128, 8, 260], BF16, "Wvo")
    P.cp(Wvo[:], wst[:, :, 128:388], [wst], [Wvo])
    for j in range(2):
        for tap in range(3):
            for kc in range(8):
                P.tt(Wqk[:, j, tap, kc, :], wst[:, kc, j * 64:(j + 1) * 64], cws[:, j, tap, :], ALU.mult, [wst, cws], [Wqk],
                     eng="pool" if kc % 2 else "dve")

    QT = P.sb([64, SEQ_ALL], F32, "QT")
    KT = P.sb([64, SEQ_ALL], F32, "KT")
    VE = P.sb([128, SEQT, 129], F32, "VE")
    P.memset(VE[:, :, 128:129], 1.0, [VE], eng="pool")
    OG = P.sb([128, SEQT, 128], BF16, "OG")
    G = P.sb([128, SEQT, 4], F32, "G")
    hb = [P.sb([128, 8, 514], BF16, f"hb{i}") for i in range(2)]

    blocks = [(0, 256, 0, 256)] + [(256 + 512 * j, 512, 256, SEQ_ALL) for j in range(16)]
    for bi, (t0, N, lo, hi) in enumerate(blocks):
        H = hb[bi % 2]
        load_hblk_halo(P, hT, H, t0, N, lo, hi)
        for j, dst in enumerate((QT, KT)):
            pq = banks[j]
            n = 0
            for tap in range(3):
                for kc in range(8):
                    P.mm(pq[0:64, 0:N], Wqk[:, j, tap, kc, :], H[:, kc, tap:tap + N], n == 0, n == 23, [Wqk, H], [pq])
                    n += 1
            P.act(dst[:, t0:t0 + N], pq[0:64, 0:N], AF.Silu, [pq], [dst], scale=1.0)
        for i in range(N // 128):
            pv = banks[2 + i % 2]
            for kc in range(8):
                P.mm(pv[:, 0:260], H[:, kc, 1 + i * 128:1 + (i + 1) * 128], Wvo[:, kc, :], kc == 0, kc == 7, [H, Wvo], [pv])
            tl = t0 // 128 + i
            P.cp(VE[:, tl, 0:128], pv[:, 0:128], [pv], [VE])
            P.act(OG[:, tl, :], pv[:, 128:256], AF.Sigmoid, [pv], [OG])
            P.tt(G[:, tl, :], pv[:, 256:260], gbs[:], ALU.add, [pv, gbs], [G])
    if stop == 1:
        return P.finish() if own else None
    P.ts(KT[:], KT[:], 0.125, None, ALU.mult, None, [KT], [KT], eng="pool")

    ge = P.sb([128, SEQT, 4], F32, "ge")
    P.act(ge[:], G[:], AF.Exp, [G], [ge], scale=-1.0)
    P.act(ge[:], ge[:], AF.Ln, [ge, one1], [ge], bias=one1[:, 0:1])
    LF = P.sb([128, 2, SEQT], F32, "LF")
    IG = P.sb([128, 2, SEQT], F32, "IG")
    for d in range(2):
        P.ts(LF[:, d, :], ge[:, :, 2 * d + 1], -1.0, None, ALU.mult, None, [ge], [LF])
        P.cp(IG[:, d, :], G[:, :, 2 * d], [G], [IG])
    BC = P.sb([128, 2, SEQT], F32, "BC")
    BT = P.sb([128, 2, SEQT], F32, "BT")
    for d in range(2):
        pb = banks[4 + d]
        P.mm(pb[:, 0:SEQT], (triU if d == 0 else triL)[:], LF[:, d, :], True, True, [triU, triL, LF], [pb])
        P.cp(BC[:, d, :], pb[:, 0:SEQT], [pb], [BC])
        pb2 = banks[6 + d]
        P.mm(pb2[:, 0:SEQT], ones[:], LF[:, d, :], True, True, [ones, LF], [pb2])
        P.cp(BT[:, d, :], pb2[:, 0:SEQT], [pb2], [BT])
    BIAS = P.sb([128, 2, SEQT], F32, "BIAS")
    WS = P.sb([128, 2, SEQT], F32, "WS")
    EB = P.sb([128, 2, SEQT], F32, "EB")
    DEC = P.sb([128, 2, SEQT], F32, "DEC")
    P.tt(BIAS[:], IG[:], BC[:], ALU.subtract, [IG, BC], [BIAS])
    P.tt(WS[:], BIAS[:], BT[:], ALU.add, [BIAS, BT], [WS])
    P.act(WS[:], WS[:], AF.Exp, [WS], [WS])
    P.act(EB[:], BC[:], AF.Exp, [BC], [EB])
    P.act(DEC[:], BT[:], AF.Exp, [BT], [DEC])

    if stop == 2:
        return P.finish() if own else None
    HS = P.sb([128, SEQT, 128], F32, "HS")
    CT = [[P.sb([64, 129], F32, f"CT{d}{i}") for i in range(2)] for d in range(2)]
    for d in range(2):
        P.memset(CT[d][0][:], 0.0, [CT[d][0]])
    lrep = [P.sb([128, 128], F32, f"lrep{i}") for i in range(2)]
    arg = [P.sb([128, 128], F32, f"arg{i}") for i in range(2)]
    ST = [P.sb([128, 128], F32, f"ST{i}") for i in range(2)]
    KW = [P.sb([128, 64], F32, f"KW{i}") for i in range(2)]
    it = [P.sb([128, 129], F32, f"it{i}") for i in range(2)]
    tot = [P.sb([128, 129], F32, f"tot{i}") for i in range(2)]
    sm = [P.sb([128, 2], F32, f"sm{i}") for i in range(2)]
    orders = chunk_orders()
    done = set()
    for step in range(SEQT):
        for d in range(2):
            c = orders[d][step]
            cs_ = slice(c * 128, (c + 1) * 128)
            Ccur, Cnew = CT[d][step % 2], CT[d][(step + 1) % 2]
            tri, neg = (triU, negU) if d == 0 else (triL, negL)
            p_brd, p_qk, p_n, p_i, p_kt, p_st = (banks[d * 4 + 0], banks[d * 4 + 1], banks[d * 4 + 2], banks[d * 4 + 3],
                                                 banks[d * 4 + 0], banks[d * 4 + 1])
            L = lrep[d]
            P.ts(L[:], ones[:], LF[:, d, c:c + 1], None, ALU.mult, None, [ones, LF], [L], eng="pool")
            P.mm(p_brd[:, 0:128], L[:], tri[:], True, True, [L, tri], [p_brd])
            A = arg[d]
            P.tt(A[:], p_brd[:, 0:128], neg[:], ALU.add, [p_brd, neg], [A])
            P.act(A[:], A[:], AF.Exp, [A, BIAS], [A], bias=BIAS[:, d, c:c + 1])
            P.mm(p_qk[:, 0:128], KT[:, cs_], QT[:, cs_], True, True, [KT, QT], [p_qk])
            S = ST[d]
            P.tt(S[:], p_qk[:, 0:128], A[:], ALU.mult, [p_qk, A], [S])
            P.mm(p_n[:, 0:129], S[:], VE[:, c, :], True, True, [S, VE], [p_n])
            P.mm(p_i[:, 0:129], QT[:, cs_], Ccur[:], True, True, [QT, Ccur], [p_i])
            I = it[d]
            P.act(I[:], p_i[:, 0:129], AF.Identity, [p_i, EB], [I], scale=EB[:, d, c:c + 1])
            T = tot[d]
            P.tt(T[:], p_n[:, 0:129], I[:], ALU.add, [p_n, I], [T])
            s_ = sm[d]
            P.act(s_[:, 0:1], T[:, 128:129], AF.Abs, [T], [s_])
            P.ts(s_[:, 0:1], s_[:, 0:1], 1.0, None, ALU.max, None, [s_], [s_])
            P.op("dve", lambda e, s_=s_: e.reciprocal(s_[:, 1:2], s_[:, 0:1]), [s_], [s_])
            if c in done:
                P.stt(HS[:, c, :], T[:, 0:128], s_[:, 1:2], HS[:, c, :], ALU.mult, ALU.add, [T, s_, HS], [HS])
            else:
                P.ts(HS[:, c, :], T[:, 0:128], s_[:, 1:2], None, ALU.mult, None, [T, s_], [HS])
                done.add(c)
            P.tr(p_kt[:, 0:64], KT[:, cs_], ident[0:64, 0:64], [KT, ident], [p_kt])
            kw = KW[d]
            P.ts(kw[:], p_kt[:, 0:64], WS[:, d, c:c + 1], None, ALU.mult, None, [p_kt, WS], [kw])
            P.mm(p_st[0:64, 0:129], kw[:], VE[:, c, :], True, True, [kw, VE], [p_st])
            P.stt(Cnew[:], Ccur[:], DEC[0:64, d, c:c + 1], p_st[0:64, 0:129], ALU.mult, ALU.add, [Ccur, DEC, p_st], [Cnew])

    if stop == 3:
        return P.finish() if own else None
    yo = ST
    junk = lrep[0]
    yob = [P.sb([128, 128], BF16, f"yob{i}") for i in range(2)]
    st = [P.sb([128, 4], F32, f"st{i}") for i in range(2)]
    for c in range(SEQT):
        Y, s_ = yo[c % 2], st[c % 2]
        P.act(junk[:], HS[:, c, :], AF.Square, [HS], [junk, s_], accum_out=s_[:, 0:1])
        P.ts(s_[:, 1:2], s_[:, 0:1], 1.0 / 128, EPS, ALU.mult, ALU.add, [s_], [s_])
        P.act(s_[:, 2:3], s_[:, 1:2], AF.Sqrt, [s_], [s_])
        P.op("dve", lambda e, s_=s_: e.reciprocal(s_[:, 3:4], s_[:, 2:3]), [s_], [s_])
        P.stt(Y[:], HS[:, c, :], s_[:, 3:4], ogs[:], ALU.mult, ALU.mult, [HS, s_, ogs], [Y])
        P.tt(Y[:], Y[:], OG[:, c, :], ALU.mult, [Y, OG], [Y])
        P.tr(banks[c % 2][:, 0:128], Y[:], ident[:], [Y, ident], [banks[c % 2]])
        Yb = yob[c % 2]
        P.cp(Yb[:], banks[c % 2][:, 0:128], [banks[c % 2]], [Yb], eng="act")
        P.dma("sp", ycols(ym, c * 128), Yb[:], reads=[Yb])
    return P.finish() if own else None


def col_offsets():
    off, o = {}, 0
    for name, wdt in IN_SPLITS:
        off[name] = o
        o += wdt
    return off


def mlstm_inputs(hTs, li, w_in, m_conv, m_gate_bias, m_out_norm):
    off = col_offsets()
    ins = []
    for b in range(2):
        hT = np.ascontiguousarray(hTs[b].reshape(8, 128, SEQ_ALL))
        for h in range(4):
            W = w_in[li]
            wq = W[:, off['m_q'] + 64 * h: off['m_q'] + 64 * (h + 1)]
            wk = W[:, off['m_k'] + 64 * h: off['m_k'] + 64 * (h + 1)]
            wvo = np.concatenate([W[:, off['m_v'] + 128 * h: off['m_v'] + 128 * (h + 1)],
                                  W[:, off['m_o'] + 128 * h: off['m_o'] + 128 * (h + 1)]], 1)
            wg = np.stack([W[:, off['m_if'] + h], W[:, off['m_ff'] + h], W[:, off['m_ib'] + h], W[:, off['m_fb'] + h]], 1)
            cq = m_conv[li][:, 64 * h:64 * (h + 1)]
            ck = m_conv[li][:, 256 + 64 * h:256 + 64 * (h + 1)]
            cw = np.stack([np.broadcast_to(cq[None], (128, 3, 64)), np.broadcast_to(ck[None], (128, 3, 64))], 0)
            gbv = m_gate_bias[li][:, h]
            ins.append({"hT": hT, "wqk": np.ascontiguousarray(np.stack([wq, wk], 0)), "wvo": np.ascontiguousarray(wvo),
                        "wg": np.ascontiguousarray(wg), "cw": np.ascontiguousarray(cw.astype(np.float32)),
                        "gb": np.ascontiguousarray(np.broadcast_to(gbv[None, :], (128, 4)).astype(np.float32)),
                        "og": np.ascontiguousarray(np.broadcast_to(m_out_norm[li][None, 128 * h:128 * (h + 1)], (128, 128)))})
    return ins


R_GN_EPS = 64e-5
W_SCALE = -0.6065306597126334


def aff_mask(P, pat, cm, op, val=1.0, base=0):
    m = P.sb([128, 128], F32)
    P.memset(m[:], val, [m], eng="pool")
    P.op("pool", lambda e: e.affine_select(m[:], m[:], pat, op, 0.0, base=base, channel_multiplier=cm), [m], [m])
    return m


def build_rwkv_v1(P=None, T=None):
    own = P is None
    P = P or Prog()
    hT = P.io(T, "hT", [8, 128, SEQ_ALL], BF16, "ExternalInput")
    wrkv = P.io(T, "wrkv", [D, 384], F32, "ExternalInput")
    crkv = P.io(T, "crkv", [128, 3, 384], F32, "ExternalInput")
    wl = P.io(T, "wl", [D, 384], F32, "ExternalInput")
    w2a2 = P.io(T, "w2a2", [2, 128, 128], F32, "ExternalInput")
    bias01 = P.io(T, "bias01", [1, 2, 256], F32, "ExternalInput")
    g2 = P.io(T, "g2", [128, 128], F32, "ExternalInput")
    vecs = P.io(T, "vecs", [128, 5, 128], F32, "ExternalInput")
    yr = P.io(T, "yr", [128, SEQ_ALL], BF16, "ExternalOutput")

    bk = [P.ps([128, 512], F32, f"bank{i}") for i in range(8)]
    ident = make_ident(P)
    ones = P.sb([128, 128], F32, "ones")
    P.memset(ones[:], 1.0, [ones])
    mI = [aff_mask(P, [[1, 128]], -1, ALU.is_ge), aff_mask(P, [[-1, 128]], 1, ALU.is_ge)]
    mS = [aff_mask(P, [[1, 128]], -1, ALU.is_gt), aff_mask(P, [[-1, 128]], 1, ALU.is_gt)]
    cI = [aff_mask(P, [[1, 128]], -1, ALU.is_ge, W_SCALE), aff_mask(P, [[-1, 128]], 1, ALU.is_ge, W_SCALE)]
    cS = [aff_mask(P, [[1, 128]], -1, ALU.is_gt, W_SCALE), aff_mask(P, [[-1, 128]], 1, ALU.is_gt, W_SCALE)]
    mSI = []
    for d in range(2):
        m = P.sb([128, 256], F32)
        P.cp(m[:, 0:128], mS[d][:], [mS[d]], [m])
        P.cp(m[:, 128:256], mI[d][:], [mI[d]], [m])
        mSI.append(m)

    wst = P.sb([128, 8, 768], F32, "wst")
    for kc in range(8):
        P.dma("sp", wst[:, kc, 0:384], wrkv[kc * 128:(kc + 1) * 128, :], writes=[wst])
        P.dma("act", wst[:, kc, 384:768], wl[kc * 128:(kc + 1) * 128, :], writes=[wst])
    cws = P.sb([128, 3, 384], F32, "cws")
    P.dma("sp", cws[:], crkv[:, :, :], writes=[cws])
    Wc = P.sb([128, 3, 8, 384], BF16, "Wc")
    for tap in range(3):
        for kc in range(8):
            P.tt(Wc[:, tap, kc, :], wst[:, kc, 0:384], cws[:, tap, :], ALU.mult, [wst, cws], [Wc])
    Wl = P.sb([128, 8, 384], BF16, "Wl")
    P.cp(Wl[:], wst[:, :, 384:768], [wst], [Wl])
    W2 = P.sb([128, 2, 128], F32, "W2")
    for d in range(2):
        P.dma("act", W2[:, d, :], w2a2[d], writes=[W2])
    B01 = P.sb([1, 2, 256], F32, "B01")
    P.dma("act", B01[:], bias01[:, :, :], writes=[B01])
    G2 = P.sb([128, 128], F32, "G2")
    P.dma("act", G2[:], g2[:, :], writes=[G2])
    VEC = P.sb([128, 5, 128], F32, "VEC")
    P.dma("act", VEC[:], vecs[:, :, :], writes=[VEC])
    epsg = P.sb([128, 1], F32, "epsg")
    P.memset(epsg[:], R_GN_EPS, [epsg])

    YF = P.sb([128, SEQT, 128], F32, "YF")
    hb = [P.sb([128, 8, 130], BF16, f"hb{i}") for i in range(2)]
    STB = P.sb([128, 128], F32, "STB")

    def TL(shape, name):
        return P.sb(shape, F32, name)

    rkv = TL([128, 384], "rkv")
    pl = TL([128, 128], "pl")
    sgg = TL([128, 128], "sgg")
    sig = TL([128, 128], "sig")
    av = TL([128, 128], "av")
    t0_ = TL([128, 128], "t0_")
    ss = TL([128, 8], "ss")
    kkn = TL([128, 128], "kkn")
    bh = TL([128, 128], "bh")
    t2 = TL([128, 128], "t2")
    key = TL([128, 128], "key")
    eG = TL([128, 256], "eG")
    enG = TL([128, 128], "enG")
    eR = TL([128, 128], "eR")
    AR = TL([128, 256], "AR")
    BtT = TL([128, 128], "BtT")
    KtT = TL([128, 128], "KtT")
    Bb = TL([128, 128], "Bb")
    Kb = TL([128, 128], "Kb")
    Mm = [TL([128, 256], f"Mm{i}") for i in range(2)]
    Ak = [TL([128, 256], f"Ak{i}") for i in range(2)]
    Lm = [TL([128, 128], f"Lm{i}") for i in range(2)]
    TtA = [TL([128, 128], f"TtA{i}") for i in range(2)]
    TA = [TL([128, 128], f"TA{i}") for i in range(2)]
    Pp = [[TL([128, 128], f"Pp{i}{j}") for j in range(2)] for i in range(2)]
    Qq = [[TL([128, 128], f"Qq{i}{j}") for j in range(2)] for i in range(2)]
    X = TL([128, 128], "X")
    U = TL([128, 128], "U")
    yb = TL([128, 128], "yb")
    gn = TL([128, 16], "gn")
    yn = TL([128, 128], "yn")
    rk = TL([128, 128], "rk")
    yo = [TL([128, 128], f"yo{i}") for i in range(2)]
    yob = [P.sb([128, 128], BF16, f"yob{i}") for i in range(2)]

    orders = chunk_orders()
    for d in range(2):
        P.memset(STB[:], 0.0, [STB])
        for step in range(SEQT):
            c = orders[d][step]
            t0 = c * 128
            lo, hi = (0, NCTX) if c < CTXT else (NCTX, SEQ_ALL)
            H = hb[step % 2]
            load_hblk_halo(P, hT, H, t0, 128, lo, hi)
            n = 0
            for tap in range(3):
                for kc in range(8):
                    P.mm(bk[0][:, 0:384], H[:, kc, tap:tap + 128], Wc[:, tap, kc, :], n == 0, n == 23, [H, Wc], [bk[0]])
                    n += 1
            P.cp(rkv[:], bk[0][:, 0:384], [bk[0]], [rkv], eng="act")
            for kc in range(8):
                P.mm(bk[1][:, 0:128], Wl[:, kc, d * 128:(d + 1) * 128], H[:, kc, 1:129], kc == 0, kc == 7, [Wl, H], [bk[1]])
            if d == 1:
                for kc in range(8):
                    P.mm(bk[1][:, 128:256], Wl[:, kc, 256:384], H[:, kc, 1:129], kc == 0, kc == 7, [Wl, H], [bk[1]])
            P.act(pl[0:64, :], bk[1][0:64, 0:128], AF.Tanh, [bk[1]], [pl])
            P.cp(pl[64:128, :], bk[1][64:128, 0:128], [bk[1]], [pl])
            if d == 1:
                P.act(sgg[:], bk[1][:, 128:256], AF.Sigmoid, [bk[1]], [sgg])
            P.mm(bk[1][:, 256:384], pl[0:64, :], W2[0:64, d, :], True, False, [pl, W2], [bk[1]])
            P.mm(bk[1][:, 256:384], ones[0:1, :], B01[0:1, d, 0:128], False, True, [ones, B01], [bk[1]])
            P.mm(bk[1][:, 384:512], pl[64:128, :], W2[64:128, d, :], True, False, [pl, W2], [bk[1]])
            P.mm(bk[1][:, 384:512], ones[0:1, :], B01[0:1, d, 128:256], False, True, [ones, B01], [bk[1]])
            P.act(sig[:], bk[1][:, 256:384], AF.Sigmoid, [bk[1]], [sig])
            P.act(av[:], bk[1][:, 384:512], AF.Sigmoid, [bk[1]], [av])
            r_, k_, v_ = rkv[:, 0:128], rkv[:, 128:256], rkv[:, 256:384]
            P.tt(t0_[:], k_, VEC[:, 0, :], ALU.mult, [rkv, VEC], [t0_])
            P.tt(t2[:], t0_[:], t0_[:], ALU.mult, [t0_], [t2])
            for hh in range(2):
                P.op("dve", lambda e, hh=hh: e.tensor_reduce(ss[:, hh:hh + 1], t2[:, hh * 64:(hh + 1) * 64], AX.X, ALU.add),
                     [t2], [ss])
            P.act(ss[:, 2:4], ss[:, 0:2], AF.Sqrt, [ss], [ss])
            P.ts(ss[:, 2:4], ss[:, 2:4], 1e-12, None, ALU.max, None, [ss], [ss])
            P.op("dve", lambda e: e.reciprocal(ss[:, 4:6], ss[:, 2:4]), [ss], [ss])
            for hh in range(2):
                hs = slice(hh * 64, (hh + 1) * 64)
                P.ts(kkn[:, hs], t0_[:, hs], ss[:, 4 + hh:5 + hh], -1.0, ALU.mult, ALU.mult, [t0_, ss], [kkn])
            P.stt(bh[:], kkn[:], -1.0, av[:], ALU.mult, ALU.mult, [kkn, av], [bh])
            P.stt(t2[:], av[:], -1.0, VEC[:, 1, :], ALU.add, ALU.mult, [av, VEC], [t2])
            P.stt(key[:], t2[:], 1.0, k_, ALU.add, ALU.mult, [t2, rkv], [key])
            P.tr(bk[2][:, 0:128], r_, ident[:], [rkv, ident], [bk[2]])
            P.tr(bk[2][:, 128:256], kkn[:], ident[:], [kkn, ident], [bk[2]])
            P.tr(bk[2][:, 256:384], bh[:], ident[:], [bh, ident], [bk[2]])
            P.tr(bk[2][:, 384:512], key[:], ident[:], [key, ident], [bk[2]])
            P.mm(bk[3][:, 0:128], sig[:], cS[d][:], True, True, [sig, cS[d]], [bk[3]])
            P.mm(bk[3][:, 128:256], sig[:], cI[d][:], True, True, [sig, cI[d]], [bk[3]])
            P.mm(bk[3][:, 256:384], cS[1 - d][:], sig[:], True, True, [sig, cS[1 - d]], [bk[3]])
            P.act(eG[:], bk[3][:, 0:256], AF.Exp, [bk[3]], [eG])
            P.act(enG[:], bk[3][:, 128:256], AF.Exp, [bk[3]], [enG], scale=-1.0)
            P.act(eR[:], bk[3][:, 256:384], AF.Exp, [bk[3]], [eR])
            P.tt(AR[:, 0:128], bk[2][:, 128:256], eG[:, 0:128], ALU.mult, [bk[2], eG], [AR])
            P.tt(AR[:, 128:256], bk[2][:, 0:128], eG[:, 128:256], ALU.mult, [bk[2], eG], [AR])
            P.tt(BtT[:], bk[2][:, 256:384], enG[:], ALU.mult, [bk[2], enG], [BtT])
            P.tt(KtT[:], bk[2][:, 384:512], enG[:], ALU.mult, [bk[2], enG], [KtT])
            P.tt(Bb[:], bh[:], eR[:], ALU.mult, [bh, eR], [Bb])
            P.tt(Kb[:], key[:], eR[:], ALU.mult, [key, eR], [Kb])
            for hh in range(2):
                hp_ = slice(hh * 64, (hh + 1) * 64)
                P.mm(bk[4][:, 0:256], BtT[hp_, :], AR[hp_, :], True, True, [BtT, AR], [bk[4]])
                P.mm(bk[5][:, 0:256], KtT[hp_, :], AR[hp_, :], True, True, [KtT, AR], [bk[5]])
                P.mm(bk[4][:, 256:384], AR[hp_, 0:128], BtT[hp_, :], True, True, [AR, BtT], [bk[4]])
                P.tt(Mm[hh][:], bk[4][:, 0:256], mSI[d][:], ALU.mult, [bk[4], mSI[d]], [Mm[hh]])
                P.tt(Ak[hh][:], bk[5][:, 0:256], mSI[d][:], ALU.mult, [bk[5], mSI[d]], [Ak[hh]])
                P.tt(Lm[hh][:], bk[4][:, 256:384], mS[1 - d][:], ALU.mult, [bk[4], mS[1 - d]], [Lm[hh]])
                P.tt(TtA[hh][:], Mm[hh][:, 0:128], ident[:], ALU.add, [Mm[hh], ident], [TtA[hh]], eng="pool")
                P.tt(TA[hh][:], Lm[hh][:], ident[:], ALU.add, [Lm[hh], ident], [TA[hh]], eng="pool")
                Pc, Qc = Mm[hh], Lm[hh]
                pc_ap, qc_ap = Mm[hh][:, 0:128], Lm[hh][:]
                for lvl in range(6):
                    last = lvl == 5
                    Pn, Qn = Pp[hh][lvl % 2], Qq[hh][lvl % 2]
                    P.mm(bk[6][:, 0:128], qc_ap, pc_ap, True, True, [Pc, Qc], [bk[6]])
                    if not last:
                        P.mm(bk[6][:, 128:256], pc_ap, qc_ap, True, True, [Pc, Qc], [bk[6]])
                    P.cp(Pn[:], bk[6][:, 0:128], [bk[6]], [Pn])
                    if not last:
                        P.cp(Qn[:], bk[6][:, 128:256], [bk[6]], [Qn], eng="act")
                    P.mm(bk[6][:, 256:384], TA[hh][:], Pn[:], True, True, [TA[hh], Pn], [bk[6]])
                    if not last:
                        P.mm(bk[6][:, 384:512], Pn[:], TA[hh][:], True, True, [TA[hh], Pn], [bk[6]])
                    P.tt(TtA[hh][:], TtA[hh][:], bk[6][:, 256:384], ALU.add, [TtA[hh], bk[6]], [TtA[hh]])
                    if not last:
                        P.tt(TA[hh][:], TA[hh][:], bk[6][:, 384:512], ALU.add, [TA[hh], bk[6]], [TA[hh]])
                    Pc, Qc = Pn, Qn
                    pc_ap, qc_ap = Pn[:], Qn[:]
            P.mm(bk[7][:, 0:128], AR[:, 0:128], STB[:], True, False, [AR, STB], [bk[7]])
            for hh in range(2):
                hs = slice(hh * 64, (hh + 1) * 64)
                P.mm(bk[7][:, hs], Ak[hh][:, 0:128], rkv[:, 256 + hh * 64:256 + (hh + 1) * 64], False, hh == 1,
                     [Ak[hh], rkv], [bk[7]])
            P.cp(X[:], bk[7][:, 0:128], [bk[7]], [X])
            for hh in range(2):
                hs = slice(hh * 64, (hh + 1) * 64)
                P.mm(bk[7][:, 128 + hh * 64:128 + (hh + 1) * 64], TtA[hh][:], X[:, hs], True, True, [TtA[hh], X], [bk[7]])
            P.cp(U[:], bk[7][:, 128:256], [bk[7]], [U])
            P.mm(bk[7][:, 256:384], AR[:, 128:256], STB[:], True, False, [AR, STB], [bk[7]])
            for hh in range(2):
                ys = slice(256 + hh * 64, 256 + (hh + 1) * 64)
                hs = slice(hh * 64, (hh + 1) * 64)
                P.mm(bk[7][:, ys], Mm[hh][:, 128:256], U[:, hs], False, False, [Mm[hh], U], [bk[7]])
                P.mm(bk[7][:, ys], Ak[hh][:, 128:256], rkv[:, 256 + hh * 64:256 + (hh + 1) * 64], False, hh == 1,
                     [Ak[hh], rkv], [bk[7]])
            P.mm(bk[7][:, 384:512], Bb[:], U[:], True, False, [Bb, U], [bk[7]])
            P.mm(bk[7][:, 384:512], Kb[:], v_, False, True, [Kb, rkv], [bk[7]])
            dcol = 255 if d == 0 else 128
            if d == 0:
                P.cp(YF[:, c, :], bk[7][:, 256:384], [bk[7]], [YF], eng="act")
            else:
                P.tt(yb[:], bk[7][:, 256:384], YF[:, c, :], ALU.add, [bk[7], YF], [yb])
            for hh in range(2):
                hs = slice(hh * 64, (hh + 1) * 64)
                P.stt(STB[hs, hs], STB[hs, hs], eG[hs, dcol:dcol + 1], bk[7][hs, 384 + hh * 64:384 + (hh + 1) * 64],
                      ALU.mult, ALU.add, [STB, eG, bk[7]], [STB])
            if d == 1:
                P.mm(bk[0][:, 384:512], sgg[:], G2[:], True, True, [sgg, G2], [bk[0]])
                P.tt(t2[:], yb[:], yb[:], ALU.mult, [yb], [t2])
                for hh in range(2):
                    hs = slice(hh * 64, (hh + 1) * 64)
                    P.op("dve", lambda e, hh=hh, hs=hs: e.tensor_reduce(gn[:, hh:hh + 1], yb[:, hs], AX.X, ALU.add), [yb], [gn])
                    P.op("dve", lambda e, hh=hh, hs=hs: e.tensor_reduce(gn[:, 2 + hh:3 + hh], t2[:, hs], AX.X, ALU.add), [t2], [gn])
                P.ts(gn[:, 4:8], gn[:, 0:4], 1.0 / 64, None, ALU.mult, None, [gn], [gn])
                P.tt(gn[:, 8:10], gn[:, 4:6], gn[:, 4:6], ALU.mult, [gn], [gn])
                P.tt(gn[:, 10:12], gn[:, 6:8], gn[:, 8:10], ALU.subtract, [gn], [gn])
                P.act(gn[:, 12:14], gn[:, 10:12], AF.Sqrt, [gn, epsg], [gn], bias=epsg[:, 0:1])
                P.op("dve", lambda e: e.reciprocal(gn[:, 14:16], gn[:, 12:14]), [gn], [gn])
                for hh in range(2):
                    hs = slice(hh * 64, (hh + 1) * 64)
                    P.ts(yn[:, hs], yb[:, hs], gn[:, 4 + hh:5 + hh], gn[:, 14 + hh:15 + hh], ALU.subtract, ALU.mult,
                         [yb, gn], [yn])
                P.tt(yn[:], yn[:], VEC[:, 3, :], ALU.mult, [yn, VEC], [yn])
                P.tt(yn[:], yn[:], VEC[:, 4, :], ALU.add, [yn, VEC], [yn])
                P.tt(rk[:], r_, k_, ALU.mult, [rkv], [rk])
                P.tt(rk[:], rk[:], VEC[:, 2, :], ALU.mult, [rk, VEC], [rk])
                for hh in range(2):
                    hs = slice(hh * 64, (hh + 1) * 64)
                    P.op("dve", lambda e, hh=hh, hs=hs: e.tensor_reduce(ss[:, 6 + hh:7 + hh], rk[:, hs], AX.X, ALU.add), [rk], [ss])
                    P.stt(yn[:, hs], rkv[:, 256 + hh * 64:256 + (hh + 1) * 64], ss[:, 6 + hh:7 + hh], yn[:, hs], ALU.mult, ALU.add,
                          [rkv, ss, yn], [yn])
                Y = yo[step % 2]
                P.tt(Y[:], yn[:], bk[0][:, 384:512], ALU.mult, [yn, bk[0]], [Y])
                P.tr(bk[3][:, 384:512], Y[:], ident[:], [Y, ident], [bk[3]])
                Yb = yob[step % 2]
                P.cp(Yb[:], bk[3][:, 384:512], [bk[3]], [Yb], eng="act")
                P.dma("sp", ycols(yr, t0), Yb[:], reads=[Yb])
    return P.finish() if own else None


def build_rwkv(P=None, T=None):
    own = P is None
    P = P or Prog()
    hT = P.io(T, "hT", [8, 128, SEQ_ALL], BF16, "ExternalInput")
    wrkv = P.io(T, "wrkv", [D, 384], F32, "ExternalInput")
    crkv = P.io(T, "crkv", [128, 3, 384], F32, "ExternalInput")
    wl = P.io(T, "wl", [D, 384], F32, "ExternalInput")
    w2a2 = P.io(T, "w2a2", [2, 128, 128], F32, "ExternalInput")
    bias01 = P.io(T, "bias01", [1, 2, 256], F32, "ExternalInput")
    g2 = P.io(T, "g2", [128, 128], F32, "ExternalInput")
    vecs = P.io(T, "vecs", [128, 5, 128], F32, "ExternalInput")
    yr = P.io(T, "yr", [128, SEQ_ALL], BF16, "ExternalOutput")

    bk = [P.ps([128, 512], F32, f"bank{i}") for i in range(8)]
    ident = make_ident(P)
    ones = P.sb([128, 128], F32, "ones")
    P.memset(ones[:], 1.0, [ones])
    mI = [aff_mask(P, [[1, 128]], -1, ALU.is_ge), aff_mask(P, [[-1, 128]], 1, ALU.is_ge)]
    mS = [aff_mask(P, [[1, 128]], -1, ALU.is_gt), aff_mask(P, [[-1, 128]], 1, ALU.is_gt)]
    cI = [aff_mask(P, [[1, 128]], -1, ALU.is_ge, W_SCALE), aff_mask(P, [[-1, 128]], 1, ALU.is_ge, W_SCALE)]
    cS = [aff_mask(P, [[1, 128]], -1, ALU.is_gt, W_SCALE), aff_mask(P, [[-1, 128]], 1, ALU.is_gt, W_SCALE)]
    mSI2 = P.sb([128, 2, 2, 256], F32, "mSI2")
    mL4 = P.sb([128, 4, 128], F32, "mL4")
    I4 = P.sb([128, 4, 128], F32, "I4")
    for d in range(2):
        for hh in range(2):
            P.cp(mSI2[:, d, hh, 0:128], mS[d][:], [mS[d]], [mSI2])
            P.cp(mSI2[:, d, hh, 128:256], mI[d][:], [mI[d]], [mSI2])
            P.cp(mL4[:, d * 2 + hh, :], mS[1 - d][:], [mS[1 - d]], [mL4])
            P.cp(I4[:, d * 2 + hh, :], ident[:], [ident], [I4])

    wst = P.sb([128, 8, 768], F32, "wst")
    for kc in range(8):
        P.dma("sp", wst[:, kc, 0:384], wrkv[kc * 128:(kc + 1) * 128, :], writes=[wst])
        P.dma("act", wst[:, kc, 384:768], wl[kc * 128:(kc + 1) * 128, :], writes=[wst])
    cws = P.sb([128, 3, 384], F32, "cws")
    P.dma("sp", cws[:], crkv[:, :, :], writes=[cws])
    Wc = P.sb([128, 3, 8, 384], BF16, "Wc")
    for tap in range(3):
        for kc in range(8):
            P.tt(Wc[:, tap, kc, :], wst[:, kc, 0:384], cws[:, tap, :], ALU.mult, [wst, cws], [Wc])
    Wl = P.sb([128, 8, 384], BF16, "Wl")
    P.cp(Wl[:], wst[:, :, 384:768], [wst], [Wl])
    W2 = P.sb([128, 2, 128], F32, "W2")
    for d in range(2):
        P.dma("act", W2[:, d, :], w2a2[d], writes=[W2])
    B01 = P.sb([1, 2, 256], F32, "B01")
    P.dma("act", B01[:], bias01[:, :, :], writes=[B01])
    G2 = P.sb([128, 128], F32, "G2")
    P.dma("act", G2[:], g2[:, :], writes=[G2])
    VEC = P.sb([128, 5, 128], F32, "VEC")
    P.dma("act", VEC[:], vecs[:, :, :], writes=[VEC])
    VK2 = P.sb([128, 2, 2, 128], F32, "VK2")
    for j in range(2):
        for d in range(2):
            P.cp(VK2[:, j, d, :], VEC[:, j, :], [VEC], [VK2])
    epsg = P.sb([128, 1], F32, "epsg")
    P.memset(epsg[:], R_GN_EPS, [epsg])

    YD = [P.sb([128, SEQT, 128], F32, f"YD{d}") for d in range(2)]
    hb = [P.sb([128, 8, 130], BF16, f"hb{i}") for i in range(2)]
    STB = [P.sb([128, 128], F32, f"STB{d}") for d in range(2)]
    for d in range(2):
        P.memset(STB[d][:], 0.0, [STB[d]])

    def TL(shape, name):
        return P.sb(shape, F32, name)

    rkv = TL([128, 2, 384], "rkv")
    pl = TL([128, 2, 128], "pl")
    sig = TL([128, 2, 128], "sig")
    av = TL([128, 2, 128], "av")
    t0_ = TL([128, 2, 128], "t0_")
    t2 = TL([128, 2, 128], "t2")
    ss = TL([128, 16], "ss")
    kkn = TL([128, 2, 128], "kkn")
    bh = TL([128, 2, 128], "bh")
    key = TL([128, 2, 128], "key")
    eG = TL([128, 2, 256], "eG")
    enG = TL([128, 2, 128], "enG")
    eR = TL([128, 2, 128], "eR")
    AR = TL([128, 2, 256], "AR")
    BtT = TL([128, 2, 128], "BtT")
    KtT = TL([128, 2, 128], "KtT")
    Bb = TL([128, 2, 128], "Bb")
    Kb = TL([128, 2, 128], "Kb")
    MA = TL([128, 2, 2, 256], "MA")
    AK = TL([128, 2, 2, 256], "AK")
    L4 = TL([128, 4, 128], "L4")
    P4 = [TL([128, 4, 128], f"P4{i}") for i in range(2)]
    Q4 = [TL([128, 4, 128], f"Q4{i}") for i in range(2)]
    TtA = TL([128, 4, 128], "TtA")
    TA = TL([128, 4, 128], "TA")
    X2 = TL([128, 2, 128], "X2")
    U2 = TL([128, 2, 128], "U2")

    orders = chunk_orders()
    for step in range(SEQT):
        cc = [orders[0][step], orders[1][step]]
        for d in range(2):
            c = cc[d]
            lo, hi = (0, NCTX) if c < CTXT else (NCTX, SEQ_ALL)
            H = hb[d]
            load_hblk_halo(P, hT, H, c * 128, 128, lo, hi, eng="sp" if d == 0 else "act")
            n = 0
            for tap in range(3):
                for kc in range(8):
                    P.mm(bk[d][:, 0:384], H[:, kc, tap:tap + 128], Wc[:, tap, kc, :], n == 0, n == 23, [H, Wc], [bk[d]])
                    n += 1
            for kc in range(8):
                P.mm(bk[d][:, 384:512], Wl[:, kc, d * 128:(d + 1) * 128], H[:, kc, 1:129], kc == 0, kc == 7, [Wl, H], [bk[d]])
            P.cp(rkv[:, d, :], bk[d][:, 0:384], [bk[d]], [rkv], eng="act")
            P.act(pl[0:64, d, :], bk[d][0:64, 384:512], AF.Tanh, [bk[d]], [pl])
            P.cp(pl[64:128, d, :], bk[d][64:128, 384:512], [bk[d]], [pl])
        for d in range(2):
            P.mm(bk[2][:, d * 256:d * 256 + 128], pl[0:64, d, :], W2[0:64, d, :], True, False, [pl, W2], [bk[2]])
            P.mm(bk[2][:, d * 256:d * 256 + 128], ones[0:1, :], B01[0:1, d, 0:128], False, True, [ones, B01], [bk[2]])
            P.mm(bk[2][:, d * 256 + 128:d * 256 + 256], pl[64:128, d, :], W2[64:128, d, :], True, False, [pl, W2], [bk[2]])
            P.mm(bk[2][:, d * 256 + 128:d * 256 + 256], ones[0:1, :], B01[0:1, d, 128:256], False, True, [ones, B01], [bk[2]])
        b2v = bk[2][:, :].rearrange("p (d j c) -> p d j c", d=2, j=2)
        P.act(sig[:], b2v[:, :, 0, :], AF.Sigmoid, [bk[2]], [sig])
        P.act(av[:], b2v[:, :, 1, :], AF.Sigmoid, [bk[2]], [av])
        k2 = rkv[:, :, 128:256]
        P.tt(t0_[:], k2, VK2[:, 0], ALU.mult, [rkv, VK2], [t0_])
        P.tt(t2[:], t0_[:], t0_[:], ALU.mult, [t0_], [t2])
        P.op("dve", lambda e: e.tensor_reduce(ss[:, 0:4], t2[:, :, :].rearrange("p d (h k) -> p (d h) k", h=2), AX.X, ALU.add),
             [t2], [ss])
        P.act(ss[:, 4:8], ss[:, 0:4], AF.Sqrt, [ss], [ss])
        P.ts(ss[:, 4:8], ss[:, 4:8], 1e-12, None, ALU.max, None, [ss], [ss])
        P.op("dve", lambda e: e.reciprocal(ss[:, 8:12], ss[:, 4:8]), [ss], [ss])
        for d in range(2):
            for hh in range(2):
                hs = slice(hh * 64, (hh + 1) * 64)
                q = d * 2 + hh
                P.ts(kkn[:, d, hs], t0_[:, d, hs], ss[:, 8 + q:9 + q], -1.0, ALU.mult, ALU.mult, [t0_, ss], [kkn])
        P.stt(bh[:], kkn[:], -1.0, av[:], ALU.mult, ALU.mult, [kkn, av], [bh])
        P.stt(t2[:], av[:], -1.0, VK2[:, 1], ALU.add, ALU.mult, [av, VK2], [t2])
        P.stt(key[:], t2[:], 1.0, k2, ALU.add, ALU.mult, [t2, rkv], [key])
        for d in range(2):
            tb = bk[3 + d]
            P.tr(tb[:, 0:128], rkv[:, d, 0:128], ident[:], [rkv, ident], [tb])
            P.tr(tb[:, 128:256], kkn[:, d, :], ident[:], [kkn, ident], [tb])
            P.tr(tb[:, 256:384], bh[:, d, :], ident[:], [bh, ident], [tb])
            P.tr(tb[:, 384:512], key[:, d, :], ident[:], [key, ident], [tb])
            gb_ = bk[5 + d]
            P.mm(gb_[:, 0:128], sig[:, d, :], cS[d][:], True, True, [sig, cS[d]], [gb_])
            P.mm(gb_[:, 128:256], sig[:, d, :], cI[d][:], True, True, [sig, cI[d]], [gb_])
            P.mm(gb_[:, 256:384], cS[1 - d][:], sig[:, d, :], True, True, [sig, cS[1 - d]], [gb_])
        for d in range(2):
            tb, gb_ = bk[3 + d], bk[5 + d]
            P.act(eG[:, d, :], gb_[:, 0:256], AF.Exp, [gb_], [eG])
            P.act(enG[:, d, :], gb_[:, 128:256], AF.Exp, [gb_], [enG], scale=-1.0)
            P.act(eR[:, d, :], gb_[:, 256:384], AF.Exp, [gb_], [eR])
            P.tt(AR[:, d, 0:128], tb[:, 128:256], eG[:, d, 0:128], ALU.mult, [tb, eG], [AR])
            P.tt(AR[:, d, 128:256], tb[:, 0:128], eG[:, d, 128:256], ALU.mult, [tb, eG], [AR])
            P.tt(BtT[:, d, :], tb[:, 256:384], enG[:, d, :], ALU.mult, [tb, enG], [BtT])
            P.tt(KtT[:, d, :], tb[:, 384:512], enG[:, d, :], ALU.mult, [tb, enG], [KtT])
        P.tt(Bb[:], bh[:], eR[:], ALU.mult, [bh, eR], [Bb], eng="pool")
        P.tt(Kb[:], key[:], eR[:], ALU.mult, [key, eR], [Kb], eng="pool")
        for d in range(2):
            for hh in range(2):
                hp_ = slice(hh * 64, (hh + 1) * 64)
                q = d * 2 + hh
                P.mm(bk[d][:, hh * 256:(hh + 1) * 256], BtT[hp_, d, :], AR[hp_, d, :], True, True, [BtT, AR], [bk[d]])
                ab = bk[2] if d == 0 else bk[7]
                P.mm(ab[:, hh * 256:(hh + 1) * 256], KtT[hp_, d, :], AR[hp_, d, :], True, True, [KtT, AR], [ab])
                P.mm(bk[3][:, q * 128:(q + 1) * 128], AR[hp_, d, 0:128], BtT[hp_, d, :], True, True, [AR, BtT], [bk[3]])
        for d in range(2):
            ab = bk[2] if d == 0 else bk[7]
            P.tt(MA[:, d], bk[d][:, :].rearrange("p (h c) -> p h c", h=2), mSI2[:, d], ALU.mult, [bk[d], mSI2], [MA])
            P.tt(AK[:, d], ab[:, :].rearrange("p (h c) -> p h c", h=2), mSI2[:, d], ALU.mult, [ab, mSI2], [AK])
        P.tt(L4[:], bk[3][:, :].rearrange("p (q c) -> p q c", q=4), mL4[:], ALU.mult, [bk[3], mL4], [L4])
        M4 = MA[:, :, :, 0:128].rearrange("p d h c -> p (d h) c")
        P.tt(TtA[:], M4, I4[:], ALU.add, [MA, I4], [TtA], eng="pool")
        P.tt(TA[:], L4[:], I4[:], ALU.add, [L4, I4], [TA], eng="pool")
        Pc_buf, Qc_buf = MA, L4
        pc = lambda q: MA[:, q // 2, q % 2, 0:128]
        qc = lambda q: L4[:, q, :]
        bP, bQ, bT, bTT = bk[4], bk[5], bk[6], bk[0]
        for lvl in range(6):
            last = lvl == 5
            Pn, Qn = P4[lvl % 2], Q4[lvl % 2]
            for q in range(4):
                P.mm(bP[:, q * 128:(q + 1) * 128], qc(q), pc(q), True, True, [Pc_buf, Qc_buf], [bP])
            if not last:
                for q in range(4):
                    P.mm(bQ[:, q * 128:(q + 1) * 128], pc(q), qc(q), True, True, [Pc_buf, Qc_buf], [bQ])
            P.cp(Pn[:], bP[:, :].rearrange("p (q c) -> p q c", q=4), [bP], [Pn])
            if not last:
                P.cp(Qn[:], bQ[:, :].rearrange("p (q c) -> p q c", q=4), [bQ], [Qn], eng="act")
            for q in range(4):
                P.mm(bT[:, q * 128:(q + 1) * 128], TA[:, q, :], Pn[:, q, :], True, True, [TA, Pn], [bT])
            if not last:
                for q in range(4):
                    P.mm(bTT[:, q * 128:(q + 1) * 128], Pn[:, q, :], TA[:, q, :], True, True, [TA, Pn], [bTT])
            P.tt(TtA[:], TtA[:], bT[:, :].rearrange("p (q c) -> p q c", q=4), ALU.add, [TtA, bT], [TtA])
            if not last:
                P.tt(TA[:], TA[:], bTT[:, :].rearrange("p (q c) -> p q c", q=4), ALU.add, [TA, bTT], [TA])
            Pc_buf, Qc_buf = Pn, Qn
            pc = lambda q, Pn=Pn: Pn[:, q, :]
            qc = lambda q, Qn=Qn: Qn[:, q, :]
        bXU, bYS = bk[1], bk[2]
        for d in range(2):
            P.mm(bXU[:, d * 128:(d + 1) * 128], AR[:, d, 0:128], STB[d][:], True, False, [AR, STB[d]], [bXU])
            for hh in range(2):
                o = d * 128 + hh * 64
                P.mm(bXU[:, o:o + 64], AK[:, d, hh, 0:128], rkv[:, d, 256 + hh * 64:256 + (hh + 1) * 64], False, hh == 1,
                     [AK, rkv], [bXU])
        P.cp(X2[:], bXU[:, 0:256].rearrange("p (d c) -> p d c", d=2), [bXU], [X2])
        for d in range(2):
            for hh in range(2):
                o = 256 + d * 128 + hh * 64
                P.mm(bXU[:, o:o + 64], TtA[:, d * 2 + hh, :], X2[:, d, hh * 64:(hh + 1) * 64], True, True, [TtA, X2], [bXU])
        P.cp(U2[:], bXU[:, 256:512].rearrange("p (d c) -> p d c", d=2), [bXU], [U2])
        for d in range(2):
            P.mm(bYS[:, d * 128:(d + 1) * 128], AR[:, d, 128:256], STB[d][:], True, False, [AR, STB[d]], [bYS])
            for hh in range(2):
                o = d * 128 + hh * 64
                hs = slice(hh * 64, (hh + 1) * 64)
                P.mm(bYS[:, o:o + 64], MA[:, d, hh, 128:256], U2[:, d, hs], False, False, [MA, U2], [bYS])
                P.mm(bYS[:, o:o + 64], AK[:, d, hh, 128:256], rkv[:, d, 256 + hh * 64:256 + (hh + 1) * 64], False, hh == 1,
                     [AK, rkv], [bYS])
        for d in range(2):
            P.mm(bYS[:, 256 + d * 128:256 + (d + 1) * 128], Bb[:, d, :], U2[:, d, :], True, False, [Bb, U2], [bYS])
            P.mm(bYS[:, 256 + d * 128:256 + (d + 1) * 128], Kb[:, d, :], rkv[:, d, 256:384], False, True, [Kb, rkv], [bYS])
        for d in range(2):
            P.cp(YD[d][:, cc[d], :], bYS[:, d * 128:(d + 1) * 128], [bYS], [YD[d]], eng="act")
            dcol = 255 if d == 0 else 128
            for hh in range(2):
                hs = slice(hh * 64, (hh + 1) * 64)
                o = 256 + d * 128 + hh * 64
                P.stt(STB[d][hs, hs], STB[d][hs, hs], eG[hs, d, dcol:dcol + 1], bYS[hs, o:o + 64], ALU.mult, ALU.add,
                      [STB[d], eG, bYS], [STB[d]])

    rk2 = [TL([128, 384], f"rk2{i}") for i in range(2)]
    sgg = [TL([128, 128], f"sgg{i}") for i in range(2)]
    ybs = [TL([128, 128], f"ybs{i}") for i in range(2)]
    tq = [TL([128, 128], f"tq{i}") for i in range(2)]
    gn = [TL([128, 16], f"gn{i}") for i in range(2)]
    yn = [TL([128, 128], f"yn{i}") for i in range(2)]
    rk = [TL([128, 128], f"rk{i}") for i in range(2)]
    yo = [TL([128, 128], f"yo{i}") for i in range(2)]
    yob = [P.sb([128, 128], BF16, f"yob{i}") for i in range(2)]
    for c in range(SEQT):
        pi = c % 2
        lo, hi = (0, NCTX) if c < CTXT else (NCTX, SEQ_ALL)
        H = hb[pi]
        load_hblk_halo(P, hT, H, c * 128, 128, lo, hi)
        b0, b1 = bk[pi * 4], bk[pi * 4 + 1]
        n = 0
        for tap in range(3):
            for kc in range(8):
                P.mm(b0[:, 0:384], H[:, kc, tap:tap + 128], Wc[:, tap, kc, :], n == 0, n == 23, [H, Wc], [b0])
                n += 1
        for kc in range(8):
            P.mm(b1[:, 0:128], Wl[:, kc, 256:384], H[:, kc, 1:129], kc == 0, kc == 7, [Wl, H], [b1])
        R_ = rk2[pi]
        P.cp(R_[:], b0[:, 0:384], [b0], [R_], eng="act")
        P.act(sgg[pi][:], b1[:, 0:128], AF.Sigmoid, [b1], [sgg[pi]])
        P.mm(b1[:, 128:256], sgg[pi][:], G2[:], True, True, [sgg[pi], G2], [b1])
        yb, t2_, g_, y_ = ybs[pi], tq[pi], gn[pi], yn[pi]
        P.tt(yb[:], YD[0][:, c, :], YD[1][:, c, :], ALU.add, [YD[0], YD[1]], [yb], eng="pool")
        P.tt(t2_[:], yb[:], yb[:], ALU.mult, [yb], [t2_], eng="pool")
        P.op("dve", lambda e, g_=g_, yb=yb: e.tensor_reduce(g_[:, 0:2], yb[:, :].rearrange("p (h k) -> p h k", h=2), AX.X, ALU.add),
             [yb], [g_])
        P.op("dve", lambda e, g_=g_, t2_=t2_: e.tensor_reduce(g_[:, 2:4], t2_[:, :].rearrange("p (h k) -> p h k", h=2), AX.X, ALU.add),
             [t2_], [g_])
        P.ts(g_[:, 4:8], g_[:, 0:4], 1.0 / 64, None, ALU.mult, None, [g_], [g_])
        P.tt(g_[:, 8:10], g_[:, 4:6], g_[:, 4:6], ALU.mult, [g_], [g_])
        P.tt(g_[:, 10:12], g_[:, 6:8], g_[:, 8:10], ALU.subtract, [g_], [g_])
        P.act(g_[:, 12:14], g_[:, 10:12], AF.Sqrt, [g_, epsg], [g_], bias=epsg[:, 0:1])
        P.op("dve", lambda e, g_=g_: e.reciprocal(g_[:, 14:16], g_[:, 12:14]), [g_], [g_])
        for hh in range(2):
            hs = slice(hh * 64, (hh + 1) * 64)
            P.ts(y_[:, hs], yb[:, hs], g_[:, 4 + hh:5 + hh], g_[:, 14 + hh:15 + hh], ALU.subtract, ALU.mult, [yb, g_], [y_])
        P.tt(y_[:], y_[:], VEC[:, 3, :], ALU.mult, [y_, VEC], [y_])
        P.tt(y_[:], y_[:], VEC[:, 4, :], ALU.add, [y_, VEC], [y_])
        rk_ = rk[pi]
        P.tt(rk_[:], R_[:, 0:128], R_[:, 128:256], ALU.mult, [R_], [rk_], eng="pool")
        P.tt(rk_[:], rk_[:], VEC[:, 2, :], ALU.mult, [rk_, VEC], [rk_], eng="pool")
        P.op("dve", lambda e, g_=g_, rk_=rk_: e.tensor_reduce(g_[:, 0:2], rk_[:, :].rearrange("p (h k) -> p h k", h=2), AX.X, ALU.add),
             [rk_], [g_])
        for hh in range(2):
            hs = slice(hh * 64, (hh + 1) * 64)
            P.stt(y_[:, hs], R_[:, 256 + hh * 64:256 + (hh + 1) * 64], g_[:, hh:hh + 1], y_[:, hs], ALU.mult, ALU.add,
                  [R_, g_, y_], [y_])
        Y = yo[pi]
        P.tt(Y[:], y_[:], b1[:, 128:256], ALU.mult, [y_, b1], [Y])
        P.tr(b1[:, 256:384], Y[:], ident[:], [Y, ident], [b1])
        Yb = yob[pi]
        P.cp(Yb[:], b1[:, 256:384], [b1], [Yb], eng="act")
        P.dma("sp", ycols(yr, c * 128), Yb[:], reads=[Yb])
    return P.finish() if own else None


def rwkv_inputs(hTs, li, w_in, r_conv, r_w0, r_w2, r_a0, r_a2, r_g2, r_kk, r_ka, r_rk, r_ln_w, r_ln_b):
    off = col_offsets()
    ins = []
    W = w_in[li]
    for b in range(2):
        hT = np.ascontiguousarray(hTs[b].reshape(8, 128, SEQ_ALL))
        for hp in range(4):
            cs_ = slice(128 * hp, 128 * (hp + 1))
            wrkv = np.concatenate([W[:, off[n] + 128 * hp: off[n] + 128 * (hp + 1)] for n in ('r_r', 'r_k', 'r_v')], 1)
            conv = np.concatenate([r_conv[li][:, j * 512 + 128 * hp: j * 512 + 128 * (hp + 1)] for j in range(3)], 1)
            wl = np.concatenate([W[:, off['r_wf']:off['r_wf'] + 64], W[:, off['r_af']:off['r_af'] + 64],
                                 W[:, off['r_wb']:off['r_wb'] + 64], W[:, off['r_ab']:off['r_ab'] + 64],
                                 W[:, off['r_g']:off['r_g'] + 128]], 1)
            w2a2 = np.stack([np.concatenate([r_w2[li, d][:, cs_], r_a2[li, d][:, cs_]], 0) for d in range(2)], 0)
            bias01 = np.stack([np.concatenate([r_w0[li, d][cs_], r_a0[li, d][cs_]], 0) for d in range(2)], 0)[None]
            vecs = np.stack([np.broadcast_to(v[li][None, cs_], (128, 128)) for v in (r_kk, r_ka, r_rk, r_ln_w, r_ln_b)], 1)
            ins.append({"hT": hT, "wrkv": np.ascontiguousarray(wrkv),
                        "crkv": np.ascontiguousarray(np.broadcast_to(conv[None], (128, 3, 384)).astype(np.float32)),
                        "wl": np.ascontiguousarray(wl), "w2a2": np.ascontiguousarray(w2a2.astype(np.float32)),
                        "bias01": np.ascontiguousarray(bias01.astype(np.float32)),
                        "g2": np.ascontiguousarray(r_g2[li][:, cs_]), "vecs": np.ascontiguousarray(vecs.astype(np.float32))})
    return ins


def build_merge(P=None, T=None, fused=False):
    own = P is None
    P = P or Prog()
    x = P.io(T, "x", [NT, 128, D], F32, "ExternalInput")
    hT = P.io(T, "hT", [8, 128, TOK], BF16, "ExternalInput")
    if fused:
        yall = T["yall"]
        sel = T["sel"]
    else:
        yT = P.io(T, "yT", [12, 128, TOK], BF16, "ExternalInput")
    wg = P.io(T, "wg", [D, 3 * D], F32, "ExternalInput")
    wb = P.io(T, "wb", [3, 512, D], F32, "ExternalInput")
    wo = P.io(T, "wo", [D, D], F32, "ExternalInput")
    gateb = P.io(T, "gateb", [2, 128, D], F32, "ExternalInput")
    xo = P.io(T, "xo", [NT, 128, D], F32, "ExternalOutput")

    Wg = P.sb([128, 8, 3 * D], BF16, "Wg")
    Pb = P.sb([128, 12, D], BF16, "Pb")
    Wo = P.sb([128, 8, D], BF16, "Wo")
    stage = [P.sb([128, 1024], F32, f"stg{i}") for i in range(2)]
    n = 0
    for kc in range(8):
        for q in range(3):
            load_cast(P, Wg, Wg[:, kc, q * 1024:(q + 1) * 1024], wg[kc * 128:(kc + 1) * 128, q * 1024:(q + 1) * 1024],
                      stage, None, n, 1024)
            n += 1
    for br in range(3):
        for c in range(4):
            load_cast(P, Pb, Pb[:, br * 4 + c, :], wb[br, c * 128:(c + 1) * 128, :], stage, None, n, 1024)
            n += 1
    for kc in range(8):
        load_cast(P, Wo, Wo[:, kc, :], wo[kc * 128:(kc + 1) * 128, :], stage, None, n, 1024)
        n += 1
    gates = P.sb([128, 2, D], F32, "gates")
    for s in range(2):
        P.dma("act", gates[:, s, :], gateb[s], writes=[gates])

    X = P.sb([128, 3, D], F32, "X")
    hb = P.sb([128, 8, 384], BF16, "hb")
    yb = P.sb([128, 12, 384], BF16, "yb")
    if fused:
        yc = [P.sb([128, 12, 384], BF16, f"yc{i}") for i in range(4)]
        sels = P.sb([128, 4], F32, "sels")
        P.dma("act", sels[:], sel[:, :], writes=[sels])

    zT = P.sb([128, 8, 384], BF16, "zT")
    sg = [P.sb([128, 384], F32, f"sg{i}") for i in range(2)]
    za = P.sb([128, 384], F32, "za")
    tm = [P.sb([128, 384], F32, f"tm{i}") for i in range(2)]
    tmp = [P.sb([128, 512], F32, f"tmp{i}") for i in range(2)]
    pg = [P.ps([128, 512], F32, f"pg{i}") for i in range(2)]
    pp = [P.ps([128, 512], F32, f"pp{i}") for i in range(2)]
    py = [P.ps([128, 512], F32, f"py{i}") for i in range(2)]

    k = 0
    for bi, (t0, nb) in enumerate(BLOCKS):
        s = 0 if bi == 0 else 1
        N = nb * 128
        for i in range(nb):
            P.dma("sp", X[:, i, :], x[t0 + i], writes=[X])
        for kc in range(8):
            P.dma("act", hb[:, kc, 0:N], hT[kc, :, t0 * 128:t0 * 128 + N], writes=[hb])
        if fused:
            for cand in range(4):
                for c in range(12):
                    j, br = c % 4, c // 4
                    P.dma("sp" if c % 2 else "act", yc[cand][:, c, 0:N],
                          yall[br, cand, j * 128:(j + 1) * 128, t0 * 128:t0 * 128 + N], writes=[yc[cand]])
            P.ts(yb[:, :, 0:N], yc[0][:, :, 0:N], sels[:, 0:1], None, ALU.mult, None, [yc[0], sels], [yb])
            for cand in range(1, 4):
                P.stt(yb[:, :, 0:N], yc[cand][:, :, 0:N], sels[:, cand:cand + 1], yb[:, :, 0:N], ALU.mult, ALU.add,
                      [yc[cand], sels, yb], [yb])
        else:
            for c in range(12):
                P.dma("sp" if c % 2 else "act", yb[:, c, 0:N], yT[c, :, t0 * 128:t0 * 128 + N], writes=[yb])
        for dc in range(8):
            for br in range(3):
                G, Q = pg[k % 2], pp[k % 2]
                S, T_ = sg[k % 2], tm[k % 2]
                k += 1
                for kc in range(8):
                    P.mm(G[:, 0:N], Wg[:, kc, br * D + dc * 128: br * D + (dc + 1) * 128], hb[:, kc, 0:N], kc == 0, kc == 7,
                         [Wg, hb], [G])
                for c in range(4):
                    P.mm(Q[:, 0:N], Pb[:, br * 4 + c, dc * 128:(dc + 1) * 128], yb[:, br * 4 + c, 0:N], c == 0, c == 3,
                         [Pb, yb], [Q])
                P.act(S[:, 0:N], G[:, 0:N], AF.Sigmoid, [G], [S])
                if br == 0:
                    P.tt(za[:, 0:N], S[:, 0:N], Q[:, 0:N], ALU.mult, [S, Q], [za])
                else:
                    P.tt(T_[:, 0:N], S[:, 0:N], Q[:, 0:N], ALU.mult, [S, Q], [T_])
                    if br == 1:
                        P.tt(za[:, 0:N], za[:, 0:N], T_[:, 0:N], ALU.add, [za, T_], [za], eng="pool")
                    else:
                        P.tt(zT[:, dc, 0:N], za[:, 0:N], T_[:, 0:N], ALU.add, [za, T_], [zT])
        for i in range(nb):
            for h in range(2):
                Y = py[(i * 2 + h) % 2]
                for dc in range(8):
                    P.mm(Y[:, :], zT[:, dc, i * 128:(i + 1) * 128], Wo[:, dc, h * 512:(h + 1) * 512], dc == 0, dc == 7,
                         [zT, Wo], [Y])
                T2 = tmp[(i * 2 + h) % 2]
                P.tt(T2[:], Y[:], gates[:, s, h * 512:(h + 1) * 512], ALU.mult, [Y, gates], [T2])
                P.tt(X[:, i, h * 512:(h + 1) * 512], X[:, i, h * 512:(h + 1) * 512], T2[:], ALU.add, [X, T2], [X], eng="pool")
            P.dma("sp", xo[t0 + i], X[:, i, :], reads=[X])
    return P.finish() if own else None


def featT_shard(y_b, nchunk):
    pad = np.zeros((4 * TOK, y_b.shape[1]), y_b.dtype)
    pad[:y_b.shape[0]] = y_b
    out = []
    for i in range(4):
        blk = pad[i * TOK:(i + 1) * TOK]
        out.append(np.ascontiguousarray(blk.T.reshape(nchunk, 128, TOK)))
    return out


def merge_inputs(xs, hTs, ymT, yrT, yaT, mods_l, li, w_in, w_branch, w_o):
    off = col_offsets()
    m = mods_l.reshape(3, 9, D)
    wg = np.ascontiguousarray(w_in[li][:, off['g_m']:off['g_m'] + 3 * D])
    ins = []
    for b in range(2):
        xsh = tok_shard(xs[b])
        hsh = featT_shard(np.ascontiguousarray(hTs[b].T), 8)
        yfull = np.concatenate([ymT[b], yrT[b], yaT[b]], axis=0)
        ypad = np.zeros((1536, 4 * TOK), yfull.dtype)
        ypad[:, :SEQ_ALL] = yfull
        for i in range(4):
            gb = np.stack([np.broadcast_to(m[2 if i == 0 else b, 5], (128, D)), np.broadcast_to(m[b, 5], (128, D))], 0)
            ins.append({"x": xsh[i], "hT": hsh[i],
                        "yT": np.ascontiguousarray(ypad[:, i * TOK:(i + 1) * TOK].reshape(12, 128, TOK)),
                        "wg": wg, "wb": w_branch[li], "wo": w_o[li], "gateb": np.ascontiguousarray(gb)})
    return ins


def emit_mod_fused(P, T):
    cT = T["cT"]
    ones = P.sb([128, 128], F32, "ones")
    P.memset(ones[:], 1.0, [ones])
    cs = P.sb([128, 8, 2], F32, "cs")
    sg = P.sb([128, 8, 2], F32, "sg")
    P.dma("sp", cs[:], cT[:, :, :], writes=[cs])
    P.act(sg[:], cs[:], AF.Sigmoid, [cs], [sg])
    P.tt(cs[:], cs[:], sg[:], ALU.mult, [cs, sg], [cs])
    crep = [P.sb([128, 8, 128], F32, f"crep{i}") for i in range(2)]
    for st in range(2):
        for kc in range(8):
            P.ts(crep[st][:, kc, :], ones[:], cs[:, kc, st:st + 1], None, ALU.mult, None, [ones, cs], [crep[st]])
    Wk = [P.sb([128, 8, 1024], F32, f"Wk{i}") for i in range(2)]
    pm = [P.ps([128, 512], F32, f"pm{i}") for i in range(2)]
    pg = [P.ps([128, 512], F32, f"pgm{i}") for i in range(2)]
    n = 0
    for l in range(2):
        bpps = P.sb([128, 72], F32, f"bpps{l}")
        P.dma("act", bpps[:], T[f"bpp{l}"][:, :], writes=[bpps])
        bgbs = P.sb([128, 3, 1024], F32, f"bgbs{l}")
        for gi in range(3):
            P.dma("act", bgbs[:, gi, :], T[f"bgb{l}"][gi], writes=[bgbs])
        ngs = P.sb([128, 24], F32, f"ngs{l}")
        P.dma("act", ngs[:], T[f"ng{l}"][:, :], writes=[ngs])
        MODT = P.sb([128, 2, 72], F32, f"MODT{l}")
        for k in range(9):
            W = Wk[n % 2]
            for kc in range(8):
                P.dma("sp", W[:, kc, :], T[f"wada{l}"][kc * 128:(kc + 1) * 128, k * 1024:(k + 1) * 1024], writes=[W])
            ps = pm[n % 2]
            for c in range(8):
                for kc in range(8):
                    P.mm(ps[:, 2 * c:2 * c + 2], W[:, kc, c * 128:(c + 1) * 128], cs[:, kc, :], kc == 0, kc == 7, [W, cs], [ps])
            for st in range(2):
                P.tt(MODT[:, st, k * 8:(k + 1) * 8], ps[:, st:16:2], bpps[:, k * 8:(k + 1) * 8], ALU.add, [ps, bpps], [MODT])
            if k in (2, 5, 8):
                gi = (2, 5, 8).index(k)
                GB = P.sb([128, 2, 1024], F32, f"GB{l}{gi}")
                for st in range(2):
                    for h in range(2):
                        pq = pg[(st * 2 + h) % 2]
                        for kc in range(8):
                            P.mm(pq[:, :], crep[st][:, kc, :], W[:, kc, h * 512:(h + 1) * 512], kc == 0, kc == 7, [crep[st], W], [pq])
                        P.tt(GB[:, st, h * 512:(h + 1) * 512], pq[:, :], bgbs[:, gi, h * 512:(h + 1) * 512], ALU.add, [pq, bgbs], [GB])
                    P.dma("act", T[f"gb{l}_{gi}"][st], GB[:, st, :], reads=[GB])
            n += 1
        for which, ks in ((0, (0, 1, None, 3, 4, None)), (1, (6, 7, None, None, None, None))):
            PP = P.sb([128, 96], F32, f"PP{l}{which}")
            P.memset(PP[:], 0.0, [PP])
            for st in range(2):
                for j, k in enumerate(ks):
                    dst = PP[:, st * 48 + j * 8: st * 48 + (j + 1) * 8]
                    if k is not None:
                        P.cp(dst, MODT[:, st, k * 8:(k + 1) * 8], [MODT], [PP])
                    elif j == 2:
                        gidx = 0 if which == 0 else 2
                        P.cp(dst, ngs[:, gidx * 8:(gidx + 1) * 8], [ngs], [PP])
                    elif j == 5 and which == 0:
                        P.cp(dst, ngs[:, 8:16], [ngs], [PP])
            P.dma("sp", T[f"pp{l}_{which}"][:, :], PP[:], reads=[PP])


def build_fused(upto=99, dbg=None):
    P = Prog()
    E = lambda name, shape, dt=F32: P.dram(name, shape, dt, "ExternalInput")
    x0 = E("x0", [NT, 128, D])
    out = P.dram("out", [NT, 128, D], F32, "ExternalOutput")
    sel = E("sel", [128, 4])
    Tm = {"cT": E("cT", [128, 8, 2])}
    pp, gb = {}, {}
    for l in range(2):
        Tm[f"wada{l}"] = E(f"wada{l}", [D, 9 * D])
        Tm[f"bpp{l}"] = E(f"bpp{l}", [128, 72])
        Tm[f"bgb{l}"] = E(f"bgb{l}", [3, 128, D])
        Tm[f"ng{l}"] = E(f"ng{l}", [128, 24])
        for w in range(2):
            pp[l, w] = Tm[f"pp{l}_{w}"] = P.idram([128, 96], F32, f"pp{l}_{w}")
        for gi in range(3):
            gb[l, gi] = Tm[f"gb{l}_{gi}"] = P.idram([2, 128, D], F32, f"gb{l}_{gi}")
    a_cs = E("a_cs", [2, 128, 8192])
    a_cmat = E("a_cmat", [2, 128, 128])
    emit_mod_fused(P, Tm)
    P.end_stage()
    groups = [[0, 1, 2, 3], [4, 5, 6, 7]]
    dummy = Buf(None, "coll")

    def done(src3):
        t = P.sb([128, D], F32, "dbgt")
        for i in range(NT):
            P.dma("sp", t[:], src3[i], writes=[t])
            P.dma("sp", out[i], t[:], reads=[t])
        return P.finish()

    xcur = x0
    for l in range(2):
        x1 = P.idram([NT, 128, D], F32, f"x1_{l}")
        hsrc = P.idram([8, 128, TOK], BF16, f"hsrc{l}")
        h3 = hsrc
        build_ffn(True, P, {"x": xcur, "wgu": E(f"w1gu{l}", [D, 2 * DFF]), "wd": E(f"w1d{l}", [DFF, D]),
                            "pp": pp[l, 0], "gateb": gb[l, 0], "xo": x1, "ho": h3})
        P.end_stage()
        if upto == 10 * l + 1:
            return done(x1)
        hall = P.idram([8, 4 * 128, TOK], BF16, f"hall{l}")
        hall.gath = True
        for kc in range(8):
            P.coll("AllGather", hsrc[kc], hall[kc], groups, writes=[dummy])
        P.end_stage()
        if upto == 10 * l + 5:
            return done(x1)
        ysrc = P.idram([3, 4, 128, TOK], BF16, f"ysrc{l}")
        zt = P.sb([128, 4 * TOK - SEQ_ALL], BF16, "zt")
        P.memset(zt[:], 0.0, [zt])
        for br in range(3):
            P.dma("act", ysrc[br, 3, :, SEQ_ALL - 3 * TOK:TOK], zt[:], reads=[zt])
        yb_ = []
        for br in range(3):
            yb_.append(Buf(ysrc[br], f"yout{br}"))
            yb_[-1].gath = True
        build_attn(False, P, {"hT": hall, "wqkv": E(f"a_wqkv{l}", [3, D, 128]), "gqk": E(f"a_gqk{l}", [128, 2]),
                              "cs": a_cs, "cmat": a_cmat, "lamp": E(f"a_lamp{l}", [128, 258]),
                              "subg": E(f"a_subg{l}", [128, 128]), "ya": yb_[2]})
        P.end_stage()
        build_mlstm(99, P, {"hT": hall, "wqk": E(f"m_wqk{l}", [2, D, 64]), "wvo": E(f"m_wvo{l}", [D, 256]),
                            "wg": E(f"m_wg{l}", [D, 4]), "cw": E(f"m_cw{l}", [2, 128, 3, 64]), "gb": E(f"m_gb{l}", [128, 4]),
                            "og": E(f"m_og{l}", [128, 128]), "ym": yb_[0]})
        P.end_stage()
        build_rwkv(P, {"hT": hall, "wrkv": E(f"r_wrkv{l}", [D, 384]), "crkv": E(f"r_crkv{l}", [128, 3, 384]),
                       "wl": E(f"r_wl{l}", [D, 384]), "w2a2": E(f"r_w2a2{l}", [2, 128, 128]),
                       "bias01": E(f"r_bias01{l}", [1, 2, 256]), "g2": E(f"r_g2{l}", [128, 128]),
                       "vecs": E(f"r_vecs{l}", [128, 5, 128]), "yr": yb_[1]})
        P.end_stage()
        yall = P.idram([3, 4, 4 * 128, TOK], BF16, f"yall{l}")
        for br in range(3):
            for q in range(4):
                P.coll("AllGather", ysrc[br, q], yall[br, q], groups, writes=[dummy])
        P.end_stage()
        x2 = P.idram([NT, 128, D], F32, f"x2_{l}")
        build_merge(P, {"x": x1, "hT": h3, "yall": yall, "sel": sel, "wg": E(f"g_wg{l}", [D, 3 * D]),
                        "wb": E(f"g_wb{l}", [3, 512, D]), "wo": E(f"g_wo{l}", [D, D]), "gateb": gb[l, 1], "xo": x2}, fused=True)
        P.end_stage()
        if upto == 10 * l + 2:
            return done(x2)
        x3 = out if l == 1 else P.idram([NT, 128, D], F32, f"x3_{l}")
        build_ffn(False, P, {"x": x2, "wgu": E(f"w2gu{l}", [D, 2 * DFF]), "wd": E(f"w2d{l}", [DFF, D]),
                             "pp": pp[l, 1], "gateb": gb[l, 2], "xo": x3})
        P.end_stage()
        if upto == 10 * l + 3 and l == 0:
            return done(x3)
        xcur = x3
    return P.finish()


def fused_inputs(x, c, ctx, c_ctx, w_ada, b_ada, norm_g, ffn1_w_gu, ffn1_w_down, ffn2_w_gu, ffn2_w_down,
                 w_in, m_conv, m_gate_bias, m_out_norm, r_conv, r_w0, r_w2, r_a0, r_a2, r_g2, r_kk, r_ka,
                 r_rk, r_ln_w, r_ln_b, a_qk_norm, a_lambda, a_subln, w_branch, w_o):
    C = np.ascontiguousarray
    xs = [np.concatenate([ctx[b], x[b]], 0) for b in range(2)]
    dummy_h = [np.zeros((D, SEQ_ALL), NPBF) for _ in range(2)]
    off = col_offsets()
    per = [dict() for _ in range(8)]
    shards = [tok_shard(xs[b]) for b in range(2)]
    cs_tab, cm = rope_tables(), attn_consts()
    for core in range(8):
        b, i = core // 4, core % 4
        d = per[core]
        d["x0"] = shards[b][i]
        sel = np.zeros((128, 4), np.float32)
        sel[:, i] = 1.0
        d["sel"] = sel
        cA = c_ctx if i == 0 else c[b]
        d["cT"] = C(np.stack([cA, c[b]], 0).reshape(2, 8, 128).transpose(2, 1, 0).astype(np.float32))
        d["a_cs"], d["a_cmat"] = cs_tab, cm
    for l in range(2):
        bpp = C(b_ada[l].reshape(9, 8, 128).transpose(2, 0, 1).reshape(128, 72))
        bgb = C(np.stack([np.broadcast_to(b_ada[l].reshape(9, D)[k][None], (128, D)) for k in (2, 5, 8)], 0))
        ng = C(norm_g[l].reshape(3, 8, 128).transpose(2, 0, 1).reshape(128, 24))
        ai = attn_inputs(dummy_h, l, w_in, a_qk_norm, a_lambda, a_subln)
        mi = mlstm_inputs(dummy_h, l, w_in, m_conv, m_gate_bias, m_out_norm)
        ri = rwkv_inputs(dummy_h, l, w_in, r_conv, r_w0, r_w2, r_a0, r_a2, r_g2, r_kk, r_ka, r_rk, r_ln_w, r_ln_b)
        wg = C(w_in[l][:, off['g_m']:off['g_m'] + 3 * D])
        for core in range(8):
            d = per[core]
            d[f"wada{l}"] = C(w_ada[l]); d[f"bpp{l}"] = bpp; d[f"bgb{l}"] = bgb; d[f"ng{l}"] = ng
            d[f"w1gu{l}"] = C(ffn1_w_gu[l]); d[f"w1d{l}"] = C(ffn1_w_down[l])
            d[f"w2gu{l}"] = C(ffn2_w_gu[l]); d[f"w2d{l}"] = C(ffn2_w_down[l])
            for k in ("wqkv", "gqk", "lamp", "subg"):
                d[f"a_{k}{l}"] = ai[core][k]
            for k in ("wqk", "wvo", "wg", "cw", "gb", "og"):
                d[f"m_{k}{l}"] = mi[core][k]
            for k in ("wrkv", "crkv", "wl", "w2a2", "bias01", "g2", "vecs"):
                d[f"r_{k}{l}"] = ri[core][k]
            d[f"g_wg{l}"] = wg; d[f"g_wb{l}"] = C(w_branch[l]); d[f"g_wo{l}"] = C(w_o[l])
    return per


def kernel(**inputs):
    f = {k: np.asarray(v, dtype=np.float32) for k, v in inputs.items()}
    nc = _prog("fused", build_fused)
    res = _run(nc, fused_inputs(**f))
    xs = tok_unshard(res, "out")
    return np.stack([xs[b][NCTX:] for b in range(2)], 0).astype(np.float32)


_PROGS = {}


def _prog(name, fn):
    if name not in _PROGS:
        _PROGS[name] = fn()
    return _PROGS[name]


def _run(nc, ins):
    return run_bass_kernel_spmd(nc, ins, core_ids=list(range(8))).results


def kernel_unfused(x, c, ctx, c_ctx, w_ada, b_ada, norm_g, ffn1_w_gu, ffn1_w_down, ffn2_w_gu, ffn2_w_down,
           w_in, m_conv, m_gate_bias, m_out_norm, r_conv, r_w0, r_w2, r_a0, r_a2, r_g2, r_kk, r_ka,
           r_rk, r_ln_w, r_ln_b, a_qk_norm, a_lambda, a_subln, w_branch, w_o):
    f = lambda a: np.asarray(a, dtype=np.float32)
    (x, c, ctx, c_ctx, w_ada, b_ada, norm_g, ffn1_w_gu, ffn1_w_down, ffn2_w_gu, ffn2_w_down, w_in, m_conv, m_gate_bias,
     m_out_norm, r_conv, r_w0, r_w2, r_a0, r_a2, r_g2, r_kk, r_ka, r_rk, r_ln_w, r_ln_b, a_qk_norm, a_lambda, a_subln,
     w_branch, w_o) = map(f, (x, c, ctx, c_ctx, w_ada, b_ada, norm_g, ffn1_w_gu, ffn1_w_down, ffn2_w_gu, ffn2_w_down, w_in,
                              m_conv, m_gate_bias, m_out_norm, r_conv, r_w0, r_w2, r_a0, r_a2, r_g2, r_kk, r_ka, r_rk,
                              r_ln_w, r_ln_b, a_qk_norm, a_lambda, a_subln, w_branch, w_o))
    mods = run_mod(c, c_ctx, w_ada, b_ada)
    xs = [np.concatenate([ctx[b], x[b]], 0) for b in range(2)]
    for li in range(2):
        res = _run(_prog("ffn_h", lambda: build_ffn(True)),
                   ffn_inputs(xs, mods[li], li, 1, norm_g, np.ascontiguousarray(ffn1_w_gu[li]), np.ascontiguousarray(ffn1_w_down[li]), True))
        xs = tok_unshard(res, "xo")
        hTs = hT_unshard(res, "ho")
        ra = _run(_prog("attn", build_attn), attn_inputs(hTs, li, w_in, a_qk_norm, a_lambda, a_subln))
        rm = _run(_prog("mlstm", build_mlstm), mlstm_inputs(hTs, li, w_in, m_conv, m_gate_bias, m_out_norm))
        rr = _run(_prog("rwkv", build_rwkv), rwkv_inputs(hTs, li, w_in, r_conv, r_w0, r_w2, r_a0, r_a2, r_g2, r_kk, r_ka,
                                                          r_rk, r_ln_w, r_ln_b))
        yas = [np.concatenate([ra[b * 4 + h]["ya"] for h in range(4)], axis=0) for b in range(2)]
        yms = [np.concatenate([rm[b * 4 + h]["ym"] for h in range(4)], axis=0) for b in range(2)]
        yrs = [np.concatenate([rr[b * 4 + h]["yr"] for h in range(4)], axis=0) for b in range(2)]
        res = _run(_prog("merge", build_merge), merge_inputs(xs, hTs, yms, yrs, yas, mods[li], li, w_in, w_branch, w_o))
        xs = tok_unshard(res, "xo")
        res = _run(_prog("ffn", lambda: build_ffn(False)),
                   ffn_inputs(xs, mods[li], li, 2, norm_g, np.ascontiguousarray(ffn2_w_gu[li]), np.ascontiguousarray(ffn2_w_down[li]), False))
        xs = tok_unshard(res, "xo")
    return np.stack([xs[b][NCTX:] for b in range(2)], 0).astype(np.float32)
```

```python
import contextlib
import numpy as np
import ml_dtypes
import concourse.bass as bass
import concourse.mybir as mybir
from concourse.bass_utils import run_bass_kernel_spmd

F32 = mybir.dt.float32
BF16 = mybir.dt.bfloat16
AF = mybir.ActivationFunctionType
ALU = mybir.AluOpType
AX = mybir.AxisListType
NPBF = ml_dtypes.bfloat16

ENGS = ("pe", "dve", "act", "pool", "sp")


class Buf:
    __slots__ = ("t", "w", "r", "name", "ds", "psum", "gath")

    def __init__(self, t=None, name=""):
        self.t = t
        self.ds = None
        self.psum = False
        self.gath = False
        self.w = None
        self.r = []
        self.name = name

    def __getitem__(self, idx):
        return self.t[idx]


class Sem:
    __slots__ = ("h", "count")

    def __init__(self, h):
        self.h = h
        self.count = 0


class Prog:
    def __init__(self, name="k"):
        self.nc = bass.Bass("TRN2", target_bir_lowering=False)
        self.es = contextlib.ExitStack()
        self.ss = contextlib.ExitStack()
        self.q = {e: [] for e in ENGS}
        self.esem = {}
        for e in ENGS:
            self.esem[e] = Sem(self.es.enter_context(self.nc.semaphore(f"s_{e}")))
        self.seen = {e: {} for e in ENGS}
        self.dsems = []
        self.free_ds = []
        self.stage_ds = []
        self.nbuf = 0

    def dram(self, name, shape, dt, kind):
        return Buf(self.nc.dram_tensor(name, list(shape), dt, kind=kind).ap(), name)

    def io(self, T, name, shape, dt, kind):
        if T is not None:
            return T[name]
        return self.dram(name, shape, dt, kind)

    def sb(self, shape, dt=F32, name=None):
        self.nbuf += 1
        name = f"{name or 'sb'}_{self.nbuf}"
        t = self.ss.enter_context(self.nc.sbuf_tensor(name, list(shape), dt))
        return Buf(t, name)

    def ps(self, shape, dt=F32, name=None):
        self.nbuf += 1
        name = f"{name or 'ps'}_{self.nbuf}"
        t = self.ss.enter_context(self.nc.psum_tensor(name, list(shape), dt))
        b = Buf(t, name)
        b.psum = True
        return b

    def dsem(self):
        if self.free_ds:
            s = self.free_ds.pop()
        else:
            s = Sem(self.es.enter_context(self.nc.semaphore(f"d{len(self.dsems)}")))
            self.dsems.append(s)
        self.stage_ds.append(s)
        return s

    def _deps(self, eng, reads, writes):
        deps = []
        for b in reads:
            if b.w is not None:
                deps.append(b.w)
            if b.psum:
                deps.extend(ev for ev in b.r if ev[2] != eng)
        for b in writes:
            if b.w is not None:
                deps.append(b.w)
            deps.extend(b.r)
        best = {}
        for (s, v, src) in deps:
            if src == "pe" and eng == "pe":
                continue
            if v > best.get(s, (0, None))[0]:
                best[s] = (v, src)
        for s, (v, src) in best.items():
            if self.seen[eng].get(s, 0) >= v:
                continue
            self.seen[eng][s] = v
            self.q[eng].append(("w", s, v))

    defer = None

    def op(self, eng, fn, reads=(), writes=()):
        if self.defer is not None:
            self.defer.append(lambda: self._op(eng, fn, reads, writes))
            return None
        return self._op(eng, fn, reads, writes)

    def _op(self, eng, fn, reads=(), writes=()):
        self._deps(eng, reads, writes)
        s = self.esem[eng]
        s.count += 1
        ev = (s, s.count, eng)
        self.q[eng].append(("i", fn, s, 1))
        for b in reads:
            b.r.append(ev)
        for b in writes:
            b.w = ev
            b.r = []
        return ev

    def dma(self, eng, out, in_, sem=None, reads=(), writes=(), **kw):
        if self.defer is not None:
            self.defer.append(lambda: self._dma(eng, out, in_, reads, writes, kw))
            return None
        return self._dma(eng, out, in_, reads, writes, kw)

    def _dma(self, eng, out, in_, reads, writes, kw):
        b0 = (list(writes) + list(reads))[0]
        if b0.ds is None:
            b0.ds = self.dsem()
        sem = b0.ds
        self._deps(eng, reads, writes)
        sem.count += 16
        ev = (sem, sem.count, "dma")
        self.q[eng].append(("i", lambda e: e.dma_start(out=out, in_=in_, **kw), sem, 16))
        for b in reads:
            b.r.append(ev)
        for b in writes:
            b.w = ev
            b.r = []
        return ev

    def coll(self, kind, in_ap, out_ap, groups, reads=(), writes=()):
        b0 = list(writes)[0]
        if b0.ds is None:
            b0.ds = self.dsem()
        sem = b0.ds
        self._deps("pool", reads, writes)
        sem.count += 1
        ev = (sem, sem.count, "dma")
        self.q["pool"].append(("i", lambda e: e.collective_compute(kind, ALU.bypass, replica_groups=groups,
                                                                   ins=[in_ap.opt()], outs=[out_ap.opt()]), sem, 1))
        for b in reads:
            b.r.append(ev)
        for b in writes:
            b.w = ev
            b.r = []
        return ev

    def raw(self, eng, fn, reads=()):
        self._deps(eng, reads, ())
        self.q[eng].append(("r", fn))

    def idram(self, shape, dt, name=None, shared=False):
        self.nbuf += 1
        t = self.nc.dram_tensor(name or f"idram{self.nbuf}", list(shape), dt, addr_space="Shared" if shared else "Local")
        return Buf(t.ap(), name or f"idram{self.nbuf}")

    def _emit_block(self):
        q = self.q

        def run(eng_obj, items):
            for it in items:
                if it[0] == "w":
                    eng_obj.wait_ge(it[1].h, it[2])
                elif it[0] == "r":
                    it[1](eng_obj)
                else:
                    it[1](eng_obj).then_inc(it[2].h, it[3])

        with self.nc.Block() as block:
            @block.tensor
            def _(e):
                run(e, q["pe"])

            @block.vector
            def _(e):
                run(e, q["dve"])

            @block.scalar
            def _(e):
                run(e, q["act"])

            @block.gpsimd
            def _(e):
                run(e, q["pool"])

            @block.sync
            def _(e):
                run(e, q["sp"])
        self.q = {e: [] for e in ENGS}

    def end_stage(self):
        sems = [x for x in self.dsems + [self.esem[e] for e in ENGS] if x.count > 0]
        for e in ENGS:
            for x in sems:
                if self.seen[e].get(x, 0) < x.count:
                    self.seen[e][x] = x.count
                    self.q[e].append(("w", x, x.count))
        self._emit_block()
        self.ss.close()
        self.ss = contextlib.ExitStack()
        self.free_ds.extend(self.stage_ds)
        self.stage_ds = []

    def finish(self):
        self.end_stage()
        self.es.close()
        return self.nc

    def mm(self, out, lhsT, rhs, start, stop, reads, writes, skip=False):
        if skip:
            return self.op("pe", lambda e: e.matmul(out, lhsT, rhs, start=start, stop=stop, skip_group_check=True), reads, writes)
        return self.op("pe", lambda e: e.matmul(out, lhsT, rhs, start=start, stop=stop), reads, writes)

    def tr(self, out, in_, ident, reads, writes):
        return self.op("pe", lambda e: e.transpose(out, in_, ident), reads, writes)

    def act(self, out, in_, func, reads, writes, bias=None, scale=None, accum_out=None):
        kw = {}
        if bias is not None:
            kw["bias"] = bias
        if scale is not None:
            kw["scale"] = scale
        if accum_out is not None:
            kw["accum_out"] = accum_out
        return self.op("act", lambda e: e.activation(out, in_, func, **kw), reads, writes)

    def tt(self, out, in0, in1, op, reads, writes, eng="dve"):
        return self.op(eng, lambda e: e.tensor_tensor(out, in0, in1, op), reads, writes)

    def ts(self, out, in0, s1, s2, op0, op1, reads, writes, eng="dve", accum_out=None):
        if op1 is None:
            return self.op(eng, lambda e: e.tensor_scalar(out, in0, s1, None, op0), reads, writes)
        if accum_out is not None:
            return self.op(eng, lambda e: e.tensor_scalar(out, in0, s1, s2, op0, op1, accum_out=accum_out), reads, writes)
        return self.op(eng, lambda e: e.tensor_scalar(out, in0, s1, s2, op0, op1), reads, writes)

    def stt(self, out, in0, scalar, in1, op0, op1, reads, writes):
        return self.op("dve", lambda e: e.scalar_tensor_tensor(out, in0, scalar, in1, op0, op1), reads, writes)

    def cp(self, out, in_, reads, writes, eng="dve"):
        if eng == "act":
            return self.op("act", lambda e: e.copy(out, in_), reads, writes)
        return self.op(eng, lambda e: e.tensor_copy(out, in_), reads, writes)

    def memset(self, ap, val, writes, eng="dve"):
        return self.op(eng, lambda e: e.memset(ap, val), (), writes)


D = 1024
DFF = 2816
NT = 17
TOK = NT * 128
CTXT = 2
SEQT = 66
EPS = 1e-6
BLOCKS = [(0, 2), (2, 3), (5, 3), (8, 3), (11, 3), (14, 3)]


def make_ident(P, n=128, dt=F32):
    ident = P.sb([n, n], dt)
    P.memset(ident[:], 1.0, [ident], eng="pool")
    P.op("pool", lambda e: e.affine_select(ident[:], ident[:], [[-1, n]], ALU.is_equal, 0.0, base=0,
                                           channel_multiplier=1), [ident], [ident])
    return ident


def load_cast(P, dst, dst_ap, src_ap, stage, sem, i, shape_cols):
    st = stage[i % len(stage)]
    P.dma("sp", st[:, 0:shape_cols], src_ap, writes=[st])
    eng = "dve" if i % 2 == 0 else "pool"
    P.cp(dst_ap, st[:, 0:shape_cols], [st], [dst], eng=eng)


def build_ffn(emit_h, P=None, T=None):
    own = P is None
    P = P or Prog()
    x = P.io(T, "x", [NT, 128, D], F32, "ExternalInput")
    wgu = P.io(T, "wgu", [D, 2 * DFF], F32, "ExternalInput")
    wd = P.io(T, "wd", [DFF, D], F32, "ExternalInput")
    pp = P.io(T, "pp", [128, 2 * 48], F32, "ExternalInput")
    gateb = P.io(T, "gateb", [2, 128, D], F32, "ExternalInput")
    xo = P.io(T, "xo", [NT, 128, D], F32, "ExternalOutput")
    if emit_h:
        ho = P.io(T, "ho", [8, 128, TOK], BF16, "ExternalOutput")

    ident = make_ident(P)
    Wgu = P.sb([128, 8, 2 * DFF], BF16, "Wgu")
    Wd = P.sb([128, 22, D], BF16, "Wd")
    stage = [P.sb([128, 704], F32, f"stg{i}") for i in range(2)]
    ssem = [P.dsem() for _ in range(2)]
    pps = P.sb([128, 96], F32, "pps")
    Gt = P.sb([128, 32], F32, "Gt")
    gates = P.sb([128, 2, D], F32, "gates")
    msem = P.dsem()
    P.dma("act", pps[:], pp[:, :], msem, writes=[pps])
    for s in range(2):
        P.dma("act", gates[:, s, :], gateb[s], msem, writes=[gates])
    for s in range(2):
        for j in range(2):
            sc = pps[:, s * 48 + j * 24 + 8: s * 48 + j * 24 + 16]
            g = pps[:, s * 48 + j * 24 + 16: s * 48 + j * 24 + 24]
            P.stt(Gt[:, s * 16 + j * 8: s * 16 + j * 8 + 8], sc, 1.0, g, ALU.add, ALU.mult, [pps], [Gt])
    P.ts(gates[:], gates[:], 0.5, None, ALU.mult, None, [gates], [gates], eng="pool")

    n = 0
    for kc in range(8):
        for q in range(8):
            load_cast(P, Wgu, Wgu[:, kc, q * 704:(q + 1) * 704], wgu[kc * 128:(kc + 1) * 128, q * 704:(q + 1) * 704],
                      stage, ssem, n, 704)
            n += 1
    for fc in range(22):
        for q in range(2):
            load_cast(P, Wd, Wd[:, fc, q * 512:(q + 1) * 512], wd[fc * 128:(fc + 1) * 128, q * 512:(q + 1) * 512], stage, ssem, n, 512)
            n += 1

    xb = [P.sb([128, 3, D], F32, "xb0")] * 2
    xsem = [P.dsem() for _ in range(2)]
    osem = [P.dsem() for _ in range(2)]
    scr = {"xn": P.sb([128, 3, D], F32, "xn"), "sq": P.sb([128, D], BF16, "sq")}
    small = {"ss": P.sb([128, 4], F32, "ss")}
    hT = P.sb([128, 8, 384], BF16, "hT")
    hT2 = hT
    hsem = P.dsem()
    uT = P.sb([128, 22, 384], BF16, "uT")
    sa = [P.sb([128, 384], F32, f"sa{i}") for i in range(2)]
    pst = [P.ps([128, 512], F32, f"pst{i}") for i in range(2)]
    pa = [P.ps([128, 512], F32, f"pa{i}") for i in range(2)]
    pb = [P.ps([128, 512], F32, f"pb{i}") for i in range(2)]
    py = [P.ps([128, 512], F32, f"py{i}") for i in range(2)]
    tmp = [P.sb([128, 512], F32, f"tmp{i}") for i in range(2)]

    for bi, (t0, nb) in enumerate(BLOCKS):
        s = 0 if bi == 0 else 1
        N = nb * 128
        X = xb[bi % 2]
        for i in range(nb):
            P.dma("sp", X[:, i, :], x[t0 + i], xsem[bi % 2], writes=[X])
        _norm(P, X, nb, Gt, s * 16, pps, s * 48, hT, ident, pst, scr, small)
        for fc in range(22):
            A, B = pa[fc % 2], pb[fc % 2]
            for kc in range(8):
                P.mm(A[:, 0:N], Wgu[:, kc, fc * 128:(fc + 1) * 128], hT[:, kc, 0:N], kc == 0, kc == 7, [Wgu, hT], [A])
            for kc in range(8):
                P.mm(B[:, 0:N], Wgu[:, kc, DFF + fc * 128:DFF + (fc + 1) * 128], hT[:, kc, 0:N], kc == 0, kc == 7,
                     [Wgu, hT], [B])
            S = sa[fc % 2]
            P.act(S[:, 0:N], A[:, 0:N], AF.Silu, [A], [S])
            P.tt(uT[:, fc, 0:N], S[:, 0:N], B[:, 0:N], ALU.mult, [S, B], [uT])
        for i in range(nb):
            for h in range(2):
                Y = py[(i * 2 + h) % 2]
                for fc in range(22):
                    P.mm(Y[:, :], uT[:, fc, i * 128:(i + 1) * 128], Wd[:, fc, h * 512:(h + 1) * 512], fc == 0, fc == 21,
                         [uT, Wd], [Y])
                T = tmp[(i * 2 + h) % 2]
                P.tt(T[:], Y[:], gates[:, s, h * 512:(h + 1) * 512], ALU.mult, [Y, gates], [T])
                P.tt(X[:, i, h * 512:(h + 1) * 512], X[:, i, h * 512:(h + 1) * 512], T[:], ALU.add, [X, T], [X], eng="pool")
            P.dma("sp", xo[t0 + i], X[:, i, :], osem[bi % 2], reads=[X])
        if emit_h:
            _norm(P, X, nb, Gt, s * 16 + 8, pps, s * 48 + 24, hT2, ident, pst, scr, small)
            for c in range(8):
                P.dma("act", ho[c, :, t0 * 128:t0 * 128 + N], hT2[:, c, 0:N], hsem, reads=[hT2])
    return P.finish() if own else None


def _norm(P, xb, nb, Gt, gcol, SHt, scol, hT, ident, pst, scr, small):
    xn = scr["xn"]
    ss = small["ss"]
    for i in range(nb):
        P.act(scr["sq"][:], xb[:, i, :], AF.Square, [xb], [scr["sq"], ss], accum_out=ss[:, 0:1])
        P.ts(ss[:, 1:2], ss[:, 0:1], 1.0 / D, EPS, ALU.mult, ALU.add, [ss], [ss])
        P.act(ss[:, 2:3], ss[:, 1:2], AF.Sqrt, [ss], [ss])
        P.op("dve", lambda e: e.reciprocal(ss[:, 3:4], ss[:, 2:3]), [ss], [ss])
        P.ts(xn[:, i, :], xb[:, i, :], ss[:, 3:4], None, ALU.mult, None, [xb, ss], [xn])
    for c in range(8):
        pt = pst[c % len(pst)]
        for i in range(nb):
            P.tr(pt[:, i * 128:(i + 1) * 128], xn[:, i, c * 128:(c + 1) * 128], ident[:], [xn, ident], [pt])
        P.act(hT[:, c, 0:nb * 128], pt[:, 0:nb * 128], AF.Identity, [pt, Gt, SHt], [hT],
              bias=SHt[:, scol + c:scol + c + 1], scale=Gt[:, gcol + c:gcol + c + 1])


def build_mod():
    P = Prog()
    cT = P.dram("cT", [128, 8, 3], F32, "ExternalInput")
    wa = P.dram("wa", [2, D, 1152], F32, "ExternalInput")
    ba = P.dram("ba", [2, 3, 1152], F32, "ExternalInput")
    mo = P.dram("mo", [2, 3, 1152], F32, "ExternalOutput")
    cs = P.sb([128, 8, 3], F32)
    sg = P.sb([128, 8, 3], F32)
    ds = P.dsem()
    P.dma("sp", cs[:], cT[:, :, :], ds, writes=[cs])
    P.act(sg[:], cs[:], AF.Sigmoid, [cs], [sg])
    P.tt(cs[:], cs[:], sg[:], ALU.mult, [cs, sg], [cs])
    W = [P.sb([128, 8, 1152], F32, f"W{l}") for l in range(2)]
    wsem = P.dsem()
    bsb = P.sb([3, 2, 1152], F32)
    osb = P.sb([3, 2, 1152], F32)
    for l in range(2):
        P.dma("act", bsb[:, l, :], ba[l], ds, writes=[bsb])
        for kc in range(8):
            P.dma("sp" if kc % 2 == 0 else "act", W[l][:, kc, :], wa[l, kc * 128:(kc + 1) * 128, :], wsem, writes=[W[l]])
    pp = [P.ps([3, 512], F32, f"pp{i}") for i in range(2)]
    n = 0
    for l in range(2):
        for (c0, cw) in ((0, 512), (512, 512), (1024, 128)):
            ps = pp[n % 2]
            n += 1
            for kc in range(8):
                P.mm(ps[:, 0:cw], cs[:, kc, :], W[l][:, kc, c0:c0 + cw], kc == 0, kc == 7, [cs, W[l]], [ps])
            P.tt(osb[:, l, c0:c0 + cw], ps[:, 0:cw], bsb[:, l, c0:c0 + cw], ALU.add, [ps, bsb], [osb])
    osem = P.dsem()
    for l in range(2):
        P.dma("sp", mo[l], osb[:, l, :], osem, reads=[osb])
    return P.finish()


def run_mod(c, c_ctx, w_ada, b_ada):
    cv = np.stack([c[0], c[1], c_ctx], 0)
    cT = np.ascontiguousarray(cv.reshape(3, 8, 128).transpose(2, 1, 0))
    nc = build_mod()
    ins = []
    for i in range(8):
        cols = slice(i * 1152, (i + 1) * 1152)
        ins.append({"cT": cT, "wa": np.ascontiguousarray(w_ada[:, :, cols]),
                    "ba": np.ascontiguousarray(np.broadcast_to(b_ada[:, None, cols], (2, 3, 1152)))})
    res = run_bass_kernel_spmd(nc, ins, core_ids=list(range(8)))
    return np.concatenate([r["mo"] for r in res.results], axis=2)


def pvec(v):
    return np.ascontiguousarray(v.reshape(8, 128).T)


def tok_shard(xfull_b):
    pad = np.zeros((4 * TOK, xfull_b.shape[1]), np.float32)
    pad[:xfull_b.shape[0]] = xfull_b
    return [np.ascontiguousarray(pad[i * TOK:(i + 1) * TOK].reshape(NT, 128, -1)) for i in range(4)]


def ffn_inputs(xs, mods_l, li, which, norm_g, wgu, wd, emit_h):
    m = mods_l.reshape(3, 9, D)
    o = 0 if which == 1 else 6
    ins = []
    for b in range(2):
        shards = tok_shard(xs[b])
        for i in range(4):
            sets = []
            for s in range(2):
                r = 2 if (s == 0 and i == 0) else b
                vecs = [m[r, o + 0], m[r, o + 1], norm_g[li, 0 if which == 1 else 2]]
                if emit_h:
                    vecs += [m[r, 3], m[r, 4], norm_g[li, 1]]
                else:
                    vecs += [m[r, 3] * 0, m[r, 3] * 0, m[r, 3] * 0]
                sets.append(np.concatenate([pvec(v) for v in vecs], axis=1))
            pp = np.ascontiguousarray(np.concatenate(sets, axis=1).astype(np.float32))
            gb = np.stack([np.broadcast_to(m[2 if i == 0 else b, o + 2], (128, D)),
                           np.broadcast_to(m[b, o + 2], (128, D))], 0)
            ins.append({"x": shards[i], "wgu": wgu, "wd": wd, "pp": pp, "gateb": np.ascontiguousarray(gb)})
    return ins


def tok_unshard(res, key):
    out = []
    for b in range(2):
        full = np.concatenate([res[b * 4 + i][key].reshape(TOK, -1) for i in range(4)], axis=0)
        out.append(full[:SEQT * 128])
    return out


def hT_unshard(res, key):
    out = []
    for b in range(2):
        full = np.concatenate([res[b * 4 + i][key].reshape(D, TOK) for i in range(4)], axis=1)
        out.append(np.ascontiguousarray(full[:, :SEQT * 128]))
    return out


SEQ_ALL = SEQT * 128
NCTX = 256


def h_pieces(hT, kc, a, b):
    if hT.gath:
        out, t = [], a
        while t < b:
            r = t // TOK
            e = min(b, (r + 1) * TOK)
            out.append((t - a, e - t, hT[kc, r * 128:(r + 1) * 128, t - r * TOK:e - r * TOK]))
            t = e
        return out
    return [(0, b - a, hT[kc, :, a:b])]


def ycols(buf, a):
    if buf.gath:
        q = a // TOK
        return buf[q, :, a - q * TOK:a - q * TOK + 128]
    return buf[:, a:a + 128]


def load_hblk(P, hT, hb, sem, t0, N, eng="sp"):
    for kc in range(8):
        for (o, n, ap) in h_pieces(hT, kc, t0, t0 + N):
            P.dma(eng, hb[:, kc, o:o + n], ap, writes=[hb], **({"allow_slow_non_contiguous": True} if n == 1 else {}))


def build_attn(debug=False, P=None, T=None):
    own = P is None
    P = P or Prog()
    hT = P.io(T, "hT", [8, 128, SEQ_ALL], BF16, "ExternalInput")
    wqkv = P.io(T, "wqkv", [3, D, 128], F32, "ExternalInput")
    gqk = P.io(T, "gqk", [128, 2], F32, "ExternalInput")
    cs = P.io(T, "cs", [2, 128, 8192], F32, "ExternalInput")
    cmat = P.io(T, "cmat", [2, 128, 128], F32, "ExternalInput")
    lamp = P.io(T, "lamp", [128, 258], F32, "ExternalInput")
    subg = P.io(T, "subg", [128, 128], F32, "ExternalInput")
    ya = P.io(T, "ya", [128, SEQ_ALL], BF16, "ExternalOutput")

    banks = [P.ps([128, 512], F32, f"bank{i}") for i in range(8)]
    ident = make_ident(P)
    csem = P.dsem()
    W = P.sb([128, 3, 8, 128], BF16, "W")
    wst = P.sb([128, 3, 8, 128], F32, "wst")
    for j in range(3):
        for kc in range(8):
            P.dma("sp", wst[:, j, kc, :], wqkv[j, kc * 128:(kc + 1) * 128, :], csem, writes=[wst])
    P.cp(W[:], wst[:], [wst], [W])
    gq = P.sb([128, 2], F32, "gq")
    P.dma("act", gq[:], gqk[:, :], csem, writes=[gq])
    P.ts(gq[:, 0:1], gq[:, 0:1], 0.125, None, ALU.mult, None, [gq], [gq])
    Bm = P.sb([128, 128], F32, "Bm")
    Rm = P.sb([128, 128], F32, "Rm")
    P.dma("act", Bm[:], cmat[0], csem, writes=[Bm])
    P.dma("act", Rm[:], cmat[1], csem, writes=[Rm])
    lp = P.sb([128, 258], F32, "lp")
    P.dma("act", lp[:], lamp[:, :], csem, writes=[lp])
    sg = P.sb([128, 128], F32, "sg")
    P.dma("act", sg[:], subg[:, :], csem, writes=[sg])
    P.ts(sg[:], sg[:], lp[:, 257:258], None, ALU.mult, None, [sg, lp], [sg])
    lt = P.sb([128, 128], F32, "lt")
    lam = P.sb([128, 4], F32, "lam")
    P.tt(lt[:, 0:64], lp[:, 0:64], lp[:, 64:128], ALU.mult, [lp], [lt])
    P.tt(lt[:, 64:128], lp[:, 128:192], lp[:, 192:256], ALU.mult, [lp], [lt])
    P.op("dve", lambda e: e.tensor_reduce(lam[:, 0:1], lt[:, 0:64], AX.X, ALU.add), [lt], [lam])
    P.op("dve", lambda e: e.tensor_reduce(lam[:, 1:2], lt[:, 64:128], AX.X, ALU.add), [lt], [lam])
    P.act(lam[:, 0:2], lam[:, 0:2], AF.Exp, [lam], [lam])
    P.tt(lam[:, 2:3], lam[:, 1:2], lam[:, 0:1], ALU.subtract, [lam], [lam])
    P.tt(lam[:, 3:4], lam[:, 2:3], lp[:, 256:257], ALU.subtract, [lam, lp], [lam])
    epsb = P.sb([128, 1], F32, "epsb")
    P.memset(epsb[:], EPS, [epsb])

    QK = [P.sb([128, SEQ_ALL], BF16, "QT"), P.sb([128, SEQ_ALL], BF16, "KT")]
    V = P.sb([128, SEQT, 129], BF16, "V")
    P.memset(V[:, :, 128:129], 1.0, [V], eng="pool")
    hb = [P.sb([128, 8, 512], BF16, f"hb{i}") for i in range(2)]
    hsem = [P.dsem() for _ in range(2)]
    cst = [P.sb([128, 2, 512], F32, f"cst{i}") for i in range(2)]
    cssem = [P.dsem() for _ in range(2)]
    sq = P.sb([128, 512], F32, "sq")
    rs = P.sb([128, 512], F32, "rs")
    xn = P.sb([128, 512], F32, "xn")
    t1 = P.sb([128, 512], F32, "t1")
    t2 = P.sb([128, 512], F32, "t2")

    blocks = [(0, 256)] + [(256 + 512 * j, 512) for j in range(16)]
    for bi, (t0, N) in enumerate(blocks):
        H = hb[bi % 2]
        load_hblk(P, hT, H, hsem[bi % 2], t0, N)
        C = cst[bi % 2]
        if bi > 0:
            for j in range(2):
                P.dma("act", C[:, j, :], cs[j, :, t0 - 256:t0 - 256 + N], cssem[bi % 2], writes=[C])
        for j in range(2):
            pq, pms, prot = banks[0 + j], banks[2 + j], banks[4 + j]
            for kc in range(8):
                P.mm(pq[:, 0:N], W[:, j, kc, :], H[:, kc, 0:N], kc == 0, kc == 7, [W, H], [pq])
            P.act(sq[:, 0:N], pq[:, 0:N], AF.Square, [pq], [sq])
            P.mm(pms[:, 0:N], Bm[:], sq[:, 0:N], True, True, [Bm, sq], [pms])
            P.act(rs[:, 0:N], pms[:, 0:N], AF.Sqrt, [pms, epsb], [rs], bias=epsb[:, 0:1])
            P.op("dve", lambda e, N=N: e.reciprocal(rs[:, 0:N], rs[:, 0:N]), [rs], [rs])
            P.stt(xn[:, 0:N], pq[:, 0:N], gq[:, j:j + 1], rs[:, 0:N], ALU.mult, ALU.mult, [pq, gq, rs], [xn])
            if bi == 0:
                P.cp(QK[j][:, t0:t0 + N], xn[:, 0:N], [xn], [QK[j]], eng="pool")
            else:
                P.mm(prot[:, 0:N], Rm[:], xn[:, 0:N], True, True, [Rm, xn], [prot])
                P.tt(t1[:, 0:N], xn[:, 0:N], C[:, 0, 0:N], ALU.mult, [xn, C], [t1], eng="pool")
                P.tt(t2[:, 0:N], prot[:, 0:N], C[:, 1, 0:N], ALU.mult, [prot, C], [t2])
                P.tt(QK[j][:, t0:t0 + N], t1[:, 0:N], t2[:, 0:N], ALU.add, [t1, t2], [QK[j]])
        for i in range(N // 128):
            pv = banks[6 + i % 2]
            for kc in range(8):
                P.mm(pv[:, 0:128], H[:, kc, i * 128:(i + 1) * 128], W[:, 2, kc, :], kc == 0, kc == 7, [H, W], [pv])
            P.cp(V[:, t0 // 128 + i, 0:128], pv[:, 0:128], [pv], [V], eng="act")

    if debug:
        dq = P.io(T, "dq", [2, 128, SEQ_ALL], BF16, "ExternalOutput")
        dv = P.io(T, "dv", [128, SEQT, 129], BF16, "ExternalOutput")
        dl = P.io(T, "dl", [128, 4], F32, "ExternalOutput")
        dsm = P.dsem()
        P.dma("sp", dq[0], QK[0][:], dsm, reads=[QK[0]])
        P.dma("sp", dq[1], QK[1][:], dsm, reads=[QK[1]])
        P.dma("sp", dv[:, :, :], V[:], dsm, reads=[V])
        P.dma("sp", dl[:, :], lam[:], dsm, reads=[lam])
    Pm = [P.sb([128, 512], BF16, f"Pm{i}") for i in range(6)]
    Sb = [banks[0], banks[1], banks[6], banks[7]]
    SKEW = 3
    accb = [banks[2], banks[3], banks[4]]
    yo = [P.sb([128, 128], F32, f"yo{i}") for i in range(2)]
    yob = [P.sb([128, 128], BF16, f"yob{i}") for i in range(2)]
    ysq = P.sb([128, 128], F32, "ysq")
    st = P.sb([128, 8], F32, "st")
    osem = [P.dsem() for _ in range(2)]

    def acc(m, qs):
        a = m * 4 + qs
        return accb[a // 3], (a % 3) * 129

    qblocks = [(0, 256, 0, CTXT)] + [(256 + 512 * j, 512, 0, SEQT) for j in range(16)]
    n = 0
    no = 0
    for (q0, N, k0, k1) in qblocks:
        nq = N // 128
        for b in accb:
            P.memset(b[:], 0.0, [b])
        its = [(kt, m) for kt in range(k0, k1) for m in range(2)]

        def issue_s(j):
            kt, m = its[j]
            S, pm = Sb[(n + j) % 4], Pm[(n + j) % 6]
            P.mm(S[:, 0:N], QK[1][m * 64:(m + 1) * 64, kt * 128:(kt + 1) * 128], QK[0][m * 64:(m + 1) * 64, q0:q0 + N],
                 True, True, [QK[0], QK[1]], [S])
            P.act(pm[:, 0:N], S[:, 0:N], AF.Exp, [S], [pm])

        for j in range(min(SKEW, len(its))):
            issue_s(j)
        for j, (kt, m) in enumerate(its):
            if j + SKEW < len(its):
                issue_s(j + SKEW)
            pm = Pm[(n + j) % 6]
            for qs in range(nq):
                ab, c0 = acc(m, qs)
                P.mm(ab[:, c0:c0 + 129], pm[:, qs * 128:(qs + 1) * 128], V[:, kt, :], False, False, [pm, V], [ab], skip=True)
        n += len(its)
        for qs in range(nq):
            a0, c0 = acc(0, qs)
            a1, c1 = acc(1, qs)
            Y = yo[no % 2]
            P.op("dve", lambda e, a0=a0, c0=c0: e.reciprocal(st[:, 0:1], a0[:, c0 + 128:c0 + 129]), [a0], [st])
            P.op("dve", lambda e, a1=a1, c1=c1: e.reciprocal(st[:, 1:2], a1[:, c1 + 128:c1 + 129]), [a1], [st])
            P.tt(st[:, 1:2], st[:, 1:2], lam[:, 3:4], ALU.mult, [st, lam], [st])
            P.ts(Y[:], a0[:, c0:c0 + 128], st[:, 0:1], None, ALU.mult, None, [a0, st], [Y])
            P.stt(Y[:], a1[:, c1:c1 + 128], st[:, 1:2], Y[:], ALU.mult, ALU.add, [a1, st, Y], [Y])
            P.act(ysq[:], Y[:], AF.Square, [Y], [ysq, st], accum_out=st[:, 2:3])
            P.ts(st[:, 3:4], st[:, 2:3], 1.0 / 128, EPS, ALU.mult, ALU.add, [st], [st])
            P.act(st[:, 4:5], st[:, 3:4], AF.Sqrt, [st], [st])
            P.op("dve", lambda e: e.reciprocal(st[:, 5:6], st[:, 4:5]), [st], [st])
            P.stt(Y[:], Y[:], st[:, 5:6], sg[:], ALU.mult, ALU.mult, [Y, st, sg], [Y])
            P.tr(banks[5][:, 0:128], Y[:], ident[:], [Y, ident], [banks[5]])
            Yb = yob[no % 2]
            P.cp(Yb[:], banks[5][:, 0:128], [banks[5]], [Yb], eng="act")
            P.dma("sp", ycols(ya, q0 + qs * 128), Yb[:], reads=[Yb])
            no += 1
    return P.finish() if own else None


def rope_tables():
    n = 8192
    rows = n // 64
    row = np.repeat(np.arange(rows, dtype=np.float32), 64)
    col = np.tile(np.arange(64, dtype=np.float32), rows)
    half = 32
    inv = (np.float32(10000.0) ** (-np.arange(0, half, 2, dtype=np.float32) / np.float32(half))).astype(np.float32)
    ang = np.concatenate([row[:, None] * inv, col[:, None] * inv], axis=-1).astype(np.float32)
    cos, sin = np.cos(ang).astype(np.float32), np.sin(ang).astype(np.float32)
    idx = (np.arange(128) % 64) // 2
    return np.ascontiguousarray(np.stack([cos[:, idx].T, sin[:, idx].T], 0))


def attn_consts():
    Bm = np.zeros((128, 128), np.float32)
    Bm[:64, :64] = 1.0 / 64
    Bm[64:, 64:] = 1.0 / 64
    Rm = np.zeros((128, 128), np.float32)
    for i in range(64):
        Rm[2 * i + 1, 2 * i] = -1.0
        Rm[2 * i, 2 * i + 1] = 1.0
    return np.stack([Bm, Rm], 0)


def lambda_init(li):
    import math
    return 0.8 - 0.6 * math.exp(-0.3 * li)


def attn_inputs(hTs, li, w_in, a_qk_norm, a_lambda, a_subln):
    off_q = 5008 - 512 - 512 - 512
    off = {}
    o = 0
    for name, wdt in IN_SPLITS:
        off[name] = o
        o += wdt
    cs = rope_tables()
    cm = attn_consts()
    ins = []
    for b in range(2):
        hT = np.ascontiguousarray(hTs[b].reshape(8, 128, SEQ_ALL))
        for h in range(4):
            wq = w_in[li][:, off['a_q'] + 128 * h: off['a_q'] + 128 * (h + 1)]
            wk = w_in[li][:, off['a_k'] + 128 * h: off['a_k'] + 128 * (h + 1)]
            wv = w_in[li][:, off['a_v'] + 128 * h: off['a_v'] + 128 * (h + 1)]
            gqk = np.stack([np.tile(a_qk_norm[li, 0], 2), np.tile(a_qk_norm[li, 1], 2)], 1).astype(np.float32)
            li_ = np.float32(lambda_init(li))
            lamp = np.concatenate([np.broadcast_to(a_lambda[li].reshape(1, 256), (128, 256)),
                                   np.full((128, 1), li_, np.float32), np.full((128, 1), np.float32(1.0) - li_, np.float32)], 1)
            ins.append({"hT": hT, "wqkv": np.ascontiguousarray(np.stack([wq, wk, wv], 0)), "gqk": np.ascontiguousarray(gqk),
                        "cs": cs, "cmat": cm, "lamp": np.ascontiguousarray(lamp.astype(np.float32)),
                        "subg": np.ascontiguousarray(np.broadcast_to(a_subln[li][None, :], (128, 128)))})
    return ins


IN_SPLITS = (
    ('m_q', 256), ('m_k', 256), ('m_v', 512), ('m_o', 512),
    ('m_if', 4), ('m_ff', 4), ('m_ib', 4), ('m_fb', 4),
    ('r_r', 512), ('r_k', 512), ('r_v', 512),
    ('r_wf', 64), ('r_wb', 64), ('r_af', 64), ('r_ab', 64), ('r_g', 128),
    ('a_q', 512), ('a_k', 512), ('a_v', 512),
    ('g_m', 1024), ('g_r', 1024), ('g_a', 1024),
)


def tri_mask(P, upper, neg=False):
    m = P.sb([128, 128], F32)
    P.memset(m[:], 0.0 if neg else 1.0, [m], eng="pool")
    pat, cm = ([[1, 128]], -1) if upper else ([[-1, 128]], 1)
    P.op("pool", lambda e: e.affine_select(m[:], m[:], pat, ALU.is_ge, -1.0e4 if neg else 0.0, base=0,
                                           channel_multiplier=cm), [m], [m])
    return m


def load_hblk_halo(P, hT, hb, t0, N, lo, hi, eng="sp"):
    a = t0 - 1 if t0 - 1 >= lo else t0
    b = t0 + N + 1 if t0 + N + 1 <= hi else t0 + N
    for kc in range(8):
        for (o, n, ap) in h_pieces(hT, kc, a, b):
            P.dma(eng, hb[:, kc, a - (t0 - 1) + o:a - (t0 - 1) + o + n], ap, writes=[hb],
                  **({"allow_slow_non_contiguous": True} if n == 1 else {}))
    if a == t0:
        P.memset(hb[:, :, 0:1], 0.0, [hb], eng="pool")
    if b == t0 + N:
        P.memset(hb[:, :, N + 1:N + 2], 0.0, [hb], eng="pool")


def chunk_orders():
    f = list(range(SEQT))
    b = [1, 0] + list(range(SEQT - 1, 1, -1))
    return f, b


def build_mlstm(stop=99, P=None, T=None):
    own = P is None
    P = P or Prog()
    hT = P.io(T, "hT", [8, 128, SEQ_ALL], BF16, "ExternalInput")
    wqk = P.io(T, "wqk", [2, D, 64], F32, "ExternalInput")
    wvo = P.io(T, "wvo", [D, 256], F32, "ExternalInput")
    wg = P.io(T, "wg", [D, 4], F32, "ExternalInput")
    cw = P.io(T, "cw", [2, 128, 3, 64], F32, "ExternalInput")
    gb = P.io(T, "gb", [128, 4], F32, "ExternalInput")
    og = P.io(T, "og", [128, 128], F32, "ExternalInput")
    ym = P.io(T, "ym", [128, SEQ_ALL], BF16, "ExternalOutput")

    banks = [P.ps([128, 512], F32, f"bank{i}") for i in range(8)]
    ident = make_ident(P)
    ones = P.sb([128, 128], F32, "ones")
    P.memset(ones[:], 1.0, [ones])
    one1 = P.sb([128, 1], F32, "one1")
    P.memset(one1[:], 1.0, [one1])
    triU = tri_mask(P, True)
    triL = tri_mask(P, False)
    negU = tri_mask(P, True, True)
    negL = tri_mask(P, False, True)

    wst = P.sb([128, 8, 388], F32, "wst")
    for kc in range(8):
        P.dma("sp", wst[:, kc, 0:64], wqk[0, kc * 128:(kc + 1) * 128, :], writes=[wst])
        P.dma("sp", wst[:, kc, 64:128], wqk[1, kc * 128:(kc + 1) * 128, :], writes=[wst])
        P.dma("sp", wst[:, kc, 128:384], wvo[kc * 128:(kc + 1) * 128, :], writes=[wst])
        P.dma("sp", wst[:, kc, 384:388], wg[kc * 128:(kc + 1) * 128, :], writes=[wst])
    cws = P.sb([128, 2, 3, 64], F32, "cws")
    P.dma("act", cws[:, 0], cw[0], writes=[cws])
    P.dma("act", cws[:, 1], cw[1], writes=[cws])
    gbs = P.sb([128, 4], F32, "gbs")
    P.dma("act", gbs[:], gb[:, :], writes=[gbs])
    ogs = P.sb([128, 128], F32, "ogs")
    P.dma("act", ogs[:], og[:, :], writes=[ogs])
    Wqk = P.sb([128, 2, 3, 8, 64], BF16, "Wqk")
    Wvo = P.sb([128, 8, 260], BF16, "Wvo")
    P.cp(Wvo[:], wst[:, :, 128:388], [wst], [Wvo])
    for j in range(2):
        for tap in range(3):
            for kc in range(8):
                P.tt(Wqk[:, j, tap, kc, :], wst[:, kc, j * 64:(j + 1) * 64], cws[:, j, tap, :], ALU.mult, [wst, cws], [Wqk],
                     eng="pool" if kc % 2 else "dve")

    QT = P.sb([64, SEQ_ALL], F32, "QT")
    KT = P.sb([64, SEQ_ALL], F32, "KT")
    VE = P.sb([128, SEQT, 129], F32, "VE")
    P.memset(VE[:, :, 128:129], 1.0, [VE], eng="pool")
    OG = P.sb([128, SEQT, 128], BF16, "OG")
    G = P.sb([128, SEQT, 4], F32, "G")
    hb = [P.sb([128, 8, 514], BF16, f"hb{i}") for i in range(2)]

    blocks = [(0, 256, 0, 256)] + [(256 + 512 * j, 512, 256, SEQ_ALL) for j in range(16)]
    for bi, (t0, N, lo, hi) in enumerate(blocks):
        H = hb[bi % 2]
        load_hblk_halo(P, hT, H, t0, N, lo, hi)
        for j, dst in enumerate((QT, KT)):
            pq = banks[j]
            n = 0
            for tap in range(3):
                for kc in range(8):
                    P.mm(pq[0:64, 0:N], Wqk[:, j, tap, kc, :], H[:, kc, tap:tap + N], n == 0, n == 23, [Wqk, H], [pq])
                    n += 1
            P.act(dst[:, t0:t0 + N], pq[0:64, 0:N], AF.Silu, [pq], [dst], scale=1.0)
        for i in range(N // 128):
            pv = banks[2 + i % 2]
            for kc in range(8):
                P.mm(pv[:, 0:260], H[:, kc, 1 + i * 128:1 + (i + 1) * 128], Wvo[:, kc, :], kc == 0, kc == 7, [H, Wvo], [pv])
            tl = t0 // 128 + i
            P.cp(VE[:, tl, 0:128], pv[:, 0:128], [pv], [VE])
            P.act(OG[:, tl, :], pv[:, 128:256], AF.Sigmoid, [pv], [OG])
            P.tt(G[:, tl, :], pv[:, 256:260], gbs[:], ALU.add, [pv, gbs], [G])
    if stop == 1:
        return P.finish() if own else None
    P.ts(KT[:], KT[:], 0.125, None, ALU.mult, None, [KT], [KT], eng="pool")

    ge = P.sb([128, SEQT, 4], F32, "ge")
    P.act(ge[:], G[:], AF.Exp, [G], [ge], scale=-1.0)
    P.act(ge[:], ge[:], AF.Ln, [ge, one1], [ge], bias=one1[:, 0:1])
    LF = P.sb([128, 2, SEQT], F32, "LF")
    IG = P.sb([128, 2, SEQT], F32, "IG")
    for d in range(2):
        P.ts(LF[:, d, :], ge[:, :, 2 * d + 1], -1.0, None, ALU.mult, None, [ge], [LF])
        P.cp(IG[:, d, :], G[:, :, 2 * d], [G], [IG])
    BC = P.sb([128, 2, SEQT], F32, "BC")
    BT = P.sb([128, 2, SEQT], F32, "BT")
    for d in range(2):
        pb = banks[4 + d]
        P.mm(pb[:, 0:SEQT], (triU if d == 0 else triL)[:], LF[:, d, :], True, True, [triU, triL, LF], [pb])
        P.cp(BC[:, d, :], pb[:, 0:SEQT], [pb], [BC])
        pb2 = banks[6 + d]
        P.mm(pb2[:, 0:SEQT], ones[:], LF[:, d, :], True, True, [ones, LF], [pb2])
        P.cp(BT[:, d, :], pb2[:, 0:SEQT], [pb2], [BT])
    BIAS = P.sb([128, 2, SEQT], F32, "BIAS")
    WS = P.sb([128, 2, SEQT], F32, "WS")
    EB = P.sb([128, 2, SEQT], F32, "EB")
    DEC = P.sb([128, 2, SEQT], F32, "DEC")
    P.tt(BIAS[:], IG[:], BC[:], ALU.subtract, [IG, BC], [BIAS])
    P.tt(WS[:], BIAS[:], BT[:], ALU.add, [BIAS, BT], [WS])
    P.act(WS[:], WS[:], AF.Exp, [WS], [WS])
    P.act(EB[:], BC[:], AF.Exp, [BC], [EB])
    P.act(DEC[:], BT[:], AF.Exp, [BT], [DEC])

    if stop == 2:
        return P.finish() if own else None
    HS = P.sb([128, SEQT, 128], F32, "HS")
    CT = [[P.sb([64, 129], F32, f"CT{d}{i}") for i in range(2)] for d in range(2)]
    for d in range(2):
        P.memset(CT[d][0][:], 0.0, [CT[d][0]])
    lrep = [P.sb([128, 128], F32, f"lrep{i}") for i in range(2)]
    arg = [P.sb([128, 128], F32, f"arg{i}") for i in range(2)]
    ST = [P.sb([128, 128], F32, f"ST{i}") for i in range(2)]
    KW = [P.sb([128, 64], F32, f"KW{i}") for i in range(2)]
    it = [P.sb([128, 129], F32, f"it{i}") for i in range(2)]
    tot = [P.sb([128, 129], F32, f"tot{i}") for i in range(2)]
    sm = [P.sb([128, 2], F32, f"sm{i}") for i in range(2)]
    orders = chunk_orders()
    done = set()
    for step in range(SEQT):
        for d in range(2):
            c = orders[d][step]
            cs_ = slice(c * 128, (c + 1) * 128)
            Ccur, Cnew = CT[d][step % 2], CT[d][(step + 1) % 2]
            tri, neg = (triU, negU) if d == 0 else (triL, negL)
            p_brd, p_qk, p_n, p_i, p_kt, p_st = (banks[d * 4 + 0], banks[d * 4 + 1], banks[d * 4 + 2], banks[d * 4 + 3],
                                                 banks[d * 4 + 0], banks[d * 4 + 1])
            L = lrep[d]
            P.ts(L[:], ones[:], LF[:, d, c:c + 1], None, ALU.mult, None, [ones, LF], [L], eng="pool")
            P.mm(p_brd[:, 0:128], L[:], tri[:], True, True, [L, tri], [p_brd])
            A = arg[d]
            P.tt(A[:], p_brd[:, 0:128], neg[:], ALU.add, [p_brd, neg], [A])
            P.act(A[:], A[:], AF.Exp, [A, BIAS], [A], bias=BIAS[:, d, c:c + 1])
            P.mm(p_qk[:, 0:128], KT[:, cs_], QT[:, cs_], True, True, [KT, QT], [p_qk])
            S = ST[d]
            P.tt(S[:], p_qk[:, 0:128], A[:], ALU.mult, [p_qk, A], [S])
            P.mm(p_n[:, 0:129], S[:], VE[:, c, :], True, True, [S, VE], [p_n])
            P.mm(p_i[:, 0:129], QT[:, cs_], Ccur[:], True, True, [QT, Ccur], [p_i])
            I = it[d]
            P.act(I[:], p_i[:, 0:129], AF.Identity, [p_i, EB], [I], scale=EB[:, d, c:c + 1])
            T = tot[d]
            P.tt(T[:], p_n[:, 0:129], I[:], ALU.add, [p_n, I], [T])
            s_ = sm[d]
            P.act(s_[:, 0:1], T[:, 128:129], AF.Abs, [T], [s_])
            P.ts(s_[:, 0:1], s_[:, 0:1], 1.0, None, ALU.max, None, [s_], [s_])
            P.op("dve", lambda e, s_=s_: e.reciprocal(s_[:, 1:2], s_[:, 0:1]), [s_], [s_])
            if c in done:
                P.stt(HS[:, c, :], T[:, 0:128], s_[:, 1:2], HS[:, c, :], ALU.mult, ALU.add, [T, s_, HS], [HS])
            else:
                P.ts(HS[:, c, :], T[:, 0:128], s_[:, 1:2], None, ALU.mult, None, [T, s_], [HS])
                done.add(c)
            P.tr(p_kt[:, 0:64], KT[:, cs_], ident[0:64, 0:64], [KT, ident], [p_kt])
            kw = KW[d]
            P.ts(kw[:], p_kt[:, 0:64], WS[:, d, c:c + 1], None, ALU.mult, None, [p_kt, WS], [kw])
            P.mm(p_st[0:64, 0:129], kw[:], VE[:, c, :], True, True, [kw, VE], [p_st])
            P.stt(Cnew[:], Ccur[:], DEC[0:64, d, c:c + 1], p_st[0:64, 0:129], ALU.mult, ALU.add, [Ccur, DEC, p_st], [Cnew])

    if stop == 3:
        return P.finish() if own else None
    yo = ST
    junk = lrep[0]
    yob = [P.sb([128, 128], BF16, f"yob{i}") for i in range(2)]
    st = [P.sb([128, 4], F32, f"st{i}") for i in range(2)]
    for c in range(SEQT):
        Y, s_ = yo[c % 2], st[c % 2]
        P.act(junk[:], HS[:, c, :], AF.Square, [HS], [junk, s_], accum_out=s_[:, 0:1])
        P.ts(s_[:, 1:2], s_[:, 0:1], 1.0 / 128, EPS, ALU.mult, ALU.add, [s_], [s_])
        P.act(s_[:, 2:3], s_[:, 1:2], AF.Sqrt, [s_], [s_])
        P.op("dve", lambda e, s_=s_: e.reciprocal(s_[:, 3:4], s_[:, 2:3]), [s_], [s_])
        P.stt(Y[:], HS[:, c, :], s_[:, 3:4], ogs[:], ALU.mult, ALU.mult, [HS, s_, ogs], [Y])
        P.tt(Y[:], Y[:], OG[:, c, :], ALU.mult, [Y, OG], [Y])
        P.tr(banks[c % 2][:, 0:128], Y[:], ident[:], [Y, ident], [banks[c % 2]])
        Yb = yob[c % 2]
        P.cp(Yb[:], banks[c % 2][:, 0:128], [banks[c % 2]], [Yb], eng="act")
        P.dma("sp", ycols(ym, c * 128), Yb[:], reads=[Yb])
    return P.finish() if own else None


def col_offsets():
    off, o = {}, 0
    for name, wdt in IN_SPLITS:
        off[name] = o
        o += wdt
    return off


def mlstm_inputs(hTs, li, w_in, m_conv, m_gate_bias, m_out_norm):
    off = col_offsets()
    ins = []
    for b in range(2):
        hT = np.ascontiguousarray(hTs[b].reshape(8, 128, SEQ_ALL))
        for h in range(4):
            W = w_in[li]
            wq = W[:, off['m_q'] + 64 * h: off['m_q'] + 64 * (h + 1)]
            wk = W[:, off['m_k'] + 64 * h: off['m_k'] + 64 * (h + 1)]
            wvo = np.concatenate([W[:, off['m_v'] + 128 * h: off['m_v'] + 128 * (h + 1)],
                                  W[:, off['m_o'] + 128 * h: off['m_o'] + 128 * (h + 1)]], 1)
            wg = np.stack([W[:, off['m_if'] + h], W[:, off['m_ff'] + h], W[:, off['m_ib'] + h], W[:, off['m_fb'] + h]], 1)
            cq = m_conv[li][:, 64 * h:64 * (h + 1)]
            ck = m_conv[li][:, 256 + 64 * h:256 + 64 * (h + 1)]
            cw = np.stack([np.broadcast_to(cq[None], (128, 3, 64)), np.broadcast_to(ck[None], (128, 3, 64))], 0)
            gbv = m_gate_bias[li][:, h]
            ins.append({"hT": hT, "wqk": np.ascontiguousarray(np.stack([wq, wk], 0)), "wvo": np.ascontiguousarray(wvo),
                        "wg": np.ascontiguousarray(wg), "cw": np.ascontiguousarray(cw.astype(np.float32)),
                        "gb": np.ascontiguousarray(np.broadcast_to(gbv[None, :], (128, 4)).astype(np.float32)),
                        "og": np.ascontiguousarray(np.broadcast_to(m_out_norm[li][None, 128 * h:128 * (h + 1)], (128, 128)))})
    return ins


R_GN_EPS = 64e-5
RWKV_INTERLEAVE = True
W_SCALE = -0.6065306597126334


def aff_mask(P, pat, cm, op, val=1.0, base=0):
    m = P.sb([128, 128], F32)
    P.memset(m[:], val, [m], eng="pool")
    P.op("pool", lambda e: e.affine_select(m[:], m[:], pat, op, 0.0, base=base, channel_multiplier=cm), [m], [m])
    return m


def build_rwkv_v1(P=None, T=None):
    own = P is None
    P = P or Prog()
    hT = P.io(T, "hT", [8, 128, SEQ_ALL], BF16, "ExternalInput")
    wrkv = P.io(T, "wrkv", [D, 384], F32, "ExternalInput")
    crkv = P.io(T, "crkv", [128, 3, 384], F32, "ExternalInput")
    wl = P.io(T, "wl", [D, 384], F32, "ExternalInput")
    w2a2 = P.io(T, "w2a2", [2, 128, 128], F32, "ExternalInput")
    bias01 = P.io(T, "bias01", [1, 2, 256], F32, "ExternalInput")
    g2 = P.io(T, "g2", [128, 128], F32, "ExternalInput")
    vecs = P.io(T, "vecs", [128, 5, 128], F32, "ExternalInput")
    yr = P.io(T, "yr", [128, SEQ_ALL], BF16, "ExternalOutput")

    bk = [P.ps([128, 512], F32, f"bank{i}") for i in range(8)]
    ident = make_ident(P)
    ones = P.sb([128, 128], F32, "ones")
    P.memset(ones[:], 1.0, [ones])
    mI = [aff_mask(P, [[1, 128]], -1, ALU.is_ge), aff_mask(P, [[-1, 128]], 1, ALU.is_ge)]
    mS = [aff_mask(P, [[1, 128]], -1, ALU.is_gt), aff_mask(P, [[-1, 128]], 1, ALU.is_gt)]
    cI = [aff_mask(P, [[1, 128]], -1, ALU.is_ge, W_SCALE), aff_mask(P, [[-1, 128]], 1, ALU.is_ge, W_SCALE)]
    cS = [aff_mask(P, [[1, 128]], -1, ALU.is_gt, W_SCALE), aff_mask(P, [[-1, 128]], 1, ALU.is_gt, W_SCALE)]
    mSI = []
    for d in range(2):
        m = P.sb([128, 256], F32)
        P.cp(m[:, 0:128], mS[d][:], [mS[d]], [m])
        P.cp(m[:, 128:256], mI[d][:], [mI[d]], [m])
        mSI.append(m)

    wst = P.sb([128, 8, 768], F32, "wst")
    for kc in range(8):
        P.dma("sp", wst[:, kc, 0:384], wrkv[kc * 128:(kc + 1) * 128, :], writes=[wst])
        P.dma("act", wst[:, kc, 384:768], wl[kc * 128:(kc + 1) * 128, :], writes=[wst])
    cws = P.sb([128, 3, 384], F32, "cws")
    P.dma("sp", cws[:], crkv[:, :, :], writes=[cws])
    Wc = P.sb([128, 3, 8, 384], BF16, "Wc")
    for tap in range(3):
        for kc in range(8):
            P.tt(Wc[:, tap, kc, :], wst[:, kc, 0:384], cws[:, tap, :], ALU.mult, [wst, cws], [Wc])
    Wl = P.sb([128, 8, 384], BF16, "Wl")
    P.cp(Wl[:], wst[:, :, 384:768], [wst], [Wl])
    W2 = P.sb([128, 2, 128], F32, "W2")
    for d in range(2):
        P.dma("act", W2[:, d, :], w2a2[d], writes=[W2])
    B01 = P.sb([1, 2, 256], F32, "B01")
    P.dma("act", B01[:], bias01[:, :, :], writes=[B01])
    G2 = P.sb([128, 128], F32, "G2")
    P.dma("act", G2[:], g2[:, :], writes=[G2])
    VEC = P.sb([128, 5, 128], F32, "VEC")
    P.dma("act", VEC[:], vecs[:, :, :], writes=[VEC])
    epsg = P.sb([128, 1], F32, "epsg")
    P.memset(epsg[:], R_GN_EPS, [epsg])

    YF = P.sb([128, SEQT, 128], F32, "YF")
    hb = [P.sb([128, 8, 130], BF16, f"hb{i}") for i in range(2)]
    STB = P.sb([128, 128], F32, "STB")

    def TL(shape, name):
        return P.sb(shape, F32, name)

    rkv = TL([128, 384], "rkv")
    pl = TL([128, 128], "pl")
    sgg = TL([128, 128], "sgg")
    sig = TL([128, 128], "sig")
    av = TL([128, 128], "av")
    t0_ = TL([128, 128], "t0_")
    ss = TL([128, 8], "ss")
    kkn = TL([128, 128], "kkn")
    bh = TL([128, 128], "bh")
    t2 = TL([128, 128], "t2")
    key = TL([128, 128], "key")
    eG = TL([128, 256], "eG")
    enG = TL([128, 128], "enG")
    eR = TL([128, 128], "eR")
    AR = TL([128, 256], "AR")
    BtT = TL([128, 128], "BtT")
    KtT = TL([128, 128], "KtT")
    Bb = TL([128, 128], "Bb")
    Kb = TL([128, 128], "Kb")
    Mm = [TL([128, 256], f"Mm{i}") for i in range(2)]
    Ak = [TL([128, 256], f"Ak{i}") for i in range(2)]
    Lm = [TL([128, 128], f"Lm{i}") for i in range(2)]
    TtA = [TL([128, 128], f"TtA{i}") for i in range(2)]
    TA = [TL([128, 128], f"TA{i}") for i in range(2)]
    Pp = [[TL([128, 128], f"Pp{i}{j}") for j in range(2)] for i in range(2)]
    Qq = [[TL([128, 128], f"Qq{i}{j}") for j in range(2)] for i in range(2)]
    X = TL([128, 128], "X")
    U = TL([128, 128], "U")
    yb = TL([128, 128], "yb")
    gn = TL([128, 16], "gn")
    yn = TL([128, 128], "yn")
    rk = TL([128, 128], "rk")
    yo = [TL([128, 128], f"yo{i}") for i in range(2)]
    yob = [P.sb([128, 128], BF16, f"yob{i}") for i in range(2)]

    orders = chunk_orders()
    for d in range(2):
        P.memset(STB[:], 0.0, [STB])
        for step in range(SEQT):
            c = orders[d][step]
            t0 = c * 128
            lo, hi = (0, NCTX) if c < CTXT else (NCTX, SEQ_ALL)
            H = hb[step % 2]
            load_hblk_halo(P, hT, H, t0, 128, lo, hi)
            n = 0
            for tap in range(3):
                for kc in range(8):
                    P.mm(bk[0][:, 0:384], H[:, kc, tap:tap + 128], Wc[:, tap, kc, :], n == 0, n == 23, [H, Wc], [bk[0]])
                    n += 1
            P.cp(rkv[:], bk[0][:, 0:384], [bk[0]], [rkv], eng="act")
            for kc in range(8):
                P.mm(bk[1][:, 0:128], Wl[:, kc, d * 128:(d + 1) * 128], H[:, kc, 1:129], kc == 0, kc == 7, [Wl, H], [bk[1]])
            if d == 1:
                for kc in range(8):
                    P.mm(bk[1][:, 128:256], Wl[:, kc, 256:384], H[:, kc, 1:129], kc == 0, kc == 7, [Wl, H], [bk[1]])
            P.act(pl[0:64, :], bk[1][0:64, 0:128], AF.Tanh, [bk[1]], [pl])
            P.cp(pl[64:128, :], bk[1][64:128, 0:128], [bk[1]], [pl])
            if d == 1:
                P.act(sgg[:], bk[1][:, 128:256], AF.Sigmoid, [bk[1]], [sgg])
            P.mm(bk[1][:, 256:384], pl[0:64, :], W2[0:64, d, :], True, False, [pl, W2], [bk[1]])
            P.mm(bk[1][:, 256:384], ones[0:1, :], B01[0:1, d, 0:128], False, True, [ones, B01], [bk[1]])
            P.mm(bk[1][:, 384:512], pl[64:128, :], W2[64:128, d, :], True, False, [pl, W2], [bk[1]])
            P.mm(bk[1][:, 384:512], ones[0:1, :], B01[0:1, d, 128:256], False, True, [ones, B01], [bk[1]])
            P.act(sig[:], bk[1][:, 256:384], AF.Sigmoid, [bk[1]], [sig])
            P.act(av[:], bk[1][:, 384:512], AF.Sigmoid, [bk[1]], [av])
            r_, k_, v_ = rkv[:, 0:128], rkv[:, 128:256], rkv[:, 256:384]
            P.tt(t0_[:], k_, VEC[:, 0, :], ALU.mult, [rkv, VEC], [t0_])
            P.tt(t2[:], t0_[:], t0_[:], ALU.mult, [t0_], [t2])
            for hh in range(2):
                P.op("dve", lambda e, hh=hh: e.tensor_reduce(ss[:, hh:hh + 1], t2[:, hh * 64:(hh + 1) * 64], AX.X, ALU.add),
                     [t2], [ss])
            P.act(ss[:, 2:4], ss[:, 0:2], AF.Sqrt, [ss], [ss])
            P.ts(ss[:, 2:4], ss[:, 2:4], 1e-12, None, ALU.max, None, [ss], [ss])
            P.op("dve", lambda e: e.reciprocal(ss[:, 4:6], ss[:, 2:4]), [ss], [ss])
            for hh in range(2):
                hs = slice(hh * 64, (hh + 1) * 64)
                P.ts(kkn[:, hs], t0_[:, hs], ss[:, 4 + hh:5 + hh], -1.0, ALU.mult, ALU.mult, [t0_, ss], [kkn])
            P.stt(bh[:], kkn[:], -1.0, av[:], ALU.mult, ALU.mult, [kkn, av], [bh])
            P.stt(t2[:], av[:], -1.0, VEC[:, 1, :], ALU.add, ALU.mult, [av, VEC], [t2])
            P.stt(key[:], t2[:], 1.0, k_, ALU.add, ALU.mult, [t2, rkv], [key])
            P.tr(bk[2][:, 0:128], r_, ident[:], [rkv, ident], [bk[2]])
            P.tr(bk[2][:, 128:256], kkn[:], ident[:], [kkn, ident], [bk[2]])
            P.tr(bk[2][:, 256:384], bh[:], ident[:], [bh, ident], [bk[2]])
            P.tr(bk[2][:, 384:512], key[:], ident[:], [key, ident], [bk[2]])
            P.mm(bk[3][:, 0:128], sig[:], cS[d][:], True, True, [sig, cS[d]], [bk[3]])
            P.mm(bk[3][:, 128:256], sig[:], cI[d][:], True, True, [sig, cI[d]], [bk[3]])
            P.mm(bk[3][:, 256:384], cS[1 - d][:], sig[:], True, True, [sig, cS[1 - d]], [bk[3]])
            P.act(eG[:], bk[3][:, 0:256], AF.Exp, [bk[3]], [eG])
            P.act(enG[:], bk[3][:, 128:256], AF.Exp, [bk[3]], [enG], scale=-1.0)
            P.act(eR[:], bk[3][:, 256:384], AF.Exp, [bk[3]], [eR])
            P.tt(AR[:, 0:128], bk[2][:, 128:256], eG[:, 0:128], ALU.mult, [bk[2], eG], [AR])
            P.tt(AR[:, 128:256], bk[2][:, 0:128], eG[:, 128:256], ALU.mult, [bk[2], eG], [AR])
            P.tt(BtT[:], bk[2][:, 256:384], enG[:], ALU.mult, [bk[2], enG], [BtT])
            P.tt(KtT[:], bk[2][:, 384:512], enG[:], ALU.mult, [bk[2], enG], [KtT])
            P.tt(Bb[:], bh[:], eR[:], ALU.mult, [bh, eR], [Bb])
            P.tt(Kb[:], key[:], eR[:], ALU.mult, [key, eR], [Kb])
            for hh in range(2):
                hp_ = slice(hh * 64, (hh + 1) * 64)
                P.mm(bk[4][:, 0:256], BtT[hp_, :], AR[hp_, :], True, True, [BtT, AR], [bk[4]])
                P.mm(bk[5][:, 0:256], KtT[hp_, :], AR[hp_, :], True, True, [KtT, AR], [bk[5]])
                P.mm(bk[4][:, 256:384], AR[hp_, 0:128], BtT[hp_, :], True, True, [AR, BtT], [bk[4]])
                P.tt(Mm[hh][:], bk[4][:, 0:256], mSI[d][:], ALU.mult, [bk[4], mSI[d]], [Mm[hh]])
                P.tt(Ak[hh][:], bk[5][:, 0:256], mSI[d][:], ALU.mult, [bk[5], mSI[d]], [Ak[hh]])
                P.tt(Lm[hh][:], bk[4][:, 256:384], mS[1 - d][:], ALU.mult, [bk[4], mS[1 - d]], [Lm[hh]])
                P.tt(TtA[hh][:], Mm[hh][:, 0:128], ident[:], ALU.add, [Mm[hh], ident], [TtA[hh]], eng="pool")
                P.tt(TA[hh][:], Lm[hh][:], ident[:], ALU.add, [Lm[hh], ident], [TA[hh]], eng="pool")
                Pc, Qc = Mm[hh], Lm[hh]
                pc_ap, qc_ap = Mm[hh][:, 0:128], Lm[hh][:]
                for lvl in range(6):
                    last = lvl == 5
                    Pn, Qn = Pp[hh][lvl % 2], Qq[hh][lvl % 2]
                    P.mm(bk[6][:, 0:128], qc_ap, pc_ap, True, True, [Pc, Qc], [bk[6]])
                    if not last:
                        P.mm(bk[6][:, 128:256], pc_ap, qc_ap, True, True, [Pc, Qc], [bk[6]])
                    P.cp(Pn[:], bk[6][:, 0:128], [bk[6]], [Pn])
                    if not last:
                        P.cp(Qn[:], bk[6][:, 128:256], [bk[6]], [Qn], eng="act")
                    P.mm(bk[6][:, 256:384], TA[hh][:], Pn[:], True, True, [TA[hh], Pn], [bk[6]])
                    if not last:
                        P.mm(bk[6][:, 384:512], Pn[:], TA[hh][:], True, True, [TA[hh], Pn], [bk[6]])
                    P.tt(TtA[hh][:], TtA[hh][:], bk[6][:, 256:384], ALU.add, [TtA[hh], bk[6]], [TtA[hh]])
                    if not last:
                        P.tt(TA[hh][:], TA[hh][:], bk[6][:, 384:512], ALU.add, [TA[hh], bk[6]], [TA[hh]])
                    Pc, Qc = Pn, Qn
                    pc_ap, qc_ap = Pn[:], Qn[:]
            P.mm(bk[7][:, 0:128], AR[:, 0:128], STB[:], True, False, [AR, STB], [bk[7]])
            for hh in range(2):
                hs = slice(hh * 64, (hh + 1) * 64)
                P.mm(bk[7][:, hs], Ak[hh][:, 0:128], rkv[:, 256 + hh * 64:256 + (hh + 1) * 64], False, hh == 1,
                     [Ak[hh], rkv], [bk[7]])
            P.cp(X[:], bk[7][:, 0:128], [bk[7]], [X])
            for hh in range(2):
                hs = slice(hh * 64, (hh + 1) * 64)
                P.mm(bk[7][:, 128 + hh * 64:128 + (hh + 1) * 64], TtA[hh][:], X[:, hs], True, True, [TtA[hh], X], [bk[7]])
            P.cp(U[:], bk[7][:, 128:256], [bk[7]], [U])
            P.mm(bk[7][:, 256:384], AR[:, 128:256], STB[:], True, False, [AR, STB], [bk[7]])
            for hh in range(2):
                ys = slice(256 + hh * 64, 256 + (hh + 1) * 64)
                hs = slice(hh * 64, (hh + 1) * 64)
                P.mm(bk[7][:, ys], Mm[hh][:, 128:256], U[:, hs], False, False, [Mm[hh], U], [bk[7]])
                P.mm(bk[7][:, ys], Ak[hh][:, 128:256], rkv[:, 256 + hh * 64:256 + (hh + 1) * 64], False, hh == 1,
                     [Ak[hh], rkv], [bk[7]])
            P.mm(bk[7][:, 384:512], Bb[:], U[:], True, False, [Bb, U], [bk[7]])
            P.mm(bk[7][:, 384:512], Kb[:], v_, False, True, [Kb, rkv], [bk[7]])
            dcol = 255 if d == 0 else 128
            if d == 0:
                P.cp(YF[:, c, :], bk[7][:, 256:384], [bk[7]], [YF], eng="act")
            else:
                P.tt(yb[:], bk[7][:, 256:384], YF[:, c, :], ALU.add, [bk[7], YF], [yb])
            for hh in range(2):
                hs = slice(hh * 64, (hh + 1) * 64)
                P.stt(STB[hs, hs], STB[hs, hs], eG[hs, dcol:dcol + 1], bk[7][hs, 384 + hh * 64:384 + (hh + 1) * 64],
                      ALU.mult, ALU.add, [STB, eG, bk[7]], [STB])
            if d == 1:
                P.mm(bk[0][:, 384:512], sgg[:], G2[:], True, True, [sgg, G2], [bk[0]])
                P.tt(t2[:], yb[:], yb[:], ALU.mult, [yb], [t2])
                for hh in range(2):
                    hs = slice(hh * 64, (hh + 1) * 64)
                    P.op("dve", lambda e, hh=hh, hs=hs: e.tensor_reduce(gn[:, hh:hh + 1], yb[:, hs], AX.X, ALU.add), [yb], [gn])
                    P.op("dve", lambda e, hh=hh, hs=hs: e.tensor_reduce(gn[:, 2 + hh:3 + hh], t2[:, hs], AX.X, ALU.add), [t2], [gn])
                P.ts(gn[:, 4:8], gn[:, 0:4], 1.0 / 64, None, ALU.mult, None, [gn], [gn])
                P.tt(gn[:, 8:10], gn[:, 4:6], gn[:, 4:6], ALU.mult, [gn], [gn])
                P.tt(gn[:, 10:12], gn[:, 6:8], gn[:, 8:10], ALU.subtract, [gn], [gn])
                P.act(gn[:, 12:14], gn[:, 10:12], AF.Sqrt, [gn, epsg], [gn], bias=epsg[:, 0:1])
                P.op("dve", lambda e: e.reciprocal(gn[:, 14:16], gn[:, 12:14]), [gn], [gn])
                for hh in range(2):
                    hs = slice(hh * 64, (hh + 1) * 64)
                    P.ts(yn[:, hs], yb[:, hs], gn[:, 4 + hh:5 + hh], gn[:, 14 + hh:15 + hh], ALU.subtract, ALU.mult,
                         [yb, gn], [yn])
                P.tt(yn[:], yn[:], VEC[:, 3, :], ALU.mult, [yn, VEC], [yn])
                P.tt(yn[:], yn[:], VEC[:, 4, :], ALU.add, [yn, VEC], [yn])
                P.tt(rk[:], r_, k_, ALU.mult, [rkv], [rk])
                P.tt(rk[:], rk[:], VEC[:, 2, :], ALU.mult, [rk, VEC], [rk])
                for hh in range(2):
                    hs = slice(hh * 64, (hh + 1) * 64)
                    P.op("dve", lambda e, hh=hh, hs=hs: e.tensor_reduce(ss[:, 6 + hh:7 + hh], rk[:, hs], AX.X, ALU.add), [rk], [ss])
                    P.stt(yn[:, hs], rkv[:, 256 + hh * 64:256 + (hh + 1) * 64], ss[:, 6 + hh:7 + hh], yn[:, hs], ALU.mult, ALU.add,
                          [rkv, ss, yn], [yn])
                Y = yo[step % 2]
                P.tt(Y[:], yn[:], bk[0][:, 384:512], ALU.mult, [yn, bk[0]], [Y])
                P.tr(bk[3][:, 384:512], Y[:], ident[:], [Y, ident], [bk[3]])
                Yb = yob[step % 2]
                P.cp(Yb[:], bk[3][:, 384:512], [bk[3]], [Yb], eng="act")
                P.dma("sp", ycols(yr, t0), Yb[:], reads=[Yb])
    return P.finish() if own else None


def build_rwkv(P=None, T=None):
    own = P is None
    P = P or Prog()
    hT = P.io(T, "hT", [8, 128, SEQ_ALL], BF16, "ExternalInput")
    wrkv = P.io(T, "wrkv", [D, 384], F32, "ExternalInput")
    crkv = P.io(T, "crkv", [128, 3, 384], F32, "ExternalInput")
    wl = P.io(T, "wl", [D, 384], F32, "ExternalInput")
    w2a2 = P.io(T, "w2a2", [2, 128, 128], F32, "ExternalInput")
    bias01 = P.io(T, "bias01", [1, 2, 256], F32, "ExternalInput")
    g2 = P.io(T, "g2", [128, 128], F32, "ExternalInput")
    vecs = P.io(T, "vecs", [128, 5, 128], F32, "ExternalInput")
    yr = P.io(T, "yr", [128, SEQ_ALL], BF16, "ExternalOutput")

    bk = [P.ps([128, 512], F32, f"bank{i}") for i in range(8)]
    ident = make_ident(P)
    ones = P.sb([128, 128], F32, "ones")
    P.memset(ones[:], 1.0, [ones])
    mI = [aff_mask(P, [[1, 128]], -1, ALU.is_ge), aff_mask(P, [[-1, 128]], 1, ALU.is_ge)]
    mS = [aff_mask(P, [[1, 128]], -1, ALU.is_gt), aff_mask(P, [[-1, 128]], 1, ALU.is_gt)]
    cI = [aff_mask(P, [[1, 128]], -1, ALU.is_ge, W_SCALE), aff_mask(P, [[-1, 128]], 1, ALU.is_ge, W_SCALE)]
    cS = [aff_mask(P, [[1, 128]], -1, ALU.is_gt, W_SCALE), aff_mask(P, [[-1, 128]], 1, ALU.is_gt, W_SCALE)]
    mSI2 = P.sb([128, 2, 2, 256], F32, "mSI2")
    mL4 = P.sb([128, 4, 128], F32, "mL4")
    I4 = P.sb([128, 4, 128], F32, "I4")
    for d in range(2):
        for hh in range(2):
            P.cp(mSI2[:, d, hh, 0:128], mS[d][:], [mS[d]], [mSI2])
            P.cp(mSI2[:, d, hh, 128:256], mI[d][:], [mI[d]], [mSI2])
            P.cp(mL4[:, d * 2 + hh, :], mS[1 - d][:], [mS[1 - d]], [mL4])
            P.cp(I4[:, d * 2 + hh, :], ident[:], [ident], [I4])

    wst = P.sb([128, 8, 384], F32, "wst")
    for kc in range(8):
        P.dma("sp", wst[:, kc, :], wrkv[kc * 128:(kc + 1) * 128, :], writes=[wst])
    cws = P.sb([128, 3, 384], F32, "cws")
    P.dma("sp", cws[:], crkv[:, :, :], writes=[cws])
    Wc = P.sb([128, 3, 8, 384], BF16, "Wc")
    for tap in range(3):
        for kc in range(8):
            P.tt(Wc[:, tap, kc, :], wst[:, kc, :], cws[:, tap, :], ALU.mult, [wst, cws], [Wc])
    Wl = P.sb([128, 8, 384], BF16, "Wl")
    for kc in range(8):
        P.dma("act", wst[:, kc, :], wl[kc * 128:(kc + 1) * 128, :], writes=[wst])
    P.cp(Wl[:], wst[:], [wst], [Wl])
    W2 = P.sb([128, 2, 128], F32, "W2")
    for d in range(2):
        P.dma("act", W2[:, d, :], w2a2[d], writes=[W2])
    B01 = P.sb([1, 2, 256], F32, "B01")
    P.dma("act", B01[:], bias01[:, :, :], writes=[B01])
    G2 = P.sb([128, 128], F32, "G2")
    P.dma("act", G2[:], g2[:, :], writes=[G2])
    VEC = P.sb([128, 5, 128], F32, "VEC")
    P.dma("act", VEC[:], vecs[:, :, :], writes=[VEC])
    VK2 = P.sb([128, 2, 2, 128], F32, "VK2")
    for j in range(2):
        for d in range(2):
            P.cp(VK2[:, j, d, :], VEC[:, j, :], [VEC], [VK2])
    epsg = P.sb([128, 1], F32, "epsg")
    P.memset(epsg[:], R_GN_EPS, [epsg])

    YD = [P.sb([128, SEQT, 128], BF16, f"YD{d}") for d in range(2)]
    hb = [P.sb([128, 8, 130], BF16, f"hb{i}") for i in range(4 if RWKV_INTERLEAVE else 2)]
    STB = [P.sb([128, 128], F32, f"STB{d}") for d in range(2)]
    for d in range(2):
        P.memset(STB[d][:], 0.0, [STB[d]])

    def TL(shape, name):
        return P.sb(shape, F32, name)

    NB_ = 2 if RWKV_INTERLEAVE else 1
    rkv_ = [TL([128, 2, 384], f"rkv{i}") for i in range(NB_)]
    pl = TL([128, 2, 128], "pl")
    sig = TL([128, 2, 128], "sig")
    av = TL([128, 2, 128], "av")
    t0_ = TL([128, 2, 128], "t0_")
    t2 = TL([128, 2, 128], "t2")
    ss = TL([128, 16], "ss")
    kkn = TL([128, 2, 128], "kkn")
    bh = TL([128, 2, 128], "bh")
    key = TL([128, 2, 128], "key")
    eG_ = [TL([128, 2, 256], f"eG{i}") for i in range(NB_)]
    enG = TL([128, 2, 128], "enG")
    eR = TL([128, 2, 128], "eR")
    AR_ = [TL([128, 2, 256], f"AR{i}") for i in range(NB_)]
    BtT = TL([128, 2, 128], "BtT")
    KtT = TL([128, 2, 128], "KtT")
    Bb_ = [TL([128, 2, 128], f"Bb{i}") for i in range(NB_)]
    Kb_ = [TL([128, 2, 128], f"Kb{i}") for i in range(NB_)]
    MA_ = [TL([128, 2, 2, 256], f"MA{i}") for i in range(NB_)]
    AK_ = [TL([128, 2, 2, 256], f"AK{i}") for i in range(NB_)]
    L4_ = [TL([128, 4, 128], f"L4{i}") for i in range(NB_)]
    P4 = [TL([128, 4, 128], f"P4{i}") for i in range(2)]
    Q4 = [TL([128, 4, 128], f"Q4{i}") for i in range(2)]
    TtA = TL([128, 4, 128], "TtA")
    TA = TL([128, 4, 128], "TA")
    X2 = TL([128, 2, 128], "X2")
    U2 = TL([128, 2, 128], "U2")

    orders = chunk_orders()

    def pre(step):
        par = (step % 2) if RWKV_INTERLEAVE else 0
        rkv, eG, AR, Bb, Kb, MA, AK, L4 = rkv_[par], eG_[par], AR_[par], Bb_[par], Kb_[par], MA_[par], AK_[par], L4_[par]
        cc = [orders[0][step], orders[1][step]]
        for d in range(2):
            c = cc[d]
            lo, hi = (0, NCTX) if c < CTXT else (NCTX, SEQ_ALL)
            H = hb[(par * 2 + d) % len(hb)]
            load_hblk_halo(P, hT, H, c * 128, 128, lo, hi, eng="sp" if d == 0 else "act")
            n = 0
            for tap in range(3):
                for kc in range(8):
                    P.mm(bk[d][:, 0:384], H[:, kc, tap:tap + 128], Wc[:, tap, kc, :], n == 0, n == 23, [H, Wc], [bk[d]])
                    n += 1
            for kc in range(8):
                P.mm(bk[d][:, 384:512], Wl[:, kc, d * 128:(d + 1) * 128], H[:, kc, 1:129], kc == 0, kc == 7, [Wl, H], [bk[d]])
            P.cp(rkv[:, d, :], bk[d][:, 0:384], [bk[d]], [rkv], eng="act")
            P.act(pl[0:64, d, :], bk[d][0:64, 384:512], AF.Tanh, [bk[d]], [pl])
            P.cp(pl[64:128, d, :], bk[d][64:128, 384:512], [bk[d]], [pl])
        for d in range(2):
            P.mm(bk[2][:, d * 256:d * 256 + 128], pl[0:64, d, :], W2[0:64, d, :], True, False, [pl, W2], [bk[2]])
            P.mm(bk[2][:, d * 256:d * 256 + 128], ones[0:1, :], B01[0:1, d, 0:128], False, True, [ones, B01], [bk[2]])
            P.mm(bk[2][:, d * 256 + 128:d * 256 + 256], pl[64:128, d, :], W2[64:128, d, :], True, False, [pl, W2], [bk[2]])
            P.mm(bk[2][:, d * 256 + 128:d * 256 + 256], ones[0:1, :], B01[0:1, d, 128:256], False, True, [ones, B01], [bk[2]])
        b2v = bk[2][:, :].rearrange("p (d j c) -> p d j c", d=2, j=2)
        P.act(sig[:], b2v[:, :, 0, :], AF.Sigmoid, [bk[2]], [sig])
        P.act(av[:], b2v[:, :, 1, :], AF.Sigmoid, [bk[2]], [av])
        k2 = rkv[:, :, 128:256]
        P.tt(t0_[:], k2, VK2[:, 0], ALU.mult, [rkv, VK2], [t0_])
        P.tt(t2[:], t0_[:], t0_[:], ALU.mult, [t0_], [t2])
        P.op("dve", lambda e: e.tensor_reduce(ss[:, 0:4], t2[:, :, :].rearrange("p d (h k) -> p (d h) k", h=2), AX.X, ALU.add),
             [t2], [ss])
        P.act(ss[:, 4:8], ss[:, 0:4], AF.Sqrt, [ss], [ss])
        P.ts(ss[:, 4:8], ss[:, 4:8], 1e-12, None, ALU.max, None, [ss], [ss])
        P.op("dve", lambda e: e.reciprocal(ss[:, 8:12], ss[:, 4:8]), [ss], [ss])
        for d in range(2):
            for hh in range(2):
                hs = slice(hh * 64, (hh + 1) * 64)
                q = d * 2 + hh
                P.ts(kkn[:, d, hs], t0_[:, d, hs], ss[:, 8 + q:9 + q], -1.0, ALU.mult, ALU.mult, [t0_, ss], [kkn])
        P.stt(bh[:], kkn[:], -1.0, av[:], ALU.mult, ALU.mult, [kkn, av], [bh])
        P.stt(t2[:], av[:], -1.0, VK2[:, 1], ALU.add, ALU.mult, [av, VK2], [t2])
        P.stt(key[:], t2[:], 1.0, k2, ALU.add, ALU.mult, [t2, rkv], [key])
        for d in range(2):
            tb = bk[3] if d == 0 else bk[7]
            P.tr(tb[:, 0:128], rkv[:, d, 0:128], ident[:], [rkv, ident], [tb])
            P.tr(tb[:, 128:256], kkn[:, d, :], ident[:], [kkn, ident], [tb])
            P.tr(tb[:, 256:384], bh[:, d, :], ident[:], [bh, ident], [tb])
            P.tr(tb[:, 384:512], key[:, d, :], ident[:], [key, ident], [tb])
            gb_ = bk[d]
            P.mm(gb_[:, 0:128], sig[:, d, :], cS[d][:], True, True, [sig, cS[d]], [gb_])
            P.mm(gb_[:, 128:256], sig[:, d, :], cI[d][:], True, True, [sig, cI[d]], [gb_])
            P.mm(gb_[:, 256:384], cS[1 - d][:], sig[:, d, :], True, True, [sig, cS[1 - d]], [gb_])
        for d in range(2):
            tb, gb_ = (bk[3] if d == 0 else bk[7]), bk[d]
            P.act(eG[:, d, :], gb_[:, 0:256], AF.Exp, [gb_], [eG])
            P.act(enG[:, d, :], gb_[:, 128:256], AF.Exp, [gb_], [enG], scale=-1.0)
            P.act(eR[:, d, :], gb_[:, 256:384], AF.Exp, [gb_], [eR])
            P.tt(AR[:, d, 0:128], tb[:, 128:256], eG[:, d, 0:128], ALU.mult, [tb, eG], [AR])
            P.tt(AR[:, d, 128:256], tb[:, 0:128], eG[:, d, 128:256], ALU.mult, [tb, eG], [AR])
            P.tt(BtT[:, d, :], tb[:, 256:384], enG[:, d, :], ALU.mult, [tb, enG], [BtT])
            P.tt(KtT[:, d, :], tb[:, 384:512], enG[:, d, :], ALU.mult, [tb, enG], [KtT])
        P.tt(Bb[:], bh[:], eR[:], ALU.mult, [bh, eR], [Bb], eng="pool")
        P.tt(Kb[:], key[:], eR[:], ALU.mult, [key, eR], [Kb], eng="pool")
        for d in range(2):
            for hh in range(2):
                hp_ = slice(hh * 64, (hh + 1) * 64)
                q = d * 2 + hh
                mb = bk[d]
                P.mm(mb[:, hh * 256:(hh + 1) * 256], BtT[hp_, d, :], AR[hp_, d, :], True, True, [BtT, AR], [mb])
                ab = bk[2] if d == 0 else bk[7]
                P.mm(ab[:, hh * 256:(hh + 1) * 256], KtT[hp_, d, :], AR[hp_, d, :], True, True, [KtT, AR], [ab])
                P.mm(bk[3][:, q * 128:(q + 1) * 128], AR[hp_, d, 0:128], BtT[hp_, d, :], True, True, [AR, BtT], [bk[3]])
        for d in range(2):
            mb, ab = bk[d], (bk[2] if d == 0 else bk[7])
            P.tt(MA[:, d], mb[:, :].rearrange("p (h c) -> p h c", h=2), mSI2[:, d], ALU.mult, [mb, mSI2], [MA])
            P.tt(AK[:, d], ab[:, :].rearrange("p (h c) -> p h c", h=2), mSI2[:, d], ALU.mult, [ab, mSI2], [AK])
        P.tt(L4[:], bk[3][:, :].rearrange("p (q c) -> p q c", q=4), mL4[:], ALU.mult, [bk[3], mL4], [L4])

    def inv_chain(step, fill):
        par = (step % 2) if RWKV_INTERLEAVE else 0
        rkv, eG, AR, Bb, Kb, MA, AK, L4 = rkv_[par], eG_[par], AR_[par], Bb_[par], Kb_[par], MA_[par], AK_[par], L4_[par]
        cc = [orders[0][step], orders[1][step]]
        npts = 16
        per = (len(fill) + npts - 1) // npts if fill else 0
        pos = [0]

        def filler():
            if not RWKV_INTERLEAVE:
                return
            for th in fill[pos[0]:pos[0] + per]:
                th()
            pos[0] += per

        M4 = MA[:, :, :, 0:128].rearrange("p d h c -> p (d h) c")
        P.tt(TtA[:], M4, I4[:], ALU.add, [MA, I4], [TtA], eng="pool")
        P.tt(TA[:], L4[:], I4[:], ALU.add, [L4, I4], [TA], eng="pool")
        Pc_buf, Qc_buf = MA, L4
        pc = lambda q: MA[:, q // 2, q % 2, 0:128]
        qc = lambda q: L4[:, q, :]
        bP, bQ, bT, bTT = bk[4], bk[5], bk[6], bk[4]
        for lvl in range(6):
            last = lvl == 5
            Pn, Qn = P4[lvl % 2], Q4[lvl % 2]
            for q in range(4):
                P.mm(bP[:, q * 128:(q + 1) * 128], qc(q), pc(q), True, True, [Pc_buf, Qc_buf], [bP])
            if not last:
                for q in range(4):
                    P.mm(bQ[:, q * 128:(q + 1) * 128], pc(q), qc(q), True, True, [Pc_buf, Qc_buf], [bQ])
            P.cp(Pn[:], bP[:, :].rearrange("p (q c) -> p q c", q=4), [bP], [Pn])
            if not last:
                P.cp(Qn[:], bQ[:, :].rearrange("p (q c) -> p q c", q=4), [bQ], [Qn], eng="act")
            filler()
            for q in range(4):
                P.mm(bT[:, q * 128:(q + 1) * 128], TA[:, q, :], Pn[:, q, :], True, True, [TA, Pn], [bT])
            if not last:
                for q in range(4):
                    P.mm(bTT[:, q * 128:(q + 1) * 128], Pn[:, q, :], TA[:, q, :], True, True, [TA, Pn], [bTT])
            P.tt(TtA[:], TtA[:], bT[:, :].rearrange("p (q c) -> p q c", q=4), ALU.add, [TtA, bT], [TtA])
            if not last:
                P.tt(TA[:], TA[:], bTT[:, :].rearrange("p (q c) -> p q c", q=4), ALU.add, [TA, bTT], [TA])
            filler()
            Pc_buf, Qc_buf = Pn, Qn
            pc = lambda q, Pn=Pn: Pn[:, q, :]
            qc = lambda q, Qn=Qn: Qn[:, q, :]
        bXU, bYS = bk[5], bk[6]
        for d in range(2):
            P.mm(bXU[:, d * 128:(d + 1) * 128], AR[:, d, 0:128], STB[d][:], True, False, [AR, STB[d]], [bXU])
            for hh in range(2):
                o = d * 128 + hh * 64
                P.mm(bXU[:, o:o + 64], AK[:, d, hh, 0:128], rkv[:, d, 256 + hh * 64:256 + (hh + 1) * 64], False, hh == 1,
                     [AK, rkv], [bXU])
        P.cp(X2[:], bXU[:, 0:256].rearrange("p (d c) -> p d c", d=2), [bXU], [X2])
        filler()
        for d in range(2):
            for hh in range(2):
                o = 256 + d * 128 + hh * 64
                P.mm(bXU[:, o:o + 64], TtA[:, d * 2 + hh, :], X2[:, d, hh * 64:(hh + 1) * 64], True, True, [TtA, X2], [bXU])
        P.cp(U2[:], bXU[:, 256:512].rearrange("p (d c) -> p d c", d=2), [bXU], [U2])
        filler()
        for d in range(2):
            P.mm(bYS[:, d * 128:(d + 1) * 128], AR[:, d, 128:256], STB[d][:], True, False, [AR, STB[d]], [bYS])
            for hh in range(2):
                o = d * 128 + hh * 64
                hs = slice(hh * 64, (hh + 1) * 64)
                P.mm(bYS[:, o:o + 64], MA[:, d, hh, 128:256], U2[:, d, hs], False, False, [MA, U2], [bYS])
                P.mm(bYS[:, o:o + 64], AK[:, d, hh, 128:256], rkv[:, d, 256 + hh * 64:256 + (hh + 1) * 64], False, hh == 1,
                     [AK, rkv], [bYS])
        for d in range(2):
            P.mm(bYS[:, 256 + d * 128:256 + (d + 1) * 128], Bb[:, d, :], U2[:, d, :], True, False, [Bb, U2], [bYS])
            P.mm(bYS[:, 256 + d * 128:256 + (d + 1) * 128], Kb[:, d, :], rkv[:, d, 256:384], False, True, [Kb, rkv], [bYS])
        for d in range(2):
            P.cp(YD[d][:, cc[d], :], bYS[:, d * 128:(d + 1) * 128], [bYS], [YD[d]], eng="act")
            dcol = 255 if d == 0 else 128
            for hh in range(2):
                hs = slice(hh * 64, (hh + 1) * 64)
                o = 256 + d * 128 + hh * 64
                P.stt(STB[d][hs, hs], STB[d][hs, hs], eG[hs, d, dcol:dcol + 1], bYS[hs, o:o + 64], ALU.mult, ALU.add,
                      [STB[d], eG, bYS], [STB[d]])
        filler()
        for th in fill[pos[0]:]:
            th()

    pre(0)
    for step in range(SEQT):
        fill = []
        if RWKV_INTERLEAVE and step + 1 < SEQT:
            P.defer = []
            pre(step + 1)
            fill, P.defer = P.defer, None
        inv_chain(step, fill)
        if not RWKV_INTERLEAVE and step + 1 < SEQT:
            pre(step + 1)

    rk2 = [TL([128, 384], f"rk2{i}") for i in range(2)]
    sgg = [TL([128, 128], f"sgg{i}") for i in range(2)]
    ybs = [TL([128, 128], f"ybs{i}") for i in range(2)]
    tq = [TL([128, 128], f"tq{i}") for i in range(2)]
    gn = [TL([128, 16], f"gn{i}") for i in range(2)]
    yn = [TL([128, 128], f"yn{i}") for i in range(2)]
    rk = [TL([128, 128], f"rk{i}") for i in range(2)]
    yo = [TL([128, 128], f"yo{i}") for i in range(2)]
    yob = [P.sb([128, 128], BF16, f"yob{i}") for i in range(2)]
    for c in range(SEQT):
        pi = c % 2
        lo, hi = (0, NCTX) if c < CTXT else (NCTX, SEQ_ALL)
        H = hb[pi]
        load_hblk_halo(P, hT, H, c * 128, 128, lo, hi)
        b0, b1 = bk[pi * 4], bk[pi * 4 + 1]
        n = 0
        for tap in range(3):
            for kc in range(8):
                P.mm(b0[:, 0:384], H[:, kc, tap:tap + 128], Wc[:, tap, kc, :], n == 0, n == 23, [H, Wc], [b0])
                n += 1
        for kc in range(8):
            P.mm(b1[:, 0:128], Wl[:, kc, 256:384], H[:, kc, 1:129], kc == 0, kc == 7, [Wl, H], [b1])
        R_ = rk2[pi]
        P.cp(R_[:], b0[:, 0:384], [b0], [R_], eng="act")
        P.act(sgg[pi][:], b1[:, 0:128], AF.Sigmoid, [b1], [sgg[pi]])
        P.mm(b1[:, 128:256], sgg[pi][:], G2[:], True, True, [sgg[pi], G2], [b1])
        yb, t2_, g_, y_ = ybs[pi], tq[pi], gn[pi], yn[pi]
        P.tt(yb[:], YD[0][:, c, :], YD[1][:, c, :], ALU.add, [YD[0], YD[1]], [yb])
        P.tt(t2_[:], yb[:], yb[:], ALU.mult, [yb], [t2_], eng="pool")
        P.op("dve", lambda e, g_=g_, yb=yb: e.tensor_reduce(g_[:, 0:2], yb[:, :].rearrange("p (h k) -> p h k", h=2), AX.X, ALU.add),
             [yb], [g_])
        P.op("dve", lambda e, g_=g_, t2_=t2_: e.tensor_reduce(g_[:, 2:4], t2_[:, :].rearrange("p (h k) -> p h k", h=2), AX.X, ALU.add),
             [t2_], [g_])
        P.ts(g_[:, 4:8], g_[:, 0:4], 1.0 / 64, None, ALU.mult, None, [g_], [g_])
        P.tt(g_[:, 8:10], g_[:, 4:6], g_[:, 4:6], ALU.mult, [g_], [g_])
        P.tt(g_[:, 10:12], g_[:, 6:8], g_[:, 8:10], ALU.subtract, [g_], [g_])
        P.act(g_[:, 12:14], g_[:, 10:12], AF.Sqrt, [g_, epsg], [g_], bias=epsg[:, 0:1])
        P.op("dve", lambda e, g_=g_: e.reciprocal(g_[:, 14:16], g_[:, 12:14]), [g_], [g_])
        for hh in range(2):
            hs = slice(hh * 64, (hh + 1) * 64)
            P.ts(y_[:, hs], yb[:, hs], g_[:, 4 + hh:5 + hh], g_[:, 14 + hh:15 + hh], ALU.subtract, ALU.mult, [yb, g_], [y_])
        P.tt(y_[:], y_[:], VEC[:, 3, :], ALU.mult, [y_, VEC], [y_])
        P.tt(y_[:], y_[:], VEC[:, 4, :], ALU.add, [y_, VEC], [y_])
        rk_ = rk[pi]
        P.tt(rk_[:], R_[:, 0:128], R_[:, 128:256], ALU.mult, [R_], [rk_], eng="pool")
        P.tt(rk_[:], rk_[:], VEC[:, 2, :], ALU.mult, [rk_, VEC], [rk_], eng="pool")
        P.op("dve", lambda e, g_=g_, rk_=rk_: e.tensor_reduce(g_[:, 0:2], rk_[:, :].rearrange("p (h k) -> p h k", h=2), AX.X, ALU.add),
             [rk_], [g_])
        for hh in range(2):
            hs = slice(hh * 64, (hh + 1) * 64)
            P.stt(y_[:, hs], R_[:, 256 + hh * 64:256 + (hh + 1) * 64], g_[:, hh:hh + 1], y_[:, hs], ALU.mult, ALU.add,
                  [R_, g_, y_], [y_])
        Y = yo[pi]
        P.tt(Y[:], y_[:], b1[:, 128:256], ALU.mult, [y_, b1], [Y])
        P.tr(b1[:, 256:384], Y[:], ident[:], [Y, ident], [b1])
        Yb = yob[pi]
        P.cp(Yb[:], b1[:, 256:384], [b1], [Yb], eng="act")
        P.dma("sp", ycols(yr, c * 128), Yb[:], reads=[Yb])
    return P.finish() if own else None


def rwkv_inputs(hTs, li, w_in, r_conv, r_w0, r_w2, r_a0, r_a2, r_g2, r_kk, r_ka, r_rk, r_ln_w, r_ln_b):
    off = col_offsets()
    ins = []
    W = w_in[li]
    for b in range(2):
        hT = np.ascontiguousarray(hTs[b].reshape(8, 128, SEQ_ALL))
        for hp in range(4):
            cs_ = slice(128 * hp, 128 * (hp + 1))
            wrkv = np.concatenate([W[:, off[n] + 128 * hp: off[n] + 128 * (hp + 1)] for n in ('r_r', 'r_k', 'r_v')], 1)
            conv = np.concatenate([r_conv[li][:, j * 512 + 128 * hp: j * 512 + 128 * (hp + 1)] for j in range(3)], 1)
            wl = np.concatenate([W[:, off['r_wf']:off['r_wf'] + 64], W[:, off['r_af']:off['r_af'] + 64],
                                 W[:, off['r_wb']:off['r_wb'] + 64], W[:, off['r_ab']:off['r_ab'] + 64],
                                 W[:, off['r_g']:off['r_g'] + 128]], 1)
            w2a2 = np.stack([np.concatenate([r_w2[li, d][:, cs_], r_a2[li, d][:, cs_]], 0) for d in range(2)], 0)
            bias01 = np.stack([np.concatenate([r_w0[li, d][cs_], r_a0[li, d][cs_]], 0) for d in range(2)], 0)[None]
            vecs = np.stack([np.broadcast_to(v[li][None, cs_], (128, 128)) for v in (r_kk, r_ka, r_rk, r_ln_w, r_ln_b)], 1)
            ins.append({"hT": hT, "wrkv": np.ascontiguousarray(wrkv),
                        "crkv": np.ascontiguousarray(np.broadcast_to(conv[None], (128, 3, 384)).astype(np.float32)),
                        "wl": np.ascontiguousarray(wl), "w2a2": np.ascontiguousarray(w2a2.astype(np.float32)),
                        "bias01": np.ascontiguousarray(bias01.astype(np.float32)),
                        "g2": np.ascontiguousarray(r_g2[li][:, cs_]), "vecs": np.ascontiguousarray(vecs.astype(np.float32))})
    return ins


def build_merge(P=None, T=None, fused=False):
    own = P is None
    P = P or Prog()
    x = P.io(T, "x", [NT, 128, D], F32, "ExternalInput")
    hT = P.io(T, "hT", [8, 128, TOK], BF16, "ExternalInput")
    if fused:
        yall = T["yall"]
        sel = T["sel"]
    else:
        yT = P.io(T, "yT", [12, 128, TOK], BF16, "ExternalInput")
    wg = P.io(T, "wg", [D, 3 * D], F32, "ExternalInput")
    wb = P.io(T, "wb", [3, 512, D], F32, "ExternalInput")
    wo = P.io(T, "wo", [D, D], F32, "ExternalInput")
    gateb = P.io(T, "gateb", [2, 128, D], F32, "ExternalInput")
    xo = P.io(T, "xo", [NT, 128, D], F32, "ExternalOutput")

    Wg = P.sb([128, 8, 3 * D], BF16, "Wg")
    Pb = P.sb([128, 12, D], BF16, "Pb")
    Wo = P.sb([128, 8, D], BF16, "Wo")
    stage = [P.sb([128, 1024], F32, f"stg{i}") for i in range(2)]
    n = 0
    for kc in range(8):
        for q in range(3):
            load_cast(P, Wg, Wg[:, kc, q * 1024:(q + 1) * 1024], wg[kc * 128:(kc + 1) * 128, q * 1024:(q + 1) * 1024],
                      stage, None, n, 1024)
            n += 1
    for br in range(3):
        for c in range(4):
            load_cast(P, Pb, Pb[:, br * 4 + c, :], wb[br, c * 128:(c + 1) * 128, :], stage, None, n, 1024)
            n += 1
    for kc in range(8):
        load_cast(P, Wo, Wo[:, kc, :], wo[kc * 128:(kc + 1) * 128, :], stage, None, n, 1024)
        n += 1
    gates = P.sb([128, 2, D], F32, "gates")
    for s in range(2):
        P.dma("act", gates[:, s, :], gateb[s], writes=[gates])

    X = P.sb([128, 3, D], F32, "X")
    hb = P.sb([128, 8, 384], BF16, "hb")
    yb = P.sb([128, 12, 384], BF16, "yb")
    if fused:
        yc = [P.sb([128, 12, 384], BF16, f"yc{i}") for i in range(4)]
        sels = P.sb([128, 4], F32, "sels")
        P.dma("act", sels[:], sel[:, :], writes=[sels])

    zT = P.sb([128, 8, 384], BF16, "zT")
    sg = [P.sb([128, 384], F32, f"sg{i}") for i in range(2)]
    za = P.sb([128, 384], F32, "za")
    tm = [P.sb([128, 384], F32, f"tm{i}") for i in range(2)]
    tmp = [P.sb([128, 512], F32, f"tmp{i}") for i in range(2)]
    pg = [P.ps([128, 512], F32, f"pg{i}") for i in range(2)]
    pp = [P.ps([128, 512], F32, f"pp{i}") for i in range(2)]
    py = [P.ps([128, 512], F32, f"py{i}") for i in range(2)]

    k = 0
    for bi, (t0, nb) in enumerate(BLOCKS):
        s = 0 if bi == 0 else 1
        N = nb * 128
        for i in range(nb):
            P.dma("sp", X[:, i, :], x[t0 + i], writes=[X])
        for kc in range(8):
            P.dma("act", hb[:, kc, 0:N], hT[kc, :, t0 * 128:t0 * 128 + N], writes=[hb])
        if fused:
            for cand in range(4):
                for c in range(12):
                    j, br = c % 4, c // 4
                    P.dma("sp" if c % 2 else "act", yc[cand][:, c, 0:N],
                          yall[br, cand, j * 128:(j + 1) * 128, t0 * 128:t0 * 128 + N], writes=[yc[cand]])
            P.ts(yb[:, :, 0:N], yc[0][:, :, 0:N], sels[:, 0:1], None, ALU.mult, None, [yc[0], sels], [yb])
            for cand in range(1, 4):
                P.stt(yb[:, :, 0:N], yc[cand][:, :, 0:N], sels[:, cand:cand + 1], yb[:, :, 0:N], ALU.mult, ALU.add,
                      [yc[cand], sels, yb], [yb])
        else:
            for c in range(12):
                P.dma("sp" if c % 2 else "act", yb[:, c, 0:N], yT[c, :, t0 * 128:t0 * 128 + N], writes=[yb])
        for dc in range(8):
            for br in range(3):
                G, Q = pg[k % 2], pp[k % 2]
                S, T_ = sg[k % 2], tm[k % 2]
                k += 1
                for kc in range(8):
                    P.mm(G[:, 0:N], Wg[:, kc, br * D + dc * 128: br * D + (dc + 1) * 128], hb[:, kc, 0:N], kc == 0, kc == 7,
                         [Wg, hb], [G])
                for c in range(4):
                    P.mm(Q[:, 0:N], Pb[:, br * 4 + c, dc * 128:(dc + 1) * 128], yb[:, br * 4 + c, 0:N], c == 0, c == 3,
                         [Pb, yb], [Q])
                P.act(S[:, 0:N], G[:, 0:N], AF.Sigmoid, [G], [S])
                if br == 0:
                    P.tt(za[:, 0:N], S[:, 0:N], Q[:, 0:N], ALU.mult, [S, Q], [za])
                else:
                    P.tt(T_[:, 0:N], S[:, 0:N], Q[:, 0:N], ALU.mult, [S, Q], [T_])
                    if br == 1:
                        P.tt(za[:, 0:N], za[:, 0:N], T_[:, 0:N], ALU.add, [za, T_], [za], eng="pool")
                    else:
                        P.tt(zT[:, dc, 0:N], za[:, 0:N], T_[:, 0:N], ALU.add, [za, T_], [zT])
        for i in range(nb):
            for h in range(2):
                Y = py[(i * 2 + h) % 2]
                for dc in range(8):
                    P.mm(Y[:, :], zT[:, dc, i * 128:(i + 1) * 128], Wo[:, dc, h * 512:(h + 1) * 512], dc == 0, dc == 7,
                         [zT, Wo], [Y])
                T2 = tmp[(i * 2 + h) % 2]
                P.tt(T2[:], Y[:], gates[:, s, h * 512:(h + 1) * 512], ALU.mult, [Y, gates], [T2])
                P.tt(X[:, i, h * 512:(h + 1) * 512], X[:, i, h * 512:(h + 1) * 512], T2[:], ALU.add, [X, T2], [X], eng="pool")
            P.dma("sp", xo[t0 + i], X[:, i, :], reads=[X])
    return P.finish() if own else None


def featT_shard(y_b, nchunk):
    pad = np.zeros((4 * TOK, y_b.shape[1]), y_b.dtype)
    pad[:y_b.shape[0]] = y_b
    out = []
    for i in range(4):
        blk = pad[i * TOK:(i + 1) * TOK]
        out.append(np.ascontiguousarray(blk.T.reshape(nchunk, 128, TOK)))
    return out


def merge_inputs(xs, hTs, ymT, yrT, yaT, mods_l, li, w_in, w_branch, w_o):
    off = col_offsets()
    m = mods_l.reshape(3, 9, D)
    wg = np.ascontiguousarray(w_in[li][:, off['g_m']:off['g_m'] + 3 * D])
    ins = []
    for b in range(2):
        xsh = tok_shard(xs[b])
        hsh = featT_shard(np.ascontiguousarray(hTs[b].T), 8)
        yfull = np.concatenate([ymT[b], yrT[b], yaT[b]], axis=0)
        ypad = np.zeros((1536, 4 * TOK), yfull.dtype)
        ypad[:, :SEQ_ALL] = yfull
        for i in range(4):
            gb = np.stack([np.broadcast_to(m[2 if i == 0 else b, 5], (128, D)), np.broadcast_to(m[b, 5], (128, D))], 0)
            ins.append({"x": xsh[i], "hT": hsh[i],
                        "yT": np.ascontiguousarray(ypad[:, i * TOK:(i + 1) * TOK].reshape(12, 128, TOK)),
                        "wg": wg, "wb": w_branch[li], "wo": w_o[li], "gateb": np.ascontiguousarray(gb)})
    return ins


def emit_mod_fused(P, T):
    cT = T["cT"]
    ones = P.sb([128, 128], F32, "ones")
    P.memset(ones[:], 1.0, [ones])
    cs = P.sb([128, 8, 2], F32, "cs")
    sg = P.sb([128, 8, 2], F32, "sg")
    P.dma("sp", cs[:], cT[:, :, :], writes=[cs])
    P.act(sg[:], cs[:], AF.Sigmoid, [cs], [sg])
    P.tt(cs[:], cs[:], sg[:], ALU.mult, [cs, sg], [cs])
    crep = [P.sb([128, 8, 128], F32, f"crep{i}") for i in range(2)]
    for st in range(2):
        for kc in range(8):
            P.ts(crep[st][:, kc, :], ones[:], cs[:, kc, st:st + 1], None, ALU.mult, None, [ones, cs], [crep[st]])
    Wk = [P.sb([128, 8, 1024], F32, f"Wk{i}") for i in range(2)]
    pm = [P.ps([128, 512], F32, f"pm{i}") for i in range(2)]
    pg = [P.ps([128, 512], F32, f"pgm{i}") for i in range(2)]
    n = 0
    for l in range(2):
        bpps = P.sb([128, 72], F32, f"bpps{l}")
        P.dma("act", bpps[:], T[f"bpp{l}"][:, :], writes=[bpps])
        bgbs = P.sb([128, 3, 1024], F32, f"bgbs{l}")
        for gi in range(3):
            P.dma("act", bgbs[:, gi, :], T[f"bgb{l}"][gi], writes=[bgbs])
        ngs = P.sb([128, 24], F32, f"ngs{l}")
        P.dma("act", ngs[:], T[f"ng{l}"][:, :], writes=[ngs])
        MODT = P.sb([128, 2, 72], F32, f"MODT{l}")
        for k in range(9):
            W = Wk[n % 2]
            for kc in range(8):
                P.dma("sp", W[:, kc, :], T[f"wada{l}"][kc * 128:(kc + 1) * 128, k * 1024:(k + 1) * 1024], writes=[W])
            ps = pm[n % 2]
            for c in range(8):
                for kc in range(8):
                    P.mm(ps[:, 2 * c:2 * c + 2], W[:, kc, c * 128:(c + 1) * 128], cs[:, kc, :], kc == 0, kc == 7, [W, cs], [ps])
            for st in range(2):
                P.tt(MODT[:, st, k * 8:(k + 1) * 8], ps[:, st:16:2], bpps[:, k * 8:(k + 1) * 8], ALU.add, [ps, bpps], [MODT])
            if k in (2, 5, 8):
                gi = (2, 5, 8).index(k)
                GB = P.sb([128, 2, 1024], F32, f"GB{l}{gi}")
                for st in range(2):
                    for h in range(2):
                        pq = pg[(st * 2 + h) % 2]
                        for kc in range(8):
                            P.mm(pq[:, :], crep[st][:, kc, :], W[:, kc, h * 512:(h + 1) * 512], kc == 0, kc == 7, [crep[st], W], [pq])
                        P.tt(GB[:, st, h * 512:(h + 1) * 512], pq[:, :], bgbs[:, gi, h * 512:(h + 1) * 512], ALU.add, [pq, bgbs], [GB])
                    P.dma("act", T[f"gb{l}_{gi}"][st], GB[:, st, :], reads=[GB])
            n += 1
        for which, ks in ((0, (0, 1, None, 3, 4, None)), (1, (6, 7, None, None, None, None))):
            PP = P.sb([128, 96], F32, f"PP{l}{which}")
            P.memset(PP[:], 0.0, [PP])
            for st in range(2):
                for j, k in enumerate(ks):
                    dst = PP[:, st * 48 + j * 8: st * 48 + (j + 1) * 8]
                    if k is not None:
                        P.cp(dst, MODT[:, st, k * 8:(k + 1) * 8], [MODT], [PP])
                    elif j == 2:
                        gidx = 0 if which == 0 else 2
                        P.cp(dst, ngs[:, gidx * 8:(gidx + 1) * 8], [ngs], [PP])
                    elif j == 5 and which == 0:
                        P.cp(dst, ngs[:, 8:16], [ngs], [PP])
            P.dma("sp", T[f"pp{l}_{which}"][:, :], PP[:], reads=[PP])


def build_fused(upto=99, dbg=None):
    P = Prog()
    E = lambda name, shape, dt=F32: P.dram(name, shape, dt, "ExternalInput")
    x0 = E("x0", [NT, 128, D])
    out = P.dram("out", [NT, 128, D], F32, "ExternalOutput")
    sel = E("sel", [128, 4])
    Tm = {"cT": E("cT", [128, 8, 2])}
    pp, gb = {}, {}
    for l in range(2):
        Tm[f"wada{l}"] = E(f"wada{l}", [D, 9 * D])
        Tm[f"bpp{l}"] = E(f"bpp{l}", [128, 72])
        Tm[f"bgb{l}"] = E(f"bgb{l}", [3, 128, D])
        Tm[f"ng{l}"] = E(f"ng{l}", [128, 24])
        for w in range(2):
            pp[l, w] = Tm[f"pp{l}_{w}"] = P.idram([128, 96], F32, f"pp{l}_{w}")
        for gi in range(3):
            gb[l, gi] = Tm[f"gb{l}_{gi}"] = P.idram([2, 128, D], F32, f"gb{l}_{gi}")
    a_cs = E("a_cs", [2, 128, 8192])
    a_cmat = E("a_cmat", [2, 128, 128])
    emit_mod_fused(P, Tm)
    P.end_stage()
    groups = [[0, 1, 2, 3], [4, 5, 6, 7]]
    dummy = Buf(None, "coll")

    def done(src3):
        t = P.sb([128, D], F32, "dbgt")
        for i in range(NT):
            P.dma("sp", t[:], src3[i], writes=[t])
            P.dma("sp", out[i], t[:], reads=[t])
        return P.finish()

    xcur = x0
    for l in range(2):
        x1 = P.idram([NT, 128, D], F32, f"x1_{l}")
        hsrc = P.idram([8, 128, TOK], BF16, f"hsrc{l}")
        h3 = hsrc
        build_ffn(True, P, {"x": xcur, "wgu": E(f"w1gu{l}", [D, 2 * DFF]), "wd": E(f"w1d{l}", [DFF, D]),
                            "pp": pp[l, 0], "gateb": gb[l, 0], "xo": x1, "ho": h3})
        P.end_stage()
        if upto == 10 * l + 1:
            return done(x1)
        hall = P.idram([8, 4 * 128, TOK], BF16, f"hall{l}")
        hall.gath = True
        for kc in range(8):
            P.coll("AllGather", hsrc[kc], hall[kc], groups, writes=[dummy])
        P.end_stage()
        if upto == 10 * l + 5:
            return done(x1)
        ysrc = P.idram([3, 4, 128, TOK], BF16, f"ysrc{l}")
        zt = P.sb([128, 4 * TOK - SEQ_ALL], BF16, "zt")
        P.memset(zt[:], 0.0, [zt])
        for br in range(3):
            P.dma("act", ysrc[br, 3, :, SEQ_ALL - 3 * TOK:TOK], zt[:], reads=[zt])
        yb_ = []
        for br in range(3):
            yb_.append(Buf(ysrc[br], f"yout{br}"))
            yb_[-1].gath = True
        build_attn(False, P, {"hT": hall, "wqkv": E(f"a_wqkv{l}", [3, D, 128]), "gqk": E(f"a_gqk{l}", [128, 2]),
                              "cs": a_cs, "cmat": a_cmat, "lamp": E(f"a_lamp{l}", [128, 258]),
                              "subg": E(f"a_subg{l}", [128, 128]), "ya": yb_[2]})
        P.end_stage()
        build_mlstm(99, P, {"hT": hall, "wqk": E(f"m_wqk{l}", [2, D, 64]), "wvo": E(f"m_wvo{l}", [D, 256]),
                            "wg": E(f"m_wg{l}", [D, 4]), "cw": E(f"m_cw{l}", [2, 128, 3, 64]), "gb": E(f"m_gb{l}", [128, 4]),
                            "og": E(f"m_og{l}", [128, 128]), "ym": yb_[0]})
        P.end_stage()
        build_rwkv(P, {"hT": hall, "wrkv": E(f"r_wrkv{l}", [D, 384]), "crkv": E(f"r_crkv{l}", [128, 3, 384]),
                       "wl": E(f"r_wl{l}", [D, 384]), "w2a2": E(f"r_w2a2{l}", [2, 128, 128]),
                       "bias01": E(f"r_bias01{l}", [1, 2, 256]), "g2": E(f"r_g2{l}", [128, 128]),
                       "vecs": E(f"r_vecs{l}", [128, 5, 128]), "yr": yb_[1]})
        P.end_stage()
        yall = P.idram([3, 4, 4 * 128, TOK], BF16, f"yall{l}")
        for br in range(3):
            for q in range(4):
                P.coll("AllGather", ysrc[br, q], yall[br, q], groups, writes=[dummy])
        P.end_stage()
        x2 = P.idram([NT, 128, D], F32, f"x2_{l}")
        build_merge(P, {"x": x1, "hT": h3, "yall": yall, "sel": sel, "wg": E(f"g_wg{l}", [D, 3 * D]),
                        "wb": E(f"g_wb{l}", [3, 512, D]), "wo": E(f"g_wo{l}", [D, D]), "gateb": gb[l, 1], "xo": x2}, fused=True)
        P.end_stage()
        if upto == 10 * l + 2:
            return done(x2)
        x3 = out if l == 1 else P.idram([NT, 128, D], F32, f"x3_{l}")
        build_ffn(False, P, {"x": x2, "wgu": E(f"w2gu{l}", [D, 2 * DFF]), "wd": E(f"w2d{l}", [DFF, D]),
                             "pp": pp[l, 1], "gateb": gb[l, 2], "xo": x3})
        P.end_stage()
        if upto == 10 * l + 3 and l == 0:
            return done(x3)
        xcur = x3
    return P.finish()


def fused_inputs(x, c, ctx, c_ctx, w_ada, b_ada, norm_g, ffn1_w_gu, ffn1_w_down, ffn2_w_gu, ffn2_w_down,
                 w_in, m_conv, m_gate_bias, m_out_norm, r_conv, r_w0, r_w2, r_a0, r_a2, r_g2, r_kk, r_ka,
                 r_rk, r_ln_w, r_ln_b, a_qk_norm, a_lambda, a_subln, w_branch, w_o):
    C = np.ascontiguousarray
    xs = [np.concatenate([ctx[b], x[b]], 0) for b in range(2)]
    dummy_h = [np.zeros((D, SEQ_ALL), NPBF) for _ in range(2)]
    off = col_offsets()
    per = [dict() for _ in range(8)]
    shards = [tok_shard(xs[b]) for b in range(2)]
    cs_tab, cm = rope_tables(), attn_consts()
    for core in range(8):
        b, i = core // 4, core % 4
        d = per[core]
        d["x0"] = shards[b][i]
        sel = np.zeros((128, 4), np.float32)
        sel[:, i] = 1.0
        d["sel"] = sel
        cA = c_ctx if i == 0 else c[b]
        d["cT"] = C(np.stack([cA, c[b]], 0).reshape(2, 8, 128).transpose(2, 1, 0).astype(np.float32))
        d["a_cs"], d["a_cmat"] = cs_tab, cm
    for l in range(2):
        bpp = C(b_ada[l].reshape(9, 8, 128).transpose(2, 0, 1).reshape(128, 72))
        bgb = C(np.stack([np.broadcast_to(b_ada[l].reshape(9, D)[k][None], (128, D)) for k in (2, 5, 8)], 0))
        ng = C(norm_g[l].reshape(3, 8, 128).transpose(2, 0, 1).reshape(128, 24))
        ai = attn_inputs(dummy_h, l, w_in, a_qk_norm, a_lambda, a_subln)
        mi = mlstm_inputs(dummy_h, l, w_in, m_conv, m_gate_bias, m_out_norm)
        ri = rwkv_inputs(dummy_h, l, w_in, r_conv, r_w0, r_w2, r_a0, r_a2, r_g2, r_kk, r_ka, r_rk, r_ln_w, r_ln_b)
        wg = C(w_in[l][:, off['g_m']:off['g_m'] + 3 * D])
        for core in range(8):
            d = per[core]
            d[f"wada{l}"] = C(w_ada[l]); d[f"bpp{l}"] = bpp; d[f"bgb{l}"] = bgb; d[f"ng{l}"] = ng
            d[f"w1gu{l}"] = C(ffn1_w_gu[l]); d[f"w1d{l}"] = C(ffn1_w_down[l])
            d[f"w2gu{l}"] = C(ffn2_w_gu[l]); d[f"w2d{l}"] = C(ffn2_w_down[l])
            for k in ("wqkv", "gqk", "lamp", "subg"):
                d[f"a_{k}{l}"] = ai[core][k]
            for k in ("wqk", "wvo", "wg", "cw", "gb", "og"):
                d[f"m_{k}{l}"] = mi[core][k]
            for k in ("wrkv", "crkv", "wl", "w2a2", "bias01", "g2", "vecs"):
                d[f"r_{k}{l}"] = ri[core][k]
            d[f"g_wg{l}"] = wg; d[f"g_wb{l}"] = C(w_branch[l]); d[f"g_wo{l}"] = C(w_o[l])
    return per


def kernel(**inputs):
    f = {k: np.asarray(v, dtype=np.float32) for k, v in inputs.items()}
    nc = _prog("fused", build_fused)
    res = _run(nc, fused_inputs(**f))
    xs = tok_unshard(res, "out")
    return np.stack([xs[b][NCTX:] for b in range(2)], 0).astype(np.float32)


_PROGS = {}


def _prog(name, fn):
    if name not in _PROGS:
        _PROGS[name] = fn()
    return _PROGS[name]


def _run(nc, ins):
    return run_bass_kernel_spmd(nc, ins, core_ids=list(range(8))).results


def kernel_unfused(x, c, ctx, c_ctx, w_ada, b_ada, norm_g, ffn1_w_gu, ffn1_w_down, ffn2_w_gu, ffn2_w_down,
           w_in, m_conv, m_gate_bias, m_out_norm, r_conv, r_w0, r_w2, r_a0, r_a2, r_g2, r_kk, r_ka,
           r_rk, r_ln_w, r_ln_b, a_qk_norm, a_lambda, a_subln, w_branch, w_o):
    f = lambda a: np.asarray(a, dtype=np.float32)
    (x, c, ctx, c_ctx, w_ada, b_ada, norm_g, ffn1_w_gu, ffn1_w_down, ffn2_w_gu, ffn2_w_down, w_in, m_conv, m_gate_bias,
     m_out_norm, r_conv, r_w0, r_w2, r_a0, r_a2, r_g2, r_kk, r_ka, r_rk, r_ln_w, r_ln_b, a_qk_norm, a_lambda, a_subln,
     w_branch, w_o) = map(f, (x, c, ctx, c_ctx, w_ada, b_ada, norm_g, ffn1_w_gu, ffn1_w_down, ffn2_w_gu, ffn2_w_down, w_in,
                              m_conv, m_gate_bias, m_out_norm, r_conv, r_w0, r_w2, r_a0, r_a2, r_g2, r_kk, r_ka, r_rk,
                              r_ln_w, r_ln_b, a_qk_norm, a_lambda, a_subln, w_branch, w_o))
    mods = run_mod(c, c_ctx, w_ada, b_ada)
    xs = [np.concatenate([ctx[b], x[b]], 0) for b in range(2)]
    for li in range(2):
        res = _run(_prog("ffn_h", lambda: build_ffn(True)),
                   ffn_inputs(xs, mods[li], li, 1, norm_g, np.ascontiguousarray(ffn1_w_gu[li]), np.ascontiguousarray(ffn1_w_down[li]), True))
        xs = tok_unshard(res, "xo")
        hTs = hT_unshard(res, "ho")
        ra = _run(_prog("attn", build_attn), attn_inputs(hTs, li, w_in, a_qk_norm, a_lambda, a_subln))
        rm = _run(_prog("mlstm", build_mlstm), mlstm_inputs(hTs, li, w_in, m_conv, m_gate_bias, m_out_norm))
        rr = _run(_prog("rwkv", build_rwkv), rwkv_inputs(hTs, li, w_in, r_conv, r_w0, r_w2, r_a0, r_a2, r_g2, r_kk, r_ka,
                                                          r_rk, r_ln_w, r_ln_b))
        yas = [np.concatenate([ra[b * 4 + h]["ya"] for h in range(4)], axis=0) for b in range(2)]
        yms = [np.concatenate([rm[b * 4 + h]["ym"] for h in range(4)], axis=0) for b in range(2)]
        yrs = [np.concatenate([rr[b * 4 + h]["yr"] for h in range(4)], axis=0) for b in range(2)]
        res = _run(_prog("merge", build_merge), merge_inputs(xs, hTs, yms, yrs, yas, mods[li], li, w_in, w_branch, w_o))
        xs = tok_unshard(res, "xo")
        res = _run(_prog("ffn", lambda: build_ffn(False)),
                   ffn_inputs(xs, mods[li], li, 2, norm_g, np.ascontiguousarray(ffn2_w_gu[li]), np.ascontiguousarray(ffn2_w_down[li]), False))
        xs = tok_unshard(res, "xo")
    return np.stack([xs[b][NCTX:] for b in range(2)], 0).astype(np.float32)
```

```python
import contextlib
import numpy as np
import ml_dtypes
import concourse.bass as bass
import concourse.mybir as mybir
from concourse.bass_utils import run_bass_kernel_spmd

F32 = mybir.dt.float32
BF16 = mybir.dt.bfloat16
AF = mybir.ActivationFunctionType
ALU = mybir.AluOpType
AX = mybir.AxisListType
NPBF = ml_dtypes.bfloat16

ENGS = ("pe", "dve", "act", "pool", "sp")


class Buf:
    __slots__ = ("t", "w", "r", "name", "ds", "psum", "gath")

    def __init__(self, t=None, name=""):
        self.t = t
        self.ds = None
        self.psum = False
        self.gath = False
        self.w = None
        self.r = []
        self.name = name

    def __getitem__(self, idx):
        return self.t[idx]


class Sem:
    __slots__ = ("h", "count")

    def __init__(self, h):
        self.h = h
        self.count = 0


class Prog:
    def __init__(self, name="k"):
        self.nc = bass.Bass("TRN2", target_bir_lowering=False)
        self.es = contextlib.ExitStack()
        self.ss = contextlib.ExitStack()
        self.q = {e: [] for e in ENGS}
        self.esem = {}
        for e in ENGS:
            self.esem[e] = Sem(self.es.enter_context(self.nc.semaphore(f"s_{e}")))
        self.seen = {e: {} for e in ENGS}
        self.dsems = []
        self.free_ds = []
        self.stage_ds = []
        self.nbuf = 0

    def dram(self, name, shape, dt, kind):
        return Buf(self.nc.dram_tensor(name, list(shape), dt, kind=kind).ap(), name)

    def io(self, T, name, shape, dt, kind):
        if T is not None:
            return T[name]
        return self.dram(name, shape, dt, kind)

    def sb(self, shape, dt=F32, name=None):
        self.nbuf += 1
        name = f"{name or 'sb'}_{self.nbuf}"
        t = self.ss.enter_context(self.nc.sbuf_tensor(name, list(shape), dt))
        return Buf(t, name)

    def ps(self, shape, dt=F32, name=None):
        self.nbuf += 1
        name = f"{name or 'ps'}_{self.nbuf}"
        t = self.ss.enter_context(self.nc.psum_tensor(name, list(shape), dt))
        b = Buf(t, name)
        b.psum = True
        return b

    def dsem(self):
        if self.free_ds:
            s = self.free_ds.pop()
        else:
            s = Sem(self.es.enter_context(self.nc.semaphore(f"d{len(self.dsems)}")))
            self.dsems.append(s)
        self.stage_ds.append(s)
        return s

    def _deps(self, eng, reads, writes):
        deps = []
        for b in reads:
            if b.w is not None:
                deps.append(b.w)
            if b.psum:
                deps.extend(ev for ev in b.r if ev[2] != eng)
        for b in writes:
            if b.w is not None:
                deps.append(b.w)
            deps.extend(b.r)
        best = {}
        for (s, v, src) in deps:
            if src == "pe" and eng == "pe":
                continue
            if v > best.get(s, (0, None))[0]:
                best[s] = (v, src)
        for s, (v, src) in best.items():
            if self.seen[eng].get(s, 0) >= v:
                continue
            self.seen[eng][s] = v
            self.q[eng].append(("w", s, v))

    defer = None

    def op(self, eng, fn, reads=(), writes=()):
        if self.defer is not None:
            self.defer.append(lambda: self._op(eng, fn, reads, writes))
            return None
        return self._op(eng, fn, reads, writes)

    def _op(self, eng, fn, reads=(), writes=()):
        self._deps(eng, reads, writes)
        s = self.esem[eng]
        s.count += 1
        ev = (s, s.count, eng)
        self.q[eng].append(("i", fn, s, 1))
        for b in reads:
            b.r.append(ev)
        for b in writes:
            b.w = ev
            b.r = []
        return ev

    def dma(self, eng, out, in_, sem=None, reads=(), writes=(), **kw):
        if self.defer is not None:
            self.defer.append(lambda: self._dma(eng, out, in_, reads, writes, kw))
            return None
        return self._dma(eng, out, in_, reads, writes, kw)

    def _dma(self, eng, out, in_, reads, writes, kw):
        b0 = (list(writes) + list(reads))[0]
        if b0.ds is None:
            b0.ds = self.dsem()
        sem = b0.ds
        self._deps(eng, reads, writes)
        sem.count += 16
        ev = (sem, sem.count, "dma")
        self.q[eng].append(("i", lambda e: e.dma_start(out=out, in_=in_, **kw), sem, 16))
        for b in reads:
            b.r.append(ev)
        for b in writes:
            b.w = ev
            b.r = []
        return ev

    def coll(self, kind, in_ap, out_ap, groups, reads=(), writes=()):
        b0 = list(writes)[0]
        if b0.ds is None:
            b0.ds = self.dsem()
        sem = b0.ds
        self._deps("pool", reads, writes)
        sem.count += 1
        ev = (sem, sem.count, "dma")
        self.q["pool"].append(("i", lambda e: e.collective_compute(kind, ALU.bypass, replica_groups=groups,
                                                                   ins=[in_ap.opt()], outs=[out_ap.opt()]), sem, 1))
        for b in reads:
            b.r.append(ev)
        for b in writes:
            b.w = ev
            b.r = []
        return ev

    def raw(self, eng, fn, reads=()):
        self._deps(eng, reads, ())
        self.q[eng].append(("r", fn))

    def idram(self, shape, dt, name=None, shared=False):
        self.nbuf += 1
        t = self.nc.dram_tensor(name or f"idram{self.nbuf}", list(shape), dt, addr_space="Shared" if shared else "Local")
        return Buf(t.ap(), name or f"idram{self.nbuf}")

    def _emit_block(self):
        q = self.q

        def run(eng_obj, items):
            for it in items:
                if it[0] == "w":
                    eng_obj.wait_ge(it[1].h, it[2])
                elif it[0] == "r":
                    it[1](eng_obj)
                else:
                    it[1](eng_obj).then_inc(it[2].h, it[3])

        with self.nc.Block() as block:
            @block.tensor
            def _(e):
                run(e, q["pe"])

            @block.vector
            def _(e):
                run(e, q["dve"])

            @block.scalar
            def _(e):
                run(e, q["act"])

            @block.gpsimd
            def _(e):
                run(e, q["pool"])

            @block.sync
            def _(e):
                run(e, q["sp"])
        self.q = {e: [] for e in ENGS}

    def end_stage(self):
        sems = [x for x in self.dsems + [self.esem[e] for e in ENGS] if x.count > 0]
        for e in ENGS:
            for x in sems:
                if self.seen[e].get(x, 0) < x.count:
                    self.seen[e][x] = x.count
                    self.q[e].append(("w", x, x.count))
        self._emit_block()
        self.ss.close()
        self.ss = contextlib.ExitStack()
        self.free_ds.extend(self.stage_ds)
        self.stage_ds = []

    def finish(self):
        self.end_stage()
        self.es.close()
        return self.nc

    def mm(self, out, lhsT, rhs, start, stop, reads, writes, skip=False):
        if skip:
            return self.op("pe", lambda e: e.matmul(out, lhsT, rhs, start=start, stop=stop, skip_group_check=True), reads, writes)
        return self.op("pe", lambda e: e.matmul(out, lhsT, rhs, start=start, stop=stop), reads, writes)

    def tr(self, out, in_, ident, reads, writes):
        return self.op("pe", lambda e: e.transpose(out, in_, ident), reads, writes)

    def act(self, out, in_, func, reads, writes, bias=None, scale=None, accum_out=None):
        kw = {}
        if bias is not None:
            kw["bias"] = bias
        if scale is not None:
            kw["scale"] = scale
        if accum_out is not None:
            kw["accum_out"] = accum_out
        return self.op("act", lambda e: e.activation(out, in_, func, **kw), reads, writes)

    def tt(self, out, in0, in1, op, reads, writes, eng="dve"):
        return self.op(eng, lambda e: e.tensor_tensor(out, in0, in1, op), reads, writes)

    def ts(self, out, in0, s1, s2, op0, op1, reads, writes, eng="dve", accum_out=None):
        if op1 is None:
            return self.op(eng, lambda e: e.tensor_scalar(out, in0, s1, None, op0), reads, writes)
        if accum_out is not None:
            return self.op(eng, lambda e: e.tensor_scalar(out, in0, s1, s2, op0, op1, accum_out=accum_out), reads, writes)
        return self.op(eng, lambda e: e.tensor_scalar(out, in0, s1, s2, op0, op1), reads, writes)

    def stt(self, out, in0, scalar, in1, op0, op1, reads, writes):
        return self.op("dve", lambda e: e.scalar_tensor_tensor(out, in0, scalar, in1, op0, op1), reads, writes)

    def cp(self, out, in_, reads, writes, eng="dve"):
        if eng == "act":
            return self.op("act", lambda e: e.copy(out, in_), reads, writes)
        return self.op(eng, lambda e: e.tensor_copy(out, in_), reads, writes)

    def memset(self, ap, val, writes, eng="dve"):
        return self.op(eng, lambda e: e.memset(ap, val), (), writes)


D = 1024
DFF = 2816
NT = 17
TOK = NT * 128
CTXT = 2
SEQT = 66
EPS = 1e-6
BLOCKS = [(0, 2), (2, 3), (5, 3), (8, 3), (11, 3), (14, 3)]


def make_ident(P, n=128, dt=F32):
    ident = P.sb([n, n], dt)
    P.memset(ident[:], 1.0, [ident], eng="pool")
    P.op("pool", lambda e: e.affine_select(ident[:], ident[:], [[-1, n]], ALU.is_equal, 0.0, base=0,
                                           channel_multiplier=1), [ident], [ident])
    return ident


def load_cast(P, dst, dst_ap, src_ap, stage, sem, i, shape_cols):
    st = stage[i % len(stage)]
    P.dma("sp", st[:, 0:shape_cols], src_ap, writes=[st])
    eng = "dve" if i % 2 == 0 else "pool"
    P.cp(dst_ap, st[:, 0:shape_cols], [st], [dst], eng=eng)


def build_ffn(emit_h, P=None, T=None):
    own = P is None
    P = P or Prog()
    x = P.io(T, "x", [NT, 128, D], F32, "ExternalInput")
    wgu = P.io(T, "wgu", [D, 2 * DFF], F32, "ExternalInput")
    wd = P.io(T, "wd", [DFF, D], F32, "ExternalInput")
    pp = P.io(T, "pp", [128, 2 * 48], F32, "ExternalInput")
    gateb = P.io(T, "gateb", [2, 128, D], F32, "ExternalInput")
    xo = P.io(T, "xo", [NT, 128, D], F32, "ExternalOutput")
    if emit_h:
        ho = P.io(T, "ho", [8, 128, TOK], BF16, "ExternalOutput")

    ident = make_ident(P)
    Wgu = P.sb([128, 8, 2 * DFF], BF16, "Wgu")
    Wd = P.sb([128, 22, D], BF16, "Wd")
    stage = [P.sb([128, 704], F32, f"stg{i}") for i in range(2)]
    ssem = [P.dsem() for _ in range(2)]
    pps = P.sb([128, 96], F32, "pps")
    Gt = P.sb([128, 32], F32, "Gt")
    gates = P.sb([128, 2, D], F32, "gates")
    msem = P.dsem()
    P.dma("act", pps[:], pp[:, :], msem, writes=[pps])
    for s in range(2):
        P.dma("act", gates[:, s, :], gateb[s], msem, writes=[gates])
    for s in range(2):
        for j in range(2):
            sc = pps[:, s * 48 + j * 24 + 8: s * 48 + j * 24 + 16]
            g = pps[:, s * 48 + j * 24 + 16: s * 48 + j * 24 + 24]
            P.stt(Gt[:, s * 16 + j * 8: s * 16 + j * 8 + 8], sc, 1.0, g, ALU.add, ALU.mult, [pps], [Gt])
    P.ts(gates[:], gates[:], 0.5, None, ALU.mult, None, [gates], [gates], eng="pool")

    n = 0
    for kc in range(8):
        for q in range(8):
            load_cast(P, Wgu, Wgu[:, kc, q * 704:(q + 1) * 704], wgu[kc * 128:(kc + 1) * 128, q * 704:(q + 1) * 704],
                      stage, ssem, n, 704)
            n += 1
    for fc in range(22):
        for q in range(2):
            load_cast(P, Wd, Wd[:, fc, q * 512:(q + 1) * 512], wd[fc * 128:(fc + 1) * 128, q * 512:(q + 1) * 512], stage, ssem, n, 512)
            n += 1

    xb = [P.sb([128, 3, D], F32, "xb0")] * 2
    xsem = [P.dsem() for _ in range(2)]
    osem = [P.dsem() for _ in range(2)]
    scr = {"xn": P.sb([128, 3, D], F32, "xn"), "sq": P.sb([128, D], BF16, "sq")}
    small = {"ss": P.sb([128, 4], F32, "ss")}
    hT = P.sb([128, 8, 384], BF16, "hT")
    hT2 = hT
    hsem = P.dsem()
    uT = P.sb([128, 22, 384], BF16, "uT")
    sa = [P.sb([128, 384], F32, f"sa{i}") for i in range(2)]
    pst = [P.ps([128, 512], F32, f"pst{i}") for i in range(2)]
    pa = [P.ps([128, 512], F32, f"pa{i}") for i in range(2)]
    pb = [P.ps([128, 512], F32, f"pb{i}") for i in range(2)]
    py = [P.ps([128, 512], F32, f"py{i}") for i in range(2)]
    tmp = [P.sb([128, 512], F32, f"tmp{i}") for i in range(2)]

    for bi, (t0, nb) in enumerate(BLOCKS):
        s = 0 if bi == 0 else 1
        N = nb * 128
        X = xb[bi % 2]
        for i in range(nb):
            P.dma("sp", X[:, i, :], x[t0 + i], xsem[bi % 2], writes=[X])
        _norm(P, X, nb, Gt, s * 16, pps, s * 48, hT, ident, pst, scr, small)
        for fc in range(22):
            A, B = pa[fc % 2], pb[fc % 2]
            for kc in range(8):
                P.mm(A[:, 0:N], Wgu[:, kc, fc * 128:(fc + 1) * 128], hT[:, kc, 0:N], kc == 0, kc == 7, [Wgu, hT], [A])
            for kc in range(8):
                P.mm(B[:, 0:N], Wgu[:, kc, DFF + fc * 128:DFF + (fc + 1) * 128], hT[:, kc, 0:N], kc == 0, kc == 7,
                     [Wgu, hT], [B])
            S = sa[fc % 2]
            P.act(S[:, 0:N], A[:, 0:N], AF.Silu, [A], [S])
            P.tt(uT[:, fc, 0:N], S[:, 0:N], B[:, 0:N], ALU.mult, [S, B], [uT])
        for i in range(nb):
            for h in range(2):
                Y = py[(i * 2 + h) % 2]
                for fc in range(22):
                    P.mm(Y[:, :], uT[:, fc, i * 128:(i + 1) * 128], Wd[:, fc, h * 512:(h + 1) * 512], fc == 0, fc == 21,
                         [uT, Wd], [Y])
                T = tmp[(i * 2 + h) % 2]
                P.tt(T[:], Y[:], gates[:, s, h * 512:(h + 1) * 512], ALU.mult, [Y, gates], [T])
                P.tt(X[:, i, h * 512:(h + 1) * 512], X[:, i, h * 512:(h + 1) * 512], T[:], ALU.add, [X, T], [X], eng="pool")
            P.dma("sp", xo[t0 + i], X[:, i, :], osem[bi % 2], reads=[X])
        if emit_h:
            _norm(P, X, nb, Gt, s * 16 + 8, pps, s * 48 + 24, hT2, ident, pst, scr, small)
            for c in range(8):
                P.dma("act", ho[c, :, t0 * 128:t0 * 128 + N], hT2[:, c, 0:N], hsem, reads=[hT2])
    return P.finish() if own else None


def _norm(P, xb, nb, Gt, gcol, SHt, scol, hT, ident, pst, scr, small):
    xn = scr["xn"]
    ss = small["ss"]
    for i in range(nb):
        P.act(scr["sq"][:], xb[:, i, :], AF.Square, [xb], [scr["sq"], ss], accum_out=ss[:, 0:1])
        P.ts(ss[:, 1:2], ss[:, 0:1], 1.0 / D, EPS, ALU.mult, ALU.add, [ss], [ss])
        P.act(ss[:, 2:3], ss[:, 1:2], AF.Sqrt, [ss], [ss])
        P.op("dve", lambda e: e.reciprocal(ss[:, 3:4], ss[:, 2:3]), [ss], [ss])
        P.ts(xn[:, i, :], xb[:, i, :], ss[:, 3:4], None, ALU.mult, None, [xb, ss], [xn])
    for c in range(8):
        pt = pst[c % len(pst)]
        for i in range(nb):
            P.tr(pt[:, i * 128:(i + 1) * 128], xn[:, i, c * 128:(c + 1) * 128], ident[:], [xn, ident], [pt])
        P.act(hT[:, c, 0:nb * 128], pt[:, 0:nb * 128], AF.Identity, [pt, Gt, SHt], [hT],
              bias=SHt[:, scol + c:scol + c + 1], scale=Gt[:, gcol + c:gcol + c + 1])


def build_mod():
    P = Prog()
    cT = P.dram("cT", [128, 8, 3], F32, "ExternalInput")
    wa = P.dram("wa", [2, D, 1152], F32, "ExternalInput")
    ba = P.dram("ba", [2, 3, 1152], F32, "ExternalInput")
    mo = P.dram("mo", [2, 3, 1152], F32, "ExternalOutput")
    cs = P.sb([128, 8, 3], F32)
    sg = P.sb([128, 8, 3], F32)
    ds = P.dsem()
    P.dma("sp", cs[:], cT[:, :, :], ds, writes=[cs])
    P.act(sg[:], cs[:], AF.Sigmoid, [cs], [sg])
    P.tt(cs[:], cs[:], sg[:], ALU.mult, [cs, sg], [cs])
    W = [P.sb([128, 8, 1152], F32, f"W{l}") for l in range(2)]
    wsem = P.dsem()
    bsb = P.sb([3, 2, 1152], F32)
    osb = P.sb([3, 2, 1152], F32)
    for l in range(2):
        P.dma("act", bsb[:, l, :], ba[l], ds, writes=[bsb])
        for kc in range(8):
            P.dma("sp" if kc % 2 == 0 else "act", W[l][:, kc, :], wa[l, kc * 128:(kc + 1) * 128, :], wsem, writes=[W[l]])
    pp = [P.ps([3, 512], F32, f"pp{i}") for i in range(2)]
    n = 0
    for l in range(2):
        for (c0, cw) in ((0, 512), (512, 512), (1024, 128)):
            ps = pp[n % 2]
            n += 1
            for kc in range(8):
                P.mm(ps[:, 0:cw], cs[:, kc, :], W[l][:, kc, c0:c0 + cw], kc == 0, kc == 7, [cs, W[l]], [ps])
            P.tt(osb[:, l, c0:c0 + cw], ps[:, 0:cw], bsb[:, l, c0:c0 + cw], ALU.add, [ps, bsb], [osb])
    osem = P.dsem()
    for l in range(2):
        P.dma("sp", mo[l], osb[:, l, :], osem, reads=[osb])
    return P.finish()


def run_mod(c, c_ctx, w_ada, b_ada):
    cv = np.stack([c[0], c[1], c_ctx], 0)
    cT = np.ascontiguousarray(cv.reshape(3, 8, 128).transpose(2, 1, 0))
    nc = build_mod()
    ins = []
    for i in range(8):
        cols = slice(i * 1152, (i + 1) * 1152)
        ins.append({"cT": cT, "wa": np.ascontiguousarray(w_ada[:, :, cols]),
                    "ba": np.ascontiguousarray(np.broadcast_to(b_ada[:, None, cols], (2, 3, 1152)))})
    res = run_bass_kernel_spmd(nc, ins, core_ids=list(range(8)))
    return np.concatenate([r["mo"] for r in res.results], axis=2)


def pvec(v):
    return np.ascontiguousarray(v.reshape(8, 128).T)


def tok_shard(xfull_b):
    pad = np.zeros((4 * TOK, xfull_b.shape[1]), np.float32)
    pad[:xfull_b.shape[0]] = xfull_b
    return [np.ascontiguousarray(pad[i * TOK:(i + 1) * TOK].reshape(NT, 128, -1)) for i in range(4)]


def ffn_inputs(xs, mods_l, li, which, norm_g, wgu, wd, emit_h):
    m = mods_l.reshape(3, 9, D)
    o = 0 if which == 1 else 6
    ins = []
    for b in range(2):
        shards = tok_shard(xs[b])
        for i in range(4):
            sets = []
            for s in range(2):
                r = 2 if (s == 0 and i == 0) else b
                vecs = [m[r, o + 0], m[r, o + 1], norm_g[li, 0 if which == 1 else 2]]
                if emit_h:
                    vecs += [m[r, 3], m[r, 4], norm_g[li, 1]]
                else:
                    vecs += [m[r, 3] * 0, m[r, 3] * 0, m[r, 3] * 0]
                sets.append(np.concatenate([pvec(v) for v in vecs], axis=1))
            pp = np.ascontiguousarray(np.concatenate(sets, axis=1).astype(np.float32))
            gb = np.stack([np.broadcast_to(m[2 if i == 0 else b, o + 2], (128, D)),
                           np.broadcast_to(m[b, o + 2], (128, D))], 0)
            ins.append({"x": shards[i], "wgu": wgu, "wd": wd, "pp": pp, "gateb": np.ascontiguousarray(gb)})
    return ins


def tok_unshard(res, key):
    out = []
    for b in range(2):
        full = np.concatenate([res[b * 4 + i][key].reshape(TOK, -1) for i in range(4)], axis=0)
        out.append(full[:SEQT * 128])
    return out


def hT_unshard(res, key):
    out = []
    for b in range(2):
        full = np.concatenate([res[b * 4 + i][key].reshape(D, TOK) for i in range(4)], axis=1)
        out.append(np.ascontiguousarray(full[:, :SEQT * 128]))
    return out


SEQ_ALL = SEQT * 128
NCTX = 256


def h_pieces(hT, kc, a, b):
    if hT.gath:
        out, t = [], a
        while t < b:
            r = t // TOK
            e = min(b, (r + 1) * TOK)
            out.append((t - a, e - t, hT[kc, r * 128:(r + 1) * 128, t - r * TOK:e - r * TOK]))
            t = e
        return out
    return [(0, b - a, hT[kc, :, a:b])]


def ycols(buf, a):
    if buf.gath:
        q = a // TOK
        return buf[q, :, a - q * TOK:a - q * TOK + 128]
    return buf[:, a:a + 128]


def load_hblk(P, hT, hb, sem, t0, N, eng="sp"):
    for kc in range(8):
        for (o, n, ap) in h_pieces(hT, kc, t0, t0 + N):
            P.dma(eng, hb[:, kc, o:o + n], ap, writes=[hb], **({"allow_slow_non_contiguous": True} if n == 1 else {}))


def build_attn(debug=False, P=None, T=None):
    own = P is None
    P = P or Prog()
    hT = P.io(T, "hT", [8, 128, SEQ_ALL], BF16, "ExternalInput")
    wqkv = P.io(T, "wqkv", [3, D, 128], F32, "ExternalInput")
    gqk = P.io(T, "gqk", [128, 2], F32, "ExternalInput")
    cs = P.io(T, "cs", [2, 128, 8192], F32, "ExternalInput")
    cmat = P.io(T, "cmat", [2, 128, 128], F32, "ExternalInput")
    lamp = P.io(T, "lamp", [128, 258], F32, "ExternalInput")
    subg = P.io(T, "subg", [128, 128], F32, "ExternalInput")
    ya = P.io(T, "ya", [128, SEQ_ALL], BF16, "ExternalOutput")

    banks = [P.ps([128, 512], F32, f"bank{i}") for i in range(8)]
    ident = make_ident(P)
    csem = P.dsem()
    W = P.sb([128, 3, 8, 128], BF16, "W")
    wst = P.sb([128, 3, 8, 128], F32, "wst")
    for j in range(3):
        for kc in range(8):
            P.dma("sp", wst[:, j, kc, :], wqkv[j, kc * 128:(kc + 1) * 128, :], csem, writes=[wst])
    P.cp(W[:], wst[:], [wst], [W])
    gq = P.sb([128, 2], F32, "gq")
    P.dma("act", gq[:], gqk[:, :], csem, writes=[gq])
    P.ts(gq[:, 0:1], gq[:, 0:1], 0.125, None, ALU.mult, None, [gq], [gq])
    Bm = P.sb([128, 128], F32, "Bm")
    Rm = P.sb([128, 128], F32, "Rm")
    P.dma("act", Bm[:], cmat[0], csem, writes=[Bm])
    P.dma("act", Rm[:], cmat[1], csem, writes=[Rm])
    lp = P.sb([128, 258], F32, "lp")
    P.dma("act", lp[:], lamp[:, :], csem, writes=[lp])
    sg = P.sb([128, 128], F32, "sg")
    P.dma("act", sg[:], subg[:, :], csem, writes=[sg])
    P.ts(sg[:], sg[:], lp[:, 257:258], None, ALU.mult, None, [sg, lp], [sg])
    lt = P.sb([128, 128], F32, "lt")
    lam = P.sb([128, 4], F32, "lam")
    P.tt(lt[:, 0:64], lp[:, 0:64], lp[:, 64:128], ALU.mult, [lp], [lt])
    P.tt(lt[:, 64:128], lp[:, 128:192], lp[:, 192:256], ALU.mult, [lp], [lt])
    P.op("dve", lambda e: e.tensor_reduce(lam[:, 0:1], lt[:, 0:64], AX.X, ALU.add), [lt], [lam])
    P.op("dve", lambda e: e.tensor_reduce(lam[:, 1:2], lt[:, 64:128], AX.X, ALU.add), [lt], [lam])
    P.act(lam[:, 0:2], lam[:, 0:2], AF.Exp, [lam], [lam])
    P.tt(lam[:, 2:3], lam[:, 1:2], lam[:, 0:1], ALU.subtract, [lam], [lam])
    P.tt(lam[:, 3:4], lam[:, 2:3], lp[:, 256:257], ALU.subtract, [lam, lp], [lam])
    epsb = P.sb([128, 1], F32, "epsb")
    P.memset(epsb[:], EPS, [epsb])

    QK = [P.sb([128, SEQ_ALL], BF16, "QT"), P.sb([128, SEQ_ALL], BF16, "KT")]
    V = P.sb([128, SEQT, 129], BF16, "V")
    P.memset(V[:, :, 128:129], 1.0, [V], eng="pool")
    hb = [P.sb([128, 8, 512], BF16, f"hb{i}") for i in range(2)]
    hsem = [P.dsem() for _ in range(2)]
    cst = [P.sb([128, 2, 512], F32, f"cst{i}") for i in range(2)]
    cssem = [P.dsem() for _ in range(2)]
    sq = P.sb([128, 512], F32, "sq")
    rs = P.sb([128, 512], F32, "rs")
    xn = P.sb([128, 512], F32, "xn")
    t1 = P.sb([128, 512], F32, "t1")
    t2 = P.sb([128, 512], F32, "t2")

    blocks = [(0, 256)] + [(256 + 512 * j, 512) for j in range(16)]
    for bi, (t0, N) in enumerate(blocks):
        H = hb[bi % 2]
        load_hblk(P, hT, H, hsem[bi % 2], t0, N)
        C = cst[bi % 2]
        if bi > 0:
            for j in range(2):
                P.dma("act", C[:, j, :], cs[j, :, t0 - 256:t0 - 256 + N], cssem[bi % 2], writes=[C])
        for j in range(2):
            pq, pms, prot = banks[0 + j], banks[2 + j], banks[4 + j]
            for kc in range(8):
                P.mm(pq[:, 0:N], W[:, j, kc, :], H[:, kc, 0:N], kc == 0, kc == 7, [W, H], [pq])
            P.act(sq[:, 0:N], pq[:, 0:N], AF.Square, [pq], [sq])
            P.mm(pms[:, 0:N], Bm[:], sq[:, 0:N], True, True, [Bm, sq], [pms])
            P.act(rs[:, 0:N], pms[:, 0:N], AF.Sqrt, [pms, epsb], [rs], bias=epsb[:, 0:1])
            P.op("dve", lambda e, N=N: e.reciprocal(rs[:, 0:N], rs[:, 0:N]), [rs], [rs])
            P.stt(xn[:, 0:N], pq[:, 0:N], gq[:, j:j + 1], rs[:, 0:N], ALU.mult, ALU.mult, [pq, gq, rs], [xn])
            if bi == 0:
                P.cp(QK[j][:, t0:t0 + N], xn[:, 0:N], [xn], [QK[j]], eng="pool")
            else:
                P.mm(prot[:, 0:N], Rm[:], xn[:, 0:N], True, True, [Rm, xn], [prot])
                P.tt(t1[:, 0:N], xn[:, 0:N], C[:, 0, 0:N], ALU.mult, [xn, C], [t1], eng="pool")
                P.tt(t2[:, 0:N], prot[:, 0:N], C[:, 1, 0:N], ALU.mult, [prot, C], [t2])
                P.tt(QK[j][:, t0:t0 + N], t1[:, 0:N], t2[:, 0:N], ALU.add, [t1, t2], [QK[j]])
        for i in range(N // 128):
            pv = banks[6 + i % 2]
            for kc in range(8):
                P.mm(pv[:, 0:128], H[:, kc, i * 128:(i + 1) * 128], W[:, 2, kc, :], kc == 0, kc == 7, [H, W], [pv])
            P.cp(V[:, t0 // 128 + i, 0:128], pv[:, 0:128], [pv], [V], eng="act")

    if debug:
        dq = P.io(T, "dq", [2, 128, SEQ_ALL], BF16, "ExternalOutput")
        dv = P.io(T, "dv", [128, SEQT, 129], BF16, "ExternalOutput")
        dl = P.io(T, "dl", [128, 4], F32, "ExternalOutput")
        dsm = P.dsem()
        P.dma("sp", dq[0], QK[0][:], dsm, reads=[QK[0]])
        P.dma("sp", dq[1], QK[1][:], dsm, reads=[QK[1]])
        P.dma("sp", dv[:, :, :], V[:], dsm, reads=[V])
        P.dma("sp", dl[:, :], lam[:], dsm, reads=[lam])
    Pm = [P.sb([128, 512], BF16, f"Pm{i}") for i in range(6)]
    Sb = [banks[0], banks[1], banks[6], banks[7]]
    SKEW = 3
    accb = [banks[2], banks[3], banks[4]]
    yo = [P.sb([128, 128], F32, f"yo{i}") for i in range(2)]
    yob = [P.sb([128, 128], BF16, f"yob{i}") for i in range(2)]
    ysq = P.sb([128, 128], F32, "ysq")
    st = P.sb([128, 8], F32, "st")
    osem = [P.dsem() for _ in range(2)]

    def acc(m, qs):
        a = m * 4 + qs
        return accb[a // 3], (a % 3) * 129

    qblocks = [(0, 256, 0, CTXT)] + [(256 + 512 * j, 512, 0, SEQT) for j in range(16)]
    accS = [[P.sb([128, 387], F32, f"accS{i}{j}") for j in range(3)] for i in range(2)]
    sts = [P.sb([128, 8], F32, f"sts{i}") for i in range(2)]
    n = 0
    no = 0
    pending = []

    def finalize(q0, nq, par, no0):
        A = accS[par]
        for qs in range(nq):
            i0, i1 = 0 * 4 + qs, 1 * 4 + qs
            a0, c0 = A[i0 // 3], (i0 % 3) * 129
            a1, c1 = A[i1 // 3], (i1 % 3) * 129
            k = no0 + qs
            Y, st_ = yo[k % 2], sts[k % 2]
            P.op("dve", lambda e, a0=a0, c0=c0, st_=st_: e.reciprocal(st_[:, 0:1], a0[:, c0 + 128:c0 + 129]), [a0], [st_])
            P.op("dve", lambda e, a1=a1, c1=c1, st_=st_: e.reciprocal(st_[:, 1:2], a1[:, c1 + 128:c1 + 129]), [a1], [st_])
            P.tt(st_[:, 1:2], st_[:, 1:2], lam[:, 3:4], ALU.mult, [st_, lam], [st_])
            P.ts(Y[:], a0[:, c0:c0 + 128], st_[:, 0:1], None, ALU.mult, None, [a0, st_], [Y])
            P.stt(Y[:], a1[:, c1:c1 + 128], st_[:, 1:2], Y[:], ALU.mult, ALU.add, [a1, st_, Y], [Y])
            P.tt(ysq[:], Y[:], Y[:], ALU.mult, [Y], [ysq])
            P.op("dve", lambda e, st_=st_: e.tensor_reduce(st_[:, 2:3], ysq[:], AX.X, ALU.add), [ysq], [st_])
            P.ts(st_[:, 3:4], st_[:, 2:3], 1.0 / 128, EPS, ALU.mult, ALU.add, [st_], [st_])
            P.act(st_[:, 4:5], st_[:, 3:4], AF.Sqrt, [st_], [st_])
            P.op("dve", lambda e, st_=st_: e.reciprocal(st_[:, 5:6], st_[:, 4:5]), [st_], [st_])
            P.stt(Y[:], Y[:], st_[:, 5:6], sg[:], ALU.mult, ALU.mult, [Y, st_, sg], [Y])
            P.tr(banks[5][:, 0:128], Y[:], ident[:], [Y, ident], [banks[5]])
            Yb = yob[k % 2]
            P.cp(Yb[:], banks[5][:, 0:128], [banks[5]], [Yb])
            P.dma("sp", ycols(ya, q0 + qs * 128), Yb[:], reads=[Yb])

    for bi, (q0, N, k0, k1) in enumerate(qblocks):
        nq = N // 128
        for b in accb:
            P.memset(b[:], 0.0, [b], eng="pool" if False else "dve")
        its = [(kt, m) for kt in range(k0, k1) for m in range(2)]
        per = (len(pending) + len(its) - 1) // max(1, len(its) - 8) if pending else 0
        ppos = 0

        def issue_s(j):
            kt, m = its[j]
            S, pm = Sb[(n + j) % 4], Pm[(n + j) % 6]
            P.mm(S[:, 0:N], QK[1][m * 64:(m + 1) * 64, kt * 128:(kt + 1) * 128], QK[0][m * 64:(m + 1) * 64, q0:q0 + N],
                 True, True, [QK[0], QK[1]], [S])
            P.act(pm[:, 0:N], S[:, 0:N], AF.Exp, [S], [pm])

        for j in range(min(SKEW, len(its))):
            issue_s(j)
        for j, (kt, m) in enumerate(its):
            if j + SKEW < len(its):
                issue_s(j + SKEW)
            pm = Pm[(n + j) % 6]
            for qs in range(nq):
                ab, c0 = acc(m, qs)
                P.mm(ab[:, c0:c0 + 129], pm[:, qs * 128:(qs + 1) * 128], V[:, kt, :], False, False, [pm, V], [ab], skip=True)
            for th in pending[ppos:ppos + per]:
                th()
            ppos += per
        for th in pending[ppos:]:
            th()
        n += len(its)
        par = bi % 2
        for j3, b in enumerate(accb):
            P.cp(accS[par][j3][:], b[:, 0:387], [b], [accS[par][j3]], eng="dve" if j3 != 1 else "act")
        P.defer = []
        finalize(q0, nq, par, no)
        pending, P.defer = P.defer, None
        no += nq
    for th in pending:
        th()
    return P.finish() if own else None


def rope_tables():
    n = 8192
    rows = n // 64
    row = np.repeat(np.arange(rows, dtype=np.float32), 64)
    col = np.tile(np.arange(64, dtype=np.float32), rows)
    half = 32
    inv = (np.float32(10000.0) ** (-np.arange(0, half, 2, dtype=np.float32) / np.float32(half))).astype(np.float32)
    ang = np.concatenate([row[:, None] * inv, col[:, None] * inv], axis=-1).astype(np.float32)
    cos, sin = np.cos(ang).astype(np.float32), np.sin(ang).astype(np.float32)
    idx = (np.arange(128) % 64) // 2
    return np.ascontiguousarray(np.stack([cos[:, idx].T, sin[:, idx].T], 0))


def attn_consts():
    Bm = np.zeros((128, 128), np.float32)
    Bm[:64, :64] = 1.0 / 64
    Bm[64:, 64:] = 1.0 / 64
    Rm = np.zeros((128, 128), np.float32)
    for i in range(64):
        Rm[2 * i + 1, 2 * i] = -1.0
        Rm[2 * i, 2 * i + 1] = 1.0
    return np.stack([Bm, Rm], 0)


def lambda_init(li):
    import math
    return 0.8 - 0.6 * math.exp(-0.3 * li)


def attn_inputs(hTs, li, w_in, a_qk_norm, a_lambda, a_subln):
    off_q = 5008 - 512 - 512 - 512
    off = {}
    o = 0
    for name, wdt in IN_SPLITS:
        off[name] = o
        o += wdt
    cs = rope_tables()
    cm = attn_consts()
    ins = []
    for b in range(2):
        hT = np.ascontiguousarray(hTs[b].reshape(8, 128, SEQ_ALL))
        for h in range(4):
            wq = w_in[li][:, off['a_q'] + 128 * h: off['a_q'] + 128 * (h + 1)]
            wk = w_in[li][:, off['a_k'] + 128 * h: off['a_k'] + 128 * (h + 1)]
            wv = w_in[li][:, off['a_v'] + 128 * h: off['a_v'] + 128 * (h + 1)]
            gqk = np.stack([np.tile(a_qk_norm[li, 0], 2), np.tile(a_qk_norm[li, 1], 2)], 1).astype(np.float32)
            li_ = np.float32(lambda_init(li))
            lamp = np.concatenate([np.broadcast_to(a_lambda[li].reshape(1, 256), (128, 256)),
                                   np.full((128, 1), li_, np.float32), np.full((128, 1), np.float32(1.0) - li_, np.float32)], 1)
            ins.append({"hT": hT, "wqkv": np.ascontiguousarray(np.stack([wq, wk, wv], 0)), "gqk": np.ascontiguousarray(gqk),
                        "cs": cs, "cmat": cm, "lamp": np.ascontiguousarray(lamp.astype(np.float32)),
                        "subg": np.ascontiguousarray(np.broadcast_to(a_subln[li][None, :], (128, 128)))})
    return ins


IN_SPLITS = (
    ('m_q', 256), ('m_k', 256), ('m_v', 512), ('m_o', 512),
    ('m_if', 4), ('m_ff', 4), ('m_ib', 4), ('m_fb', 4),
    ('r_r', 512), ('r_k', 512), ('r_v', 512),
    ('r_wf', 64), ('r_wb', 64), ('r_af', 64), ('r_ab', 64), ('r_g', 128),
    ('a_q', 512), ('a_k', 512), ('a_v', 512),
    ('g_m', 1024), ('g_r', 1024), ('g_a', 1024),
)


def tri_mask(P, upper, neg=False):
    m = P.sb([128, 128], F32)
    P.memset(m[:], 0.0 if neg else 1.0, [m], eng="pool")
    pat, cm = ([[1, 128]], -1) if upper else ([[-1, 128]], 1)
    P.op("pool", lambda e: e.affine_select(m[:], m[:], pat, ALU.is_ge, -1.0e4 if neg else 0.0, base=0,
                                           channel_multiplier=cm), [m], [m])
    return m


def load_hblk_halo(P, hT, hb, t0, N, lo, hi, eng="sp"):
    a = t0 - 1 if t0 - 1 >= lo else t0
    b = t0 + N + 1 if t0 + N + 1 <= hi else t0 + N
    for kc in range(8):
        for (o, n, ap) in h_pieces(hT, kc, a, b):
            P.dma(eng, hb[:, kc, a - (t0 - 1) + o:a - (t0 - 1) + o + n], ap, writes=[hb],
                  **({"allow_slow_non_contiguous": True} if n == 1 else {}))
    if a == t0:
        P.memset(hb[:, :, 0:1], 0.0, [hb], eng="pool")
    if b == t0 + N:
        P.memset(hb[:, :, N + 1:N + 2], 0.0, [hb], eng="pool")


def chunk_orders():
    f = list(range(SEQT))
    b = [1, 0] + list(range(SEQT - 1, 1, -1))
    return f, b


def build_mlstm(stop=99, P=None, T=None):
    own = P is None
    P = P or Prog()
    hT = P.io(T, "hT", [8, 128, SEQ_ALL], BF16, "ExternalInput")
    wqk = P.io(T, "wqk", [2, D, 64], F32, "ExternalInput")
    wvo = P.io(T, "wvo", [D, 256], F32, "ExternalInput")
    wg = P.io(T, "wg", [D, 4], F32, "ExternalInput")
    cw = P.io(T, "cw", [2, 128, 3, 64], F32, "ExternalInput")
    gb = P.io(T, "gb", [128, 4], F32, "ExternalInput")
    og = P.io(T, "og", [128, 128], F32, "ExternalInput")
    ym = P.io(T, "ym", [128, SEQ_ALL], BF16, "ExternalOutput")

    banks = [P.ps([128, 512], F32, f"bank{i}") for i in range(8)]
    ident = make_ident(P)
    ones = P.sb([128, 128], F32, "ones")
    P.memset(ones[:], 1.0, [ones])
    one1 = P.sb([128, 1], F32, "one1")
    P.memset(one1[:], 1.0, [one1])
    triU = tri_mask(P, True)
    triL = tri_mask(P, False)
    negU = tri_mask(P, True, True)
    negL = tri_mask(P, False, True)

    wst = P.sb([128, 8, 388], F32, "wst")
    for kc in range(8):
        P.dma("sp", wst[:, kc, 0:64], wqk[0, kc * 128:(kc + 1) * 128, :], writes=[wst])
        P.dma("sp", wst[:, kc, 64:128], wqk[1, kc * 128:(kc + 1) * 128, :], writes=[wst])
        P.dma("sp", wst[:, kc, 128:384], wvo[kc * 128:(kc + 1) * 128, :], writes=[wst])
        P.dma("sp", wst[:, kc, 384:388], wg[kc * 128:(kc + 1) * 128, :], writes=[wst])
    cws = P.sb([128, 2, 3, 64], F32, "cws")
    P.dma("act", cws[:, 0], cw[0], writes=[cws])
    P.dma("act", cws[:, 1], cw[1], writes=[cws])
    gbs = P.sb([128, 4], F32, "gbs")
    P.dma("act", gbs[:], gb[:, :], writes=[gbs])
    ogs = P.sb([128, 128], F32, "ogs")
    P.dma("act", ogs[:], og[:, :], writes=[ogs])
    Wqk = P.sb([128, 2, 3, 8, 64], BF16, "Wqk")
    Wvo = P.sb([128, 8, 260], BF16, "Wvo")
    P.cp(Wvo[:], wst[:, :, 128:388], [wst], [Wvo])
    for j in range(2):
        for tap in range(3):
            for kc in range(8):
                P.tt(Wqk[:, j, tap, kc, :], wst[:, kc, j * 64:(j + 1) * 64], cws[:, j, tap, :], ALU.mult, [wst, cws], [Wqk],
                     eng="pool" if kc % 2 else "dve")

    QT = P.sb([64, SEQ_ALL], F32, "QT")
    KT = P.sb([64, SEQ_ALL], F32, "KT")
    VE = P.sb([128, SEQT, 129], F32, "VE")
    P.memset(VE[:, :, 128:129], 1.0, [VE], eng="pool")
    OG = P.sb([128, SEQT, 128], BF16, "OG")
    G = P.sb([128, SEQT, 4], F32, "G")
    hb = [P.sb([128, 8, 514], BF16, f"hb{i}") for i in range(2)]

    blocks = [(0, 256, 0, 256)] + [(256 + 512 * j, 512, 256, SEQ_ALL) for j in range(16)]
    for bi, (t0, N, lo, hi) in enumerate(blocks):
        H = hb[bi % 2]
        load_hblk_halo(P, hT, H, t0, N, lo, hi)
        for j, dst in enumerate((QT, KT)):
            pq = banks[j]
            n = 0
            for tap in range(3):
                for kc in range(8):
                    P.mm(pq[0:64, 0:N], Wqk[:, j, tap, kc, :], H[:, kc, tap:tap + N], n == 0, n == 23, [Wqk, H], [pq])
                    n += 1
            P.act(dst[:, t0:t0 + N], pq[0:64, 0:N], AF.Silu, [pq], [dst], scale=1.0)
        for i in range(N // 128):
            pv = banks[2 + i % 2]
            for kc in range(8):
                P.mm(pv[:, 0:260], H[:, kc, 1 + i * 128:1 + (i + 1) * 128], Wvo[:, kc, :], kc == 0, kc == 7, [H, Wvo], [pv])
            tl = t0 // 128 + i
            P.cp(VE[:, tl, 0:128], pv[:, 0:128], [pv], [VE])
            P.act(OG[:, tl, :], pv[:, 128:256], AF.Sigmoid, [pv], [OG])
            P.tt(G[:, tl, :], pv[:, 256:260], gbs[:], ALU.add, [pv, gbs], [G])
    if stop == 1:
        return P.finish() if own else None
    P.ts(KT[:], KT[:], 0.125, None, ALU.mult, None, [KT], [KT], eng="pool")

    ge = P.sb([128, SEQT, 4], F32, "ge")
    P.act(ge[:], G[:], AF.Exp, [G], [ge], scale=-1.0)
    P.act(ge[:], ge[:], AF.Ln, [ge, one1], [ge], bias=one1[:, 0:1])
    LF = P.sb([128, 2, SEQT], F32, "LF")
    IG = P.sb([128, 2, SEQT], F32, "IG")
    for d in range(2):
        P.ts(LF[:, d, :], ge[:, :, 2 * d + 1], -1.0, None, ALU.mult, None, [ge], [LF])
        P.cp(IG[:, d, :], G[:, :, 2 * d], [G], [IG])
    BC = P.sb([128, 2, SEQT], F32, "BC")
    BT = P.sb([128, 2, SEQT], F32, "BT")
    for d in range(2):
        pb = banks[4 + d]
        P.mm(pb[:, 0:SEQT], (triU if d == 0 else triL)[:], LF[:, d, :], True, True, [triU, triL, LF], [pb])
        P.cp(BC[:, d, :], pb[:, 0:SEQT], [pb], [BC])
        pb2 = banks[6 + d]
        P.mm(pb2[:, 0:SEQT], ones[:], LF[:, d, :], True, True, [ones, LF], [pb2])
        P.cp(BT[:, d, :], pb2[:, 0:SEQT], [pb2], [BT])
    BIAS = P.sb([128, 2, SEQT], F32, "BIAS")
    WS = P.sb([128, 2, SEQT], F32, "WS")
    EB = P.sb([128, 2, SEQT], F32, "EB")
    DEC = P.sb([128, 2, SEQT], F32, "DEC")
    P.tt(BIAS[:], IG[:], BC[:], ALU.subtract, [IG, BC], [BIAS])
    P.tt(WS[:], BIAS[:], BT[:], ALU.add, [BIAS, BT], [WS])
    P.act(WS[:], WS[:], AF.Exp, [WS], [WS])
    P.act(EB[:], BC[:], AF.Exp, [BC], [EB])
    P.act(DEC[:], BT[:], AF.Exp, [BT], [DEC])

    if stop == 2:
        return P.finish() if own else None
    HS = P.sb([128, SEQT, 128], F32, "HS")
    CT = [[P.sb([64, 129], F32, f"CT{d}{i}") for i in range(2)] for d in range(2)]
    for d in range(2):
        P.memset(CT[d][0][:], 0.0, [CT[d][0]])
    lrep = [P.sb([128, 128], F32, f"lrep{i}") for i in range(2)]
    arg = [P.sb([128, 128], F32, f"arg{i}") for i in range(2)]
    ST = [P.sb([128, 128], F32, f"ST{i}") for i in range(2)]
    KW = [P.sb([128, 64], F32, f"KW{i}") for i in range(2)]
    it = [P.sb([128, 129], F32, f"it{i}") for i in range(2)]
    tot = [P.sb([128, 129], F32, f"tot{i}") for i in range(2)]
    sm = [P.sb([128, 2], F32, f"sm{i}") for i in range(2)]
    orders = chunk_orders()
    done = set()
    for step in range(SEQT):
        for d in range(2):
            c = orders[d][step]
            cs_ = slice(c * 128, (c + 1) * 128)
            Ccur, Cnew = CT[d][step % 2], CT[d][(step + 1) % 2]
            tri, neg = (triU, negU) if d == 0 else (triL, negL)
            p_brd, p_qk, p_n, p_i, p_kt, p_st = (banks[d * 4 + 0], banks[d * 4 + 1], banks[d * 4 + 2], banks[d * 4 + 3],
                                                 banks[d * 4 + 0], banks[d * 4 + 1])
            L = lrep[d]
            P.ts(L[:], ones[:], LF[:, d, c:c + 1], None, ALU.mult, None, [ones, LF], [L], eng="pool")
            P.mm(p_brd[:, 0:128], L[:], tri[:], True, True, [L, tri], [p_brd])
            A = arg[d]
            P.tt(A[:], p_brd[:, 0:128], neg[:], ALU.add, [p_brd, neg], [A])
            P.act(A[:], A[:], AF.Exp, [A, BIAS], [A], bias=BIAS[:, d, c:c + 1])
            P.mm(p_qk[:, 0:128], KT[:, cs_], QT[:, cs_], True, True, [KT, QT], [p_qk])
            S = ST[d]
            P.tt(S[:], p_qk[:, 0:128], A[:], ALU.mult, [p_qk, A], [S])
            P.mm(p_n[:, 0:129], S[:], VE[:, c, :], True, True, [S, VE], [p_n])
            P.mm(p_i[:, 0:129], QT[:, cs_], Ccur[:], True, True, [QT, Ccur], [p_i])
            I = it[d]
            P.act(I[:], p_i[:, 0:129], AF.Identity, [p_i, EB], [I], scale=EB[:, d, c:c + 1])
            T = tot[d]
            P.tt(T[:], p_n[:, 0:129], I[:], ALU.add, [p_n, I], [T])
            s_ = sm[d]
            P.act(s_[:, 0:1], T[:, 128:129], AF.Abs, [T], [s_])
            P.ts(s_[:, 0:1], s_[:, 0:1], 1.0, None, ALU.max, None, [s_], [s_])
            P.op("dve", lambda e, s_=s_: e.reciprocal(s_[:, 1:2], s_[:, 0:1]), [s_], [s_])
            if c in done:
                P.stt(HS[:, c, :], T[:, 0:128], s_[:, 1:2], HS[:, c, :], ALU.mult, ALU.add, [T, s_, HS], [HS])
            else:
                P.ts(HS[:, c, :], T[:, 0:128], s_[:, 1:2], None, ALU.mult, None, [T, s_], [HS])
                done.add(c)
            P.tr(p_kt[:, 0:64], KT[:, cs_], ident[0:64, 0:64], [KT, ident], [p_kt])
            kw = KW[d]
            P.ts(kw[:], p_kt[:, 0:64], WS[:, d, c:c + 1], None, ALU.mult, None, [p_kt, WS], [kw])
            P.mm(p_st[0:64, 0:129], kw[:], VE[:, c, :], True, True, [kw, VE], [p_st])
            P.stt(Cnew[:], Ccur[:], DEC[0:64, d, c:c + 1], p_st[0:64, 0:129], ALU.mult, ALU.add, [Ccur, DEC, p_st], [Cnew])

    if stop == 3:
        return P.finish() if own else None
    yo = ST
    junk = lrep[0]
    yob = [P.sb([128, 128], BF16, f"yob{i}") for i in range(2)]
    st = [P.sb([128, 4], F32, f"st{i}") for i in range(2)]
    for c in range(SEQT):
        Y, s_ = yo[c % 2], st[c % 2]
        P.act(junk[:], HS[:, c, :], AF.Square, [HS], [junk, s_], accum_out=s_[:, 0:1])
        P.ts(s_[:, 1:2], s_[:, 0:1], 1.0 / 128, EPS, ALU.mult, ALU.add, [s_], [s_])
        P.act(s_[:, 2:3], s_[:, 1:2], AF.Sqrt, [s_], [s_])
        P.op("dve", lambda e, s_=s_: e.reciprocal(s_[:, 3:4], s_[:, 2:3]), [s_], [s_])
        P.stt(Y[:], HS[:, c, :], s_[:, 3:4], ogs[:], ALU.mult, ALU.mult, [HS, s_, ogs], [Y])
        P.tt(Y[:], Y[:], OG[:, c, :], ALU.mult, [Y, OG], [Y])
        P.tr(banks[c % 2][:, 0:128], Y[:], ident[:], [Y, ident], [banks[c % 2]])
        Yb = yob[c % 2]
        P.cp(Yb[:], banks[c % 2][:, 0:128], [banks[c % 2]], [Yb], eng="act")
        P.dma("sp", ycols(ym, c * 128), Yb[:], reads=[Yb])
    return P.finish() if own else None


def col_offsets():
    off, o = {}, 0
    for name, wdt in IN_SPLITS:
        off[name] = o
        o += wdt
    return off


def mlstm_inputs(hTs, li, w_in, m_conv, m_gate_bias, m_out_norm):
    off = col_offsets()
    ins = []
    for b in range(2):
        hT = np.ascontiguousarray(hTs[b].reshape(8, 128, SEQ_ALL))
        for h in range(4):
            W = w_in[li]
            wq = W[:, off['m_q'] + 64 * h: off['m_q'] + 64 * (h + 1)]
            wk = W[:, off['m_k'] + 64 * h: off['m_k'] + 64 * (h + 1)]
            wvo = np.concatenate([W[:, off['m_v'] + 128 * h: off['m_v'] + 128 * (h + 1)],
                                  W[:, off['m_o'] + 128 * h: off['m_o'] + 128 * (h + 1)]], 1)
            wg = np.stack([W[:, off['m_if'] + h], W[:, off['m_ff'] + h], W[:, off['m_ib'] + h], W[:, off['m_fb'] + h]], 1)
            cq = m_conv[li][:, 64 * h:64 * (h + 1)]
            ck = m_conv[li][:, 256 + 64 * h:256 + 64 * (h + 1)]
            cw = np.stack([np.broadcast_to(cq[None], (128, 3, 64)), np.broadcast_to(ck[None], (128, 3, 64))], 0)
            gbv = m_gate_bias[li][:, h]
            ins.append({"hT": hT, "wqk": np.ascontiguousarray(np.stack([wq, wk], 0)), "wvo": np.ascontiguousarray(wvo),
                        "wg": np.ascontiguousarray(wg), "cw": np.ascontiguousarray(cw.astype(np.float32)),
                        "gb": np.ascontiguousarray(np.broadcast_to(gbv[None, :], (128, 4)).astype(np.float32)),
                        "og": np.ascontiguousarray(np.broadcast_to(m_out_norm[li][None, 128 * h:128 * (h + 1)], (128, 128)))})
    return ins


R_GN_EPS = 64e-5
RWKV_INTERLEAVE = True
W_SCALE = -0.6065306597126334


def aff_mask(P, pat, cm, op, val=1.0, base=0):
    m = P.sb([128, 128], F32)
    P.memset(m[:], val, [m], eng="pool")
    P.op("pool", lambda e: e.affine_select(m[:], m[:], pat, op, 0.0, base=base, channel_multiplier=cm), [m], [m])
    return m


def build_rwkv_v1(P=None, T=None):
    own = P is None
    P = P or Prog()
    hT = P.io(T, "hT", [8, 128, SEQ_ALL], BF16, "ExternalInput")
    wrkv = P.io(T, "wrkv", [D, 384], F32, "ExternalInput")
    crkv = P.io(T, "crkv", [128, 3, 384], F32, "ExternalInput")
    wl = P.io(T, "wl", [D, 384], F32, "ExternalInput")
    w2a2 = P.io(T, "w2a2", [2, 128, 128], F32, "ExternalInput")
    bias01 = P.io(T, "bias01", [1, 2, 256], F32, "ExternalInput")
    g2 = P.io(T, "g2", [128, 128], F32, "ExternalInput")
    vecs = P.io(T, "vecs", [128, 5, 128], F32, "ExternalInput")
    yr = P.io(T, "yr", [128, SEQ_ALL], BF16, "ExternalOutput")

    bk = [P.ps([128, 512], F32, f"bank{i}") for i in range(8)]
    ident = make_ident(P)
    ones = P.sb([128, 128], F32, "ones")
    P.memset(ones[:], 1.0, [ones])
    mI = [aff_mask(P, [[1, 128]], -1, ALU.is_ge), aff_mask(P, [[-1, 128]], 1, ALU.is_ge)]
    mS = [aff_mask(P, [[1, 128]], -1, ALU.is_gt), aff_mask(P, [[-1, 128]], 1, ALU.is_gt)]
    cI = [aff_mask(P, [[1, 128]], -1, ALU.is_ge, W_SCALE), aff_mask(P, [[-1, 128]], 1, ALU.is_ge, W_SCALE)]
    cS = [aff_mask(P, [[1, 128]], -1, ALU.is_gt, W_SCALE), aff_mask(P, [[-1, 128]], 1, ALU.is_gt, W_SCALE)]
    mSI = []
    for d in range(2):
        m = P.sb([128, 256], F32)
        P.cp(m[:, 0:128], mS[d][:], [mS[d]], [m])
        P.cp(m[:, 128:256], mI[d][:], [mI[d]], [m])
        mSI.append(m)

    wst = P.sb([128, 8, 768], F32, "wst")
    for kc in range(8):
        P.dma("sp", wst[:, kc, 0:384], wrkv[kc * 128:(kc + 1) * 128, :], writes=[wst])
        P.dma("act", wst[:, kc, 384:768], wl[kc * 128:(kc + 1) * 128, :], writes=[wst])
    cws = P.sb([128, 3, 384], F32, "cws")
    P.dma("sp", cws[:], crkv[:, :, :], writes=[cws])
    Wc = P.sb([128, 3, 8, 384], BF16, "Wc")
    for tap in range(3):
        for kc in range(8):
            P.tt(Wc[:, tap, kc, :], wst[:, kc, 0:384], cws[:, tap, :], ALU.mult, [wst, cws], [Wc])
    Wl = P.sb([128, 8, 384], BF16, "Wl")
    P.cp(Wl[:], wst[:, :, 384:768], [wst], [Wl])
    W2 = P.sb([128, 2, 128], F32, "W2")
    for d in range(2):
        P.dma("act", W2[:, d, :], w2a2[d], writes=[W2])
    B01 = P.sb([1, 2, 256], F32, "B01")
    P.dma("act", B01[:], bias01[:, :, :], writes=[B01])
    G2 = P.sb([128, 128], F32, "G2")
    P.dma("act", G2[:], g2[:, :], writes=[G2])
    VEC = P.sb([128, 5, 128], F32, "VEC")
    P.dma("act", VEC[:], vecs[:, :, :], writes=[VEC])
    epsg = P.sb([128, 1], F32, "epsg")
    P.memset(epsg[:], R_GN_EPS, [epsg])

    YF = P.sb([128, SEQT, 128], F32, "YF")
    hb = [P.sb([128, 8, 130], BF16, f"hb{i}") for i in range(2)]
    STB = P.sb([128, 128], F32, "STB")

    def TL(shape, name):
        return P.sb(shape, F32, name)

    rkv = TL([128, 384], "rkv")
    pl = TL([128, 128], "pl")
    sgg = TL([128, 128], "sgg")
    sig = TL([128, 128], "sig")
    av = TL([128, 128], "av")
    t0_ = TL([128, 128], "t0_")
    ss = TL([128, 8], "ss")
    kkn = TL([128, 128], "kkn")
    bh = TL([128, 128], "bh")
    t2 = TL([128, 128], "t2")
    key = TL([128, 128], "key")
    eG = TL([128, 256], "eG")
    enG = TL([128, 128], "enG")
    eR = TL([128, 128], "eR")
    AR = TL([128, 256], "AR")
    BtT = TL([128, 128], "BtT")
    KtT = TL([128, 128], "KtT")
    Bb = TL([128, 128], "Bb")
    Kb = TL([128, 128], "Kb")
    Mm = [TL([128, 256], f"Mm{i}") for i in range(2)]
    Ak = [TL([128, 256], f"Ak{i}") for i in range(2)]
    Lm = [TL([128, 128], f"Lm{i}") for i in range(2)]
    TtA = [TL([128, 128], f"TtA{i}") for i in range(2)]
    TA = [TL([128, 128], f"TA{i}") for i in range(2)]
    Pp = [[TL([128, 128], f"Pp{i}{j}") for j in range(2)] for i in range(2)]
    Qq = [[TL([128, 128], f"Qq{i}{j}") for j in range(2)] for i in range(2)]
    X = TL([128, 128], "X")
    U = TL([128, 128], "U")
    yb = TL([128, 128], "yb")
    gn = TL([128, 16], "gn")
    yn = TL([128, 128], "yn")
    rk = TL([128, 128], "rk")
    yo = [TL([128, 128], f"yo{i}") for i in range(2)]
    yob = [P.sb([128, 128], BF16, f"yob{i}") for i in range(2)]

    orders = chunk_orders()
    for d in range(2):
        P.memset(STB[:], 0.0, [STB])
        for step in range(SEQT):
            c = orders[d][step]
            t0 = c * 128
            lo, hi = (0, NCTX) if c < CTXT else (NCTX, SEQ_ALL)
            H = hb[step % 2]
            load_hblk_halo(P, hT, H, t0, 128, lo, hi)
            n = 0
            for tap in range(3):
                for kc in range(8):
                    P.mm(bk[0][:, 0:384], H[:, kc, tap:tap + 128], Wc[:, tap, kc, :], n == 0, n == 23, [H, Wc], [bk[0]])
                    n += 1
            P.cp(rkv[:], bk[0][:, 0:384], [bk[0]], [rkv], eng="act")
            for kc in range(8):
                P.mm(bk[1][:, 0:128], Wl[:, kc, d * 128:(d + 1) * 128], H[:, kc, 1:129], kc == 0, kc == 7, [Wl, H], [bk[1]])
            if d == 1:
                for kc in range(8):
                    P.mm(bk[1][:, 128:256], Wl[:, kc, 256:384], H[:, kc, 1:129], kc == 0, kc == 7, [Wl, H], [bk[1]])
            P.act(pl[0:64, :], bk[1][0:64, 0:128], AF.Tanh, [bk[1]], [pl])
            P.cp(pl[64:128, :], bk[1][64:128, 0:128], [bk[1]], [pl])
            if d == 1:
                P.act(sgg[:], bk[1][:, 128:256], AF.Sigmoid, [bk[1]], [sgg])
            P.mm(bk[1][:, 256:384], pl[0:64, :], W2[0:64, d, :], True, False, [pl, W2], [bk[1]])
            P.mm(bk[1][:, 256:384], ones[0:1, :], B01[0:1, d, 0:128], False, True, [ones, B01], [bk[1]])
            P.mm(bk[1][:, 384:512], pl[64:128, :], W2[64:128, d, :], True, False, [pl, W2], [bk[1]])
            P.mm(bk[1][:, 384:512], ones[0:1, :], B01[0:1, d, 128:256], False, True, [ones, B01], [bk[1]])
            P.act(sig[:], bk[1][:, 256:384], AF.Sigmoid, [bk[1]], [sig])
            P.act(av[:], bk[1][:, 384:512], AF.Sigmoid, [bk[1]], [av])
            r_, k_, v_ = rkv[:, 0:128], rkv[:, 128:256], rkv[:, 256:384]
            P.tt(t0_[:], k_, VEC[:, 0, :], ALU.mult, [rkv, VEC], [t0_])
            P.tt(t2[:], t0_[:], t0_[:], ALU.mult, [t0_], [t2])
            for hh in range(2):
                P.op("dve", lambda e, hh=hh: e.tensor_reduce(ss[:, hh:hh + 1], t2[:, hh * 64:(hh + 1) * 64], AX.X, ALU.add),
                     [t2], [ss])
            P.act(ss[:, 2:4], ss[:, 0:2], AF.Sqrt, [ss], [ss])
            P.ts(ss[:, 2:4], ss[:, 2:4], 1e-12, None, ALU.max, None, [ss], [ss])
            P.op("dve", lambda e: e.reciprocal(ss[:, 4:6], ss[:, 2:4]), [ss], [ss])
            for hh in range(2):
                hs = slice(hh * 64, (hh + 1) * 64)
                P.ts(kkn[:, hs], t0_[:, hs], ss[:, 4 + hh:5 + hh], -1.0, ALU.mult, ALU.mult, [t0_, ss], [kkn])
            P.stt(bh[:], kkn[:], -1.0, av[:], ALU.mult, ALU.mult, [kkn, av], [bh])
            P.stt(t2[:], av[:], -1.0, VEC[:, 1, :], ALU.add, ALU.mult, [av, VEC], [t2])
            P.stt(key[:], t2[:], 1.0, k_, ALU.add, ALU.mult, [t2, rkv], [key])
            P.tr(bk[2][:, 0:128], r_, ident[:], [rkv, ident], [bk[2]])
            P.tr(bk[2][:, 128:256], kkn[:], ident[:], [kkn, ident], [bk[2]])
            P.tr(bk[2][:, 256:384], bh[:], ident[:], [bh, ident], [bk[2]])
            P.tr(bk[2][:, 384:512], key[:], ident[:], [key, ident], [bk[2]])
            P.mm(bk[3][:, 0:128], sig[:], cS[d][:], True, True, [sig, cS[d]], [bk[3]])
            P.mm(bk[3][:, 128:256], sig[:], cI[d][:], True, True, [sig, cI[d]], [bk[3]])
            P.mm(bk[3][:, 256:384], cS[1 - d][:], sig[:], True, True, [sig, cS[1 - d]], [bk[3]])
            P.act(eG[:], bk[3][:, 0:256], AF.Exp, [bk[3]], [eG])
            P.act(enG[:], bk[3][:, 128:256], AF.Exp, [bk[3]], [enG], scale=-1.0)
            P.act(eR[:], bk[3][:, 256:384], AF.Exp, [bk[3]], [eR])
            P.tt(AR[:, 0:128], bk[2][:, 128:256], eG[:, 0:128], ALU.mult, [bk[2], eG], [AR])
            P.tt(AR[:, 128:256], bk[2][:, 0:128], eG[:, 128:256], ALU.mult, [bk[2], eG], [AR])
            P.tt(BtT[:], bk[2][:, 256:384], enG[:], ALU.mult, [bk[2], enG], [BtT])
            P.tt(KtT[:], bk[2][:, 384:512], enG[:], ALU.mult, [bk[2], enG], [KtT])
            P.tt(Bb[:], bh[:], eR[:], ALU.mult, [bh, eR], [Bb])
            P.tt(Kb[:], key[:], eR[:], ALU.mult, [key, eR], [Kb])
            for hh in range(2):
                hp_ = slice(hh * 64, (hh + 1) * 64)
                P.mm(bk[4][:, 0:256], BtT[hp_, :], AR[hp_, :], True, True, [BtT, AR], [bk[4]])
                P.mm(bk[5][:, 0:256], KtT[hp_, :], AR[hp_, :], True, True, [KtT, AR], [bk[5]])
                P.mm(bk[4][:, 256:384], AR[hp_, 0:128], BtT[hp_, :], True, True, [AR, BtT], [bk[4]])
                P.tt(Mm[hh][:], bk[4][:, 0:256], mSI[d][:], ALU.mult, [bk[4], mSI[d]], [Mm[hh]])
                P.tt(Ak[hh][:], bk[5][:, 0:256], mSI[d][:], ALU.mult, [bk[5], mSI[d]], [Ak[hh]])
                P.tt(Lm[hh][:], bk[4][:, 256:384], mS[1 - d][:], ALU.mult, [bk[4], mS[1 - d]], [Lm[hh]])
                P.tt(TtA[hh][:], Mm[hh][:, 0:128], ident[:], ALU.add, [Mm[hh], ident], [TtA[hh]], eng="pool")
                P.tt(TA[hh][:], Lm[hh][:], ident[:], ALU.add, [Lm[hh], ident], [TA[hh]], eng="pool")
                Pc, Qc = Mm[hh], Lm[hh]
                pc_ap, qc_ap = Mm[hh][:, 0:128], Lm[hh][:]
                for lvl in range(6):
                    last = lvl == 5
                    Pn, Qn = Pp[hh][lvl % 2], Qq[hh][lvl % 2]
                    P.mm(bk[6][:, 0:128], qc_ap, pc_ap, True, True, [Pc, Qc], [bk[6]])
                    if not last:
                        P.mm(bk[6][:, 128:256], pc_ap, qc_ap, True, True, [Pc, Qc], [bk[6]])
                    P.cp(Pn[:], bk[6][:, 0:128], [bk[6]], [Pn])
                    if not last:
                        P.cp(Qn[:], bk[6][:, 128:256], [bk[6]], [Qn], eng="act")
                    P.mm(bk[6][:, 256:384], TA[hh][:], Pn[:], True, True, [TA[hh], Pn], [bk[6]])
                    if not last:
                        P.mm(bk[6][:, 384:512], Pn[:], TA[hh][:], True, True, [TA[hh], Pn], [bk[6]])
                    P.tt(TtA[hh][:], TtA[hh][:], bk[6][:, 256:384], ALU.add, [TtA[hh], bk[6]], [TtA[hh]])
                    if not last:
                        P.tt(TA[hh][:], TA[hh][:], bk[6][:, 384:512], ALU.add, [TA[hh], bk[6]], [TA[hh]])
                    Pc, Qc = Pn, Qn
                    pc_ap, qc_ap = Pn[:], Qn[:]
            P.mm(bk[7][:, 0:128], AR[:, 0:128], STB[:], True, False, [AR, STB], [bk[7]])
            for hh in range(2):
                hs = slice(hh * 64, (hh + 1) * 64)
                P.mm(bk[7][:, hs], Ak[hh][:, 0:128], rkv[:, 256 + hh * 64:256 + (hh + 1) * 64], False, hh == 1,
                     [Ak[hh], rkv], [bk[7]])
            P.cp(X[:], bk[7][:, 0:128], [bk[7]], [X])
            for hh in range(2):
                hs = slice(hh * 64, (hh + 1) * 64)
                P.mm(bk[7][:, 128 + hh * 64:128 + (hh + 1) * 64], TtA[hh][:], X[:, hs], True, True, [TtA[hh], X], [bk[7]])
            P.cp(U[:], bk[7][:, 128:256], [bk[7]], [U])
            P.mm(bk[7][:, 256:384], AR[:, 128:256], STB[:], True, False, [AR, STB], [bk[7]])
            for hh in range(2):
                ys = slice(256 + hh * 64, 256 + (hh + 1) * 64)
                hs = slice(hh * 64, (hh + 1) * 64)
                P.mm(bk[7][:, ys], Mm[hh][:, 128:256], U[:, hs], False, False, [Mm[hh], U], [bk[7]])
                P.mm(bk[7][:, ys], Ak[hh][:, 128:256], rkv[:, 256 + hh * 64:256 + (hh + 1) * 64], False, hh == 1,
                     [Ak[hh], rkv], [bk[7]])
            P.mm(bk[7][:, 384:512], Bb[:], U[:], True, False, [Bb, U], [bk[7]])
            P.mm(bk[7][:, 384:512], Kb[:], v_, False, True, [Kb, rkv], [bk[7]])
            dcol = 255 if d == 0 else 128
            if d == 0:
                P.cp(YF[:, c, :], bk[7][:, 256:384], [bk[7]], [YF], eng="act")
            else:
                P.tt(yb[:], bk[7][:, 256:384], YF[:, c, :], ALU.add, [bk[7], YF], [yb])
            for hh in range(2):
                hs = slice(hh * 64, (hh + 1) * 64)
                P.stt(STB[hs, hs], STB[hs, hs], eG[hs, dcol:dcol + 1], bk[7][hs, 384 + hh * 64:384 + (hh + 1) * 64],
                      ALU.mult, ALU.add, [STB, eG, bk[7]], [STB])
            if d == 1:
                P.mm(bk[0][:, 384:512], sgg[:], G2[:], True, True, [sgg, G2], [bk[0]])
                P.tt(t2[:], yb[:], yb[:], ALU.mult, [yb], [t2])
                for hh in range(2):
                    hs = slice(hh * 64, (hh + 1) * 64)
                    P.op("dve", lambda e, hh=hh, hs=hs: e.tensor_reduce(gn[:, hh:hh + 1], yb[:, hs], AX.X, ALU.add), [yb], [gn])
                    P.op("dve", lambda e, hh=hh, hs=hs: e.tensor_reduce(gn[:, 2 + hh:3 + hh], t2[:, hs], AX.X, ALU.add), [t2], [gn])
                P.ts(gn[:, 4:8], gn[:, 0:4], 1.0 / 64, None, ALU.mult, None, [gn], [gn])
                P.tt(gn[:, 8:10], gn[:, 4:6], gn[:, 4:6], ALU.mult, [gn], [gn])
                P.tt(gn[:, 10:12], gn[:, 6:8], gn[:, 8:10], ALU.subtract, [gn], [gn])
                P.act(gn[:, 12:14], gn[:, 10:12], AF.Sqrt, [gn, epsg], [gn], bias=epsg[:, 0:1])
                P.op("dve", lambda e: e.reciprocal(gn[:, 14:16], gn[:, 12:14]), [gn], [gn])
                for hh in range(2):
                    hs = slice(hh * 64, (hh + 1) * 64)
                    P.ts(yn[:, hs], yb[:, hs], gn[:, 4 + hh:5 + hh], gn[:, 14 + hh:15 + hh], ALU.subtract, ALU.mult,
                         [yb, gn], [yn])
                P.tt(yn[:], yn[:], VEC[:, 3, :], ALU.mult, [yn, VEC], [yn])
                P.tt(yn[:], yn[:], VEC[:, 4, :], ALU.add, [yn, VEC], [yn])
                P.tt(rk[:], r_, k_, ALU.mult, [rkv], [rk])
                P.tt(rk[:], rk[:], VEC[:, 2, :], ALU.mult, [rk, VEC], [rk])
                for hh in range(2):
                    hs = slice(hh * 64, (hh + 1) * 64)
                    P.op("dve", lambda e, hh=hh, hs=hs: e.tensor_reduce(ss[:, 6 + hh:7 + hh], rk[:, hs], AX.X, ALU.add), [rk], [ss])
                    P.stt(yn[:, hs], rkv[:, 256 + hh * 64:256 + (hh + 1) * 64], ss[:, 6 + hh:7 + hh], yn[:, hs], ALU.mult, ALU.add,
                          [rkv, ss, yn], [yn])
                Y = yo[step % 2]
                P.tt(Y[:], yn[:], bk[0][:, 384:512], ALU.mult, [yn, bk[0]], [Y])
                P.tr(bk[3][:, 384:512], Y[:], ident[:], [Y, ident], [bk[3]])
                Yb = yob[step % 2]
                P.cp(Yb[:], bk[3][:, 384:512], [bk[3]], [Yb], eng="act")
                P.dma("sp", ycols(yr, t0), Yb[:], reads=[Yb])
    return P.finish() if own else None


def build_rwkv(P=None, T=None):
    own = P is None
    P = P or Prog()
    hT = P.io(T, "hT", [8, 128, SEQ_ALL], BF16, "ExternalInput")
    wrkv = P.io(T, "wrkv", [D, 384], F32, "ExternalInput")
    crkv = P.io(T, "crkv", [128, 3, 384], F32, "ExternalInput")
    wl = P.io(T, "wl", [D, 384], F32, "ExternalInput")
    w2a2 = P.io(T, "w2a2", [2, 128, 128], F32, "ExternalInput")
    bias01 = P.io(T, "bias01", [1, 2, 256], F32, "ExternalInput")
    g2 = P.io(T, "g2", [128, 128], F32, "ExternalInput")
    vecs = P.io(T, "vecs", [128, 5, 128], F32, "ExternalInput")
    yr = P.io(T, "yr", [128, SEQ_ALL], BF16, "ExternalOutput")

    bk = [P.ps([128, 512], F32, f"bank{i}") for i in range(8)]
    ident = make_ident(P)
    ones = P.sb([128, 128], F32, "ones")
    P.memset(ones[:], 1.0, [ones])
    mI = [aff_mask(P, [[1, 128]], -1, ALU.is_ge), aff_mask(P, [[-1, 128]], 1, ALU.is_ge)]
    mS = [aff_mask(P, [[1, 128]], -1, ALU.is_gt), aff_mask(P, [[-1, 128]], 1, ALU.is_gt)]
    cI = [aff_mask(P, [[1, 128]], -1, ALU.is_ge, W_SCALE), aff_mask(P, [[-1, 128]], 1, ALU.is_ge, W_SCALE)]
    cS = [aff_mask(P, [[1, 128]], -1, ALU.is_gt, W_SCALE), aff_mask(P, [[-1, 128]], 1, ALU.is_gt, W_SCALE)]
    mSI2 = P.sb([128, 2, 2, 256], F32, "mSI2")
    mL4 = P.sb([128, 4, 128], F32, "mL4")
    I4 = P.sb([128, 4, 128], F32, "I4")
    for d in range(2):
        for hh in range(2):
            P.cp(mSI2[:, d, hh, 0:128], mS[d][:], [mS[d]], [mSI2])
            P.cp(mSI2[:, d, hh, 128:256], mI[d][:], [mI[d]], [mSI2])
            P.cp(mL4[:, d * 2 + hh, :], mS[1 - d][:], [mS[1 - d]], [mL4])
            P.cp(I4[:, d * 2 + hh, :], ident[:], [ident], [I4])

    wst = P.sb([128, 8, 384], F32, "wst")
    for kc in range(8):
        P.dma("sp", wst[:, kc, :], wrkv[kc * 128:(kc + 1) * 128, :], writes=[wst])
    cws = P.sb([128, 3, 384], F32, "cws")
    P.dma("sp", cws[:], crkv[:, :, :], writes=[cws])
    Wc = P.sb([128, 3, 8, 384], BF16, "Wc")
    for tap in range(3):
        for kc in range(8):
            P.tt(Wc[:, tap, kc, :], wst[:, kc, :], cws[:, tap, :], ALU.mult, [wst, cws], [Wc])
    Wl = P.sb([128, 8, 384], BF16, "Wl")
    for kc in range(8):
        P.dma("act", wst[:, kc, :], wl[kc * 128:(kc + 1) * 128, :], writes=[wst])
    P.cp(Wl[:], wst[:], [wst], [Wl])
    W2 = P.sb([128, 2, 128], F32, "W2")
    for d in range(2):
        P.dma("act", W2[:, d, :], w2a2[d], writes=[W2])
    B01 = P.sb([1, 2, 256], F32, "B01")
    P.dma("act", B01[:], bias01[:, :, :], writes=[B01])
    G2 = P.sb([128, 128], F32, "G2")
    P.dma("act", G2[:], g2[:, :], writes=[G2])
    VEC = P.sb([128, 5, 128], F32, "VEC")
    P.dma("act", VEC[:], vecs[:, :, :], writes=[VEC])
    VK2 = P.sb([128, 2, 2, 128], F32, "VK2")
    for j in range(2):
        for d in range(2):
            P.cp(VK2[:, j, d, :], VEC[:, j, :], [VEC], [VK2])
    epsg = P.sb([128, 1], F32, "epsg")
    P.memset(epsg[:], R_GN_EPS, [epsg])

    YD = [P.sb([128, SEQT, 128], BF16, f"YD{d}") for d in range(2)]
    hb = [P.sb([128, 8, 130], BF16, f"hb{i}") for i in range(4 if RWKV_INTERLEAVE else 2)]
    STB = [P.sb([128, 128], F32, f"STB{d}") for d in range(2)]
    for d in range(2):
        P.memset(STB[d][:], 0.0, [STB[d]])

    def TL(shape, name):
        return P.sb(shape, F32, name)

    NB_ = 2 if RWKV_INTERLEAVE else 1
    rkv_ = [TL([128, 2, 384], f"rkv{i}") for i in range(NB_)]
    pl = TL([128, 2, 128], "pl")
    sig = TL([128, 2, 128], "sig")
    av = TL([128, 2, 128], "av")
    t0_ = TL([128, 2, 128], "t0_")
    t2 = TL([128, 2, 128], "t2")
    ss = TL([128, 16], "ss")
    kkn = TL([128, 2, 128], "kkn")
    bh = TL([128, 2, 128], "bh")
    key = TL([128, 2, 128], "key")
    eG_ = [TL([128, 2, 256], f"eG{i}") for i in range(NB_)]
    enG = TL([128, 2, 128], "enG")
    eR = TL([128, 2, 128], "eR")
    AR_ = [TL([128, 2, 256], f"AR{i}") for i in range(NB_)]
    BtT = TL([128, 2, 128], "BtT")
    KtT = TL([128, 2, 128], "KtT")
    Bb_ = [TL([128, 2, 128], f"Bb{i}") for i in range(NB_)]
    Kb_ = [TL([128, 2, 128], f"Kb{i}") for i in range(NB_)]
    MA_ = [TL([128, 2, 2, 256], f"MA{i}") for i in range(NB_)]
    AK_ = [TL([128, 2, 2, 256], f"AK{i}") for i in range(NB_)]
    L4_ = [TL([128, 4, 128], f"L4{i}") for i in range(NB_)]
    P4 = [TL([128, 4, 128], f"P4{i}") for i in range(2)]
    Q4 = [TL([128, 4, 128], f"Q4{i}") for i in range(2)]
    TtA = TL([128, 4, 128], "TtA")
    TA = TL([128, 4, 128], "TA")
    X2 = TL([128, 2, 128], "X2")
    U2 = TL([128, 2, 128], "U2")

    orders = chunk_orders()

    def pre(step):
        par = (step % 2) if RWKV_INTERLEAVE else 0
        rkv, eG, AR, Bb, Kb, MA, AK, L4 = rkv_[par], eG_[par], AR_[par], Bb_[par], Kb_[par], MA_[par], AK_[par], L4_[par]
        cc = [orders[0][step], orders[1][step]]
        for d in range(2):
            c = cc[d]
            lo, hi = (0, NCTX) if c < CTXT else (NCTX, SEQ_ALL)
            H = hb[(par * 2 + d) % len(hb)]
            load_hblk_halo(P, hT, H, c * 128, 128, lo, hi, eng="sp" if d == 0 else "act")
            n = 0
            for tap in range(3):
                for kc in range(8):
                    P.mm(bk[d][:, 0:384], H[:, kc, tap:tap + 128], Wc[:, tap, kc, :], n == 0, n == 23, [H, Wc], [bk[d]])
                    n += 1
            for kc in range(8):
                P.mm(bk[d][:, 384:512], Wl[:, kc, d * 128:(d + 1) * 128], H[:, kc, 1:129], kc == 0, kc == 7, [Wl, H], [bk[d]])
            P.cp(rkv[:, d, :], bk[d][:, 0:384], [bk[d]], [rkv], eng="act")
            P.act(pl[0:64, d, :], bk[d][0:64, 384:512], AF.Tanh, [bk[d]], [pl])
            P.cp(pl[64:128, d, :], bk[d][64:128, 384:512], [bk[d]], [pl])
        for d in range(2):
            P.mm(bk[2][:, d * 256:d * 256 + 128], pl[0:64, d, :], W2[0:64, d, :], True, False, [pl, W2], [bk[2]])
            P.mm(bk[2][:, d * 256:d * 256 + 128], ones[0:1, :], B01[0:1, d, 0:128], False, True, [ones, B01], [bk[2]])
            P.mm(bk[2][:, d * 256 + 128:d * 256 + 256], pl[64:128, d, :], W2[64:128, d, :], True, False, [pl, W2], [bk[2]])
            P.mm(bk[2][:, d * 256 + 128:d * 256 + 256], ones[0:1, :], B01[0:1, d, 128:256], False, True, [ones, B01], [bk[2]])
        b2v = bk[2][:, :].rearrange("p (d j c) -> p d j c", d=2, j=2)
        P.act(sig[:], b2v[:, :, 0, :], AF.Sigmoid, [bk[2]], [sig])
        P.act(av[:], b2v[:, :, 1, :], AF.Sigmoid, [bk[2]], [av])
        k2 = rkv[:, :, 128:256]
        P.tt(t0_[:], k2, VK2[:, 0], ALU.mult, [rkv, VK2], [t0_])
        P.tt(t2[:], t0_[:], t0_[:], ALU.mult, [t0_], [t2])
        P.op("dve", lambda e: e.tensor_reduce(ss[:, 0:4], t2[:, :, :].rearrange("p d (h k) -> p (d h) k", h=2), AX.X, ALU.add),
             [t2], [ss])
        P.act(ss[:, 4:8], ss[:, 0:4], AF.Sqrt, [ss], [ss])
        P.ts(ss[:, 4:8], ss[:, 4:8], 1e-12, None, ALU.max, None, [ss], [ss])
        P.op("dve", lambda e: e.reciprocal(ss[:, 8:12], ss[:, 4:8]), [ss], [ss])
        for d in range(2):
            for hh in range(2):
                hs = slice(hh * 64, (hh + 1) * 64)
                q = d * 2 + hh
                P.ts(kkn[:, d, hs], t0_[:, d, hs], ss[:, 8 + q:9 + q], -1.0, ALU.mult, ALU.mult, [t0_, ss], [kkn])
        P.stt(bh[:], kkn[:], -1.0, av[:], ALU.mult, ALU.mult, [kkn, av], [bh])
        P.stt(t2[:], av[:], -1.0, VK2[:, 1], ALU.add, ALU.mult, [av, VK2], [t2])
        P.stt(key[:], t2[:], 1.0, k2, ALU.add, ALU.mult, [t2, rkv], [key])
        for d in range(2):
            tb = bk[3] if d == 0 else bk[7]
            P.tr(tb[:, 0:128], rkv[:, d, 0:128], ident[:], [rkv, ident], [tb])
            P.tr(tb[:, 128:256], kkn[:, d, :], ident[:], [kkn, ident], [tb])
            P.tr(tb[:, 256:384], bh[:, d, :], ident[:], [bh, ident], [tb])
            P.tr(tb[:, 384:512], key[:, d, :], ident[:], [key, ident], [tb])
            gb_ = bk[d]
            P.mm(gb_[:, 0:128], sig[:, d, :], cS[d][:], True, True, [sig, cS[d]], [gb_])
            P.mm(gb_[:, 128:256], sig[:, d, :], cI[d][:], True, True, [sig, cI[d]], [gb_])
            P.mm(gb_[:, 256:384], cS[1 - d][:], sig[:, d, :], True, True, [sig, cS[1 - d]], [gb_])
        for d in range(2):
            tb, gb_ = (bk[3] if d == 0 else bk[7]), bk[d]
            P.act(eG[:, d, :], gb_[:, 0:256], AF.Exp, [gb_], [eG])
            P.act(enG[:, d, :], gb_[:, 128:256], AF.Exp, [gb_], [enG], scale=-1.0)
            P.act(eR[:, d, :], gb_[:, 256:384], AF.Exp, [gb_], [eR])
            P.tt(AR[:, d, 0:128], tb[:, 128:256], eG[:, d, 0:128], ALU.mult, [tb, eG], [AR])
            P.tt(AR[:, d, 128:256], tb[:, 0:128], eG[:, d, 128:256], ALU.mult, [tb, eG], [AR])
            P.tt(BtT[:, d, :], tb[:, 256:384], enG[:, d, :], ALU.mult, [tb, enG], [BtT])
            P.tt(KtT[:, d, :], tb[:, 384:512], enG[:, d, :], ALU.mult, [tb, enG], [KtT])
        P.tt(Bb[:], bh[:], eR[:], ALU.mult, [bh, eR], [Bb], eng="pool")
        P.tt(Kb[:], key[:], eR[:], ALU.mult, [key, eR], [Kb], eng="pool")
        for d in range(2):
            for hh in range(2):
                hp_ = slice(hh * 64, (hh + 1) * 64)
                q = d * 2 + hh
                mb = bk[d]
                P.mm(mb[:, hh * 256:(hh + 1) * 256], BtT[hp_, d, :], AR[hp_, d, :], True, True, [BtT, AR], [mb])
                ab = bk[2] if d == 0 else bk[7]
                P.mm(ab[:, hh * 256:(hh + 1) * 256], KtT[hp_, d, :], AR[hp_, d, :], True, True, [KtT, AR], [ab])
                P.mm(bk[3][:, q * 128:(q + 1) * 128], AR[hp_, d, 0:128], BtT[hp_, d, :], True, True, [AR, BtT], [bk[3]])
        for d in range(2):
            mb, ab = bk[d], (bk[2] if d == 0 else bk[7])
            P.tt(MA[:, d], mb[:, :].rearrange("p (h c) -> p h c", h=2), mSI2[:, d], ALU.mult, [mb, mSI2], [MA])
            P.tt(AK[:, d], ab[:, :].rearrange("p (h c) -> p h c", h=2), mSI2[:, d], ALU.mult, [ab, mSI2], [AK])
        P.tt(L4[:], bk[3][:, :].rearrange("p (q c) -> p q c", q=4), mL4[:], ALU.mult, [bk[3], mL4], [L4])

    def inv_chain(step, fill):
        par = (step % 2) if RWKV_INTERLEAVE else 0
        rkv, eG, AR, Bb, Kb, MA, AK, L4 = rkv_[par], eG_[par], AR_[par], Bb_[par], Kb_[par], MA_[par], AK_[par], L4_[par]
        cc = [orders[0][step], orders[1][step]]
        npts = 16
        per = (len(fill) + npts - 1) // npts if fill else 0
        pos = [0]

        def filler():
            if not RWKV_INTERLEAVE:
                return
            for th in fill[pos[0]:pos[0] + per]:
                th()
            pos[0] += per

        M4 = MA[:, :, :, 0:128].rearrange("p d h c -> p (d h) c")
        P.tt(TtA[:], M4, I4[:], ALU.add, [MA, I4], [TtA], eng="pool")
        P.tt(TA[:], L4[:], I4[:], ALU.add, [L4, I4], [TA], eng="pool")
        Pc_buf, Qc_buf = MA, L4
        pc = lambda q: MA[:, q // 2, q % 2, 0:128]
        qc = lambda q: L4[:, q, :]
        bP, bQ, bT, bTT = bk[4], bk[5], bk[6], bk[4]
        for lvl in range(6):
            last = lvl == 5
            Pn, Qn = P4[lvl % 2], Q4[lvl % 2]
            for q in range(4):
                P.mm(bP[:, q * 128:(q + 1) * 128], qc(q), pc(q), True, True, [Pc_buf, Qc_buf], [bP])
            if not last:
                for q in range(4):
                    P.mm(bQ[:, q * 128:(q + 1) * 128], pc(q), qc(q), True, True, [Pc_buf, Qc_buf], [bQ])
            P.cp(Pn[:], bP[:, :].rearrange("p (q c) -> p q c", q=4), [bP], [Pn])
            if not last:
                P.cp(Qn[:], bQ[:, :].rearrange("p (q c) -> p q c", q=4), [bQ], [Qn], eng="act")
            filler()
            for q in range(4):
                P.mm(bT[:, q * 128:(q + 1) * 128], TA[:, q, :], Pn[:, q, :], True, True, [TA, Pn], [bT])
            if not last:
                for q in range(4):
                    P.mm(bTT[:, q * 128:(q + 1) * 128], Pn[:, q, :], TA[:, q, :], True, True, [TA, Pn], [bTT])
            P.tt(TtA[:], TtA[:], bT[:, :].rearrange("p (q c) -> p q c", q=4), ALU.add, [TtA, bT], [TtA])
            if not last:
                P.tt(TA[:], TA[:], bTT[:, :].rearrange("p (q c) -> p q c", q=4), ALU.add, [TA, bTT], [TA])
            filler()
            Pc_buf, Qc_buf = Pn, Qn
            pc = lambda q, Pn=Pn: Pn[:, q, :]
            qc = lambda q, Qn=Qn: Qn[:, q, :]
        bXU, bYS = bk[5], bk[6]
        for d in range(2):
            P.mm(bXU[:, d * 128:(d + 1) * 128], AR[:, d, 0:128], STB[d][:], True, False, [AR, STB[d]], [bXU])
            for hh in range(2):
                o = d * 128 + hh * 64
                P.mm(bXU[:, o:o + 64], AK[:, d, hh, 0:128], rkv[:, d, 256 + hh * 64:256 + (hh + 1) * 64], False, hh == 1,
                     [AK, rkv], [bXU])
        P.cp(X2[:], bXU[:, 0:256].rearrange("p (d c) -> p d c", d=2), [bXU], [X2])
        filler()
        for d in range(2):
            for hh in range(2):
                o = 256 + d * 128 + hh * 64
                P.mm(bXU[:, o:o + 64], TtA[:, d * 2 + hh, :], X2[:, d, hh * 64:(hh + 1) * 64], True, True, [TtA, X2], [bXU])
        P.cp(U2[:], bXU[:, 256:512].rearrange("p (d c) -> p d c", d=2), [bXU], [U2])
        filler()
        for d in range(2):
            P.mm(bYS[:, d * 128:(d + 1) * 128], AR[:, d, 128:256], STB[d][:], True, False, [AR, STB[d]], [bYS])
            for hh in range(2):
                o = d * 128 + hh * 64
                hs = slice(hh * 64, (hh + 1) * 64)
                P.mm(bYS[:, o:o + 64], MA[:, d, hh, 128:256], U2[:, d, hs], False, False, [MA, U2], [bYS])
                P.mm(bYS[:, o:o + 64], AK[:, d, hh, 128:256], rkv[:, d, 256 + hh * 64:256 + (hh + 1) * 64], False, hh == 1,
                     [AK, rkv], [bYS])
        for d in range(2):
            P.mm(bYS[:, 256 + d * 128:256 + (d + 1) * 128], Bb[:, d, :], U2[:, d, :], True, False, [Bb, U2], [bYS])
            P.mm(bYS[:, 256 + d * 128:256 + (d + 1) * 128], Kb[:, d, :], rkv[:, d, 256:384], False, True, [Kb, rkv], [bYS])
        for d in range(2):
            P.cp(YD[d][:, cc[d], :], bYS[:, d * 128:(d + 1) * 128], [bYS], [YD[d]], eng="act")
            dcol = 255 if d == 0 else 128
            for hh in range(2):
                hs = slice(hh * 64, (hh + 1) * 64)
                o = 256 + d * 128 + hh * 64
                P.stt(STB[d][hs, hs], STB[d][hs, hs], eG[hs, d, dcol:dcol + 1], bYS[hs, o:o + 64], ALU.mult, ALU.add,
                      [STB[d], eG, bYS], [STB[d]])
        filler()
        for th in fill[pos[0]:]:
            th()

    pre(0)
    for step in range(SEQT):
        fill = []
        if RWKV_INTERLEAVE and step + 1 < SEQT:
            P.defer = []
            pre(step + 1)
            fill, P.defer = P.defer, None
        inv_chain(step, fill)
        if not RWKV_INTERLEAVE and step + 1 < SEQT:
            pre(step + 1)

    rk2 = [TL([128, 384], f"rk2{i}") for i in range(2)]
    sgg = [TL([128, 128], f"sgg{i}") for i in range(2)]
    ybs = [TL([128, 128], f"ybs{i}") for i in range(2)]
    tq = [TL([128, 128], f"tq{i}") for i in range(2)]
    gn = [TL([128, 16], f"gn{i}") for i in range(2)]
    yn = [TL([128, 128], f"yn{i}") for i in range(2)]
    rk = [TL([128, 128], f"rk{i}") for i in range(2)]
    yo = [TL([128, 128], f"yo{i}") for i in range(2)]
    yob = [P.sb([128, 128], BF16, f"yob{i}") for i in range(2)]
    for c in range(SEQT):
        pi = c % 2
        lo, hi = (0, NCTX) if c < CTXT else (NCTX, SEQ_ALL)
        H = hb[pi]
        load_hblk_halo(P, hT, H, c * 128, 128, lo, hi)
        b0, b1 = bk[pi * 4], bk[pi * 4 + 1]
        n = 0
        for tap in range(3):
            for kc in range(8):
                P.mm(b0[:, 0:384], H[:, kc, tap:tap + 128], Wc[:, tap, kc, :], n == 0, n == 23, [H, Wc], [b0])
                n += 1
        for kc in range(8):
            P.mm(b1[:, 0:128], Wl[:, kc, 256:384], H[:, kc, 1:129], kc == 0, kc == 7, [Wl, H], [b1])
        R_ = rk2[pi]
        P.cp(R_[:], b0[:, 0:384], [b0], [R_], eng="act")
        P.act(sgg[pi][:], b1[:, 0:128], AF.Sigmoid, [b1], [sgg[pi]])
        P.mm(b1[:, 128:256], sgg[pi][:], G2[:], True, True, [sgg[pi], G2], [b1])
        yb, t2_, g_, y_ = ybs[pi], tq[pi], gn[pi], yn[pi]
        P.tt(yb[:], YD[0][:, c, :], YD[1][:, c, :], ALU.add, [YD[0], YD[1]], [yb])
        P.tt(t2_[:], yb[:], yb[:], ALU.mult, [yb], [t2_], eng="pool")
        P.op("dve", lambda e, g_=g_, yb=yb: e.tensor_reduce(g_[:, 0:2], yb[:, :].rearrange("p (h k) -> p h k", h=2), AX.X, ALU.add),
             [yb], [g_])
        P.op("dve", lambda e, g_=g_, t2_=t2_: e.tensor_reduce(g_[:, 2:4], t2_[:, :].rearrange("p (h k) -> p h k", h=2), AX.X, ALU.add),
             [t2_], [g_])
        P.ts(g_[:, 4:8], g_[:, 0:4], 1.0 / 64, None, ALU.mult, None, [g_], [g_])
        P.tt(g_[:, 8:10], g_[:, 4:6], g_[:, 4:6], ALU.mult, [g_], [g_])
        P.tt(g_[:, 10:12], g_[:, 6:8], g_[:, 8:10], ALU.subtract, [g_], [g_])
        P.act(g_[:, 12:14], g_[:, 10:12], AF.Sqrt, [g_, epsg], [g_], bias=epsg[:, 0:1])
        P.op("dve", lambda e, g_=g_: e.reciprocal(g_[:, 14:16], g_[:, 12:14]), [g_], [g_])
        for hh in range(2):
            hs = slice(hh * 64, (hh + 1) * 64)
            P.ts(y_[:, hs], yb[:, hs], g_[:, 4 + hh:5 + hh], g_[:, 14 + hh:15 + hh], ALU.subtract, ALU.mult, [yb, g_], [y_])
        P.tt(y_[:], y_[:], VEC[:, 3, :], ALU.mult, [y_, VEC], [y_])
        P.tt(y_[:], y_[:], VEC[:, 4, :], ALU.add, [y_, VEC], [y_])
        rk_ = rk[pi]
        P.tt(rk_[:], R_[:, 0:128], R_[:, 128:256], ALU.mult, [R_], [rk_], eng="pool")
        P.tt(rk_[:], rk_[:], VEC[:, 2, :], ALU.mult, [rk_, VEC], [rk_], eng="pool")
        P.op("dve", lambda e, g_=g_, rk_=rk_: e.tensor_reduce(g_[:, 0:2], rk_[:, :].rearrange("p (h k) -> p h k", h=2), AX.X, ALU.add),
             [rk_], [g_])
        for hh in range(2):
            hs = slice(hh * 64, (hh + 1) * 64)
            P.stt(y_[:, hs], R_[:, 256 + hh * 64:256 + (hh + 1) * 64], g_[:, hh:hh + 1], y_[:, hs], ALU.mult, ALU.add,
                  [R_, g_, y_], [y_])
        Y = yo[pi]
        P.tt(Y[:], y_[:], b1[:, 128:256], ALU.mult, [y_, b1], [Y])
        P.tr(b1[:, 256:384], Y[:], ident[:], [Y, ident], [b1])
        Yb = yob[pi]
        P.cp(Yb[:], b1[:, 256:384], [b1], [Yb], eng="act")
        P.dma("sp", ycols(yr, c * 128), Yb[:], reads=[Yb])
    return P.finish() if own else None


def rwkv_inputs(hTs, li, w_in, r_conv, r_w0, r_w2, r_a0, r_a2, r_g2, r_kk, r_ka, r_rk, r_ln_w, r_ln_b):
    off = col_offsets()
    ins = []
    W = w_in[li]
    for b in range(2):
        hT = np.ascontiguousarray(hTs[b].reshape(8, 128, SEQ_ALL))
        for hp in range(4):
            cs_ = slice(128 * hp, 128 * (hp + 1))
            wrkv = np.concatenate([W[:, off[n] + 128 * hp: off[n] + 128 * (hp + 1)] for n in ('r_r', 'r_k', 'r_v')], 1)
            conv = np.concatenate([r_conv[li][:, j * 512 + 128 * hp: j * 512 + 128 * (hp + 1)] for j in range(3)], 1)
            wl = np.concatenate([W[:, off['r_wf']:off['r_wf'] + 64], W[:, off['r_af']:off['r_af'] + 64],
                                 W[:, off['r_wb']:off['r_wb'] + 64], W[:, off['r_ab']:off['r_ab'] + 64],
                                 W[:, off['r_g']:off['r_g'] + 128]], 1)
            w2a2 = np.stack([np.concatenate([r_w2[li, d][:, cs_], r_a2[li, d][:, cs_]], 0) for d in range(2)], 0)
            bias01 = np.stack([np.concatenate([r_w0[li, d][cs_], r_a0[li, d][cs_]], 0) for d in range(2)], 0)[None]
            vecs = np.stack([np.broadcast_to(v[li][None, cs_], (128, 128)) for v in (r_kk, r_ka, r_rk, r_ln_w, r_ln_b)], 1)
            ins.append({"hT": hT, "wrkv": np.ascontiguousarray(wrkv),
                        "crkv": np.ascontiguousarray(np.broadcast_to(conv[None], (128, 3, 384)).astype(np.float32)),
                        "wl": np.ascontiguousarray(wl), "w2a2": np.ascontiguousarray(w2a2.astype(np.float32)),
                        "bias01": np.ascontiguousarray(bias01.astype(np.float32)),
                        "g2": np.ascontiguousarray(r_g2[li][:, cs_]), "vecs": np.ascontiguousarray(vecs.astype(np.float32))})
    return ins


def build_merge(P=None, T=None, fused=False):
    own = P is None
    P = P or Prog()
    x = P.io(T, "x", [NT, 128, D], F32, "ExternalInput")
    hT = P.io(T, "hT", [8, 128, TOK], BF16, "ExternalInput")
    if fused:
        yall = T["yall"]
        sel = T["sel"]
    else:
        yT = P.io(T, "yT", [12, 128, TOK], BF16, "ExternalInput")
    wg = P.io(T, "wg", [D, 3 * D], F32, "ExternalInput")
    wb = P.io(T, "wb", [3, 512, D], F32, "ExternalInput")
    wo = P.io(T, "wo", [D, D], F32, "ExternalInput")
    gateb = P.io(T, "gateb", [2, 128, D], F32, "ExternalInput")
    xo = P.io(T, "xo", [NT, 128, D], F32, "ExternalOutput")

    Wg = P.sb([128, 8, 3 * D], BF16, "Wg")
    Pb = P.sb([128, 12, D], BF16, "Pb")
    Wo = P.sb([128, 8, D], BF16, "Wo")
    stage = [P.sb([128, 1024], F32, f"stg{i}") for i in range(2)]
    n = 0
    for kc in range(8):
        for q in range(3):
            load_cast(P, Wg, Wg[:, kc, q * 1024:(q + 1) * 1024], wg[kc * 128:(kc + 1) * 128, q * 1024:(q + 1) * 1024],
                      stage, None, n, 1024)
            n += 1
    for br in range(3):
        for c in range(4):
            load_cast(P, Pb, Pb[:, br * 4 + c, :], wb[br, c * 128:(c + 1) * 128, :], stage, None, n, 1024)
            n += 1
    for kc in range(8):
        load_cast(P, Wo, Wo[:, kc, :], wo[kc * 128:(kc + 1) * 128, :], stage, None, n, 1024)
        n += 1
    gates = P.sb([128, 2, D], F32, "gates")
    for s in range(2):
        P.dma("act", gates[:, s, :], gateb[s], writes=[gates])

    X = P.sb([128, 3, D], F32, "X")
    hb = P.sb([128, 8, 384], BF16, "hb")
    yb = P.sb([128, 12, 384], BF16, "yb")
    if fused:
        yc = [P.sb([128, 12, 384], BF16, f"yc{i}") for i in range(4)]
        sels = P.sb([128, 4], F32, "sels")
        P.dma("act", sels[:], sel[:, :], writes=[sels])

    zT = P.sb([128, 8, 384], BF16, "zT")
    sg = [P.sb([128, 384], F32, f"sg{i}") for i in range(2)]
    za = P.sb([128, 384], F32, "za")
    tm = [P.sb([128, 384], F32, f"tm{i}") for i in range(2)]
    tmp = [P.sb([128, 512], F32, f"tmp{i}") for i in range(2)]
    pg = [P.ps([128, 512], F32, f"pg{i}") for i in range(2)]
    pp = [P.ps([128, 512], F32, f"pp{i}") for i in range(2)]
    py = [P.ps([128, 512], F32, f"py{i}") for i in range(2)]

    k = 0
    for bi, (t0, nb) in enumerate(BLOCKS):
        s = 0 if bi == 0 else 1
        N = nb * 128
        for i in range(nb):
            P.dma("sp", X[:, i, :], x[t0 + i], writes=[X])
        for kc in range(8):
            P.dma("act", hb[:, kc, 0:N], hT[kc, :, t0 * 128:t0 * 128 + N], writes=[hb])
        if fused:
            for cand in range(4):
                for c in range(12):
                    j, br = c % 4, c // 4
                    P.dma("sp" if c % 2 else "act", yc[cand][:, c, 0:N],
                          yall[br, cand, j * 128:(j + 1) * 128, t0 * 128:t0 * 128 + N], writes=[yc[cand]])
            P.ts(yb[:, :, 0:N], yc[0][:, :, 0:N], sels[:, 0:1], None, ALU.mult, None, [yc[0], sels], [yb])
            for cand in range(1, 4):
                P.stt(yb[:, :, 0:N], yc[cand][:, :, 0:N], sels[:, cand:cand + 1], yb[:, :, 0:N], ALU.mult, ALU.add,
                      [yc[cand], sels, yb], [yb])
        else:
            for c in range(12):
                P.dma("sp" if c % 2 else "act", yb[:, c, 0:N], yT[c, :, t0 * 128:t0 * 128 + N], writes=[yb])
        for dc in range(8):
            for br in range(3):
                G, Q = pg[k % 2], pp[k % 2]
                S, T_ = sg[k % 2], tm[k % 2]
                k += 1
                for kc in range(8):
                    P.mm(G[:, 0:N], Wg[:, kc, br * D + dc * 128: br * D + (dc + 1) * 128], hb[:, kc, 0:N], kc == 0, kc == 7,
                         [Wg, hb], [G])
                for c in range(4):
                    P.mm(Q[:, 0:N], Pb[:, br * 4 + c, dc * 128:(dc + 1) * 128], yb[:, br * 4 + c, 0:N], c == 0, c == 3,
                         [Pb, yb], [Q])
                P.act(S[:, 0:N], G[:, 0:N], AF.Sigmoid, [G], [S])
                if br == 0:
                    P.tt(za[:, 0:N], S[:, 0:N], Q[:, 0:N], ALU.mult, [S, Q], [za])
                else:
                    P.tt(T_[:, 0:N], S[:, 0:N], Q[:, 0:N], ALU.mult, [S, Q], [T_])
                    if br == 1:
                        P.tt(za[:, 0:N], za[:, 0:N], T_[:, 0:N], ALU.add, [za, T_], [za], eng="pool")
                    else:
                        P.tt(zT[:, dc, 0:N], za[:, 0:N], T_[:, 0:N], ALU.add, [za, T_], [zT])
        for i in range(nb):
            for h in range(2):
                Y = py[(i * 2 + h) % 2]
                for dc in range(8):
                    P.mm(Y[:, :], zT[:, dc, i * 128:(i + 1) * 128], Wo[:, dc, h * 512:(h + 1) * 512], dc == 0, dc == 7,
                         [zT, Wo], [Y])
                T2 = tmp[(i * 2 + h) % 2]
                P.tt(T2[:], Y[:], gates[:, s, h * 512:(h + 1) * 512], ALU.mult, [Y, gates], [T2])
                P.tt(X[:, i, h * 512:(h + 1) * 512], X[:, i, h * 512:(h + 1) * 512], T2[:], ALU.add, [X, T2], [X], eng="pool")
            P.dma("sp", xo[t0 + i], X[:, i, :], reads=[X])
    return P.finish() if own else None


def featT_shard(y_b, nchunk):
    pad = np.zeros((4 * TOK, y_b.shape[1]), y_b.dtype)
    pad[:y_b.shape[0]] = y_b
    out = []
    for i in range(4):
        blk = pad[i * TOK:(i + 1) * TOK]
        out.append(np.ascontiguousarray(blk.T.reshape(nchunk, 128, TOK)))
    return out


def merge_inputs(xs, hTs, ymT, yrT, yaT, mods_l, li, w_in, w_branch, w_o):
    off = col_offsets()
    m = mods_l.reshape(3, 9, D)
    wg = np.ascontiguousarray(w_in[li][:, off['g_m']:off['g_m'] + 3 * D])
    ins = []
    for b in range(2):
        xsh = tok_shard(xs[b])
        hsh = featT_shard(np.ascontiguousarray(hTs[b].T), 8)
        yfull = np.concatenate([ymT[b], yrT[b], yaT[b]], axis=0)
        ypad = np.zeros((1536, 4 * TOK), yfull.dtype)
        ypad[:, :SEQ_ALL] = yfull
        for i in range(4):
            gb = np.stack([np.broadcast_to(m[2 if i == 0 else b, 5], (128, D)), np.broadcast_to(m[b, 5], (128, D))], 0)
            ins.append({"x": xsh[i], "hT": hsh[i],
                        "yT": np.ascontiguousarray(ypad[:, i * TOK:(i + 1) * TOK].reshape(12, 128, TOK)),
                        "wg": wg, "wb": w_branch[li], "wo": w_o[li], "gateb": np.ascontiguousarray(gb)})
    return ins


def emit_mod_fused(P, T):
    cT = T["cT"]
    ones = P.sb([128, 128], F32, "ones")
    P.memset(ones[:], 1.0, [ones])
    cs = P.sb([128, 8, 2], F32, "cs")
    sg = P.sb([128, 8, 2], F32, "sg")
    P.dma("sp", cs[:], cT[:, :, :], writes=[cs])
    P.act(sg[:], cs[:], AF.Sigmoid, [cs], [sg])
    P.tt(cs[:], cs[:], sg[:], ALU.mult, [cs, sg], [cs])
    crep = [P.sb([128, 8, 128], F32, f"crep{i}") for i in range(2)]
    for st in range(2):
        for kc in range(8):
            P.ts(crep[st][:, kc, :], ones[:], cs[:, kc, st:st + 1], None, ALU.mult, None, [ones, cs], [crep[st]])
    Wk = [P.sb([128, 8, 1024], F32, f"Wk{i}") for i in range(2)]
    pm = [P.ps([128, 512], F32, f"pm{i}") for i in range(2)]
    pg = [P.ps([128, 512], F32, f"pgm{i}") for i in range(2)]
    n = 0
    for l in range(2):
        bpps = P.sb([128, 72], F32, f"bpps{l}")
        P.dma("act", bpps[:], T[f"bpp{l}"][:, :], writes=[bpps])
        bgbs = P.sb([128, 3, 1024], F32, f"bgbs{l}")
        for gi in range(3):
            P.dma("act", bgbs[:, gi, :], T[f"bgb{l}"][gi], writes=[bgbs])
        ngs = P.sb([128, 24], F32, f"ngs{l}")
        P.dma("act", ngs[:], T[f"ng{l}"][:, :], writes=[ngs])
        MODT = P.sb([128, 2, 72], F32, f"MODT{l}")
        for k in range(9):
            W = Wk[n % 2]
            for kc in range(8):
                P.dma("sp", W[:, kc, :], T[f"wada{l}"][kc * 128:(kc + 1) * 128, k * 1024:(k + 1) * 1024], writes=[W])
            ps = pm[n % 2]
            for c in range(8):
                for kc in range(8):
                    P.mm(ps[:, 2 * c:2 * c + 2], W[:, kc, c * 128:(c + 1) * 128], cs[:, kc, :], kc == 0, kc == 7, [W, cs], [ps])
            for st in range(2):
                P.tt(MODT[:, st, k * 8:(k + 1) * 8], ps[:, st:16:2], bpps[:, k * 8:(k + 1) * 8], ALU.add, [ps, bpps], [MODT])
            if k in (2, 5, 8):
                gi = (2, 5, 8).index(k)
                GB = P.sb([128, 2, 1024], F32, f"GB{l}{gi}")
                for st in range(2):
                    for h in range(2):
                        pq = pg[(st * 2 + h) % 2]
                        for kc in range(8):
                            P.mm(pq[:, :], crep[st][:, kc, :], W[:, kc, h * 512:(h + 1) * 512], kc == 0, kc == 7, [crep[st], W], [pq])
                        P.tt(GB[:, st, h * 512:(h + 1) * 512], pq[:, :], bgbs[:, gi, h * 512:(h + 1) * 512], ALU.add, [pq, bgbs], [GB])
                    P.dma("act", T[f"gb{l}_{gi}"][st], GB[:, st, :], reads=[GB])
            n += 1
        for which, ks in ((0, (0, 1, None, 3, 4, None)), (1, (6, 7, None, None, None, None))):
            PP = P.sb([128, 96], F32, f"PP{l}{which}")
            P.memset(PP[:], 0.0, [PP])
            for st in range(2):
                for j, k in enumerate(ks):
                    dst = PP[:, st * 48 + j * 8: st * 48 + (j + 1) * 8]
                    if k is not None:
                        P.cp(dst, MODT[:, st, k * 8:(k + 1) * 8], [MODT], [PP])
                    elif j == 2:
                        gidx = 0 if which == 0 else 2
                        P.cp(dst, ngs[:, gidx * 8:(gidx + 1) * 8], [ngs], [PP])
                    elif j == 5 and which == 0:
                        P.cp(dst, ngs[:, 8:16], [ngs], [PP])
            P.dma("sp", T[f"pp{l}_{which}"][:, :], PP[:], reads=[PP])


def build_fused(upto=99, dbg=None):
    P = Prog()
    E = lambda name, shape, dt=F32: P.dram(name, shape, dt, "ExternalInput")
    x0 = E("x0", [NT, 128, D])
    out = P.dram("out", [NT, 128, D], F32, "ExternalOutput")
    sel = E("sel", [128, 4])
    Tm = {"cT": E("cT", [128, 8, 2])}
    pp, gb = {}, {}
    for l in range(2):
        Tm[f"wada{l}"] = E(f"wada{l}", [D, 9 * D])
        Tm[f"bpp{l}"] = E(f"bpp{l}", [128, 72])
        Tm[f"bgb{l}"] = E(f"bgb{l}", [3, 128, D])
        Tm[f"ng{l}"] = E(f"ng{l}", [128, 24])
        for w in range(2):
            pp[l, w] = Tm[f"pp{l}_{w}"] = P.idram([128, 96], F32, f"pp{l}_{w}")
        for gi in range(3):
            gb[l, gi] = Tm[f"gb{l}_{gi}"] = P.idram([2, 128, D], F32, f"gb{l}_{gi}")
    a_cs = E("a_cs", [2, 128, 8192])
    a_cmat = E("a_cmat", [2, 128, 128])
    emit_mod_fused(P, Tm)
    P.end_stage()
    groups = [[0, 1, 2, 3], [4, 5, 6, 7]]
    dummy = Buf(None, "coll")

    def done(src3):
        t = P.sb([128, D], F32, "dbgt")
        for i in range(NT):
            P.dma("sp", t[:], src3[i], writes=[t])
            P.dma("sp", out[i], t[:], reads=[t])
        return P.finish()

    xcur = x0
    for l in range(2):
        x1 = P.idram([NT, 128, D], F32, f"x1_{l}")
        hsrc = P.idram([8, 128, TOK], BF16, f"hsrc{l}")
        h3 = hsrc
        build_ffn(True, P, {"x": xcur, "wgu": E(f"w1gu{l}", [D, 2 * DFF]), "wd": E(f"w1d{l}", [DFF, D]),
                            "pp": pp[l, 0], "gateb": gb[l, 0], "xo": x1, "ho": h3})
        P.end_stage()
        if upto == 10 * l + 1:
            return done(x1)
        hall = P.idram([8, 4 * 128, TOK], BF16, f"hall{l}")
        hall.gath = True
        for kc in range(8):
            P.coll("AllGather", hsrc[kc], hall[kc], groups, writes=[dummy])
        P.end_stage()
        if upto == 10 * l + 5:
            return done(x1)
        ysrc = P.idram([3, 4, 128, TOK], BF16, f"ysrc{l}")
        zt = P.sb([128, 4 * TOK - SEQ_ALL], BF16, "zt")
        P.memset(zt[:], 0.0, [zt])
        for br in range(3):
            P.dma("act", ysrc[br, 3, :, SEQ_ALL - 3 * TOK:TOK], zt[:], reads=[zt])
        yb_ = []
        for br in range(3):
            yb_.append(Buf(ysrc[br], f"yout{br}"))
            yb_[-1].gath = True
        build_attn(False, P, {"hT": hall, "wqkv": E(f"a_wqkv{l}", [3, D, 128]), "gqk": E(f"a_gqk{l}", [128, 2]),
                              "cs": a_cs, "cmat": a_cmat, "lamp": E(f"a_lamp{l}", [128, 258]),
                              "subg": E(f"a_subg{l}", [128, 128]), "ya": yb_[2]})
        P.end_stage()
        build_mlstm(99, P, {"hT": hall, "wqk": E(f"m_wqk{l}", [2, D, 64]), "wvo": E(f"m_wvo{l}", [D, 256]),
                            "wg": E(f"m_wg{l}", [D, 4]), "cw": E(f"m_cw{l}", [2, 128, 3, 64]), "gb": E(f"m_gb{l}", [128, 4]),
                            "og": E(f"m_og{l}", [128, 128]), "ym": yb_[0]})
        P.end_stage()
        build_rwkv(P, {"hT": hall, "wrkv": E(f"r_wrkv{l}", [D, 384]), "crkv": E(f"r_crkv{l}", [128, 3, 384]),
                       "wl": E(f"r_wl{l}", [D, 384]), "w2a2": E(f"r_w2a2{l}", [2, 128, 128]),
                       "bias01": E(f"r_bias01{l}", [1, 2, 256]), "g2": E(f"r_g2{l}", [128, 128]),
                       "vecs": E(f"r_vecs{l}", [128, 5, 128]), "yr": yb_[1]})
        P.end_stage()
        yall = P.idram([3, 4, 4 * 128, TOK], BF16, f"yall{l}")
        for br in range(3):
            for q in range(4):
                P.coll("AllGather", ysrc[br, q], yall[br, q], groups, writes=[dummy])
        P.end_stage()
        x2 = P.idram([NT, 128, D], F32, f"x2_{l}")
        build_merge(P, {"x": x1, "hT": h3, "yall": yall, "sel": sel, "wg": E(f"g_wg{l}", [D, 3 * D]),
                        "wb": E(f"g_wb{l}", [3, 512, D]), "wo": E(f"g_wo{l}", [D, D]), "gateb": gb[l, 1], "xo": x2}, fused=True)
        P.end_stage()
        if upto == 10 * l + 2:
            return done(x2)
        x3 = out if l == 1 else P.idram([NT, 128, D], F32, f"x3_{l}")
        build_ffn(False, P, {"x": x2, "wgu": E(f"w2gu{l}", [D, 2 * DFF]), "wd": E(f"w2d{l}", [DFF, D]),
                             "pp": pp[l, 1], "gateb": gb[l, 2], "xo": x3})
        P.end_stage()
        if upto == 10 * l + 3 and l == 0:
            return done(x3)
        xcur = x3
    return P.finish()


def fused_inputs(x, c, ctx, c_ctx, w_ada, b_ada, norm_g, ffn1_w_gu, ffn1_w_down, ffn2_w_gu, ffn2_w_down,
                 w_in, m_conv, m_gate_bias, m_out_norm, r_conv, r_w0, r_w2, r_a0, r_a2, r_g2, r_kk, r_ka,
                 r_rk, r_ln_w, r_ln_b, a_qk_norm, a_lambda, a_subln, w_branch, w_o):
    C = np.ascontiguousarray
    xs = [np.concatenate([ctx[b], x[b]], 0) for b in range(2)]
    dummy_h = [np.zeros((D, SEQ_ALL), NPBF) for _ in range(2)]
    off = col_offsets()
    per = [dict() for _ in range(8)]
    shards = [tok_shard(xs[b]) for b in range(2)]
    cs_tab, cm = rope_tables(), attn_consts()
    for core in range(8):
        b, i = core // 4, core % 4
        d = per[core]
        d["x0"] = shards[b][i]
        sel = np.zeros((128, 4), np.float32)
        sel[:, i] = 1.0
        d["sel"] = sel
        cA = c_ctx if i == 0 else c[b]
        d["cT"] = C(np.stack([cA, c[b]], 0).reshape(2, 8, 128).transpose(2, 1, 0).astype(np.float32))
        d["a_cs"], d["a_cmat"] = cs_tab, cm
    for l in range(2):
        bpp = C(b_ada[l].reshape(9, 8, 128).transpose(2, 0, 1).reshape(128, 72))
        bgb = C(np.stack([np.broadcast_to(b_ada[l].reshape(9, D)[k][None], (128, D)) for k in (2, 5, 8)], 0))
        ng = C(norm_g[l].reshape(3, 8, 128).transpose(2, 0, 1).reshape(128, 24))
        ai = attn_inputs(dummy_h, l, w_in, a_qk_norm, a_lambda, a_subln)
        mi = mlstm_inputs(dummy_h, l, w_in, m_conv, m_gate_bias, m_out_norm)
        ri = rwkv_inputs(dummy_h, l, w_in, r_conv, r_w0, r_w2, r_a0, r_a2, r_g2, r_kk, r_ka, r_rk, r_ln_w, r_ln_b)
        wg = C(w_in[l][:, off['g_m']:off['g_m'] + 3 * D])
        for core in range(8):
            d = per[core]
            d[f"wada{l}"] = C(w_ada[l]); d[f"bpp{l}"] = bpp; d[f"bgb{l}"] = bgb; d[f"ng{l}"] = ng
            d[f"w1gu{l}"] = C(ffn1_w_gu[l]); d[f"w1d{l}"] = C(ffn1_w_down[l])
            d[f"w2gu{l}"] = C(ffn2_w_gu[l]); d[f"w2d{l}"] = C(ffn2_w_down[l])
            for k in ("wqkv", "gqk", "lamp", "subg"):
                d[f"a_{k}{l}"] = ai[core][k]
            for k in ("wqk", "wvo", "wg", "cw", "gb", "og"):
                d[f"m_{k}{l}"] = mi[core][k]
            for k in ("wrkv", "crkv", "wl", "w2a2", "bias01", "g2", "vecs"):
                d[f"r_{k}{l}"] = ri[core][k]
            d[f"g_wg{l}"] = wg; d[f"g_wb{l}"] = C(w_branch[l]); d[f"g_wo{l}"] = C(w_o[l])
    return per


def kernel(**inputs):
    f = {k: np.asarray(v, dtype=np.float32) for k, v in inputs.items()}
    nc = _prog("fused", build_fused)
    res = _run(nc, fused_inputs(**f))
    xs = tok_unshard(res, "out")
    return np.stack([xs[b][NCTX:] for b in range(2)], 0).astype(np.float32)


_PROGS = {}


def _prog(name, fn):
    if name not in _PROGS:
        _PROGS[name] = fn()
    return _PROGS[name]


def _run(nc, ins):
    return run_bass_kernel_spmd(nc, ins, core_ids=list(range(8))).results


def kernel_unfused(x, c, ctx, c_ctx, w_ada, b_ada, norm_g, ffn1_w_gu, ffn1_w_down, ffn2_w_gu, ffn2_w_down,
           w_in, m_conv, m_gate_bias, m_out_norm, r_conv, r_w0, r_w2, r_a0, r_a2, r_g2, r_kk, r_ka,
           r_rk, r_ln_w, r_ln_b, a_qk_norm, a_lambda, a_subln, w_branch, w_o):
    f = lambda a: np.asarray(a, dtype=np.float32)
    (x, c, ctx, c_ctx, w_ada, b_ada, norm_g, ffn1_w_gu, ffn1_w_down, ffn2_w_gu, ffn2_w_down, w_in, m_conv, m_gate_bias,
     m_out_norm, r_conv, r_w0, r_w2, r_a0, r_a2, r_g2, r_kk, r_ka, r_rk, r_ln_w, r_ln_b, a_qk_norm, a_lambda, a_subln,
     w_branch, w_o) = map(f, (x, c, ctx, c_ctx, w_ada, b_ada, norm_g, ffn1_w_gu, ffn1_w_down, ffn2_w_gu, ffn2_w_down, w_in,
                              m_conv, m_gate_bias, m_out_norm, r_conv, r_w0, r_w2, r_a0, r_a2, r_g2, r_kk, r_ka, r_rk,
                              r_ln_w, r_ln_b, a_qk_norm, a_lambda, a_subln, w_branch, w_o))
    mods = run_mod(c, c_ctx, w_ada, b_ada)
    xs = [np.concatenate([ctx[b], x[b]], 0) for b in range(2)]
    for li in range(2):
        res = _run(_prog("ffn_h", lambda: build_ffn(True)),
                   ffn_inputs(xs, mods[li], li, 1, norm_g, np.ascontiguousarray(ffn1_w_gu[li]), np.ascontiguousarray(ffn1_w_down[li]), True))
        xs = tok_unshard(res, "xo")
        hTs = hT_unshard(res, "ho")
        ra = _run(_prog("attn", build_attn), attn_inputs(hTs, li, w_in, a_qk_norm, a_lambda, a_subln))
        rm = _run(_prog("mlstm", build_mlstm), mlstm_inputs(hTs, li, w_in, m_conv, m_gate_bias, m_out_norm))
        rr = _run(_prog("rwkv", build_rwkv), rwkv_inputs(hTs, li, w_in, r_conv, r_w0, r_w2, r_a0, r_a2, r_g2, r_kk, r_ka,
                                                          r_rk, r_ln_w, r_ln_b))
        yas = [np.concatenate([ra[b * 4 + h]["ya"] for h in range(4)], axis=0) for b in range(2)]
        yms = [np.concatenate([rm[b * 4 + h]["ym"] for h in range(4)], axis=0) for b in range(2)]
        yrs = [np.concatenate([rr[b * 4 + h]["yr"] for h in range(4)], axis=0) for b in range(2)]
        res = _run(_prog("merge", build_merge), merge_inputs(xs, hTs, yms, yrs, yas, mods[li], li, w_in, w_branch, w_o))
        xs = tok_unshard(res, "xo")
        res = _run(_prog("ffn", lambda: build_ffn(False)),
                   ffn_inputs(xs, mods[li], li, 2, norm_g, np.ascontiguousarray(ffn2_w_gu[li]), np.ascontiguousarray(ffn2_w_down[li]), False))
        xs = tok_unshard(res, "xo")
    return np.stack([xs[b][NCTX:] for b in range(2)], 0).astype(np.float32)
```

```python
import contextlib
import numpy as np
import ml_dtypes
import concourse.bass as bass
import concourse.mybir as mybir
from concourse.bass_utils import run_bass_kernel_spmd

F32 = mybir.dt.float32
BF16 = mybir.dt.bfloat16
AF = mybir.ActivationFunctionType
ALU = mybir.AluOpType
AX = mybir.AxisListType
NPBF = ml_dtypes.bfloat16

ENGS = ("pe", "dve", "act", "pool", "sp")


class Buf:
    __slots__ = ("t", "w", "r", "name", "ds", "psum", "gath")

    def __init__(self, t=None, name=""):
        self.t = t
        self.ds = None
        self.psum = False
        self.gath = False
        self.w = None
        self.r = []
        self.name = name

    def __getitem__(self, idx):
        return self.t[idx]


class Sem:
    __slots__ = ("h", "count")

    def __init__(self, h):
        self.h = h
        self.count = 0


class Prog:
    def __init__(self, name="k"):
        self.nc = bass.Bass("TRN2", target_bir_lowering=False)
        self.es = contextlib.ExitStack()
        self.ss = contextlib.ExitStack()
        self.q = {e: [] for e in ENGS}
        self.esem = {}
        for e in ENGS:
            self.esem[e] = Sem(self.es.enter_context(self.nc.semaphore(f"s_{e}")))
        self.seen = {e: {} for e in ENGS}
        self.dsems = []
        self.free_ds = []
        self.stage_ds = []
        self.nbuf = 0

    def dram(self, name, shape, dt, kind):
        return Buf(self.nc.dram_tensor(name, list(shape), dt, kind=kind).ap(), name)

    def io(self, T, name, shape, dt, kind):
        if T is not None:
            return T[name]
        return self.dram(name, shape, dt, kind)

    def sb(self, shape, dt=F32, name=None):
        self.nbuf += 1
        name = f"{name or 'sb'}_{self.nbuf}"
        t = self.ss.enter_context(self.nc.sbuf_tensor(name, list(shape), dt))
        return Buf(t, name)

    def ps(self, shape, dt=F32, name=None):
        self.nbuf += 1
        name = f"{name or 'ps'}_{self.nbuf}"
        t = self.ss.enter_context(self.nc.psum_tensor(name, list(shape), dt))
        b = Buf(t, name)
        b.psum = True
        return b

    def dsem(self):
        if self.free_ds:
            s = self.free_ds.pop()
        else:
            s = Sem(self.es.enter_context(self.nc.semaphore(f"d{len(self.dsems)}")))
            self.dsems.append(s)
        self.stage_ds.append(s)
        return s

    def _deps(self, eng, reads, writes):
        deps = []
        for b in reads:
            if b.w is not None:
                deps.append(b.w)
            if b.psum:
                deps.extend(ev for ev in b.r if ev[2] != eng)
        for b in writes:
            if b.w is not None:
                deps.append(b.w)
            deps.extend(b.r)
        best = {}
        for (s, v, src) in deps:
            if src == "pe" and eng == "pe":
                continue
            if v > best.get(s, (0, None))[0]:
                best[s] = (v, src)
        for s, (v, src) in best.items():
            if self.seen[eng].get(s, 0) >= v:
                continue
            self.seen[eng][s] = v
            self.q[eng].append(("w", s, v))

    defer = None

    def op(self, eng, fn, reads=(), writes=()):
        if self.defer is not None:
            self.defer.append(lambda: self._op(eng, fn, reads, writes))
            return None
        return self._op(eng, fn, reads, writes)

    def _op(self, eng, fn, reads=(), writes=()):
        self._deps(eng, reads, writes)
        s = self.esem[eng]
        s.count += 1
        ev = (s, s.count, eng)
        self.q[eng].append(("i", fn, s, 1))
        for b in reads:
            b.r.append(ev)
        for b in writes:
            b.w = ev
            b.r = []
        return ev

    def dma(self, eng, out, in_, sem=None, reads=(), writes=(), **kw):
        if self.defer is not None:
            self.defer.append(lambda: self._dma(eng, out, in_, reads, writes, kw))
            return None
        return self._dma(eng, out, in_, reads, writes, kw)

    def _dma(self, eng, out, in_, reads, writes, kw):
        b0 = (list(writes) + list(reads))[0]
        if b0.ds is None:
            b0.ds = self.dsem()
        sem = b0.ds
        self._deps(eng, reads, writes)
        sem.count += 16
        ev = (sem, sem.count, "dma")
        self.q[eng].append(("i", lambda e: e.dma_start(out=out, in_=in_, **kw), sem, 16))
        for b in reads:
            b.r.append(ev)
        for b in writes:
            b.w = ev
            b.r = []
        return ev

    def coll(self, kind, in_ap, out_ap, groups, reads=(), writes=()):
        b0 = list(writes)[0]
        if b0.ds is None:
            b0.ds = self.dsem()
        sem = b0.ds
        self._deps("pool", reads, writes)
        sem.count += 1
        ev = (sem, sem.count, "dma")
        self.q["pool"].append(("i", lambda e: e.collective_compute(kind, ALU.bypass, replica_groups=groups,
                                                                   ins=[in_ap.opt()], outs=[out_ap.opt()]), sem, 1))
        for b in reads:
            b.r.append(ev)
        for b in writes:
            b.w = ev
            b.r = []
        return ev

    def raw(self, eng, fn, reads=()):
        self._deps(eng, reads, ())
        self.q[eng].append(("r", fn))

    def idram(self, shape, dt, name=None, shared=False):
        self.nbuf += 1
        t = self.nc.dram_tensor(name or f"idram{self.nbuf}", list(shape), dt, addr_space="Shared" if shared else "Local")
        return Buf(t.ap(), name or f"idram{self.nbuf}")

    def _emit_block(self):
        q = self.q

        def run(eng_obj, items):
            for it in items:
                if it[0] == "w":
                    eng_obj.wait_ge(it[1].h, it[2])
                elif it[0] == "r":
                    it[1](eng_obj)
                else:
                    it[1](eng_obj).then_inc(it[2].h, it[3])

        with self.nc.Block() as block:
            @block.tensor
            def _(e):
                run(e, q["pe"])

            @block.vector
            def _(e):
                run(e, q["dve"])

            @block.scalar
            def _(e):
                run(e, q["act"])

            @block.gpsimd
            def _(e):
                run(e, q["pool"])

            @block.sync
            def _(e):
                run(e, q["sp"])
        self.q = {e: [] for e in ENGS}

    def end_stage(self):
        sems = [x for x in self.dsems + [self.esem[e] for e in ENGS] if x.count > 0]
        for e in ENGS:
            for x in sems:
                if self.seen[e].get(x, 0) < x.count:
                    self.seen[e][x] = x.count
                    self.q[e].append(("w", x, x.count))
        self._emit_block()
        self.ss.close()
        self.ss = contextlib.ExitStack()
        self.free_ds.extend(self.stage_ds)
        self.stage_ds = []

    def finish(self):
        self.end_stage()
        self.es.close()
        return self.nc

    def mm(self, out, lhsT, rhs, start, stop, reads, writes, skip=False):
        if skip:
            return self.op("pe", lambda e: e.matmul(out, lhsT, rhs, start=start, stop=stop, skip_group_check=True), reads, writes)
        return self.op("pe", lambda e: e.matmul(out, lhsT, rhs, start=start, stop=stop), reads, writes)

    def tr(self, out, in_, ident, reads, writes):
        return self.op("pe", lambda e: e.transpose(out, in_, ident), reads, writes)

    def act(self, out, in_, func, reads, writes, bias=None, scale=None, accum_out=None):
        kw = {}
        if bias is not None:
            kw["bias"] = bias
        if scale is not None:
            kw["scale"] = scale
        if accum_out is not None:
            kw["accum_out"] = accum_out
        return self.op("act", lambda e: e.activation(out, in_, func, **kw), reads, writes)

    def tt(self, out, in0, in1, op, reads, writes, eng="dve"):
        return self.op(eng, lambda e: e.tensor_tensor(out, in0, in1, op), reads, writes)

    def ts(self, out, in0, s1, s2, op0, op1, reads, writes, eng="dve", accum_out=None):
        if op1 is None:
            return self.op(eng, lambda e: e.tensor_scalar(out, in0, s1, None, op0), reads, writes)
        if accum_out is not None:
            return self.op(eng, lambda e: e.tensor_scalar(out, in0, s1, s2, op0, op1, accum_out=accum_out), reads, writes)
        return self.op(eng, lambda e: e.tensor_scalar(out, in0, s1, s2, op0, op1), reads, writes)

    def stt(self, out, in0, scalar, in1, op0, op1, reads, writes):
        return self.op("dve", lambda e: e.scalar_tensor_tensor(out, in0, scalar, in1, op0, op1), reads, writes)

    def cp(self, out, in_, reads, writes, eng="dve"):
        if eng == "act":
            return self.op("act", lambda e: e.copy(out, in_), reads, writes)
        return self.op(eng, lambda e: e.tensor_copy(out, in_), reads, writes)

    def memset(self, ap, val, writes, eng="dve"):
        return self.op(eng, lambda e: e.memset(ap, val), (), writes)


D = 1024
DFF = 2816
NT = 17
TOK = NT * 128
CTXT = 2
SEQT = 66
EPS = 1e-6
BLOCKS = [(0, 2), (2, 3), (5, 3), (8, 3), (11, 3), (14, 3)]


def make_ident(P, n=128, dt=F32):
    ident = P.sb([n, n], dt)
    P.memset(ident[:], 1.0, [ident], eng="pool")
    P.op("pool", lambda e: e.affine_select(ident[:], ident[:], [[-1, n]], ALU.is_equal, 0.0, base=0,
                                           channel_multiplier=1), [ident], [ident])
    return ident


def load_cast(P, dst, dst_ap, src_ap, stage, sem, i, shape_cols):
    st = stage[i % len(stage)]
    P.dma("sp", st[:, 0:shape_cols], src_ap, writes=[st])
    eng = "dve" if i % 2 == 0 else "pool"
    P.cp(dst_ap, st[:, 0:shape_cols], [st], [dst], eng=eng)


def build_ffn(emit_h, P=None, T=None):
    own = P is None
    P = P or Prog()
    x = P.io(T, "x", [NT, 128, D], F32, "ExternalInput")
    wgu = P.io(T, "wgu", [D, 2 * DFF], F32, "ExternalInput")
    wd = P.io(T, "wd", [DFF, D], F32, "ExternalInput")
    pp = P.io(T, "pp", [128, 2 * 48], F32, "ExternalInput")
    gateb = P.io(T, "gateb", [2, 128, D], F32, "ExternalInput")
    xo = P.io(T, "xo", [NT, 128, D], F32, "ExternalOutput")
    if emit_h:
        ho = P.io(T, "ho", [8, 128, TOK], BF16, "ExternalOutput")

    ident = make_ident(P)
    Wgu = P.sb([128, 8, 2 * DFF], BF16, "Wgu")
    Wd = P.sb([128, 22, D], BF16, "Wd")
    stage = [P.sb([128, 704], F32, f"stg{i}") for i in range(2)]
    ssem = [P.dsem() for _ in range(2)]
    pps = P.sb([128, 96], F32, "pps")
    Gt = P.sb([128, 32], F32, "Gt")
    gates = P.sb([128, 2, D], F32, "gates")
    msem = P.dsem()
    P.dma("act", pps[:], pp[:, :], msem, writes=[pps])
    for s in range(2):
        P.dma("act", gates[:, s, :], gateb[s], msem, writes=[gates])
    for s in range(2):
        for j in range(2):
            sc = pps[:, s * 48 + j * 24 + 8: s * 48 + j * 24 + 16]
            g = pps[:, s * 48 + j * 24 + 16: s * 48 + j * 24 + 24]
            P.stt(Gt[:, s * 16 + j * 8: s * 16 + j * 8 + 8], sc, 1.0, g, ALU.add, ALU.mult, [pps], [Gt])
    P.ts(gates[:], gates[:], 0.5, None, ALU.mult, None, [gates], [gates], eng="pool")

    n = 0
    for kc in range(8):
        for q in range(8):
            load_cast(P, Wgu, Wgu[:, kc, q * 704:(q + 1) * 704], wgu[kc * 128:(kc + 1) * 128, q * 704:(q + 1) * 704],
                      stage, ssem, n, 704)
            n += 1
    for fc in range(22):
        for q in range(2):
            load_cast(P, Wd, Wd[:, fc, q * 512:(q + 1) * 512], wd[fc * 128:(fc + 1) * 128, q * 512:(q + 1) * 512], stage, ssem, n, 512)
            n += 1

    xb = [P.sb([128, 3, D], F32, "xb0")] * 2
    xsem = [P.dsem() for _ in range(2)]
    osem = [P.dsem() for _ in range(2)]
    scr = {"xn": P.sb([128, 3, D], F32, "xn"), "sq": P.sb([128, D], BF16, "sq")}
    small = {"ss": P.sb([128, 4], F32, "ss")}
    hT = P.sb([128, 8, 384], BF16, "hT")
    hT2 = hT
    hsem = P.dsem()
    uT = P.sb([128, 22, 384], BF16, "uT")
    sa = [P.sb([128, 384], F32, f"sa{i}") for i in range(2)]
    pst = [P.ps([128, 512], F32, f"pst{i}") for i in range(2)]
    pa = [P.ps([128, 512], F32, f"pa{i}") for i in range(2)]
    pb = [P.ps([128, 512], F32, f"pb{i}") for i in range(2)]
    py = [P.ps([128, 512], F32, f"py{i}") for i in range(2)]
    tmp = [P.sb([128, 512], F32, f"tmp{i}") for i in range(2)]

    for bi, (t0, nb) in enumerate(BLOCKS):
        s = 0 if bi == 0 else 1
        N = nb * 128
        X = xb[bi % 2]
        for i in range(nb):
            P.dma("sp", X[:, i, :], x[t0 + i], xsem[bi % 2], writes=[X])
        _norm(P, X, nb, Gt, s * 16, pps, s * 48, hT, ident, pst, scr, small)
        for fc in range(22):
            A, B = pa[fc % 2], pb[fc % 2]
            for kc in range(8):
                P.mm(A[:, 0:N], Wgu[:, kc, fc * 128:(fc + 1) * 128], hT[:, kc, 0:N], kc == 0, kc == 7, [Wgu, hT], [A])
            for kc in range(8):
                P.mm(B[:, 0:N], Wgu[:, kc, DFF + fc * 128:DFF + (fc + 1) * 128], hT[:, kc, 0:N], kc == 0, kc == 7,
                     [Wgu, hT], [B])
            S = sa[fc % 2]
            P.act(S[:, 0:N], A[:, 0:N], AF.Silu, [A], [S])
            P.tt(uT[:, fc, 0:N], S[:, 0:N], B[:, 0:N], ALU.mult, [S, B], [uT])
        for i in range(nb):
            for h in range(2):
                Y = py[(i * 2 + h) % 2]
                for fc in range(22):
                    P.mm(Y[:, :], uT[:, fc, i * 128:(i + 1) * 128], Wd[:, fc, h * 512:(h + 1) * 512], fc == 0, fc == 21,
                         [uT, Wd], [Y])
                T = tmp[(i * 2 + h) % 2]
                P.tt(T[:], Y[:], gates[:, s, h * 512:(h + 1) * 512], ALU.mult, [Y, gates], [T])
                P.tt(X[:, i, h * 512:(h + 1) * 512], X[:, i, h * 512:(h + 1) * 512], T[:], ALU.add, [X, T], [X], eng="pool")
            P.dma("sp", xo[t0 + i], X[:, i, :], osem[bi % 2], reads=[X])
        if emit_h:
            _norm(P, X, nb, Gt, s * 16 + 8, pps, s * 48 + 24, hT2, ident, pst, scr, small)
            for c in range(8):
                P.dma("act", ho[c, :, t0 * 128:t0 * 128 + N], hT2[:, c, 0:N], hsem, reads=[hT2])
    return P.finish() if own else None


def _norm(P, xb, nb, Gt, gcol, SHt, scol, hT, ident, pst, scr, small):
    xn = scr["xn"]
    ss = small["ss"]
    for i in range(nb):
        P.act(scr["sq"][:], xb[:, i, :], AF.Square, [xb], [scr["sq"], ss], accum_out=ss[:, 0:1])
        P.ts(ss[:, 1:2], ss[:, 0:1], 1.0 / D, EPS, ALU.mult, ALU.add, [ss], [ss])
        P.act(ss[:, 2:3], ss[:, 1:2], AF.Sqrt, [ss], [ss])
        P.op("dve", lambda e: e.reciprocal(ss[:, 3:4], ss[:, 2:3]), [ss], [ss])
        P.ts(xn[:, i, :], xb[:, i, :], ss[:, 3:4], None, ALU.mult, None, [xb, ss], [xn])
    for c in range(8):
        pt = pst[c % len(pst)]
        for i in range(nb):
            P.tr(pt[:, i * 128:(i + 1) * 128], xn[:, i, c * 128:(c + 1) * 128], ident[:], [xn, ident], [pt])
        P.act(hT[:, c, 0:nb * 128], pt[:, 0:nb * 128], AF.Identity, [pt, Gt, SHt], [hT],
              bias=SHt[:, scol + c:scol + c + 1], scale=Gt[:, gcol + c:gcol + c + 1])


def build_mod():
    P = Prog()
    cT = P.dram("cT", [128, 8, 3], F32, "ExternalInput")
    wa = P.dram("wa", [2, D, 1152], F32, "ExternalInput")
    ba = P.dram("ba", [2, 3, 1152], F32, "ExternalInput")
    mo = P.dram("mo", [2, 3, 1152], F32, "ExternalOutput")
    cs = P.sb([128, 8, 3], F32)
    sg = P.sb([128, 8, 3], F32)
    ds = P.dsem()
    P.dma("sp", cs[:], cT[:, :, :], ds, writes=[cs])
    P.act(sg[:], cs[:], AF.Sigmoid, [cs], [sg])
    P.tt(cs[:], cs[:], sg[:], ALU.mult, [cs, sg], [cs])
    W = [P.sb([128, 8, 1152], F32, f"W{l}") for l in range(2)]
    wsem = P.dsem()
    bsb = P.sb([3, 2, 1152], F32)
    osb = P.sb([3, 2, 1152], F32)
    for l in range(2):
        P.dma("act", bsb[:, l, :], ba[l], ds, writes=[bsb])
        for kc in range(8):
            P.dma("sp" if kc % 2 == 0 else "act", W[l][:, kc, :], wa[l, kc * 128:(kc + 1) * 128, :], wsem, writes=[W[l]])
    pp = [P.ps([3, 512], F32, f"pp{i}") for i in range(2)]
    n = 0
    for l in range(2):
        for (c0, cw) in ((0, 512), (512, 512), (1024, 128)):
            ps = pp[n % 2]
            n += 1
            for kc in range(8):
                P.mm(ps[:, 0:cw], cs[:, kc, :], W[l][:, kc, c0:c0 + cw], kc == 0, kc == 7, [cs, W[l]], [ps])
            P.tt(osb[:, l, c0:c0 + cw], ps[:, 0:cw], bsb[:, l, c0:c0 + cw], ALU.add, [ps, bsb], [osb])
    osem = P.dsem()
    for l in range(2):
        P.dma("sp", mo[l], osb[:, l, :], osem, reads=[osb])
    return P.finish()


def run_mod(c, c_ctx, w_ada, b_ada):
    cv = np.stack([c[0], c[1], c_ctx], 0)
    cT = np.ascontiguousarray(cv.reshape(3, 8, 128).transpose(2, 1, 0))
    nc = build_mod()
    ins = []
    for i in range(8):
        cols = slice(i * 1152, (i + 1) * 1152)
        ins.append({"cT": cT, "wa": np.ascontiguousarray(w_ada[:, :, cols]),
                    "ba": np.ascontiguousarray(np.broadcast_to(b_ada[:, None, cols], (2, 3, 1152)))})
    res = run_bass_kernel_spmd(nc, ins, core_ids=list(range(8)))
    return np.concatenate([r["mo"] for r in res.results], axis=2)


def pvec(v):
    return np.ascontiguousarray(v.reshape(8, 128).T)


def tok_shard(xfull_b):
    pad = np.zeros((4 * TOK, xfull_b.shape[1]), np.float32)
    pad[:xfull_b.shape[0]] = xfull_b
    return [np.ascontiguousarray(pad[i * TOK:(i + 1) * TOK].reshape(NT, 128, -1)) for i in range(4)]


def ffn_inputs(xs, mods_l, li, which, norm_g, wgu, wd, emit_h):
    m = mods_l.reshape(3, 9, D)
    o = 0 if which == 1 else 6
    ins = []
    for b in range(2):
        shards = tok_shard(xs[b])
        for i in range(4):
            sets = []
            for s in range(2):
                r = 2 if (s == 0 and i == 0) else b
                vecs = [m[r, o + 0], m[r, o + 1], norm_g[li, 0 if which == 1 else 2]]
                if emit_h:
                    vecs += [m[r, 3], m[r, 4], norm_g[li, 1]]
                else:
                    vecs += [m[r, 3] * 0, m[r, 3] * 0, m[r, 3] * 0]
                sets.append(np.concatenate([pvec(v) for v in vecs], axis=1))
            pp = np.ascontiguousarray(np.concatenate(sets, axis=1).astype(np.float32))
            gb = np.stack([np.broadcast_to(m[2 if i == 0 else b, o + 2], (128, D)),
                           np.broadcast_to(m[b, o + 2], (128, D))], 0)
            ins.append({"x": shards[i], "wgu": wgu, "wd": wd, "pp": pp, "gateb": np.ascontiguousarray(gb)})
    return ins


def tok_unshard(res, key):
    out = []
    for b in range(2):
        full = np.concatenate([res[b * 4 + i][key].reshape(TOK, -1) for i in range(4)], axis=0)
        out.append(full[:SEQT * 128])
    return out


def hT_unshard(res, key):
    out = []
    for b in range(2):
        full = np.concatenate([res[b * 4 + i][key].reshape(D, TOK) for i in range(4)], axis=1)
        out.append(np.ascontiguousarray(full[:, :SEQT * 128]))
    return out


SEQ_ALL = SEQT * 128
NCTX = 256


def h_pieces(hT, kc, a, b):
    if hT.gath:
        out, t = [], a
        while t < b:
            r = t // TOK
            e = min(b, (r + 1) * TOK)
            out.append((t - a, e - t, hT[kc, r * 128:(r + 1) * 128, t - r * TOK:e - r * TOK]))
            t = e
        return out
    return [(0, b - a, hT[kc, :, a:b])]


def ycols(buf, a):
    if buf.gath:
        q = a // TOK
        return buf[q, :, a - q * TOK:a - q * TOK + 128]
    return buf[:, a:a + 128]


def load_hblk(P, hT, hb, sem, t0, N, eng="sp"):
    for kc in range(8):
        for (o, n, ap) in h_pieces(hT, kc, t0, t0 + N):
            P.dma(eng, hb[:, kc, o:o + n], ap, writes=[hb], **({"allow_slow_non_contiguous": True} if n == 1 else {}))


def build_attn(debug=False, P=None, T=None):
    own = P is None
    P = P or Prog()
    hT = P.io(T, "hT", [8, 128, SEQ_ALL], BF16, "ExternalInput")
    wqkv = P.io(T, "wqkv", [3, D, 128], F32, "ExternalInput")
    gqk = P.io(T, "gqk", [128, 2], F32, "ExternalInput")
    cs = P.io(T, "cs", [2, 128, 8192], F32, "ExternalInput")
    cmat = P.io(T, "cmat", [2, 128, 128], F32, "ExternalInput")
    lamp = P.io(T, "lamp", [128, 258], F32, "ExternalInput")
    subg = P.io(T, "subg", [128, 128], F32, "ExternalInput")
    ya = P.io(T, "ya", [128, SEQ_ALL], BF16, "ExternalOutput")

    banks = [P.ps([128, 512], F32, f"bank{i}") for i in range(8)]
    ident = make_ident(P)
    csem = P.dsem()
    W = P.sb([128, 3, 8, 128], BF16, "W")
    wst = P.sb([128, 3, 8, 128], F32, "wst")
    for j in range(3):
        for kc in range(8):
            P.dma("sp", wst[:, j, kc, :], wqkv[j, kc * 128:(kc + 1) * 128, :], csem, writes=[wst])
    P.cp(W[:], wst[:], [wst], [W])
    gq = P.sb([128, 2], F32, "gq")
    P.dma("act", gq[:], gqk[:, :], csem, writes=[gq])
    P.ts(gq[:, 0:1], gq[:, 0:1], 0.125, None, ALU.mult, None, [gq], [gq])
    Bm = P.sb([128, 128], F32, "Bm")
    Rm = P.sb([128, 128], F32, "Rm")
    P.dma("act", Bm[:], cmat[0], csem, writes=[Bm])
    P.dma("act", Rm[:], cmat[1], csem, writes=[Rm])
    lp = P.sb([128, 258], F32, "lp")
    P.dma("act", lp[:], lamp[:, :], csem, writes=[lp])
    sg = P.sb([128, 128], F32, "sg")
    P.dma("act", sg[:], subg[:, :], csem, writes=[sg])
    P.ts(sg[:], sg[:], lp[:, 257:258], None, ALU.mult, None, [sg, lp], [sg])
    lt = P.sb([128, 128], F32, "lt")
    lam = P.sb([128, 4], F32, "lam")
    P.tt(lt[:, 0:64], lp[:, 0:64], lp[:, 64:128], ALU.mult, [lp], [lt])
    P.tt(lt[:, 64:128], lp[:, 128:192], lp[:, 192:256], ALU.mult, [lp], [lt])
    P.op("dve", lambda e: e.tensor_reduce(lam[:, 0:1], lt[:, 0:64], AX.X, ALU.add), [lt], [lam])
    P.op("dve", lambda e: e.tensor_reduce(lam[:, 1:2], lt[:, 64:128], AX.X, ALU.add), [lt], [lam])
    P.act(lam[:, 0:2], lam[:, 0:2], AF.Exp, [lam], [lam])
    P.tt(lam[:, 2:3], lam[:, 1:2], lam[:, 0:1], ALU.subtract, [lam], [lam])
    P.tt(lam[:, 3:4], lam[:, 2:3], lp[:, 256:257], ALU.subtract, [lam, lp], [lam])
    epsb = P.sb([128, 1], F32, "epsb")
    P.memset(epsb[:], EPS, [epsb])

    QK = [P.sb([128, SEQ_ALL], BF16, "QT"), P.sb([128, SEQ_ALL], BF16, "KT")]
    V = P.sb([128, SEQT, 129], BF16, "V")
    P.memset(V[:, :, 128:129], 1.0, [V], eng="pool")
    hb = [P.sb([128, 8, 512], BF16, f"hb{i}") for i in range(2)]
    hsem = [P.dsem() for _ in range(2)]
    cst = [P.sb([128, 2, 512], F32, f"cst{i}") for i in range(2)]
    cssem = [P.dsem() for _ in range(2)]
    sq = P.sb([128, 512], F32, "sq")
    rs = P.sb([128, 512], F32, "rs")
    xn = P.sb([128, 512], F32, "xn")
    t1 = P.sb([128, 512], F32, "t1")
    t2 = P.sb([128, 512], F32, "t2")

    blocks = [(0, 256)] + [(256 + 512 * j, 512) for j in range(16)]
    for bi, (t0, N) in enumerate(blocks):
        H = hb[bi % 2]
        load_hblk(P, hT, H, hsem[bi % 2], t0, N)
        C = cst[bi % 2]
        if bi > 0:
            for j in range(2):
                P.dma("act", C[:, j, :], cs[j, :, t0 - 256:t0 - 256 + N], cssem[bi % 2], writes=[C])
        for j in range(2):
            pq, pms, prot = banks[0 + j], banks[2 + j], banks[4 + j]
            for kc in range(8):
                P.mm(pq[:, 0:N], W[:, j, kc, :], H[:, kc, 0:N], kc == 0, kc == 7, [W, H], [pq])
            P.act(sq[:, 0:N], pq[:, 0:N], AF.Square, [pq], [sq])
            P.mm(pms[:, 0:N], Bm[:], sq[:, 0:N], True, True, [Bm, sq], [pms])
            P.act(rs[:, 0:N], pms[:, 0:N], AF.Sqrt, [pms, epsb], [rs], bias=epsb[:, 0:1])
            P.op("dve", lambda e, N=N: e.reciprocal(rs[:, 0:N], rs[:, 0:N]), [rs], [rs])
            P.stt(xn[:, 0:N], pq[:, 0:N], gq[:, j:j + 1], rs[:, 0:N], ALU.mult, ALU.mult, [pq, gq, rs], [xn])
            if bi == 0:
                P.cp(QK[j][:, t0:t0 + N], xn[:, 0:N], [xn], [QK[j]], eng="pool")
            else:
                P.mm(prot[:, 0:N], Rm[:], xn[:, 0:N], True, True, [Rm, xn], [prot])
                P.tt(t1[:, 0:N], xn[:, 0:N], C[:, 0, 0:N], ALU.mult, [xn, C], [t1], eng="pool")
                P.tt(t2[:, 0:N], prot[:, 0:N], C[:, 1, 0:N], ALU.mult, [prot, C], [t2])
                P.tt(QK[j][:, t0:t0 + N], t1[:, 0:N], t2[:, 0:N], ALU.add, [t1, t2], [QK[j]])
        for i in range(N // 128):
            pv = banks[6 + i % 2]
            for kc in range(8):
                P.mm(pv[:, 0:128], H[:, kc, i * 128:(i + 1) * 128], W[:, 2, kc, :], kc == 0, kc == 7, [H, W], [pv])
            P.cp(V[:, t0 // 128 + i, 0:128], pv[:, 0:128], [pv], [V], eng="act")

    if debug:
        dq = P.io(T, "dq", [2, 128, SEQ_ALL], BF16, "ExternalOutput")
        dv = P.io(T, "dv", [128, SEQT, 129], BF16, "ExternalOutput")
        dl = P.io(T, "dl", [128, 4], F32, "ExternalOutput")
        dsm = P.dsem()
        P.dma("sp", dq[0], QK[0][:], dsm, reads=[QK[0]])
        P.dma("sp", dq[1], QK[1][:], dsm, reads=[QK[1]])
        P.dma("sp", dv[:, :, :], V[:], dsm, reads=[V])
        P.dma("sp", dl[:, :], lam[:], dsm, reads=[lam])
    Pm = [P.sb([128, 512], BF16, f"Pm{i}") for i in range(6)]
    Sb = [banks[0], banks[1], banks[6], banks[7]]
    SKEW = 3
    accb = [banks[2], banks[3], banks[4]]
    yo = [P.sb([128, 128], F32, f"yo{i}") for i in range(2)]
    yob = [P.sb([128, 128], BF16, f"yob{i}") for i in range(2)]
    ysq = P.sb([128, 128], F32, "ysq")
    st = P.sb([128, 8], F32, "st")
    osem = [P.dsem() for _ in range(2)]

    def acc(m, qs):
        a = m * 4 + qs
        return accb[a // 3], (a % 3) * 129

    qblocks = [(0, 256, 0, CTXT)] + [(256 + 512 * j, 512, 0, SEQT) for j in range(16)]
    accS = [[P.sb([128, 387], F32, f"accS{i}{j}") for j in range(3)] for i in range(2)]
    sts = [P.sb([128, 8], F32, f"sts{i}") for i in range(2)]
    n = 0
    no = 0
    pending = []

    def finalize(q0, nq, par, no0):
        A = accS[par]
        for qs in range(nq):
            i0, i1 = 0 * 4 + qs, 1 * 4 + qs
            a0, c0 = A[i0 // 3], (i0 % 3) * 129
            a1, c1 = A[i1 // 3], (i1 % 3) * 129
            k = no0 + qs
            Y, st_ = yo[k % 2], sts[k % 2]
            P.op("dve", lambda e, a0=a0, c0=c0, st_=st_: e.reciprocal(st_[:, 0:1], a0[:, c0 + 128:c0 + 129]), [a0], [st_])
            P.op("dve", lambda e, a1=a1, c1=c1, st_=st_: e.reciprocal(st_[:, 1:2], a1[:, c1 + 128:c1 + 129]), [a1], [st_])
            P.tt(st_[:, 1:2], st_[:, 1:2], lam[:, 3:4], ALU.mult, [st_, lam], [st_])
            P.ts(Y[:], a0[:, c0:c0 + 128], st_[:, 0:1], None, ALU.mult, None, [a0, st_], [Y])
            P.stt(Y[:], a1[:, c1:c1 + 128], st_[:, 1:2], Y[:], ALU.mult, ALU.add, [a1, st_, Y], [Y])
            P.tt(ysq[:], Y[:], Y[:], ALU.mult, [Y], [ysq])
            P.op("dve", lambda e, st_=st_: e.tensor_reduce(st_[:, 2:3], ysq[:], AX.X, ALU.add), [ysq], [st_])
            P.ts(st_[:, 3:4], st_[:, 2:3], 1.0 / 128, EPS, ALU.mult, ALU.add, [st_], [st_])
            P.act(st_[:, 4:5], st_[:, 3:4], AF.Sqrt, [st_], [st_])
            P.op("dve", lambda e, st_=st_: e.reciprocal(st_[:, 5:6], st_[:, 4:5]), [st_], [st_])
            P.stt(Y[:], Y[:], st_[:, 5:6], sg[:], ALU.mult, ALU.mult, [Y, st_, sg], [Y])
            P.tr(banks[5][:, 0:128], Y[:], ident[:], [Y, ident], [banks[5]])
            Yb = yob[k % 2]
            P.cp(Yb[:], banks[5][:, 0:128], [banks[5]], [Yb])
            P.dma("sp", ycols(ya, q0 + qs * 128), Yb[:], reads=[Yb])

    for bi, (q0, N, k0, k1) in enumerate(qblocks):
        nq = N // 128
        for b in accb:
            P.memset(b[:], 0.0, [b], eng="pool" if False else "dve")
        its = [(kt, m) for kt in range(k0, k1) for m in range(2)]
        per = (len(pending) + len(its) - 1) // max(1, len(its) - 8) if pending else 0
        ppos = 0

        def issue_s(j):
            kt, m = its[j]
            S, pm = Sb[(n + j) % 4], Pm[(n + j) % 6]
            P.mm(S[:, 0:N], QK[1][m * 64:(m + 1) * 64, kt * 128:(kt + 1) * 128], QK[0][m * 64:(m + 1) * 64, q0:q0 + N],
                 True, True, [QK[0], QK[1]], [S])
            P.act(pm[:, 0:N], S[:, 0:N], AF.Exp, [S], [pm])

        for j in range(min(SKEW, len(its))):
            issue_s(j)
        for j, (kt, m) in enumerate(its):
            if j + SKEW < len(its):
                issue_s(j + SKEW)
            pm = Pm[(n + j) % 6]
            for qs in range(nq):
                ab, c0 = acc(m, qs)
                P.mm(ab[:, c0:c0 + 129], pm[:, qs * 128:(qs + 1) * 128], V[:, kt, :], False, False, [pm, V], [ab], skip=True)
            for th in pending[ppos:ppos + per]:
                th()
            ppos += per
        for th in pending[ppos:]:
            th()
        n += len(its)
        par = bi % 2
        for j3, b in enumerate(accb):
            P.cp(accS[par][j3][:], b[:, 0:387], [b], [accS[par][j3]], eng="dve" if j3 != 1 else "act")
        P.defer = []
        finalize(q0, nq, par, no)
        pending, P.defer = P.defer, None
        no += nq
    for th in pending:
        th()
    return P.finish() if own else None


def rope_tables():
    n = 8192
    rows = n // 64
    row = np.repeat(np.arange(rows, dtype=np.float32), 64)
    col = np.tile(np.arange(64, dtype=np.float32), rows)
    half = 32
    inv = (np.float32(10000.0) ** (-np.arange(0, half, 2, dtype=np.float32) / np.float32(half))).astype(np.float32)
    ang = np.concatenate([row[:, None] * inv, col[:, None] * inv], axis=-1).astype(np.float32)
    cos, sin = np.cos(ang).astype(np.float32), np.sin(ang).astype(np.float32)
    idx = (np.arange(128) % 64) // 2
    return np.ascontiguousarray(np.stack([cos[:, idx].T, sin[:, idx].T], 0))


def attn_consts():
    Bm = np.zeros((128, 128), np.float32)
    Bm[:64, :64] = 1.0 / 64
    Bm[64:, 64:] = 1.0 / 64
    Rm = np.zeros((128, 128), np.float32)
    for i in range(64):
        Rm[2 * i + 1, 2 * i] = -1.0
        Rm[2 * i, 2 * i + 1] = 1.0
    return np.stack([Bm, Rm], 0)


def lambda_init(li):
    import math
    return 0.8 - 0.6 * math.exp(-0.3 * li)


def attn_inputs(hTs, li, w_in, a_qk_norm, a_lambda, a_subln):
    off_q = 5008 - 512 - 512 - 512
    off = {}
    o = 0
    for name, wdt in IN_SPLITS:
        off[name] = o
        o += wdt
    cs = rope_tables()
    cm = attn_consts()
    ins = []
    for b in range(2):
        hT = np.ascontiguousarray(hTs[b].reshape(8, 128, SEQ_ALL))
        for h in range(4):
            wq = w_in[li][:, off['a_q'] + 128 * h: off['a_q'] + 128 * (h + 1)]
            wk = w_in[li][:, off['a_k'] + 128 * h: off['a_k'] + 128 * (h + 1)]
            wv = w_in[li][:, off['a_v'] + 128 * h: off['a_v'] + 128 * (h + 1)]
            gqk = np.stack([np.tile(a_qk_norm[li, 0], 2), np.tile(a_qk_norm[li, 1], 2)], 1).astype(np.float32)
            li_ = np.float32(lambda_init(li))
            lamp = np.concatenate([np.broadcast_to(a_lambda[li].reshape(1, 256), (128, 256)),
                                   np.full((128, 1), li_, np.float32), np.full((128, 1), np.float32(1.0) - li_, np.float32)], 1)
            ins.append({"hT": hT, "wqkv": np.ascontiguousarray(np.stack([wq, wk, wv], 0)), "gqk": np.ascontiguousarray(gqk),
                        "cs": cs, "cmat": cm, "lamp": np.ascontiguousarray(lamp.astype(np.float32)),
                        "subg": np.ascontiguousarray(np.broadcast_to(a_subln[li][None, :], (128, 128)))})
    return ins


IN_SPLITS = (
    ('m_q', 256), ('m_k', 256), ('m_v', 512), ('m_o', 512),
    ('m_if', 4), ('m_ff', 4), ('m_ib', 4), ('m_fb', 4),
    ('r_r', 512), ('r_k', 512), ('r_v', 512),
    ('r_wf', 64), ('r_wb', 64), ('r_af', 64), ('r_ab', 64), ('r_g', 128),
    ('a_q', 512), ('a_k', 512), ('a_v', 512),
    ('g_m', 1024), ('g_r', 1024), ('g_a', 1024),
)


def tri_mask(P, upper, neg=False):
    m = P.sb([128, 128], F32)
    P.memset(m[:], 0.0 if neg else 1.0, [m], eng="pool")
    pat, cm = ([[1, 128]], -1) if upper else ([[-1, 128]], 1)
    P.op("pool", lambda e: e.affine_select(m[:], m[:], pat, ALU.is_ge, -1.0e4 if neg else 0.0, base=0,
                                           channel_multiplier=cm), [m], [m])
    return m


def load_hblk_halo(P, hT, hb, t0, N, lo, hi, eng="sp"):
    a = t0 - 1 if t0 - 1 >= lo else t0
    b = t0 + N + 1 if t0 + N + 1 <= hi else t0 + N
    for kc in range(8):
        for (o, n, ap) in h_pieces(hT, kc, a, b):
            P.dma(eng, hb[:, kc, a - (t0 - 1) + o:a - (t0 - 1) + o + n], ap, writes=[hb],
                  **({"allow_slow_non_contiguous": True} if n == 1 else {}))
    if a == t0:
        P.memset(hb[:, :, 0:1], 0.0, [hb], eng="pool")
    if b == t0 + N:
        P.memset(hb[:, :, N + 1:N + 2], 0.0, [hb], eng="pool")


def chunk_orders():
    f = list(range(SEQT))
    b = [1, 0] + list(range(SEQT - 1, 1, -1))
    return f, b


def build_mlstm(stop=99, P=None, T=None):
    own = P is None
    P = P or Prog()
    hT = P.io(T, "hT", [8, 128, SEQ_ALL], BF16, "ExternalInput")
    wqk = P.io(T, "wqk", [2, D, 64], F32, "ExternalInput")
    wvo = P.io(T, "wvo", [D, 256], F32, "ExternalInput")
    wg = P.io(T, "wg", [D, 4], F32, "ExternalInput")
    cw = P.io(T, "cw", [2, 128, 3, 64], F32, "ExternalInput")
    gb = P.io(T, "gb", [128, 4], F32, "ExternalInput")
    og = P.io(T, "og", [128, 128], F32, "ExternalInput")
    ym = P.io(T, "ym", [128, SEQ_ALL], BF16, "ExternalOutput")

    banks = [P.ps([128, 512], F32, f"bank{i}") for i in range(8)]
    ident = make_ident(P)
    ones = P.sb([128, 128], F32, "ones")
    P.memset(ones[:], 1.0, [ones])
    one1 = P.sb([128, 1], F32, "one1")
    P.memset(one1[:], 1.0, [one1])
    triU = tri_mask(P, True)
    triL = tri_mask(P, False)
    negU = tri_mask(P, True, True)
    negL = tri_mask(P, False, True)

    wst = P.sb([128, 8, 388], F32, "wst")
    for kc in range(8):
        P.dma("sp", wst[:, kc, 0:64], wqk[0, kc * 128:(kc + 1) * 128, :], writes=[wst])
        P.dma("sp", wst[:, kc, 64:128], wqk[1, kc * 128:(kc + 1) * 128, :], writes=[wst])
        P.dma("sp", wst[:, kc, 128:384], wvo[kc * 128:(kc + 1) * 128, :], writes=[wst])
        P.dma("sp", wst[:, kc, 384:388], wg[kc * 128:(kc + 1) * 128, :], writes=[wst])
    cws = P.sb([128, 2, 3, 64], F32, "cws")
    P.dma("act", cws[:, 0], cw[0], writes=[cws])
    P.dma("act", cws[:, 1], cw[1], writes=[cws])
    gbs = P.sb([128, 4], F32, "gbs")
    P.dma("act", gbs[:], gb[:, :], writes=[gbs])
    ogs = P.sb([128, 128], F32, "ogs")
    P.dma("act", ogs[:], og[:, :], writes=[ogs])
    Wqk = P.sb([128, 2, 3, 8, 64], BF16, "Wqk")
    Wvo = P.sb([128, 8, 260], BF16, "Wvo")
    P.cp(Wvo[:], wst[:, :, 128:388], [wst], [Wvo])
    for j in range(2):
        for tap in range(3):
            for kc in range(8):
                P.tt(Wqk[:, j, tap, kc, :], wst[:, kc, j * 64:(j + 1) * 64], cws[:, j, tap, :], ALU.mult, [wst, cws], [Wqk],
                     eng="pool" if kc % 2 else "dve")

    QT = P.sb([64, SEQ_ALL], F32, "QT")
    KT = P.sb([64, SEQ_ALL], F32, "KT")
    VE = P.sb([128, SEQT, 129], F32, "VE")
    P.memset(VE[:, :, 128:129], 1.0, [VE], eng="pool")
    OG = P.sb([128, SEQT, 128], BF16, "OG")
    G = P.sb([128, SEQT, 4], F32, "G")
    hb = [P.sb([128, 8, 514], BF16, f"hb{i}") for i in range(2)]

    blocks = [(0, 256, 0, 256)] + [(256 + 512 * j, 512, 256, SEQ_ALL) for j in range(16)]
    for bi, (t0, N, lo, hi) in enumerate(blocks):
        H = hb[bi % 2]
        load_hblk_halo(P, hT, H, t0, N, lo, hi)
        for j, dst in enumerate((QT, KT)):
            pq = banks[j]
            n = 0
            for tap in range(3):
                for kc in range(8):
                    P.mm(pq[0:64, 0:N], Wqk[:, j, tap, kc, :], H[:, kc, tap:tap + N], n == 0, n == 23, [Wqk, H], [pq])
                    n += 1
            P.act(dst[:, t0:t0 + N], pq[0:64, 0:N], AF.Silu, [pq], [dst], scale=1.0)
        for i in range(N // 128):
            pv = banks[2 + i % 2]
            for kc in range(8):
                P.mm(pv[:, 0:260], H[:, kc, 1 + i * 128:1 + (i + 1) * 128], Wvo[:, kc, :], kc == 0, kc == 7, [H, Wvo], [pv])
            tl = t0 // 128 + i
            P.cp(VE[:, tl, 0:128], pv[:, 0:128], [pv], [VE])
            P.act(OG[:, tl, :], pv[:, 128:256], AF.Sigmoid, [pv], [OG])
            P.tt(G[:, tl, :], pv[:, 256:260], gbs[:], ALU.add, [pv, gbs], [G])
    if stop == 1:
        return P.finish() if own else None
    P.ts(KT[:], KT[:], 0.125, None, ALU.mult, None, [KT], [KT], eng="pool")

    ge = P.sb([128, SEQT, 4], F32, "ge")
    P.act(ge[:], G[:], AF.Exp, [G], [ge], scale=-1.0)
    P.act(ge[:], ge[:], AF.Ln, [ge, one1], [ge], bias=one1[:, 0:1])
    LF = P.sb([128, 2, SEQT], F32, "LF")
    IG = P.sb([128, 2, SEQT], F32, "IG")
    for d in range(2):
        P.ts(LF[:, d, :], ge[:, :, 2 * d + 1], -1.0, None, ALU.mult, None, [ge], [LF])
        P.cp(IG[:, d, :], G[:, :, 2 * d], [G], [IG])
    BC = P.sb([128, 2, SEQT], F32, "BC")
    BT = P.sb([128, 2, SEQT], F32, "BT")
    for d in range(2):
        pb = banks[4 + d]
        P.mm(pb[:, 0:SEQT], (triU if d == 0 else triL)[:], LF[:, d, :], True, True, [triU, triL, LF], [pb])
        P.cp(BC[:, d, :], pb[:, 0:SEQT], [pb], [BC])
        pb2 = banks[6 + d]
        P.mm(pb2[:, 0:SEQT], ones[:], LF[:, d, :], True, True, [ones, LF], [pb2])
        P.cp(BT[:, d, :], pb2[:, 0:SEQT], [pb2], [BT])
    BIAS = P.sb([128, 2, SEQT], F32, "BIAS")
    WS = P.sb([128, 2, SEQT], F32, "WS")
    EB = P.sb([128, 2, SEQT], F32, "EB")
    DEC = P.sb([128, 2, SEQT], F32, "DEC")
    P.tt(BIAS[:], IG[:], BC[:], ALU.subtract, [IG, BC], [BIAS])
    P.tt(WS[:], BIAS[:], BT[:], ALU.add, [BIAS, BT], [WS])
    P.act(WS[:], WS[:], AF.Exp, [WS], [WS])
    P.act(EB[:], BC[:], AF.Exp, [BC], [EB])
    P.act(DEC[:], BT[:], AF.Exp, [BT], [DEC])

    if stop == 2:
        return P.finish() if own else None
    HS = P.sb([128, SEQT, 128], F32, "HS")
    CT = [[P.sb([64, 129], F32, f"CT{d}{i}") for i in range(2)] for d in range(2)]
    for d in range(2):
        P.memset(CT[d][0][:], 0.0, [CT[d][0]])
    lrep = [P.sb([128, 128], F32, f"lrep{i}") for i in range(2)]
    arg = [P.sb([128, 128], F32, f"arg{i}") for i in range(2)]
    ST = [P.sb([128, 128], F32, f"ST{i}") for i in range(2)]
    KW = [P.sb([128, 64], F32, f"KW{i}") for i in range(2)]
    it = [P.sb([128, 129], F32, f"it{i}") for i in range(2)]
    tot = [P.sb([128, 129], F32, f"tot{i}") for i in range(2)]
    sm = [P.sb([128, 2], F32, f"sm{i}") for i in range(2)]
    orders = chunk_orders()
    done = set()
    for step in range(SEQT):
        for d in range(2):
            c = orders[d][step]
            cs_ = slice(c * 128, (c + 1) * 128)
            Ccur, Cnew = CT[d][step % 2], CT[d][(step + 1) % 2]
            tri, neg = (triU, negU) if d == 0 else (triL, negL)
            p_brd, p_qk, p_n, p_i, p_kt, p_st = (banks[d * 4 + 0], banks[d * 4 + 1], banks[d * 4 + 2], banks[d * 4 + 3],
                                                 banks[d * 4 + 0], banks[d * 4 + 1])
            L = lrep[d]
            P.ts(L[:], ones[:], LF[:, d, c:c + 1], None, ALU.mult, None, [ones, LF], [L], eng="pool")
            P.mm(p_brd[:, 0:128], L[:], tri[:], True, True, [L, tri], [p_brd])
            A = arg[d]
            P.tt(A[:], p_brd[:, 0:128], neg[:], ALU.add, [p_brd, neg], [A])
            P.act(A[:], A[:], AF.Exp, [A, BIAS], [A], bias=BIAS[:, d, c:c + 1])
            P.mm(p_qk[:, 0:128], KT[:, cs_], QT[:, cs_], True, True, [KT, QT], [p_qk])
            S = ST[d]
            P.tt(S[:], p_qk[:, 0:128], A[:], ALU.mult, [p_qk, A], [S])
            P.mm(p_n[:, 0:129], S[:], VE[:, c, :], True, True, [S, VE], [p_n])
            P.mm(p_i[:, 0:129], QT[:, cs_], Ccur[:], True, True, [QT, Ccur], [p_i])
            I = it[d]
            P.act(I[:], p_i[:, 0:129], AF.Identity, [p_i, EB], [I], scale=EB[:, d, c:c + 1])
            T = tot[d]
            P.tt(T[:], p_n[:, 0:129], I[:], ALU.add, [p_n, I], [T])
            s_ = sm[d]
            P.act(s_[:, 0:1], T[:, 128:129], AF.Abs, [T], [s_])
            P.ts(s_[:, 0:1], s_[:, 0:1], 1.0, None, ALU.max, None, [s_], [s_])
            P.op("dve", lambda e, s_=s_: e.reciprocal(s_[:, 1:2], s_[:, 0:1]), [s_], [s_])
            if c in done:
                P.stt(HS[:, c, :], T[:, 0:128], s_[:, 1:2], HS[:, c, :], ALU.mult, ALU.add, [T, s_, HS], [HS])
            else:
                P.ts(HS[:, c, :], T[:, 0:128], s_[:, 1:2], None, ALU.mult, None, [T, s_], [HS])
                done.add(c)
            P.tr(p_kt[:, 0:64], KT[:, cs_], ident[0:64, 0:64], [KT, ident], [p_kt])
            kw = KW[d]
            P.ts(kw[:], p_kt[:, 0:64], WS[:, d, c:c + 1], None, ALU.mult, None, [p_kt, WS], [kw])
            P.mm(p_st[0:64, 0:129], kw[:], VE[:, c, :], True, True, [kw, VE], [p_st])
            P.stt(Cnew[:], Ccur[:], DEC[0:64, d, c:c + 1], p_st[0:64, 0:129], ALU.mult, ALU.add, [Ccur, DEC, p_st], [Cnew])

    if stop == 3:
        return P.finish() if own else None
    yo = ST
    junk = lrep[0]
    yob = [P.sb([128, 128], BF16, f"yob{i}") for i in range(2)]
    st = [P.sb([128, 4], F32, f"st{i}") for i in range(2)]
    for c in range(SEQT):
        Y, s_ = yo[c % 2], st[c % 2]
        P.act(junk[:], HS[:, c, :], AF.Square, [HS], [junk, s_], accum_out=s_[:, 0:1])
        P.ts(s_[:, 1:2], s_[:, 0:1], 1.0 / 128, EPS, ALU.mult, ALU.add, [s_], [s_])
        P.act(s_[:, 2:3], s_[:, 1:2], AF.Sqrt, [s_], [s_])
        P.op("dve", lambda e, s_=s_: e.reciprocal(s_[:, 3:4], s_[:, 2:3]), [s_], [s_])
        P.stt(Y[:], HS[:, c, :], s_[:, 3:4], ogs[:], ALU.mult, ALU.mult, [HS, s_, ogs], [Y])
        P.tt(Y[:], Y[:], OG[:, c, :], ALU.mult, [Y, OG], [Y])
        P.tr(banks[c % 2][:, 0:128], Y[:], ident[:], [Y, ident], [banks[c % 2]])
        Yb = yob[c % 2]
        P.cp(Yb[:], banks[c % 2][:, 0:128], [banks[c % 2]], [Yb], eng="act")
        P.dma("sp", ycols(ym, c * 128), Yb[:], reads=[Yb])
    return P.finish() if own else None


def col_offsets():
    off, o = {}, 0
    for name, wdt in IN_SPLITS:
        off[name] = o
        o += wdt
    return off


def mlstm_inputs(hTs, li, w_in, m_conv, m_gate_bias, m_out_norm):
    off = col_offsets()
    ins = []
    for b in range(2):
        hT = np.ascontiguousarray(hTs[b].reshape(8, 128, SEQ_ALL))
        for h in range(4):
            W = w_in[li]
            wq = W[:, off['m_q'] + 64 * h: off['m_q'] + 64 * (h + 1)]
            wk = W[:, off['m_k'] + 64 * h: off['m_k'] + 64 * (h + 1)]
            wvo = np.concatenate([W[:, off['m_v'] + 128 * h: off['m_v'] + 128 * (h + 1)],
                                  W[:, off['m_o'] + 128 * h: off['m_o'] + 128 * (h + 1)]], 1)
            wg = np.stack([W[:, off['m_if'] + h], W[:, off['m_ff'] + h], W[:, off['m_ib'] + h], W[:, off['m_fb'] + h]], 1)
            cq = m_conv[li][:, 64 * h:64 * (h + 1)]
            ck = m_conv[li][:, 256 + 64 * h:256 + 64 * (h + 1)]
            cw = np.stack([np.broadcast_to(cq[None], (128, 3, 64)), np.broadcast_to(ck[None], (128, 3, 64))], 0)
            gbv = m_gate_bias[li][:, h]
            ins.append({"hT": hT, "wqk": np.ascontiguousarray(np.stack([wq, wk], 0)), "wvo": np.ascontiguousarray(wvo),
                        "wg": np.ascontiguousarray(wg), "cw": np.ascontiguousarray(cw.astype(np.float32)),
                        "gb": np.ascontiguousarray(np.broadcast_to(gbv[None, :], (128, 4)).astype(np.float32)),
                        "og": np.ascontiguousarray(np.broadcast_to(m_out_norm[li][None, 128 * h:128 * (h + 1)], (128, 128)))})
    return ins


R_GN_EPS = 64e-5
RWKV_INTERLEAVE = True
W_SCALE = -0.6065306597126334


def aff_mask(P, pat, cm, op, val=1.0, base=0):
    m = P.sb([128, 128], F32)
    P.memset(m[:], val, [m], eng="pool")
    P.op("pool", lambda e: e.affine_select(m[:], m[:], pat, op, 0.0, base=base, channel_multiplier=cm), [m], [m])
    return m


def build_rwkv_v1(P=None, T=None):
    own = P is None
    P = P or Prog()
    hT = P.io(T, "hT", [8, 128, SEQ_ALL], BF16, "ExternalInput")
    wrkv = P.io(T, "wrkv", [D, 384], F32, "ExternalInput")
    crkv = P.io(T, "crkv", [128, 3, 384], F32, "ExternalInput")
    wl = P.io(T, "wl", [D, 384], F32, "ExternalInput")
    w2a2 = P.io(T, "w2a2", [2, 128, 128], F32, "ExternalInput")
    bias01 = P.io(T, "bias01", [1, 2, 256], F32, "ExternalInput")
    g2 = P.io(T, "g2", [128, 128], F32, "ExternalInput")
    vecs = P.io(T, "vecs", [128, 5, 128], F32, "ExternalInput")
    yr = P.io(T, "yr", [128, SEQ_ALL], BF16, "ExternalOutput")

    bk = [P.ps([128, 512], F32, f"bank{i}") for i in range(8)]
    ident = make_ident(P)
    ones = P.sb([128, 128], F32, "ones")
    P.memset(ones[:], 1.0, [ones])
    mI = [aff_mask(P, [[1, 128]], -1, ALU.is_ge), aff_mask(P, [[-1, 128]], 1, ALU.is_ge)]
    mS = [aff_mask(P, [[1, 128]], -1, ALU.is_gt), aff_mask(P, [[-1, 128]], 1, ALU.is_gt)]
    cI = [aff_mask(P, [[1, 128]], -1, ALU.is_ge, W_SCALE), aff_mask(P, [[-1, 128]], 1, ALU.is_ge, W_SCALE)]
    cS = [aff_mask(P, [[1, 128]], -1, ALU.is_gt, W_SCALE), aff_mask(P, [[-1, 128]], 1, ALU.is_gt, W_SCALE)]
    mSI = []
    for d in range(2):
        m = P.sb([128, 256], F32)
        P.cp(m[:, 0:128], mS[d][:], [mS[d]], [m])
        P.cp(m[:, 128:256], mI[d][:], [mI[d]], [m])
        mSI.append(m)

    wst = P.sb([128, 8, 768], F32, "wst")
    for kc in range(8):
        P.dma("sp", wst[:, kc, 0:384], wrkv[kc * 128:(kc + 1) * 128, :], writes=[wst])
        P.dma("act", wst[:, kc, 384:768], wl[kc * 128:(kc + 1) * 128, :], writes=[wst])
    cws = P.sb([128, 3, 384], F32, "cws")
    P.dma("sp", cws[:], crkv[:, :, :], writes=[cws])
    Wc = P.sb([128, 3, 8, 384], BF16, "Wc")
    for tap in range(3):
        for kc in range(8):
            P.tt(Wc[:, tap, kc, :], wst[:, kc, 0:384], cws[:, tap, :], ALU.mult, [wst, cws], [Wc])
    Wl = P.sb([128, 8, 384], BF16, "Wl")
    P.cp(Wl[:], wst[:, :, 384:768], [wst], [Wl])
    W2 = P.sb([128, 2, 128], F32, "W2")
    for d in range(2):
        P.dma("act", W2[:, d, :], w2a2[d], writes=[W2])
    B01 = P.sb([1, 2, 256], F32, "B01")
    P.dma("act", B01[:], bias01[:, :, :], writes=[B01])
    G2 = P.sb([128, 128], F32, "G2")
    P.dma("act", G2[:], g2[:, :], writes=[G2])
    VEC = P.sb([128, 5, 128], F32, "VEC")
    P.dma("act", VEC[:], vecs[:, :, :], writes=[VEC])
    epsg = P.sb([128, 1], F32, "epsg")
    P.memset(epsg[:], R_GN_EPS, [epsg])

    YF = P.sb([128, SEQT, 128], F32, "YF")
    hb = [P.sb([128, 8, 130], BF16, f"hb{i}") for i in range(2)]
    STB = P.sb([128, 128], F32, "STB")

    def TL(shape, name):
        return P.sb(shape, F32, name)

    rkv = TL([128, 384], "rkv")
    pl = TL([128, 128], "pl")
    sgg = TL([128, 128], "sgg")
    sig = TL([128, 128], "sig")
    av = TL([128, 128], "av")
    t0_ = TL([128, 128], "t0_")
    ss = TL([128, 8], "ss")
    kkn = TL([128, 128], "kkn")
    bh = TL([128, 128], "bh")
    t2 = TL([128, 128], "t2")
    key = TL([128, 128], "key")
    eG = TL([128, 256], "eG")
    enG = TL([128, 128], "enG")
    eR = TL([128, 128], "eR")
    AR = TL([128, 256], "AR")
    BtT = TL([128, 128], "BtT")
    KtT = TL([128, 128], "KtT")
    Bb = TL([128, 128], "Bb")
    Kb = TL([128, 128], "Kb")
    Mm = [TL([128, 256], f"Mm{i}") for i in range(2)]
    Ak = [TL([128, 256], f"Ak{i}") for i in range(2)]
    Lm = [TL([128, 128], f"Lm{i}") for i in range(2)]
    TtA = [TL([128, 128], f"TtA{i}") for i in range(2)]
    TA = [TL([128, 128], f"TA{i}") for i in range(2)]
    Pp = [[TL([128, 128], f"Pp{i}{j}") for j in range(2)] for i in range(2)]
    Qq = [[TL([128, 128], f"Qq{i}{j}") for j in range(2)] for i in range(2)]
    X = TL([128, 128], "X")
    U = TL([128, 128], "U")
    yb = TL([128, 128], "yb")
    gn = TL([128, 16], "gn")
    yn = TL([128, 128], "yn")
    rk = TL([128, 128], "rk")
    yo = [TL([128, 128], f"yo{i}") for i in range(2)]
    yob = [P.sb([128, 128], BF16, f"yob{i}") for i in range(2)]

    orders = chunk_orders()
    for d in range(2):
        P.memset(STB[:], 0.0, [STB])
        for step in range(SEQT):
            c = orders[d][step]
            t0 = c * 128
            lo, hi = (0, NCTX) if c < CTXT else (NCTX, SEQ_ALL)
            H = hb[step % 2]
            load_hblk_halo(P, hT, H, t0, 128, lo, hi)
            n = 0
            for tap in range(3):
                for kc in range(8):
                    P.mm(bk[0][:, 0:384], H[:, kc, tap:tap + 128], Wc[:, tap, kc, :], n == 0, n == 23, [H, Wc], [bk[0]])
                    n += 1
            P.cp(rkv[:], bk[0][:, 0:384], [bk[0]], [rkv], eng="act")
            for kc in range(8):
                P.mm(bk[1][:, 0:128], Wl[:, kc, d * 128:(d + 1) * 128], H[:, kc, 1:129], kc == 0, kc == 7, [Wl, H], [bk[1]])
            if d == 1:
                for kc in range(8):
                    P.mm(bk[1][:, 128:256], Wl[:, kc, 256:384], H[:, kc, 1:129], kc == 0, kc == 7, [Wl, H], [bk[1]])
            P.act(pl[0:64, :], bk[1][0:64, 0:128], AF.Tanh, [bk[1]], [pl])
            P.cp(pl[64:128, :], bk[1][64:128, 0:128], [bk[1]], [pl])
            if d == 1:
                P.act(sgg[:], bk[1][:, 128:256], AF.Sigmoid, [bk[1]], [sgg])
            P.mm(bk[1][:, 256:384], pl[0:64, :], W2[0:64, d, :], True, False, [pl, W2], [bk[1]])
            P.mm(bk[1][:, 256:384], ones[0:1, :], B01[0:1, d, 0:128], False, True, [ones, B01], [bk[1]])
            P.mm(bk[1][:, 384:512], pl[64:128, :], W2[64:128, d, :], True, False, [pl, W2], [bk[1]])
            P.mm(bk[1][:, 384:512], ones[0:1, :], B01[0:1, d, 128:256], False, True, [ones, B01], [bk[1]])
            P.act(sig[:], bk[1][:, 256:384], AF.Sigmoid, [bk[1]], [sig])
            P.act(av[:], bk[1][:, 384:512], AF.Sigmoid, [bk[1]], [av])
            r_, k_, v_ = rkv[:, 0:128], rkv[:, 128:256], rkv[:, 256:384]
            P.tt(t0_[:], k_, VEC[:, 0, :], ALU.mult, [rkv, VEC], [t0_])
            P.tt(t2[:], t0_[:], t0_[:], ALU.mult, [t0_], [t2])
            for hh in range(2):
                P.op("dve", lambda e, hh=hh: e.tensor_reduce(ss[:, hh:hh + 1], t2[:, hh * 64:(hh + 1) * 64], AX.X, ALU.add),
                     [t2], [ss])
            P.act(ss[:, 2:4], ss[:, 0:2], AF.Sqrt, [ss], [ss])
            P.ts(ss[:, 2:4], ss[:, 2:4], 1e-12, None, ALU.max, None, [ss], [ss])
            P.op("dve", lambda e: e.reciprocal(ss[:, 4:6], ss[:, 2:4]), [ss], [ss])
            for hh in range(2):
                hs = slice(hh * 64, (hh + 1) * 64)
                P.ts(kkn[:, hs], t0_[:, hs], ss[:, 4 + hh:5 + hh], -1.0, ALU.mult, ALU.mult, [t0_, ss], [kkn])
            P.stt(bh[:], kkn[:], -1.0, av[:], ALU.mult, ALU.mult, [kkn, av], [bh])
            P.stt(t2[:], av[:], -1.0, VEC[:, 1, :], ALU.add, ALU.mult, [av, VEC], [t2])
            P.stt(key[:], t2[:], 1.0, k_, ALU.add, ALU.mult, [t2, rkv], [key])
            P.tr(bk[2][:, 0:128], r_, ident[:], [rkv, ident], [bk[2]])
            P.tr(bk[2][:, 128:256], kkn[:], ident[:], [kkn, ident], [bk[2]])
            P.tr(bk[2][:, 256:384], bh[:], ident[:], [bh, ident], [bk[2]])
            P.tr(bk[2][:, 384:512], key[:], ident[:], [key, ident], [bk[2]])
            P.mm(bk[3][:, 0:128], sig[:], cS[d][:], True, True, [sig, cS[d]], [bk[3]])
            P.mm(bk[3][:, 128:256], sig[:], cI[d][:], True, True, [sig, cI[d]], [bk[3]])
            P.mm(bk[3][:, 256:384], cS[1 - d][:], sig[:], True, True, [sig, cS[1 - d]], [bk[3]])
            P.act(eG[:], bk[3][:, 0:256], AF.Exp, [bk[3]], [eG])
            P.act(enG[:], bk[3][:, 128:256], AF.Exp, [bk[3]], [enG], scale=-1.0)
            P.act(eR[:], bk[3][:, 256:384], AF.Exp, [bk[3]], [eR])
            P.tt(AR[:, 0:128], bk[2][:, 128:256], eG[:, 0:128], ALU.mult, [bk[2], eG], [AR])
            P.tt(AR[:, 128:256], bk[2][:, 0:128], eG[:, 128:256], ALU.mult, [bk[2], eG], [AR])
            P.tt(BtT[:], bk[2][:, 256:384], enG[:], ALU.mult, [bk[2], enG], [BtT])
            P.tt(KtT[:], bk[2][:, 384:512], enG[:], ALU.mult, [bk[2], enG], [KtT])
            P.tt(Bb[:], bh[:], eR[:], ALU.mult, [bh, eR], [Bb])
            P.tt(Kb[:], key[:], eR[:], ALU.mult, [key, eR], [Kb])
            for hh in range(2):
                hp_ = slice(hh * 64, (hh + 1) * 64)
                P.mm(bk[4][:, 0:256], BtT[hp_, :], AR[hp_, :], True, True, [BtT, AR], [bk[4]])
                P.mm(bk[5][:, 0:256], KtT[hp_, :], AR[hp_, :], True, True, [KtT, AR], [bk[5]])
                P.mm(bk[4][:, 256:384], AR[hp_, 0:128], BtT[hp_, :], True, True, [AR, BtT], [bk[4]])
                P.tt(Mm[hh][:], bk[4][:, 0:256], mSI[d][:], ALU.mult, [bk[4], mSI[d]], [Mm[hh]])
                P.tt(Ak[hh][:], bk[5][:, 0:256], mSI[d][:], ALU.mult, [bk[5], mSI[d]], [Ak[hh]])
                P.tt(Lm[hh][:], bk[4][:, 256:384], mS[1 - d][:], ALU.mult, [bk[4], mS[1 - d]], [Lm[hh]])
                P.tt(TtA[hh][:], Mm[hh][:, 0:128], ident[:], ALU.add, [Mm[hh], ident], [TtA[hh]], eng="pool")
                P.tt(TA[hh][:], Lm[hh][:], ident[:], ALU.add, [Lm[hh], ident], [TA[hh]], eng="pool")
                Pc, Qc = Mm[hh], Lm[hh]
                pc_ap, qc_ap = Mm[hh][:, 0:128], Lm[hh][:]
                for lvl in range(6):
                    last = lvl == 5
                    Pn, Qn = Pp[hh][lvl % 2], Qq[hh][lvl % 2]
                    P.mm(bk[6][:, 0:128], qc_ap, pc_ap, True, True, [Pc, Qc], [bk[6]])
                    if not last:
                        P.mm(bk[6][:, 128:256], pc_ap, qc_ap, True, True, [Pc, Qc], [bk[6]])
                    P.cp(Pn[:], bk[6][:, 0:128], [bk[6]], [Pn])
                    if not last:
                        P.cp(Qn[:], bk[6][:, 128:256], [bk[6]], [Qn], eng="act")
                    P.mm(bk[6][:, 256:384], TA[hh][:], Pn[:], True, True, [TA[hh], Pn], [bk[6]])
                    if not last:
                        P.mm(bk[6][:, 384:512], Pn[:], TA[hh][:], True, True, [TA[hh], Pn], [bk[6]])
                    P.tt(TtA[hh][:], TtA[hh][:], bk[6][:, 256:384], ALU.add, [TtA[hh], bk[6]], [TtA[hh]])
                    if not last:
                        P.tt(TA[hh][:], TA[hh][:], bk[6][:, 384:512], ALU.add, [TA[hh], bk[6]], [TA[hh]])
                    Pc, Qc = Pn, Qn
                    pc_ap, qc_ap = Pn[:], Qn[:]
            P.mm(bk[7][:, 0:128], AR[:, 0:128], STB[:], True, False, [AR, STB], [bk[7]])
            for hh in range(2):
                hs = slice(hh * 64, (hh + 1) * 64)
                P.mm(bk[7][:, hs], Ak[hh][:, 0:128], rkv[:, 256 + hh * 64:256 + (hh + 1) * 64], False, hh == 1,
                     [Ak[hh], rkv], [bk[7]])
            P.cp(X[:], bk[7][:, 0:128], [bk[7]], [X])
            for hh in range(2):
                hs = slice(hh * 64, (hh + 1) * 64)
                P.mm(bk[7][:, 128 + hh * 64:128 + (hh + 1) * 64], TtA[hh][:], X[:, hs], True, True, [TtA[hh], X], [bk[7]])
            P.cp(U[:], bk[7][:, 128:256], [bk[7]], [U])
            P.mm(bk[7][:, 256:384], AR[:, 128:256], STB[:], True, False, [AR, STB], [bk[7]])
            for hh in range(2):
                ys = slice(256 + hh * 64, 256 + (hh + 1) * 64)
                hs = slice(hh * 64, (hh + 1) * 64)
                P.mm(bk[7][:, ys], Mm[hh][:, 128:256], U[:, hs], False, False, [Mm[hh], U], [bk[7]])
                P.mm(bk[7][:, ys], Ak[hh][:, 128:256], rkv[:, 256 + hh * 64:256 + (hh + 1) * 64], False, hh == 1,
                     [Ak[hh], rkv], [bk[7]])
            P.mm(bk[7][:, 384:512], Bb[:], U[:], True, False, [Bb, U], [bk[7]])
            P.mm(bk[7][:, 384:512], Kb[:], v_, False, True, [Kb, rkv], [bk[7]])
            dcol = 255 if d == 0 else 128
            if d == 0:
                P.cp(YF[:, c, :], bk[7][:, 256:384], [bk[7]], [YF], eng="act")
            else:
                P.tt(yb[:], bk[7][:, 256:384], YF[:, c, :], ALU.add, [bk[7], YF], [yb])
            for hh in range(2):
                hs = slice(hh * 64, (hh + 1) * 64)
                P.stt(STB[hs, hs], STB[hs, hs], eG[hs, dcol:dcol + 1], bk[7][hs, 384 + hh * 64:384 + (hh + 1) * 64],
                      ALU.mult, ALU.add, [STB, eG, bk[7]], [STB])
            if d == 1:
                P.mm(bk[0][:, 384:512], sgg[:], G2[:], True, True, [sgg, G2], [bk[0]])
                P.tt(t2[:], yb[:], yb[:], ALU.mult, [yb], [t2])
                for hh in range(2):
                    hs = slice(hh * 64, (hh + 1) * 64)
                    P.op("dve", lambda e, hh=hh, hs=hs: e.tensor_reduce(gn[:, hh:hh + 1], yb[:, hs], AX.X, ALU.add), [yb], [gn])
                    P.op("dve", lambda e, hh=hh, hs=hs: e.tensor_reduce(gn[:, 2 + hh:3 + hh], t2[:, hs], AX.X, ALU.add), [t2], [gn])
                P.ts(gn[:, 4:8], gn[:, 0:4], 1.0 / 64, None, ALU.mult, None, [gn], [gn])
                P.tt(gn[:, 8:10], gn[:, 4:6], gn[:, 4:6], ALU.mult, [gn], [gn])
                P.tt(gn[:, 10:12], gn[:, 6:8], gn[:, 8:10], ALU.subtract, [gn], [gn])
                P.act(gn[:, 12:14], gn[:, 10:12], AF.Sqrt, [gn, epsg], [gn], bias=epsg[:, 0:1])
                P.op("dve", lambda e: e.reciprocal(gn[:, 14:16], gn[:, 12:14]), [gn], [gn])
                for hh in range(2):
                    hs = slice(hh * 64, (hh + 1) * 64)
                    P.ts(yn[:, hs], yb[:, hs], gn[:, 4 + hh:5 + hh], gn[:, 14 + hh:15 + hh], ALU.subtract, ALU.mult,
                         [yb, gn], [yn])
                P.tt(yn[:], yn[:], VEC[:, 3, :], ALU.mult, [yn, VEC], [yn])
                P.tt(yn[:], yn[:], VEC[:, 4, :], ALU.add, [yn, VEC], [yn])
                P.tt(rk[:], r_, k_, ALU.mult, [rkv], [rk])
                P.tt(rk[:], rk[:], VEC[:, 2, :], ALU.mult, [rk, VEC], [rk])
                for hh in range(2):
                    hs = slice(hh * 64, (hh + 1) * 64)
                    P.op("dve", lambda e, hh=hh, hs=hs: e.tensor_reduce(ss[:, 6 + hh:7 + hh], rk[:, hs], AX.X, ALU.add), [rk], [ss])
                    P.stt(yn[:, hs], rkv[:, 256 + hh * 64:256 + (hh + 1) * 64], ss[:, 6 + hh:7 + hh], yn[:, hs], ALU.mult, ALU.add,
                          [rkv, ss, yn], [yn])
                Y = yo[step % 2]
                P.tt(Y[:], yn[:], bk[0][:, 384:512], ALU.mult, [yn, bk[0]], [Y])
                P.tr(bk[3][:, 384:512], Y[:], ident[:], [Y, ident], [bk[3]])
                Yb = yob[step % 2]
                P.cp(Yb[:], bk[3][:, 384:512], [bk[3]], [Yb], eng="act")
                P.dma("sp", ycols(yr, t0), Yb[:], reads=[Yb])
    return P.finish() if own else None


def build_rwkv(P=None, T=None):
    own = P is None
    P = P or Prog()
    hT = P.io(T, "hT", [8, 128, SEQ_ALL], BF16, "ExternalInput")
    wrkv = P.io(T, "wrkv", [D, 384], F32, "ExternalInput")
    crkv = P.io(T, "crkv", [128, 3, 384], F32, "ExternalInput")
    wl = P.io(T, "wl", [D, 384], F32, "ExternalInput")
    w2a2 = P.io(T, "w2a2", [2, 128, 128], F32, "ExternalInput")
    bias01 = P.io(T, "bias01", [1, 2, 256], F32, "ExternalInput")
    g2 = P.io(T, "g2", [128, 128], F32, "ExternalInput")
    vecs = P.io(T, "vecs", [128, 5, 128], F32, "ExternalInput")
    yr = P.io(T, "yr", [128, SEQ_ALL], BF16, "ExternalOutput")

    bk = [P.ps([128, 512], F32, f"bank{i}") for i in range(8)]
    ident = make_ident(P)
    ones = P.sb([128, 128], F32, "ones")
    P.memset(ones[:], 1.0, [ones])
    mI = [aff_mask(P, [[1, 128]], -1, ALU.is_ge), aff_mask(P, [[-1, 128]], 1, ALU.is_ge)]
    mS = [aff_mask(P, [[1, 128]], -1, ALU.is_gt), aff_mask(P, [[-1, 128]], 1, ALU.is_gt)]
    cI = [aff_mask(P, [[1, 128]], -1, ALU.is_ge, W_SCALE), aff_mask(P, [[-1, 128]], 1, ALU.is_ge, W_SCALE)]
    cS = [aff_mask(P, [[1, 128]], -1, ALU.is_gt, W_SCALE), aff_mask(P, [[-1, 128]], 1, ALU.is_gt, W_SCALE)]
    mSI2 = P.sb([128, 2, 2, 256], F32, "mSI2")
    mL4 = P.sb([128, 4, 128], F32, "mL4")
    I4 = P.sb([128, 4, 128], F32, "I4")
    for d in range(2):
        for hh in range(2):
            P.cp(mSI2[:, d, hh, 0:128], mS[d][:], [mS[d]], [mSI2])
            P.cp(mSI2[:, d, hh, 128:256], mI[d][:], [mI[d]], [mSI2])
            P.cp(mL4[:, d * 2 + hh, :], mS[1 - d][:], [mS[1 - d]], [mL4])
            P.cp(I4[:, d * 2 + hh, :], ident[:], [ident], [I4])

    wst = P.sb([128, 8, 384], F32, "wst")
    for kc in range(8):
        P.dma("sp", wst[:, kc, :], wrkv[kc * 128:(kc + 1) * 128, :], writes=[wst])
    cws = P.sb([128, 3, 384], F32, "cws")
    P.dma("sp", cws[:], crkv[:, :, :], writes=[cws])
    Wc = P.sb([128, 3, 8, 384], BF16, "Wc")
    for tap in range(3):
        for kc in range(8):
            P.tt(Wc[:, tap, kc, :], wst[:, kc, :], cws[:, tap, :], ALU.mult, [wst, cws], [Wc])
    Wl = P.sb([128, 8, 384], BF16, "Wl")
    for kc in range(8):
        P.dma("act", wst[:, kc, :], wl[kc * 128:(kc + 1) * 128, :], writes=[wst])
    P.cp(Wl[:], wst[:], [wst], [Wl])
    W2 = P.sb([128, 2, 128], F32, "W2")
    for d in range(2):
        P.dma("act", W2[:, d, :], w2a2[d], writes=[W2])
    B01 = P.sb([1, 2, 256], F32, "B01")
    P.dma("act", B01[:], bias01[:, :, :], writes=[B01])
    G2 = P.sb([128, 128], F32, "G2")
    P.dma("act", G2[:], g2[:, :], writes=[G2])
    VEC = P.sb([128, 5, 128], F32, "VEC")
    P.dma("act", VEC[:], vecs[:, :, :], writes=[VEC])
    VK2 = P.sb([128, 2, 2, 128], F32, "VK2")
    for j in range(2):
        for d in range(2):
            P.cp(VK2[:, j, d, :], VEC[:, j, :], [VEC], [VK2])
    epsg = P.sb([128, 1], F32, "epsg")
    P.memset(epsg[:], R_GN_EPS, [epsg])

    YD = [P.sb([128, SEQT, 128], BF16, f"YD{d}") for d in range(2)]
    hb = [P.sb([128, 8, 130], BF16, f"hb{i}") for i in range(4 if RWKV_INTERLEAVE else 2)]
    STB = [P.sb([128, 128], F32, f"STB{d}") for d in range(2)]
    for d in range(2):
        P.memset(STB[d][:], 0.0, [STB[d]])

    def TL(shape, name):
        return P.sb(shape, F32, name)

    NB_ = 2 if RWKV_INTERLEAVE else 1
    rkv_ = [TL([128, 2, 384], f"rkv{i}") for i in range(NB_)]
    pl = TL([128, 2, 128], "pl")
    sig = TL([128, 2, 128], "sig")
    av = TL([128, 2, 128], "av")
    t0_ = TL([128, 2, 128], "t0_")
    t2 = TL([128, 2, 128], "t2")
    ss = TL([128, 16], "ss")
    kkn = TL([128, 2, 128], "kkn")
    bh = TL([128, 2, 128], "bh")
    key = TL([128, 2, 128], "key")
    eG_ = [TL([128, 2, 256], f"eG{i}") for i in range(NB_)]
    enG = TL([128, 2, 128], "enG")
    eR = TL([128, 2, 128], "eR")
    AR_ = [TL([128, 2, 256], f"AR{i}") for i in range(NB_)]
    BtT = TL([128, 2, 128], "BtT")
    KtT = TL([128, 2, 128], "KtT")
    Bb_ = [TL([128, 2, 128], f"Bb{i}") for i in range(NB_)]
    Kb_ = [TL([128, 2, 128], f"Kb{i}") for i in range(NB_)]
    MA_ = [TL([128, 2, 2, 256], f"MA{i}") for i in range(NB_)]
    AK_ = [TL([128, 2, 2, 256], f"AK{i}") for i in range(NB_)]
    L4_ = [TL([128, 4, 128], f"L4{i}") for i in range(NB_)]
    P4 = [TL([128, 4, 128], f"P4{i}") for i in range(2)]
    Q4 = [TL([128, 4, 128], f"Q4{i}") for i in range(2)]
    TtA = TL([128, 4, 128], "TtA")
    TA = TL([128, 4, 128], "TA")
    X2 = TL([128, 2, 128], "X2")
    U2 = TL([128, 2, 128], "U2")

    orders = chunk_orders()

    def pre(step):
        par = (step % 2) if RWKV_INTERLEAVE else 0
        rkv, eG, AR, Bb, Kb, MA, AK, L4 = rkv_[par], eG_[par], AR_[par], Bb_[par], Kb_[par], MA_[par], AK_[par], L4_[par]
        cc = [orders[0][step], orders[1][step]]
        for d in range(2):
            c = cc[d]
            lo, hi = (0, NCTX) if c < CTXT else (NCTX, SEQ_ALL)
            H = hb[(par * 2 + d) % len(hb)]
            load_hblk_halo(P, hT, H, c * 128, 128, lo, hi, eng="sp" if d == 0 else "act")
            n = 0
            for tap in range(3):
                for kc in range(8):
                    P.mm(bk[d][:, 0:384], H[:, kc, tap:tap + 128], Wc[:, tap, kc, :], n == 0, n == 23, [H, Wc], [bk[d]])
                    n += 1
            for kc in range(8):
                P.mm(bk[d][:, 384:512], Wl[:, kc, d * 128:(d + 1) * 128], H[:, kc, 1:129], kc == 0, kc == 7, [Wl, H], [bk[d]])
            P.cp(rkv[:, d, :], bk[d][:, 0:384], [bk[d]], [rkv], eng="act")
            P.act(pl[0:64, d, :], bk[d][0:64, 384:512], AF.Tanh, [bk[d]], [pl])
            P.cp(pl[64:128, d, :], bk[d][64:128, 384:512], [bk[d]], [pl])
        for d in range(2):
            P.mm(bk[2][:, d * 256:d * 256 + 128], pl[0:64, d, :], W2[0:64, d, :], True, False, [pl, W2], [bk[2]])
            P.mm(bk[2][:, d * 256:d * 256 + 128], ones[0:1, :], B01[0:1, d, 0:128], False, True, [ones, B01], [bk[2]])
            P.mm(bk[2][:, d * 256 + 128:d * 256 + 256], pl[64:128, d, :], W2[64:128, d, :], True, False, [pl, W2], [bk[2]])
            P.mm(bk[2][:, d * 256 + 128:d * 256 + 256], ones[0:1, :], B01[0:1, d, 128:256], False, True, [ones, B01], [bk[2]])
        b2v = bk[2][:, :].rearrange("p (d j c) -> p d j c", d=2, j=2)
        P.act(sig[:], b2v[:, :, 0, :], AF.Sigmoid, [bk[2]], [sig])
        P.act(av[:], b2v[:, :, 1, :], AF.Sigmoid, [bk[2]], [av])
        k2 = rkv[:, :, 128:256]
        P.tt(t0_[:], k2, VK2[:, 0], ALU.mult, [rkv, VK2], [t0_])
        P.tt(t2[:], t0_[:], t0_[:], ALU.mult, [t0_], [t2])
        P.op("dve", lambda e: e.tensor_reduce(ss[:, 0:4], t2[:, :, :].rearrange("p d (h k) -> p (d h) k", h=2), AX.X, ALU.add),
             [t2], [ss])
        P.act(ss[:, 4:8], ss[:, 0:4], AF.Sqrt, [ss], [ss])
        P.ts(ss[:, 4:8], ss[:, 4:8], 1e-12, None, ALU.max, None, [ss], [ss])
        P.op("dve", lambda e: e.reciprocal(ss[:, 8:12], ss[:, 4:8]), [ss], [ss])
        for d in range(2):
            for hh in range(2):
                hs = slice(hh * 64, (hh + 1) * 64)
                q = d * 2 + hh
                P.ts(kkn[:, d, hs], t0_[:, d, hs], ss[:, 8 + q:9 + q], -1.0, ALU.mult, ALU.mult, [t0_, ss], [kkn])
        P.stt(bh[:], kkn[:], -1.0, av[:], ALU.mult, ALU.mult, [kkn, av], [bh])
        P.stt(t2[:], av[:], -1.0, VK2[:, 1], ALU.add, ALU.mult, [av, VK2], [t2])
        P.stt(key[:], t2[:], 1.0, k2, ALU.add, ALU.mult, [t2, rkv], [key])
        for d in range(2):
            tb = bk[3] if d == 0 else bk[7]
            P.tr(tb[:, 0:128], rkv[:, d, 0:128], ident[:], [rkv, ident], [tb])
            P.tr(tb[:, 128:256], kkn[:, d, :], ident[:], [kkn, ident], [tb])
            P.tr(tb[:, 256:384], bh[:, d, :], ident[:], [bh, ident], [tb])
            P.tr(tb[:, 384:512], key[:, d, :], ident[:], [key, ident], [tb])
            gb_ = bk[d]
            P.mm(gb_[:, 0:128], sig[:, d, :], cS[d][:], True, True, [sig, cS[d]], [gb_])
            P.mm(gb_[:, 128:256], sig[:, d, :], cI[d][:], True, True, [sig, cI[d]], [gb_])
            P.mm(gb_[:, 256:384], cS[1 - d][:], sig[:, d, :], True, True, [sig, cS[1 - d]], [gb_])
        for d in range(2):
            tb, gb_ = (bk[3] if d == 0 else bk[7]), bk[d]
            P.act(eG[:, d, :], gb_[:, 0:256], AF.Exp, [gb_], [eG])
            P.act(enG[:, d, :], gb_[:, 128:256], AF.Exp, [gb_], [enG], scale=-1.0)
            P.act(eR[:, d, :], gb_[:, 256:384], AF.Exp, [gb_], [eR])
            P.tt(AR[:, d, 0:128], tb[:, 128:256], eG[:, d, 0:128], ALU.mult, [tb, eG], [AR])
            P.tt(AR[:, d, 128:256], tb[:, 0:128], eG[:, d, 128:256], ALU.mult, [tb, eG], [AR])
            P.tt(BtT[:, d, :], tb[:, 256:384], enG[:, d, :], ALU.mult, [tb, enG], [BtT])
            P.tt(KtT[:, d, :], tb[:, 384:512], enG[:, d, :], ALU.mult, [tb, enG], [KtT])
        P.tt(Bb[:], bh[:], eR[:], ALU.mult, [bh, eR], [Bb], eng="pool")
        P.tt(Kb[:], key[:], eR[:], ALU.mult, [key, eR], [Kb], eng="pool")
        for d in range(2):
            for hh in range(2):
                hp_ = slice(hh * 64, (hh + 1) * 64)
                q = d * 2 + hh
                mb = bk[d]
                P.mm(mb[:, hh * 256:(hh + 1) * 256], BtT[hp_, d, :], AR[hp_, d, :], True, True, [BtT, AR], [mb])
                ab = bk[2] if d == 0 else bk[7]
                P.mm(ab[:, hh * 256:(hh + 1) * 256], KtT[hp_, d, :], AR[hp_, d, :], True, True, [KtT, AR], [ab])
                P.mm(bk[3][:, q * 128:(q + 1) * 128], AR[hp_, d, 0:128], BtT[hp_, d, :], True, True, [AR, BtT], [bk[3]])
        for d in range(2):
            mb, ab = bk[d], (bk[2] if d == 0 else bk[7])
            P.tt(MA[:, d], mb[:, :].rearrange("p (h c) -> p h c", h=2), mSI2[:, d], ALU.mult, [mb, mSI2], [MA])
            P.tt(AK[:, d], ab[:, :].rearrange("p (h c) -> p h c", h=2), mSI2[:, d], ALU.mult, [ab, mSI2], [AK])
        P.tt(L4[:], bk[3][:, :].rearrange("p (q c) -> p q c", q=4), mL4[:], ALU.mult, [bk[3], mL4], [L4])

    def inv_chain(step, fill):
        par = (step % 2) if RWKV_INTERLEAVE else 0
        rkv, eG, AR, Bb, Kb, MA, AK, L4 = rkv_[par], eG_[par], AR_[par], Bb_[par], Kb_[par], MA_[par], AK_[par], L4_[par]
        cc = [orders[0][step], orders[1][step]]
        npts = 16
        per = (len(fill) + npts - 1) // npts if fill else 0
        pos = [0]

        def filler():
            if not RWKV_INTERLEAVE:
                return
            for th in fill[pos[0]:pos[0] + per]:
                th()
            pos[0] += per

        M4 = MA[:, :, :, 0:128].rearrange("p d h c -> p (d h) c")
        P.tt(TtA[:], M4, I4[:], ALU.add, [MA, I4], [TtA], eng="pool")
        P.tt(TA[:], L4[:], I4[:], ALU.add, [L4, I4], [TA], eng="pool")
        Pc_buf, Qc_buf = MA, L4
        pc = lambda q: MA[:, q // 2, q % 2, 0:128]
        qc = lambda q: L4[:, q, :]
        bP, bQ, bT, bTT = bk[4], bk[5], bk[6], bk[4]
        for lvl in range(6):
            last = lvl == 5
            Pn, Qn = P4[lvl % 2], Q4[lvl % 2]
            for q in range(4):
                P.mm(bP[:, q * 128:(q + 1) * 128], qc(q), pc(q), True, True, [Pc_buf, Qc_buf], [bP])
            if not last:
                for q in range(4):
                    P.mm(bQ[:, q * 128:(q + 1) * 128], pc(q), qc(q), True, True, [Pc_buf, Qc_buf], [bQ])
            P.cp(Pn[:], bP[:, :].rearrange("p (q c) -> p q c", q=4), [bP], [Pn])
            if not last:
                P.cp(Qn[:], bQ[:, :].rearrange("p (q c) -> p q c", q=4), [bQ], [Qn], eng="act")
            filler()
            for q in range(4):
                P.mm(bT[:, q * 128:(q + 1) * 128], TA[:, q, :], Pn[:, q, :], True, True, [TA, Pn], [bT])
            if not last:
                for q in range(4):
                    P.mm(bTT[:, q * 128:(q + 1) * 128], Pn[:, q, :], TA[:, q, :], True, True, [TA, Pn], [bTT])
            P.tt(TtA[:], TtA[:], bT[:, :].rearrange("p (q c) -> p q c", q=4), ALU.add, [TtA, bT], [TtA])
            if not last:
                P.tt(TA[:], TA[:], bTT[:, :].rearrange("p (q c) -> p q c", q=4), ALU.add, [TA, bTT], [TA])
            filler()
            Pc_buf, Qc_buf = Pn, Qn
            pc = lambda q, Pn=Pn: Pn[:, q, :]
            qc = lambda q, Qn=Qn: Qn[:, q, :]
        bXU, bYS = bk[5], bk[6]
        for d in range(2):
            P.mm(bXU[:, d * 128:(d + 1) * 128], AR[:, d, 0:128], STB[d][:], True, False, [AR, STB[d]], [bXU])
            for hh in range(2):
                o = d * 128 + hh * 64
                P.mm(bXU[:, o:o + 64], AK[:, d, hh, 0:128], rkv[:, d, 256 + hh * 64:256 + (hh + 1) * 64], False, hh == 1,
                     [AK, rkv], [bXU])
        P.cp(X2[:], bXU[:, 0:256].rearrange("p (d c) -> p d c", d=2), [bXU], [X2])
        filler()
        for d in range(2):
            for hh in range(2):
                o = 256 + d * 128 + hh * 64
                P.mm(bXU[:, o:o + 64], TtA[:, d * 2 + hh, :], X2[:, d, hh * 64:(hh + 1) * 64], True, True, [TtA, X2], [bXU])
        P.cp(U2[:], bXU[:, 256:512].rearrange("p (d c) -> p d c", d=2), [bXU], [U2])
        filler()
        for d in range(2):
            P.mm(bYS[:, d * 128:(d + 1) * 128], AR[:, d, 128:256], STB[d][:], True, False, [AR, STB[d]], [bYS])
            for hh in range(2):
                o = d * 128 + hh * 64
                hs = slice(hh * 64, (hh + 1) * 64)
                P.mm(bYS[:, o:o + 64], MA[:, d, hh, 128:256], U2[:, d, hs], False, False, [MA, U2], [bYS])
                P.mm(bYS[:, o:o + 64], AK[:, d, hh, 128:256], rkv[:, d, 256 + hh * 64:256 + (hh + 1) * 64], False, hh == 1,
                     [AK, rkv], [bYS])
        for d in range(2):
            P.mm(bYS[:, 256 + d * 128:256 + (d + 1) * 128], Bb[:, d, :], U2[:, d, :], True, False, [Bb, U2], [bYS])
            P.mm(bYS[:, 256 + d * 128:256 + (d + 1) * 128], Kb[:, d, :], rkv[:, d, 256:384], False, True, [Kb, rkv], [bYS])
        for d in range(2):
            P.cp(YD[d][:, cc[d], :], bYS[:, d * 128:(d + 1) * 128], [bYS], [YD[d]], eng="act")
            dcol = 255 if d == 0 else 128
            for hh in range(2):
                hs = slice(hh * 64, (hh + 1) * 64)
                o = 256 + d * 128 + hh * 64
                P.stt(STB[d][hs, hs], STB[d][hs, hs], eG[hs, d, dcol:dcol + 1], bYS[hs, o:o + 64], ALU.mult, ALU.add,
                      [STB[d], eG, bYS], [STB[d]])
        filler()
        for th in fill[pos[0]:]:
            th()

    pre(0)
    for step in range(SEQT):
        fill = []
        if RWKV_INTERLEAVE and step + 1 < SEQT:
            P.defer = []
            pre(step + 1)
            fill, P.defer = P.defer, None
        inv_chain(step, fill)
        if not RWKV_INTERLEAVE and step + 1 < SEQT:
            pre(step + 1)

    NT_ = 4
    rk2 = [TL([128, 384], f"rk2{i}") for i in range(NT_)]
    sgg = [TL([128, 128], f"sgg{i}") for i in range(NT_)]
    ybs = [TL([128, 128], f"ybs{i}") for i in range(NT_)]
    tq = [TL([128, 128], f"tq{i}") for i in range(NT_)]
    gn = [TL([128, 16], f"gn{i}") for i in range(NT_)]
    yn = [TL([128, 128], f"yn{i}") for i in range(NT_)]
    rk = [TL([128, 128], f"rk{i}") for i in range(NT_)]
    yo = [TL([128, 128], f"yo{i}") for i in range(NT_)]
    yob = [P.sb([128, 128], BF16, f"yob{i}") for i in range(NT_)]
    hbo = [P.sb([128, 8, 130], BF16, f"hbo{i}") for i in range(NT_)]

    def out_tile(c):
        pi = c % NT_
        lo, hi = (0, NCTX) if c < CTXT else (NCTX, SEQ_ALL)
        H = hbo[pi]
        load_hblk_halo(P, hT, H, c * 128, 128, lo, hi)
        b0, b1 = bk[pi * 2], bk[pi * 2 + 1]
        n = 0
        for tap in range(3):
            for kc in range(8):
                P.mm(b0[:, 0:384], H[:, kc, tap:tap + 128], Wc[:, tap, kc, :], n == 0, n == 23, [H, Wc], [b0])
                n += 1
        for kc in range(8):
            P.mm(b1[:, 0:128], Wl[:, kc, 256:384], H[:, kc, 1:129], kc == 0, kc == 7, [Wl, H], [b1])
        R_ = rk2[pi]
        P.cp(R_[:], b0[:, 0:384], [b0], [R_], eng="act")
        P.act(sgg[pi][:], b1[:, 0:128], AF.Sigmoid, [b1], [sgg[pi]])
        P.mm(b1[:, 128:256], sgg[pi][:], G2[:], True, True, [sgg[pi], G2], [b1])
        yb, t2_, g_, y_ = ybs[pi], tq[pi], gn[pi], yn[pi]
        P.tt(yb[:], YD[0][:, c, :], YD[1][:, c, :], ALU.add, [YD[0], YD[1]], [yb])
        P.tt(t2_[:], yb[:], yb[:], ALU.mult, [yb], [t2_], eng="pool")
        P.op("dve", lambda e, g_=g_, yb=yb: e.tensor_reduce(g_[:, 0:2], yb[:, :].rearrange("p (h k) -> p h k", h=2), AX.X, ALU.add),
             [yb], [g_])
        P.op("dve", lambda e, g_=g_, t2_=t2_: e.tensor_reduce(g_[:, 2:4], t2_[:, :].rearrange("p (h k) -> p h k", h=2), AX.X, ALU.add),
             [t2_], [g_])
        P.ts(g_[:, 4:8], g_[:, 0:4], 1.0 / 64, None, ALU.mult, None, [g_], [g_])
        P.tt(g_[:, 8:10], g_[:, 4:6], g_[:, 4:6], ALU.mult, [g_], [g_])
        P.tt(g_[:, 10:12], g_[:, 6:8], g_[:, 8:10], ALU.subtract, [g_], [g_])
        P.act(g_[:, 12:14], g_[:, 10:12], AF.Sqrt, [g_, epsg], [g_], bias=epsg[:, 0:1])
        P.op("dve", lambda e, g_=g_: e.reciprocal(g_[:, 14:16], g_[:, 12:14]), [g_], [g_])
        for hh in range(2):
            hs = slice(hh * 64, (hh + 1) * 64)
            P.ts(y_[:, hs], yb[:, hs], g_[:, 4 + hh:5 + hh], g_[:, 14 + hh:15 + hh], ALU.subtract, ALU.mult, [yb, g_], [y_])
        P.tt(y_[:], y_[:], VEC[:, 3, :], ALU.mult, [y_, VEC], [y_])
        P.tt(y_[:], y_[:], VEC[:, 4, :], ALU.add, [y_, VEC], [y_])
        rk_ = rk[pi]
        P.tt(rk_[:], R_[:, 0:128], R_[:, 128:256], ALU.mult, [R_], [rk_], eng="pool")
        P.tt(rk_[:], rk_[:], VEC[:, 2, :], ALU.mult, [rk_, VEC], [rk_], eng="pool")
        P.op("dve", lambda e, g_=g_, rk_=rk_: e.tensor_reduce(g_[:, 0:2], rk_[:, :].rearrange("p (h k) -> p h k", h=2), AX.X, ALU.add),
             [rk_], [g_])
        for hh in range(2):
            hs = slice(hh * 64, (hh + 1) * 64)
            P.stt(y_[:, hs], R_[:, 256 + hh * 64:256 + (hh + 1) * 64], g_[:, hh:hh + 1], y_[:, hs], ALU.mult, ALU.add,
                  [R_, g_, y_], [y_])
        Y = yo[pi]
        P.tt(Y[:], y_[:], b1[:, 128:256], ALU.mult, [y_, b1], [Y])
        P.tr(b1[:, 256:384], Y[:], ident[:], [Y, ident], [b1])
        Yb = yob[pi]
        P.cp(Yb[:], b1[:, 256:384], [b1], [Yb], eng="act")
        P.dma("sp", ycols(yr, c * 128), Yb[:], reads=[Yb])

    for c0 in range(0, SEQT, NT_):
        streams = []
        for c in range(c0, min(SEQT, c0 + NT_)):
            P.defer = []
            out_tile(c)
            streams.append(P.defer)
            P.defer = None
        for k in range(max(len(st_) for st_ in streams)):
            for st_ in streams:
                if k < len(st_):
                    st_[k]()
    return P.finish() if own else None


def rwkv_inputs(hTs, li, w_in, r_conv, r_w0, r_w2, r_a0, r_a2, r_g2, r_kk, r_ka, r_rk, r_ln_w, r_ln_b):
    off = col_offsets()
    ins = []
    W = w_in[li]
    for b in range(2):
        hT = np.ascontiguousarray(hTs[b].reshape(8, 128, SEQ_ALL))
        for hp in range(4):
            cs_ = slice(128 * hp, 128 * (hp + 1))
            wrkv = np.concatenate([W[:, off[n] + 128 * hp: off[n] + 128 * (hp + 1)] for n in ('r_r', 'r_k', 'r_v')], 1)
            conv = np.concatenate([r_conv[li][:, j * 512 + 128 * hp: j * 512 + 128 * (hp + 1)] for j in range(3)], 1)
            wl = np.concatenate([W[:, off['r_wf']:off['r_wf'] + 64], W[:, off['r_af']:off['r_af'] + 64],
                                 W[:, off['r_wb']:off['r_wb'] + 64], W[:, off['r_ab']:off['r_ab'] + 64],
                                 W[:, off['r_g']:off['r_g'] + 128]], 1)
            w2a2 = np.stack([np.concatenate([r_w2[li, d][:, cs_], r_a2[li, d][:, cs_]], 0) for d in range(2)], 0)
            bias01 = np.stack([np.concatenate([r_w0[li, d][cs_], r_a0[li, d][cs_]], 0) for d in range(2)], 0)[None]
            vecs = np.stack([np.broadcast_to(v[li][None, cs_], (128, 128)) for v in (r_kk, r_ka, r_rk, r_ln_w, r_ln_b)], 1)
            ins.append({"hT": hT, "wrkv": np.ascontiguousarray(wrkv),
                        "crkv": np.ascontiguousarray(np.broadcast_to(conv[None], (128, 3, 384)).astype(np.float32)),
                        "wl": np.ascontiguousarray(wl), "w2a2": np.ascontiguousarray(w2a2.astype(np.float32)),
                        "bias01": np.ascontiguousarray(bias01.astype(np.float32)),
                        "g2": np.ascontiguousarray(r_g2[li][:, cs_]), "vecs": np.ascontiguousarray(vecs.astype(np.float32))})
    return ins


def build_merge(P=None, T=None, fused=False):
    own = P is None
    P = P or Prog()
    x = P.io(T, "x", [NT, 128, D], F32, "ExternalInput")
    hT = P.io(T, "hT", [8, 128, TOK], BF16, "ExternalInput")
    if fused:
        yall = T["yall"]
        sel = T["sel"]
    else:
        yT = P.io(T, "yT", [12, 128, TOK], BF16, "ExternalInput")
    wg = P.io(T, "wg", [D, 3 * D], F32, "ExternalInput")
    wb = P.io(T, "wb", [3, 512, D], F32, "ExternalInput")
    wo = P.io(T, "wo", [D, D], F32, "ExternalInput")
    gateb = P.io(T, "gateb", [2, 128, D], F32, "ExternalInput")
    xo = P.io(T, "xo", [NT, 128, D], F32, "ExternalOutput")

    Wg = P.sb([128, 8, 3 * D], BF16, "Wg")
    Pb = P.sb([128, 12, D], BF16, "Pb")
    Wo = P.sb([128, 8, D], BF16, "Wo")
    stage = [P.sb([128, 1024], F32, f"stg{i}") for i in range(2)]
    n = 0
    for kc in range(8):
        for q in range(3):
            load_cast(P, Wg, Wg[:, kc, q * 1024:(q + 1) * 1024], wg[kc * 128:(kc + 1) * 128, q * 1024:(q + 1) * 1024],
                      stage, None, n, 1024)
            n += 1
    for br in range(3):
        for c in range(4):
            load_cast(P, Pb, Pb[:, br * 4 + c, :], wb[br, c * 128:(c + 1) * 128, :], stage, None, n, 1024)
            n += 1
    for kc in range(8):
        load_cast(P, Wo, Wo[:, kc, :], wo[kc * 128:(kc + 1) * 128, :], stage, None, n, 1024)
        n += 1
    gates = P.sb([128, 2, D], F32, "gates")
    for s in range(2):
        P.dma("act", gates[:, s, :], gateb[s], writes=[gates])

    X = P.sb([128, 3, D], F32, "X")
    hb = P.sb([128, 8, 384], BF16, "hb")
    yb = P.sb([128, 12, 384], BF16, "yb")
    if fused:
        yc = [P.sb([128, 12, 384], BF16, f"yc{i}") for i in range(4)]
        sels = P.sb([128, 4], F32, "sels")
        P.dma("act", sels[:], sel[:, :], writes=[sels])

    zT = P.sb([128, 8, 384], BF16, "zT")
    sg = [P.sb([128, 384], F32, f"sg{i}") for i in range(2)]
    za = P.sb([128, 384], F32, "za")
    tm = [P.sb([128, 384], F32, f"tm{i}") for i in range(2)]
    tmp = [P.sb([128, 512], F32, f"tmp{i}") for i in range(2)]
    pg = [P.ps([128, 512], F32, f"pg{i}") for i in range(2)]
    pp = [P.ps([128, 512], F32, f"pp{i}") for i in range(2)]
    py = [P.ps([128, 512], F32, f"py{i}") for i in range(2)]

    k = 0
    for bi, (t0, nb) in enumerate(BLOCKS):
        s = 0 if bi == 0 else 1
        N = nb * 128
        for i in range(nb):
            P.dma("sp", X[:, i, :], x[t0 + i], writes=[X])
        for kc in range(8):
            P.dma("act", hb[:, kc, 0:N], hT[kc, :, t0 * 128:t0 * 128 + N], writes=[hb])
        if fused:
            for cand in range(4):
                for c in range(12):
                    j, br = c % 4, c // 4
                    P.dma("sp" if c % 2 else "act", yc[cand][:, c, 0:N],
                          yall[br, cand, j * 128:(j + 1) * 128, t0 * 128:t0 * 128 + N], writes=[yc[cand]])
            P.ts(yb[:, :, 0:N], yc[0][:, :, 0:N], sels[:, 0:1], None, ALU.mult, None, [yc[0], sels], [yb])
            for cand in range(1, 4):
                P.stt(yb[:, :, 0:N], yc[cand][:, :, 0:N], sels[:, cand:cand + 1], yb[:, :, 0:N], ALU.mult, ALU.add,
                      [yc[cand], sels, yb], [yb])
        else:
            for c in range(12):
                P.dma("sp" if c % 2 else "act", yb[:, c, 0:N], yT[c, :, t0 * 128:t0 * 128 + N], writes=[yb])
        for dc in range(8):
            for br in range(3):
                G, Q = pg[k % 2], pp[k % 2]
                S, T_ = sg[k % 2], tm[k % 2]
                k += 1
                for kc in range(8):
                    P.mm(G[:, 0:N], Wg[:, kc, br * D + dc * 128: br * D + (dc + 1) * 128], hb[:, kc, 0:N], kc == 0, kc == 7,
                         [Wg, hb], [G])
                for c in range(4):
                    P.mm(Q[:, 0:N], Pb[:, br * 4 + c, dc * 128:(dc + 1) * 128], yb[:, br * 4 + c, 0:N], c == 0, c == 3,
                         [Pb, yb], [Q])
                P.act(S[:, 0:N], G[:, 0:N], AF.Sigmoid, [G], [S])
                if br == 0:
                    P.tt(za[:, 0:N], S[:, 0:N], Q[:, 0:N], ALU.mult, [S, Q], [za])
                else:
                    P.tt(T_[:, 0:N], S[:, 0:N], Q[:, 0:N], ALU.mult, [S, Q], [T_])
                    if br == 1:
                        P.tt(za[:, 0:N], za[:, 0:N], T_[:, 0:N], ALU.add, [za, T_], [za], eng="pool")
                    else:
                        P.tt(zT[:, dc, 0:N], za[:, 0:N], T_[:, 0:N], ALU.add, [za, T_], [zT])
        for i in range(nb):
            for h in range(2):
                Y = py[(i * 2 + h) % 2]
                for dc in range(8):
                    P.mm(Y[:, :], zT[:, dc, i * 128:(i + 1) * 128], Wo[:, dc, h * 512:(h + 1) * 512], dc == 0, dc == 7,
                         [zT, Wo], [Y])
                T2 = tmp[(i * 2 + h) % 2]
                P.tt(T2[:], Y[:], gates[:, s, h * 512:(h + 1) * 512], ALU.mult, [Y, gates], [T2])
                P.tt(X[:, i, h * 512:(h + 1) * 512], X[:, i, h * 512:(h + 1) * 512], T2[:], ALU.add, [X, T2], [X], eng="pool")
            P.dma("sp", xo[t0 + i], X[:, i, :], reads=[X])
    return P.finish() if own else None


def featT_shard(y_b, nchunk):
    pad = np.zeros((4 * TOK, y_b.shape[1]), y_b.dtype)
    pad[:y_b.shape[0]] = y_b
    out = []
    for i in range(4):
        blk = pad[i * TOK:(i + 1) * TOK]
        out.append(np.ascontiguousarray(blk.T.reshape(nchunk, 128, TOK)))
    return out


def merge_inputs(xs, hTs, ymT, yrT, yaT, mods_l, li, w_in, w_branch, w_o):
    off = col_offsets()
    m = mods_l.reshape(3, 9, D)
    wg = np.ascontiguousarray(w_in[li][:, off['g_m']:off['g_m'] + 3 * D])
    ins = []
    for b in range(2):
        xsh = tok_shard(xs[b])
        hsh = featT_shard(np.ascontiguousarray(hTs[b].T), 8)
        yfull = np.concatenate([ymT[b], yrT[b], yaT[b]], axis=0)
        ypad = np.zeros((1536, 4 * TOK), yfull.dtype)
        ypad[:, :SEQ_ALL] = yfull
        for i in range(4):
            gb = np.stack([np.broadcast_to(m[2 if i == 0 else b, 5], (128, D)), np.broadcast_to(m[b, 5], (128, D))], 0)
            ins.append({"x": xsh[i], "hT": hsh[i],
                        "yT": np.ascontiguousarray(ypad[:, i * TOK:(i + 1) * TOK].reshape(12, 128, TOK)),
                        "wg": wg, "wb": w_branch[li], "wo": w_o[li], "gateb": np.ascontiguousarray(gb)})
    return ins


def emit_mod_fused(P, T):
    cT = T["cT"]
    ones = P.sb([128, 128], F32, "ones")
    P.memset(ones[:], 1.0, [ones])
    cs = P.sb([128, 8, 2], F32, "cs")
    sg = P.sb([128, 8, 2], F32, "sg")
    P.dma("sp", cs[:], cT[:, :, :], writes=[cs])
    P.act(sg[:], cs[:], AF.Sigmoid, [cs], [sg])
    P.tt(cs[:], cs[:], sg[:], ALU.mult, [cs, sg], [cs])
    crep = [P.sb([128, 8, 128], F32, f"crep{i}") for i in range(2)]
    for st in range(2):
        for kc in range(8):
            P.ts(crep[st][:, kc, :], ones[:], cs[:, kc, st:st + 1], None, ALU.mult, None, [ones, cs], [crep[st]])
    Wk = [P.sb([128, 8, 1024], F32, f"Wk{i}") for i in range(2)]
    pm = [P.ps([128, 512], F32, f"pm{i}") for i in range(2)]
    pg = [P.ps([128, 512], F32, f"pgm{i}") for i in range(2)]
    n = 0
    for l in range(2):
        bpps = P.sb([128, 72], F32, f"bpps{l}")
        P.dma("act", bpps[:], T[f"bpp{l}"][:, :], writes=[bpps])
        bgbs = P.sb([128, 3, 1024], F32, f"bgbs{l}")
        for gi in range(3):
            P.dma("act", bgbs[:, gi, :], T[f"bgb{l}"][gi], writes=[bgbs])
        ngs = P.sb([128, 24], F32, f"ngs{l}")
        P.dma("act", ngs[:], T[f"ng{l}"][:, :], writes=[ngs])
        MODT = P.sb([128, 2, 72], F32, f"MODT{l}")
        for k in range(9):
            W = Wk[n % 2]
            for kc in range(8):
                P.dma("sp", W[:, kc, :], T[f"wada{l}"][kc * 128:(kc + 1) * 128, k * 1024:(k + 1) * 1024], writes=[W])
            ps = pm[n % 2]
            for c in range(8):
                for kc in range(8):
                    P.mm(ps[:, 2 * c:2 * c + 2], W[:, kc, c * 128:(c + 1) * 128], cs[:, kc, :], kc == 0, kc == 7, [W, cs], [ps])
            for st in range(2):
                P.tt(MODT[:, st, k * 8:(k + 1) * 8], ps[:, st:16:2], bpps[:, k * 8:(k + 1) * 8], ALU.add, [ps, bpps], [MODT])
            if k in (2, 5, 8):
                gi = (2, 5, 8).index(k)
                GB = P.sb([128, 2, 1024], F32, f"GB{l}{gi}")
                for st in range(2):
                    for h in range(2):
                        pq = pg[(st * 2 + h) % 2]
                        for kc in range(8):
                            P.mm(pq[:, :], crep[st][:, kc, :], W[:, kc, h * 512:(h + 1) * 512], kc == 0, kc == 7, [crep[st], W], [pq])
                        P.tt(GB[:, st, h * 512:(h + 1) * 512], pq[:, :], bgbs[:, gi, h * 512:(h + 1) * 512], ALU.add, [pq, bgbs], [GB])
                    P.dma("act", T[f"gb{l}_{gi}"][st], GB[:, st, :], reads=[GB])
            n += 1
        for which, ks in ((0, (0, 1, None, 3, 4, None)), (1, (6, 7, None, None, None, None))):
            PP = P.sb([128, 96], F32, f"PP{l}{which}")
            P.memset(PP[:], 0.0, [PP])
            for st in range(2):
                for j, k in enumerate(ks):
                    dst = PP[:, st * 48 + j * 8: st * 48 + (j + 1) * 8]
                    if k is not None:
                        P.cp(dst, MODT[:, st, k * 8:(k + 1) * 8], [MODT], [PP])
                    elif j == 2:
                        gidx = 0 if which == 0 else 2
                        P.cp(dst, ngs[:, gidx * 8:(gidx + 1) * 8], [ngs], [PP])
                    elif j == 5 and which == 0:
                        P.cp(dst, ngs[:, 8:16], [ngs], [PP])
            P.dma("sp", T[f"pp{l}_{which}"][:, :], PP[:], reads=[PP])


def build_fused(upto=99, dbg=None):
    P = Prog()
    E = lambda name, shape, dt=F32: P.dram(name, shape, dt, "ExternalInput")
    x0 = E("x0", [NT, 128, D])
    out = P.dram("out", [NT, 128, D], F32, "ExternalOutput")
    sel = E("sel", [128, 4])
    Tm = {"cT": E("cT", [128, 8, 2])}
    pp, gb = {}, {}
    for l in range(2):
        Tm[f"wada{l}"] = E(f"wada{l}", [D, 9 * D])
        Tm[f"bpp{l}"] = E(f"bpp{l}", [128, 72])
        Tm[f"bgb{l}"] = E(f"bgb{l}", [3, 128, D])
        Tm[f"ng{l}"] = E(f"ng{l}", [128, 24])
        for w in range(2):
            pp[l, w] = Tm[f"pp{l}_{w}"] = P.idram([128, 96], F32, f"pp{l}_{w}")
        for gi in range(3):
            gb[l, gi] = Tm[f"gb{l}_{gi}"] = P.idram([2, 128, D], F32, f"gb{l}_{gi}")
    a_cs = E("a_cs", [2, 128, 8192])
    a_cmat = E("a_cmat", [2, 128, 128])
    emit_mod_fused(P, Tm)
    P.end_stage()
    groups = [[0, 1, 2, 3], [4, 5, 6, 7]]
    dummy = Buf(None, "coll")

    def done(src3):
        t = P.sb([128, D], F32, "dbgt")
        for i in range(NT):
            P.dma("sp", t[:], src3[i], writes=[t])
            P.dma("sp", out[i], t[:], reads=[t])
        return P.finish()

    xcur = x0
    for l in range(2):
        x1 = P.idram([NT, 128, D], F32, f"x1_{l}")
        hsrc = P.idram([8, 128, TOK], BF16, f"hsrc{l}")
        h3 = hsrc
        build_ffn(True, P, {"x": xcur, "wgu": E(f"w1gu{l}", [D, 2 * DFF]), "wd": E(f"w1d{l}", [DFF, D]),
                            "pp": pp[l, 0], "gateb": gb[l, 0], "xo": x1, "ho": h3})
        P.end_stage()
        if upto == 10 * l + 1:
            return done(x1)
        hall = P.idram([8, 4 * 128, TOK], BF16, f"hall{l}")
        hall.gath = True
        for kc in range(8):
            P.coll("AllGather", hsrc[kc], hall[kc], groups, writes=[dummy])
        P.end_stage()
        if upto == 10 * l + 5:
            return done(x1)
        ysrc = P.idram([3, 4, 128, TOK], BF16, f"ysrc{l}")
        zt = P.sb([128, 4 * TOK - SEQ_ALL], BF16, "zt")
        P.memset(zt[:], 0.0, [zt])
        for br in range(3):
            P.dma("act", ysrc[br, 3, :, SEQ_ALL - 3 * TOK:TOK], zt[:], reads=[zt])
        yb_ = []
        for br in range(3):
            yb_.append(Buf(ysrc[br], f"yout{br}"))
            yb_[-1].gath = True
        build_attn(False, P, {"hT": hall, "wqkv": E(f"a_wqkv{l}", [3, D, 128]), "gqk": E(f"a_gqk{l}", [128, 2]),
                              "cs": a_cs, "cmat": a_cmat, "lamp": E(f"a_lamp{l}", [128, 258]),
                              "subg": E(f"a_subg{l}", [128, 128]), "ya": yb_[2]})
        P.end_stage()
        build_mlstm(99, P, {"hT": hall, "wqk": E(f"m_wqk{l}", [2, D, 64]), "wvo": E(f"m_wvo{l}", [D, 256]),
                            "wg": E(f"m_wg{l}", [D, 4]), "cw": E(f"m_cw{l}", [2, 128, 3, 64]), "gb": E(f"m_gb{l}", [128, 4]),
                            "og": E(f"m_og{l}", [128, 128]), "ym": yb_[0]})
        P.end_stage()
        build_rwkv(P, {"hT": hall, "wrkv": E(f"r_wrkv{l}", [D, 384]), "crkv": E(f"r_crkv{l}", [128, 3, 384]),
                       "wl": E(f"r_wl{l}", [D, 384]), "w2a2": E(f"r_w2a2{l}", [2, 128, 128]),
                       "bias01": E(f"r_bias01{l}", [1, 2, 256]), "g2": E(f"r_g2{l}", [128, 128]),
                       "vecs": E(f"r_vecs{l}", [128, 5, 128]), "yr": yb_[1]})
        P.end_stage()
        yall = P.idram([3, 4, 4 * 128, TOK], BF16, f"yall{l}")
        for br in range(3):
            for q in range(4):
                P.coll("AllGather", ysrc[br, q], yall[br, q], groups, writes=[dummy])
        P.end_stage()
        x2 = P.idram([NT, 128, D], F32, f"x2_{l}")
        build_merge(P, {"x": x1, "hT": h3, "yall": yall, "sel": sel, "wg": E(f"g_wg{l}", [D, 3 * D]),
                        "wb": E(f"g_wb{l}", [3, 512, D]), "wo": E(f"g_wo{l}", [D, D]), "gateb": gb[l, 1], "xo": x2}, fused=True)
        P.end_stage()
        if upto == 10 * l + 2:
            return done(x2)
        x3 = out if l == 1 else P.idram([NT, 128, D], F32, f"x3_{l}")
        build_ffn(False, P, {"x": x2, "wgu": E(f"w2gu{l}", [D, 2 * DFF]), "wd": E(f"w2d{l}", [DFF, D]),
                             "pp": pp[l, 1], "gateb": gb[l, 2], "xo": x3})
        P.end_stage()
        if upto == 10 * l + 3 and l == 0:
            return done(x3)
        xcur = x3
    return P.finish()


def fused_inputs(x, c, ctx, c_ctx, w_ada, b_ada, norm_g, ffn1_w_gu, ffn1_w_down, ffn2_w_gu, ffn2_w_down,
                 w_in, m_conv, m_gate_bias, m_out_norm, r_conv, r_w0, r_w2, r_a0, r_a2, r_g2, r_kk, r_ka,
                 r_rk, r_ln_w, r_ln_b, a_qk_norm, a_lambda, a_subln, w_branch, w_o):
    C = np.ascontiguousarray
    xs = [np.concatenate([ctx[b], x[b]], 0) for b in range(2)]
    dummy_h = [np.zeros((D, SEQ_ALL), NPBF) for _ in range(2)]
    off = col_offsets()
    per = [dict() for _ in range(8)]
    shards = [tok_shard(xs[b]) for b in range(2)]
    cs_tab, cm = rope_tables(), attn_consts()
    for core in range(8):
        b, i = core // 4, core % 4
        d = per[core]
        d["x0"] = shards[b][i]
        sel = np.zeros((128, 4), np.float32)
        sel[:, i] = 1.0
        d["sel"] = sel
        cA = c_ctx if i == 0 else c[b]
        d["cT"] = C(np.stack([cA, c[b]], 0).reshape(2, 8, 128).transpose(2, 1, 0).astype(np.float32))
        d["a_cs"], d["a_cmat"] = cs_tab, cm
    for l in range(2):
        bpp = C(b_ada[l].reshape(9, 8, 128).transpose(2, 0, 1).reshape(128, 72))
        bgb = C(np.stack([np.broadcast_to(b_ada[l].reshape(9, D)[k][None], (128, D)) for k in (2, 5, 8)], 0))
        ng = C(norm_g[l].reshape(3, 8, 128).transpose(2, 0, 1).reshape(128, 24))
        ai = attn_inputs(dummy_h, l, w_in, a_qk_norm, a_lambda, a_subln)
        mi = mlstm_inputs(dummy_h, l, w_in, m_conv, m_gate_bias, m_out_norm)
        ri = rwkv_inputs(dummy_h, l, w_in, r_conv, r_w0, r_w2, r_a0, r_a2, r_g2, r_kk, r_ka, r_rk, r_ln_w, r_ln_b)
        wg = C(w_in[l][:, off['g_m']:off['g_m'] + 3 * D])
        for core in range(8):
            d = per[core]
            d[f"wada{l}"] = C(w_ada[l]); d[f"bpp{l}"] = bpp; d[f"bgb{l}"] = bgb; d[f"ng{l}"] = ng
            d[f"w1gu{l}"] = C(ffn1_w_gu[l]); d[f"w1d{l}"] = C(ffn1_w_down[l])
            d[f"w2gu{l}"] = C(ffn2_w_gu[l]); d[f"w2d{l}"] = C(ffn2_w_down[l])
            for k in ("wqkv", "gqk", "lamp", "subg"):
                d[f"a_{k}{l}"] = ai[core][k]
            for k in ("wqk", "wvo", "wg", "cw", "gb", "og"):
                d[f"m_{k}{l}"] = mi[core][k]
            for k in ("wrkv", "crkv", "wl", "w2a2", "bias01", "g2", "vecs"):
                d[f"r_{k}{l}"] = ri[core][k]
            d[f"g_wg{l}"] = wg; d[f"g_wb{l}"] = C(w_branch[l]); d[f"g_wo{l}"] = C(w_o[l])
    return per


def kernel(**inputs):
    f = {k: np.asarray(v, dtype=np.float32) for k, v in inputs.items()}
    nc = _prog("fused", build_fused)
    res = _run(nc, fused_inputs(**f))
    xs = tok_unshard(res, "out")
    return np.stack([xs[b][NCTX:] for b in range(2)], 0).astype(np.float32)


_PROGS = {}


def _prog(name, fn):
    if name not in _PROGS:
        _PROGS[name] = fn()
    return _PROGS[name]


def _run(nc, ins):
    return run_bass_kernel_spmd(nc, ins, core_ids=list(range(8))).results


def kernel_unfused(x, c, ctx, c_ctx, w_ada, b_ada, norm_g, ffn1_w_gu, ffn1_w_down, ffn2_w_gu, ffn2_w_down,
           w_in, m_conv, m_gate_bias, m_out_norm, r_conv, r_w0, r_w2, r_a0, r_a2, r_g2, r_kk, r_ka,
           r_rk, r_ln_w, r_ln_b, a_qk_norm, a_lambda, a_subln, w_branch, w_o):
    f = lambda a: np.asarray(a, dtype=np.float32)
    (x, c, ctx, c_ctx, w_ada, b_ada, norm_g, ffn1_w_gu, ffn1_w_down, ffn2_w_gu, ffn2_w_down, w_in, m_conv, m_gate_bias,
     m_out_norm, r_conv, r_w0, r_w2, r_a0, r_a2, r_g2, r_kk, r_ka, r_rk, r_ln_w, r_ln_b, a_qk_norm, a_lambda, a_subln,
     w_branch, w_o) = map(f, (x, c, ctx, c_ctx, w_ada, b_ada, norm_g, ffn1_w_gu, ffn1_w_down, ffn2_w_gu, ffn2_w_down, w_in,
                              m_conv, m_gate_bias, m_out_norm, r_conv, r_w0, r_w2, r_a0, r_a2, r_g2, r_kk, r_ka, r_rk,
                              r_ln_w, r_ln_b, a_qk_norm, a_lambda, a_subln, w_branch, w_o))
    mods = run_mod(c, c_ctx, w_ada, b_ada)
    xs = [np.concatenate([ctx[b], x[b]], 0) for b in range(2)]
    for li in range(2):
        res = _run(_prog("ffn_h", lambda: build_ffn(True)),
                   ffn_inputs(xs, mods[li], li, 1, norm_g, np.ascontiguousarray(ffn1_w_gu[li]), np.ascontiguousarray(ffn1_w_down[li]), True))
        xs = tok_unshard(res, "xo")
        hTs = hT_unshard(res, "ho")
        ra = _run(_prog("attn", build_attn), attn_inputs(hTs, li, w_in, a_qk_norm, a_lambda, a_subln))
        rm = _run(_prog("mlstm", build_mlstm), mlstm_inputs(hTs, li, w_in, m_conv, m_gate_bias, m_out_norm))
        rr = _run(_prog("rwkv", build_rwkv), rwkv_inputs(hTs, li, w_in, r_conv, r_w0, r_w2, r_a0, r_a2, r_g2, r_kk, r_ka,
                                                          r_rk, r_ln_w, r_ln_b))
        yas = [np.concatenate([ra[b * 4 + h]["ya"] for h in range(4)], axis=0) for b in range(2)]
        yms = [np.concatenate([rm[b * 4 + h]["ym"] for h in range(4)], axis=0) for b in range(2)]
        yrs = [np.concatenate([rr[b * 4 + h]["yr"] for h in range(4)], axis=0) for b in range(2)]
        res = _run(_prog("merge", build_merge), merge_inputs(xs, hTs, yms, yrs, yas, mods[li], li, w_in, w_branch, w_o))
        xs = tok_unshard(res, "xo")
        res = _run(_prog("ffn", lambda: build_ffn(False)),
                   ffn_inputs(xs, mods[li], li, 2, norm_g, np.ascontiguousarray(ffn2_w_gu[li]), np.ascontiguousarray(ffn2_w_down[li]), False))
        xs = tok_unshard(res, "xo")
    return np.stack([xs[b][NCTX:] for b in range(2)], 0).astype(np.float32)
```

```python
import contextlib
import numpy as np
import ml_dtypes
import concourse.bass as bass
import concourse.mybir as mybir
from concourse.bass_utils import run_bass_kernel_spmd

F32 = mybir.dt.float32
BF16 = mybir.dt.bfloat16
AF = mybir.ActivationFunctionType
ALU = mybir.AluOpType
AX = mybir.AxisListType
NPBF = ml_dtypes.bfloat16

ENGS = ("pe", "dve", "act", "pool", "sp")


class Buf:
    __slots__ = ("t", "w", "r", "name", "ds", "psum", "gath")

    def __init__(self, t=None, name=""):
        self.t = t
        self.ds = None
        self.psum = False
        self.gath = False
        self.w = None
        self.r = []
        self.name = name

    def __getitem__(self, idx):
        return self.t[idx]


class Sem:
    __slots__ = ("h", "count")

    def __init__(self, h):
        self.h = h
        self.count = 0


class Prog:
    def __init__(self, name="k"):
        self.nc = bass.Bass("TRN2", target_bir_lowering=False)
        self.es = contextlib.ExitStack()
        self.ss = contextlib.ExitStack()
        self.q = {e: [] for e in ENGS}
        self.esem = {}
        for e in ENGS:
            self.esem[e] = Sem(self.es.enter_context(self.nc.semaphore(f"s_{e}")))
        self.seen = {e: {} for e in ENGS}
        self.dsems = []
        self.free_ds = []
        self.stage_ds = []
        self.nbuf = 0

    def dram(self, name, shape, dt, kind):
        return Buf(self.nc.dram_tensor(name, list(shape), dt, kind=kind).ap(), name)

    def io(self, T, name, shape, dt, kind):
        if T is not None:
            return T[name]
        return self.dram(name, shape, dt, kind)

    def sb(self, shape, dt=F32, name=None):
        self.nbuf += 1
        name = f"{name or 'sb'}_{self.nbuf}"
        t = self.ss.enter_context(self.nc.sbuf_tensor(name, list(shape), dt))
        return Buf(t, name)

    def ps(self, shape, dt=F32, name=None):
        self.nbuf += 1
        name = f"{name or 'ps'}_{self.nbuf}"
        t = self.ss.enter_context(self.nc.psum_tensor(name, list(shape), dt))
        b = Buf(t, name)
        b.psum = True
        return b

    def dsem(self):
        if self.free_ds:
            s = self.free_ds.pop()
        else:
            s = Sem(self.es.enter_context(self.nc.semaphore(f"d{len(self.dsems)}")))
            self.dsems.append(s)
        self.stage_ds.append(s)
        return s

    def _deps(self, eng, reads, writes):
        deps = []
        for b in reads:
            if b.w is not None:
                deps.append(b.w)
            if b.psum:
                deps.extend(ev for ev in b.r if ev[2] != eng)
        for b in writes:
            if b.w is not None:
                deps.append(b.w)
            deps.extend(b.r)
        best = {}
        for (s, v, src) in deps:
            if src == "pe" and eng == "pe":
                continue
            if v > best.get(s, (0, None))[0]:
                best[s] = (v, src)
        for s, (v, src) in best.items():
            if self.seen[eng].get(s, 0) >= v:
                continue
            self.seen[eng][s] = v
            self.q[eng].append(("w", s, v))

    defer = None

    def op(self, eng, fn, reads=(), writes=()):
        if self.defer is not None:
            self.defer.append(lambda: self._op(eng, fn, reads, writes))
            return None
        return self._op(eng, fn, reads, writes)

    def _op(self, eng, fn, reads=(), writes=()):
        self._deps(eng, reads, writes)
        s = self.esem[eng]
        s.count += 1
        ev = (s, s.count, eng)
        self.q[eng].append(("i", fn, s, 1))
        for b in reads:
            b.r.append(ev)
        for b in writes:
            b.w = ev
            b.r = []
        return ev

    def dma(self, eng, out, in_, sem=None, reads=(), writes=(), **kw):
        if self.defer is not None:
            self.defer.append(lambda: self._dma(eng, out, in_, reads, writes, kw))
            return None
        return self._dma(eng, out, in_, reads, writes, kw)

    def _dma(self, eng, out, in_, reads, writes, kw):
        b0 = (list(writes) + list(reads))[0]
        if b0.ds is None:
            b0.ds = self.dsem()
        sem = b0.ds
        self._deps(eng, reads, writes)
        sem.count += 16
        ev = (sem, sem.count, "dma")
        self.q[eng].append(("i", lambda e: e.dma_start(out=out, in_=in_, **kw), sem, 16))
        for b in reads:
            b.r.append(ev)
        for b in writes:
            b.w = ev
            b.r = []
        return ev

    def coll(self, kind, in_ap, out_ap, groups, reads=(), writes=()):
        b0 = list(writes)[0]
        if b0.ds is None:
            b0.ds = self.dsem()
        sem = b0.ds
        self._deps("pool", reads, writes)
        sem.count += 1
        ev = (sem, sem.count, "dma")
        self.q["pool"].append(("i", lambda e: e.collective_compute(kind, ALU.bypass, replica_groups=groups,
                                                                   ins=[in_ap.opt()], outs=[out_ap.opt()]), sem, 1))
        for b in reads:
            b.r.append(ev)
        for b in writes:
            b.w = ev
            b.r = []
        return ev

    def raw(self, eng, fn, reads=()):
        self._deps(eng, reads, ())
        self.q[eng].append(("r", fn))

    def idram(self, shape, dt, name=None, shared=False):
        self.nbuf += 1
        t = self.nc.dram_tensor(name or f"idram{self.nbuf}", list(shape), dt, addr_space="Shared" if shared else "Local")
        return Buf(t.ap(), name or f"idram{self.nbuf}")

    def _emit_block(self):
        q = self.q

        def run(eng_obj, items):
            for it in items:
                if it[0] == "w":
                    eng_obj.wait_ge(it[1].h, it[2])
                elif it[0] == "r":
                    it[1](eng_obj)
                else:
                    it[1](eng_obj).then_inc(it[2].h, it[3])

        with self.nc.Block() as block:
            @block.tensor
            def _(e):
                run(e, q["pe"])

            @block.vector
            def _(e):
                run(e, q["dve"])

            @block.scalar
            def _(e):
                run(e, q["act"])

            @block.gpsimd
            def _(e):
                run(e, q["pool"])

            @block.sync
            def _(e):
                run(e, q["sp"])
        self.q = {e: [] for e in ENGS}

    def end_stage(self):
        sems = [x for x in self.dsems + [self.esem[e] for e in ENGS] if x.count > 0]
        for e in ENGS:
            for x in sems:
                if self.seen[e].get(x, 0) < x.count:
                    self.seen[e][x] = x.count
                    self.q[e].append(("w", x, x.count))
        self._emit_block()
        self.ss.close()
        self.ss = contextlib.ExitStack()
        self.free_ds.extend(self.stage_ds)
        self.stage_ds = []

    def finish(self):
        self.end_stage()
        self.es.close()
        return self.nc

    def mm(self, out, lhsT, rhs, start, stop, reads, writes, skip=False):
        if skip:
            return self.op("pe", lambda e: e.matmul(out, lhsT, rhs, start=start, stop=stop, skip_group_check=True), reads, writes)
        return self.op("pe", lambda e: e.matmul(out, lhsT, rhs, start=start, stop=stop), reads, writes)

    def tr(self, out, in_, ident, reads, writes):
        return self.op("pe", lambda e: e.transpose(out, in_, ident), reads, writes)

    def act(self, out, in_, func, reads, writes, bias=None, scale=None, accum_out=None):
        kw = {}
        if bias is not None:
            kw["bias"] = bias
        if scale is not None:
            kw["scale"] = scale
        if accum_out is not None:
            kw["accum_out"] = accum_out
        return self.op("act", lambda e: e.activation(out, in_, func, **kw), reads, writes)

    def tt(self, out, in0, in1, op, reads, writes, eng="dve"):
        return self.op(eng, lambda e: e.tensor_tensor(out, in0, in1, op), reads, writes)

    def ts(self, out, in0, s1, s2, op0, op1, reads, writes, eng="dve", accum_out=None):
        if op1 is None:
            return self.op(eng, lambda e: e.tensor_scalar(out, in0, s1, None, op0), reads, writes)
        if accum_out is not None:
            return self.op(eng, lambda e: e.tensor_scalar(out, in0, s1, s2, op0, op1, accum_out=accum_out), reads, writes)
        return self.op(eng, lambda e: e.tensor_scalar(out, in0, s1, s2, op0, op1), reads, writes)

    def stt(self, out, in0, scalar, in1, op0, op1, reads, writes):
        return self.op("dve", lambda e: e.scalar_tensor_tensor(out, in0, scalar, in1, op0, op1), reads, writes)

    def cp(self, out, in_, reads, writes, eng="dve"):
        if eng == "act":
            return self.op("act", lambda e: e.copy(out, in_), reads, writes)
        return self.op(eng, lambda e: e.tensor_copy(out, in_), reads, writes)

    def memset(self, ap, val, writes, eng="dve"):
        return self.op(eng, lambda e: e.memset(ap, val), (), writes)


D = 1024
DFF = 2816
NT = 17
TOK = NT * 128
CTXT = 2
SEQT = 66
EPS = 1e-6
BLOCKS = [(0, 2), (2, 3), (5, 3), (8, 3), (11, 3), (14, 3)]


def make_ident(P, n=128, dt=F32):
    ident = P.sb([n, n], dt)
    P.memset(ident[:], 1.0, [ident], eng="pool")
    P.op("pool", lambda e: e.affine_select(ident[:], ident[:], [[-1, n]], ALU.is_equal, 0.0, base=0,
                                           channel_multiplier=1), [ident], [ident])
    return ident


def load_cast(P, dst, dst_ap, src_ap, stage, sem, i, shape_cols):
    st = stage[i % len(stage)]
    P.dma("sp", st[:, 0:shape_cols], src_ap, writes=[st])
    eng = "dve" if i % 2 == 0 else "pool"
    P.cp(dst_ap, st[:, 0:shape_cols], [st], [dst], eng=eng)


def build_ffn(emit_h, P=None, T=None):
    own = P is None
    P = P or Prog()
    x = P.io(T, "x", [NT, 128, D], F32, "ExternalInput")
    wgu = P.io(T, "wgu", [D, 2 * DFF], F32, "ExternalInput")
    wd = P.io(T, "wd", [DFF, D], F32, "ExternalInput")
    pp = P.io(T, "pp", [128, 2 * 48], F32, "ExternalInput")
    gateb = P.io(T, "gateb", [2, 128, D], F32, "ExternalInput")
    xo = P.io(T, "xo", [NT, 128, D], F32, "ExternalOutput")
    if emit_h:
        ho = P.io(T, "ho", [8, 128, TOK], BF16, "ExternalOutput")

    ident = make_ident(P)
    Wgu = P.sb([128, 8, 2 * DFF], BF16, "Wgu")
    Wd = P.sb([128, 22, D], BF16, "Wd")
    stage = [P.sb([128, 704], F32, f"stg{i}") for i in range(2)]
    ssem = [P.dsem() for _ in range(2)]
    pps = P.sb([128, 96], F32, "pps")
    Gt = P.sb([128, 32], F32, "Gt")
    gates = P.sb([128, 2, D], F32, "gates")
    msem = P.dsem()
    P.dma("act", pps[:], pp[:, :], msem, writes=[pps])
    for s in range(2):
        P.dma("act", gates[:, s, :], gateb[s], msem, writes=[gates])
    for s in range(2):
        for j in range(2):
            sc = pps[:, s * 48 + j * 24 + 8: s * 48 + j * 24 + 16]
            g = pps[:, s * 48 + j * 24 + 16: s * 48 + j * 24 + 24]
            P.stt(Gt[:, s * 16 + j * 8: s * 16 + j * 8 + 8], sc, 1.0, g, ALU.add, ALU.mult, [pps], [Gt])
    P.ts(gates[:], gates[:], 0.5, None, ALU.mult, None, [gates], [gates], eng="pool")

    n = 0
    for kc in range(8):
        for q in range(8):
            load_cast(P, Wgu, Wgu[:, kc, q * 704:(q + 1) * 704], wgu[kc * 128:(kc + 1) * 128, q * 704:(q + 1) * 704],
                      stage, ssem, n, 704)
            n += 1
    for fc in range(22):
        for q in range(2):
            load_cast(P, Wd, Wd[:, fc, q * 512:(q + 1) * 512], wd[fc * 128:(fc + 1) * 128, q * 512:(q + 1) * 512], stage, ssem, n, 512)
            n += 1

    xb = [P.sb([128, 3, D], F32, "xb0")] * 2
    xsem = [P.dsem() for _ in range(2)]
    osem = [P.dsem() for _ in range(2)]
    scr = {"xn": P.sb([128, 3, D], F32, "xn"), "sq": P.sb([128, D], BF16, "sq")}
    small = {"ss": P.sb([128, 4], F32, "ss")}
    hT = P.sb([128, 8, 384], BF16, "hT")
    hT2 = hT
    hsem = P.dsem()
    uT = P.sb([128, 22, 384], BF16, "uT")
    sa = [P.sb([128, 384], F32, f"sa{i}") for i in range(2)]
    pst = [P.ps([128, 512], F32, f"pst{i}") for i in range(2)]
    pa = [P.ps([128, 512], F32, f"pa{i}") for i in range(2)]
    pb = [P.ps([128, 512], F32, f"pb{i}") for i in range(2)]
    py = [P.ps([128, 512], F32, f"py{i}") for i in range(2)]
    tmp = [P.sb([128, 512], F32, f"tmp{i}") for i in range(2)]

    for bi, (t0, nb) in enumerate(BLOCKS):
        s = 0 if bi == 0 else 1
        N = nb * 128
        X = xb[bi % 2]
        for i in range(nb):
            P.dma("sp", X[:, i, :], x[t0 + i], xsem[bi % 2], writes=[X])
        _norm(P, X, nb, Gt, s * 16, pps, s * 48, hT, ident, pst, scr, small)
        for fc in range(22):
            A, B = pa[fc % 2], pb[fc % 2]
            for kc in range(8):
                P.mm(A[:, 0:N], Wgu[:, kc, fc * 128:(fc + 1) * 128], hT[:, kc, 0:N], kc == 0, kc == 7, [Wgu, hT], [A])
            for kc in range(8):
                P.mm(B[:, 0:N], Wgu[:, kc, DFF + fc * 128:DFF + (fc + 1) * 128], hT[:, kc, 0:N], kc == 0, kc == 7,
                     [Wgu, hT], [B])
            S = sa[fc % 2]
            P.act(S[:, 0:N], A[:, 0:N], AF.Silu, [A], [S])
            P.tt(uT[:, fc, 0:N], S[:, 0:N], B[:, 0:N], ALU.mult, [S, B], [uT])
        for i in range(nb):
            for h in range(2):
                Y = py[(i * 2 + h) % 2]
                for fc in range(22):
                    P.mm(Y[:, :], uT[:, fc, i * 128:(i + 1) * 128], Wd[:, fc, h * 512:(h + 1) * 512], fc == 0, fc == 21,
                         [uT, Wd], [Y])
                T = tmp[(i * 2 + h) % 2]
                P.tt(T[:], Y[:], gates[:, s, h * 512:(h + 1) * 512], ALU.mult, [Y, gates], [T])
                P.tt(X[:, i, h * 512:(h + 1) * 512], X[:, i, h * 512:(h + 1) * 512], T[:], ALU.add, [X, T], [X], eng="pool")
            P.dma("sp", xo[t0 + i], X[:, i, :], osem[bi % 2], reads=[X])
        if emit_h:
            _norm(P, X, nb, Gt, s * 16 + 8, pps, s * 48 + 24, hT2, ident, pst, scr, small)
            for c in range(8):
                P.dma("act", ho[c, :, t0 * 128:t0 * 128 + N], hT2[:, c, 0:N], hsem, reads=[hT2])
    return P.finish() if own else None


def _norm(P, xb, nb, Gt, gcol, SHt, scol, hT, ident, pst, scr, small):
    xn = scr["xn"]
    ss = small["ss"]
    for i in range(nb):
        P.act(scr["sq"][:], xb[:, i, :], AF.Square, [xb], [scr["sq"], ss], accum_out=ss[:, 0:1])
        P.ts(ss[:, 1:2], ss[:, 0:1], 1.0 / D, EPS, ALU.mult, ALU.add, [ss], [ss])
        P.act(ss[:, 2:3], ss[:, 1:2], AF.Sqrt, [ss], [ss])
        P.op("dve", lambda e: e.reciprocal(ss[:, 3:4], ss[:, 2:3]), [ss], [ss])
        P.ts(xn[:, i, :], xb[:, i, :], ss[:, 3:4], None, ALU.mult, None, [xb, ss], [xn])
    for c in range(8):
        pt = pst[c % len(pst)]
        for i in range(nb):
            P.tr(pt[:, i * 128:(i + 1) * 128], xn[:, i, c * 128:(c + 1) * 128], ident[:], [xn, ident], [pt])
        P.act(hT[:, c, 0:nb * 128], pt[:, 0:nb * 128], AF.Identity, [pt, Gt, SHt], [hT],
              bias=SHt[:, scol + c:scol + c + 1], scale=Gt[:, gcol + c:gcol + c + 1])


def build_mod():
    P = Prog()
    cT = P.dram("cT", [128, 8, 3], F32, "ExternalInput")
    wa = P.dram("wa", [2, D, 1152], F32, "ExternalInput")
    ba = P.dram("ba", [2, 3, 1152], F32, "ExternalInput")
    mo = P.dram("mo", [2, 3, 1152], F32, "ExternalOutput")
    cs = P.sb([128, 8, 3], F32)
    sg = P.sb([128, 8, 3], F32)
    ds = P.dsem()
    P.dma("sp", cs[:], cT[:, :, :], ds, writes=[cs])
    P.act(sg[:], cs[:], AF.Sigmoid, [cs], [sg])
    P.tt(cs[:], cs[:], sg[:], ALU.mult, [cs, sg], [cs])
    W = [P.sb([128, 8, 1152], F32, f"W{l}") for l in range(2)]
    wsem = P.dsem()
    bsb = P.sb([3, 2, 1152], F32)
    osb = P.sb([3, 2, 1152], F32)
    for l in range(2):
        P.dma("act", bsb[:, l, :], ba[l], ds, writes=[bsb])
        for kc in range(8):
            P.dma("sp" if kc % 2 == 0 else "act", W[l][:, kc, :], wa[l, kc * 128:(kc + 1) * 128, :], wsem, writes=[W[l]])
    pp = [P.ps([3, 512], F32, f"pp{i}") for i in range(2)]
    n = 0
    for l in range(2):
        for (c0, cw) in ((0, 512), (512, 512), (1024, 128)):
            ps = pp[n % 2]
            n += 1
            for kc in range(8):
                P.mm(ps[:, 0:cw], cs[:, kc, :], W[l][:, kc, c0:c0 + cw], kc == 0, kc == 7, [cs, W[l]], [ps])
            P.tt(osb[:, l, c0:c0 + cw], ps[:, 0:cw], bsb[:, l, c0:c0 + cw], ALU.add, [ps, bsb], [osb])
    osem = P.dsem()
    for l in range(2):
        P.dma("sp", mo[l], osb[:, l, :], osem, reads=[osb])
    return P.finish()


def run_mod(c, c_ctx, w_ada, b_ada):
    cv = np.stack([c[0], c[1], c_ctx], 0)
    cT = np.ascontiguousarray(cv.reshape(3, 8, 128).transpose(2, 1, 0))
    nc = build_mod()
    ins = []
    for i in range(8):
        cols = slice(i * 1152, (i + 1) * 1152)
        ins.append({"cT": cT, "wa": np.ascontiguousarray(w_ada[:, :, cols]),
                    "ba": np.ascontiguousarray(np.broadcast_to(b_ada[:, None, cols], (2, 3, 1152)))})
    res = run_bass_kernel_spmd(nc, ins, core_ids=list(range(8)))
    return np.concatenate([r["mo"] for r in res.results], axis=2)


def pvec(v):
    return np.ascontiguousarray(v.reshape(8, 128).T)


def tok_shard(xfull_b):
    pad = np.zeros((4 * TOK, xfull_b.shape[1]), np.float32)
    pad[:xfull_b.shape[0]] = xfull_b
    return [np.ascontiguousarray(pad[i * TOK:(i + 1) * TOK].reshape(NT, 128, -1)) for i in range(4)]


def ffn_inputs(xs, mods_l, li, which, norm_g, wgu, wd, emit_h):
    m = mods_l.reshape(3, 9, D)
    o = 0 if which == 1 else 6
    ins = []
    for b in range(2):
        shards = tok_shard(xs[b])
        for i in range(4):
            sets = []
            for s in range(2):
                r = 2 if (s == 0 and i == 0) else b
                vecs = [m[r, o + 0], m[r, o + 1], norm_g[li, 0 if which == 1 else 2]]
                if emit_h:
                    vecs += [m[r, 3], m[r, 4], norm_g[li, 1]]
                else:
                    vecs += [m[r, 3] * 0, m[r, 3] * 0, m[r, 3] * 0]
                sets.append(np.concatenate([pvec(v) for v in vecs], axis=1))
            pp = np.ascontiguousarray(np.concatenate(sets, axis=1).astype(np.float32))
            gb = np.stack([np.broadcast_to(m[2 if i == 0 else b, o + 2], (128, D)),
                           np.broadcast_to(m[b, o + 2], (128, D))], 0)
            ins.append({"x": shards[i], "wgu": wgu, "wd": wd, "pp": pp, "gateb": np.ascontiguousarray(gb)})
    return ins


def tok_unshard(res, key):
    out = []
    for b in range(2):
        full = np.concatenate([res[b * 4 + i][key].reshape(TOK, -1) for i in range(4)], axis=0)
        out.append(full[:SEQT * 128])
    return out


def hT_unshard(res, key):
    out = []
    for b in range(2):
        full = np.concatenate([res[b * 4 + i][key].reshape(D, TOK) for i in range(4)], axis=1)
        out.append(np.ascontiguousarray(full[:, :SEQT * 128]))
    return out


SEQ_ALL = SEQT * 128
NCTX = 256


def h_pieces(hT, kc, a, b):
    if hT.gath:
        out, t = [], a
        while t < b:
            r = t // TOK
            e = min(b, (r + 1) * TOK)
            out.append((t - a, e - t, hT[kc, r * 128:(r + 1) * 128, t - r * TOK:e - r * TOK]))
            t = e
        return out
    return [(0, b - a, hT[kc, :, a:b])]


def ycols(buf, a):
    if buf.gath:
        q = a // TOK
        return buf[q, :, a - q * TOK:a - q * TOK + 128]
    return buf[:, a:a + 128]


def load_hblk(P, hT, hb, sem, t0, N, eng="sp"):
    for kc in range(8):
        for (o, n, ap) in h_pieces(hT, kc, t0, t0 + N):
            P.dma(eng, hb[:, kc, o:o + n], ap, writes=[hb], **({"allow_slow_non_contiguous": True} if n == 1 else {}))


def build_attn(debug=False, P=None, T=None):
    own = P is None
    P = P or Prog()
    hT = P.io(T, "hT", [8, 128, SEQ_ALL], BF16, "ExternalInput")
    wqkv = P.io(T, "wqkv", [3, D, 128], F32, "ExternalInput")
    gqk = P.io(T, "gqk", [128, 2], F32, "ExternalInput")
    cs = P.io(T, "cs", [2, 128, 8192], F32, "ExternalInput")
    cmat = P.io(T, "cmat", [2, 128, 128], F32, "ExternalInput")
    lamp = P.io(T, "lamp", [128, 258], F32, "ExternalInput")
    subg = P.io(T, "subg", [128, 128], F32, "ExternalInput")
    ya = P.io(T, "ya", [128, SEQ_ALL], BF16, "ExternalOutput")

    banks = [P.ps([128, 512], F32, f"bank{i}") for i in range(8)]
    ident = make_ident(P)
    csem = P.dsem()
    W = P.sb([128, 3, 8, 128], BF16, "W")
    wst = P.sb([128, 3, 8, 128], F32, "wst")
    for j in range(3):
        for kc in range(8):
            P.dma("sp", wst[:, j, kc, :], wqkv[j, kc * 128:(kc + 1) * 128, :], csem, writes=[wst])
    P.cp(W[:], wst[:], [wst], [W])
    gq = P.sb([128, 2], F32, "gq")
    P.dma("act", gq[:], gqk[:, :], csem, writes=[gq])
    P.ts(gq[:, 0:1], gq[:, 0:1], 0.125, None, ALU.mult, None, [gq], [gq])
    Bm = P.sb([128, 128], F32, "Bm")
    Rm = P.sb([128, 128], F32, "Rm")
    P.dma("act", Bm[:], cmat[0], csem, writes=[Bm])
    P.dma("act", Rm[:], cmat[1], csem, writes=[Rm])
    lp = P.sb([128, 258], F32, "lp")
    P.dma("act", lp[:], lamp[:, :], csem, writes=[lp])
    sg = P.sb([128, 128], F32, "sg")
    P.dma("act", sg[:], subg[:, :], csem, writes=[sg])
    P.ts(sg[:], sg[:], lp[:, 257:258], None, ALU.mult, None, [sg, lp], [sg])
    lt = P.sb([128, 128], F32, "lt")
    lam = P.sb([128, 4], F32, "lam")
    P.tt(lt[:, 0:64], lp[:, 0:64], lp[:, 64:128], ALU.mult, [lp], [lt])
    P.tt(lt[:, 64:128], lp[:, 128:192], lp[:, 192:256], ALU.mult, [lp], [lt])
    P.op("dve", lambda e: e.tensor_reduce(lam[:, 0:1], lt[:, 0:64], AX.X, ALU.add), [lt], [lam])
    P.op("dve", lambda e: e.tensor_reduce(lam[:, 1:2], lt[:, 64:128], AX.X, ALU.add), [lt], [lam])
    P.act(lam[:, 0:2], lam[:, 0:2], AF.Exp, [lam], [lam])
    P.tt(lam[:, 2:3], lam[:, 1:2], lam[:, 0:1], ALU.subtract, [lam], [lam])
    P.tt(lam[:, 3:4], lam[:, 2:3], lp[:, 256:257], ALU.subtract, [lam, lp], [lam])
    epsb = P.sb([128, 1], F32, "epsb")
    P.memset(epsb[:], EPS, [epsb])

    QK = [P.sb([128, SEQ_ALL], BF16, "QT"), P.sb([128, SEQ_ALL], BF16, "KT")]
    V = P.sb([128, SEQT, 129], BF16, "V")
    P.memset(V[:, :, 128:129], 1.0, [V], eng="pool")
    hb = [P.sb([128, 8, 512], BF16, f"hb{i}") for i in range(2)]
    hsem = [P.dsem() for _ in range(2)]
    cst = [P.sb([128, 2, 512], F32, f"cst{i}") for i in range(2)]
    cssem = [P.dsem() for _ in range(2)]
    sq = P.sb([128, 512], F32, "sq")
    rs = P.sb([128, 512], F32, "rs")
    xn = P.sb([128, 512], F32, "xn")
    t1 = P.sb([128, 512], F32, "t1")
    t2 = P.sb([128, 512], F32, "t2")

    blocks = [(0, 256)] + [(256 + 512 * j, 512) for j in range(16)]
    for bi, (t0, N) in enumerate(blocks):
        H = hb[bi % 2]
        load_hblk(P, hT, H, hsem[bi % 2], t0, N)
        C = cst[bi % 2]
        if bi > 0:
            for j in range(2):
                P.dma("act", C[:, j, :], cs[j, :, t0 - 256:t0 - 256 + N], cssem[bi % 2], writes=[C])
        for j in range(2):
            pq, pms, prot = banks[0 + j], banks[2 + j], banks[4 + j]
            for kc in range(8):
                P.mm(pq[:, 0:N], W[:, j, kc, :], H[:, kc, 0:N], kc == 0, kc == 7, [W, H], [pq])
            P.act(sq[:, 0:N], pq[:, 0:N], AF.Square, [pq], [sq])
            P.mm(pms[:, 0:N], Bm[:], sq[:, 0:N], True, True, [Bm, sq], [pms])
            P.act(rs[:, 0:N], pms[:, 0:N], AF.Sqrt, [pms, epsb], [rs], bias=epsb[:, 0:1])
            P.op("dve", lambda e, N=N: e.reciprocal(rs[:, 0:N], rs[:, 0:N]), [rs], [rs])
            P.stt(xn[:, 0:N], pq[:, 0:N], gq[:, j:j + 1], rs[:, 0:N], ALU.mult, ALU.mult, [pq, gq, rs], [xn])
            if bi == 0:
                P.cp(QK[j][:, t0:t0 + N], xn[:, 0:N], [xn], [QK[j]], eng="pool")
            else:
                P.mm(prot[:, 0:N], Rm[:], xn[:, 0:N], True, True, [Rm, xn], [prot])
                P.tt(t1[:, 0:N], xn[:, 0:N], C[:, 0, 0:N], ALU.mult, [xn, C], [t1], eng="pool")
                P.tt(t2[:, 0:N], prot[:, 0:N], C[:, 1, 0:N], ALU.mult, [prot, C], [t2])
                P.tt(QK[j][:, t0:t0 + N], t1[:, 0:N], t2[:, 0:N], ALU.add, [t1, t2], [QK[j]])
        for i in range(N // 128):
            pv = banks[6 + i % 2]
            for kc in range(8):
                P.mm(pv[:, 0:128], H[:, kc, i * 128:(i + 1) * 128], W[:, 2, kc, :], kc == 0, kc == 7, [H, W], [pv])
            P.cp(V[:, t0 // 128 + i, 0:128], pv[:, 0:128], [pv], [V], eng="act")

    if debug:
        dq = P.io(T, "dq", [2, 128, SEQ_ALL], BF16, "ExternalOutput")
        dv = P.io(T, "dv", [128, SEQT, 129], BF16, "ExternalOutput")
        dl = P.io(T, "dl", [128, 4], F32, "ExternalOutput")
        dsm = P.dsem()
        P.dma("sp", dq[0], QK[0][:], dsm, reads=[QK[0]])
        P.dma("sp", dq[1], QK[1][:], dsm, reads=[QK[1]])
        P.dma("sp", dv[:, :, :], V[:], dsm, reads=[V])
        P.dma("sp", dl[:, :], lam[:], dsm, reads=[lam])
    Pm = [P.sb([128, 512], BF16, f"Pm{i}") for i in range(6)]
    Sb = [banks[0], banks[1], banks[6], banks[7]]
    SKEW = 3
    accb = [banks[2], banks[3], banks[4]]
    yo = [P.sb([128, 128], F32, f"yo{i}") for i in range(2)]
    yob = [P.sb([128, 128], BF16, f"yob{i}") for i in range(2)]
    ysq = P.sb([128, 128], F32, "ysq")
    st = P.sb([128, 8], F32, "st")
    osem = [P.dsem() for _ in range(2)]

    def acc(m, qs):
        a = m * 4 + qs
        return accb[a // 3], (a % 3) * 129

    qblocks = [(0, 256, 0, CTXT)] + [(256 + 512 * j, 512, 0, SEQT) for j in range(16)]
    accS = [[P.sb([128, 387], F32, f"accS{i}{j}") for j in range(3)] for i in range(2)]
    sts = [P.sb([128, 8], F32, f"sts{i}") for i in range(2)]
    n = 0
    no = 0
    pending = []

    def finalize(q0, nq, par, no0):
        A = accS[par]
        for qs in range(nq):
            i0, i1 = 0 * 4 + qs, 1 * 4 + qs
            a0, c0 = A[i0 // 3], (i0 % 3) * 129
            a1, c1 = A[i1 // 3], (i1 % 3) * 129
            k = no0 + qs
            Y, st_ = yo[k % 2], sts[k % 2]
            P.op("dve", lambda e, a0=a0, c0=c0, st_=st_: e.reciprocal(st_[:, 0:1], a0[:, c0 + 128:c0 + 129]), [a0], [st_])
            P.op("dve", lambda e, a1=a1, c1=c1, st_=st_: e.reciprocal(st_[:, 1:2], a1[:, c1 + 128:c1 + 129]), [a1], [st_])
            P.tt(st_[:, 1:2], st_[:, 1:2], lam[:, 3:4], ALU.mult, [st_, lam], [st_])
            P.ts(Y[:], a0[:, c0:c0 + 128], st_[:, 0:1], None, ALU.mult, None, [a0, st_], [Y])
            P.stt(Y[:], a1[:, c1:c1 + 128], st_[:, 1:2], Y[:], ALU.mult, ALU.add, [a1, st_, Y], [Y])
            P.tt(ysq[:], Y[:], Y[:], ALU.mult, [Y], [ysq])
            P.op("dve", lambda e, st_=st_: e.tensor_reduce(st_[:, 2:3], ysq[:], AX.X, ALU.add), [ysq], [st_])
            P.ts(st_[:, 3:4], st_[:, 2:3], 1.0 / 128, EPS, ALU.mult, ALU.add, [st_], [st_])
            P.act(st_[:, 4:5], st_[:, 3:4], AF.Sqrt, [st_], [st_])
            P.op("dve", lambda e, st_=st_: e.reciprocal(st_[:, 5:6], st_[:, 4:5]), [st_], [st_])
            P.stt(Y[:], Y[:], st_[:, 5:6], sg[:], ALU.mult, ALU.mult, [Y, st_, sg], [Y])
            P.tr(banks[5][:, 0:128], Y[:], ident[:], [Y, ident], [banks[5]])
            Yb = yob[k % 2]
            P.cp(Yb[:], banks[5][:, 0:128], [banks[5]], [Yb])
            P.dma("sp", ycols(ya, q0 + qs * 128), Yb[:], reads=[Yb])

    for bi, (q0, N, k0, k1) in enumerate(qblocks):
        nq = N // 128
        for b in accb:
            P.memset(b[:], 0.0, [b], eng="pool" if False else "dve")
        its = [(kt, m) for kt in range(k0, k1) for m in range(2)]
        per = (len(pending) + len(its) - 1) // max(1, len(its) - 8) if pending else 0
        ppos = 0

        def issue_s(j):
            kt, m = its[j]
            S, pm = Sb[(n + j) % 4], Pm[(n + j) % 6]
            P.mm(S[:, 0:N], QK[1][m * 64:(m + 1) * 64, kt * 128:(kt + 1) * 128], QK[0][m * 64:(m + 1) * 64, q0:q0 + N],
                 True, True, [QK[0], QK[1]], [S])
            P.act(pm[:, 0:N], S[:, 0:N], AF.Exp, [S], [pm])

        for j in range(min(SKEW, len(its))):
            issue_s(j)
        for j, (kt, m) in enumerate(its):
            if j + SKEW < len(its):
                issue_s(j + SKEW)
            pm = Pm[(n + j) % 6]
            for qs in range(nq):
                ab, c0 = acc(m, qs)
                P.mm(ab[:, c0:c0 + 129], pm[:, qs * 128:(qs + 1) * 128], V[:, kt, :], False, False, [pm, V], [ab], skip=True)
            for th in pending[ppos:ppos + per]:
                th()
            ppos += per
        for th in pending[ppos:]:
            th()
        n += len(its)
        par = bi % 2
        for j3, b in enumerate(accb):
            P.cp(accS[par][j3][:], b[:, 0:387], [b], [accS[par][j3]], eng="dve" if j3 != 1 else "act")
        P.defer = []
        finalize(q0, nq, par, no)
        pending, P.defer = P.defer, None
        no += nq
    for th in pending:
        th()
    return P.finish() if own else None


def rope_tables():
    n = 8192
    rows = n // 64
    row = np.repeat(np.arange(rows, dtype=np.float32), 64)
    col = np.tile(np.arange(64, dtype=np.float32), rows)
    half = 32
    inv = (np.float32(10000.0) ** (-np.arange(0, half, 2, dtype=np.float32) / np.float32(half))).astype(np.float32)
    ang = np.concatenate([row[:, None] * inv, col[:, None] * inv], axis=-1).astype(np.float32)
    cos, sin = np.cos(ang).astype(np.float32), np.sin(ang).astype(np.float32)
    idx = (np.arange(128) % 64) // 2
    return np.ascontiguousarray(np.stack([cos[:, idx].T, sin[:, idx].T], 0))


def attn_consts():
    Bm = np.zeros((128, 128), np.float32)
    Bm[:64, :64] = 1.0 / 64
    Bm[64:, 64:] = 1.0 / 64
    Rm = np.zeros((128, 128), np.float32)
    for i in range(64):
        Rm[2 * i + 1, 2 * i] = -1.0
        Rm[2 * i, 2 * i + 1] = 1.0
    return np.stack([Bm, Rm], 0)


def lambda_init(li):
    import math
    return 0.8 - 0.6 * math.exp(-0.3 * li)


def attn_inputs(hTs, li, w_in, a_qk_norm, a_lambda, a_subln):
    off_q = 5008 - 512 - 512 - 512
    off = {}
    o = 0
    for name, wdt in IN_SPLITS:
        off[name] = o
        o += wdt
    cs = rope_tables()
    cm = attn_consts()
    ins = []
    for b in range(2):
        hT = np.ascontiguousarray(hTs[b].reshape(8, 128, SEQ_ALL))
        for h in range(4):
            wq = w_in[li][:, off['a_q'] + 128 * h: off['a_q'] + 128 * (h + 1)]
            wk = w_in[li][:, off['a_k'] + 128 * h: off['a_k'] + 128 * (h + 1)]
            wv = w_in[li][:, off['a_v'] + 128 * h: off['a_v'] + 128 * (h + 1)]
            gqk = np.stack([np.tile(a_qk_norm[li, 0], 2), np.tile(a_qk_norm[li, 1], 2)], 1).astype(np.float32)
            li_ = np.float32(lambda_init(li))
            lamp = np.concatenate([np.broadcast_to(a_lambda[li].reshape(1, 256), (128, 256)),
                                   np.full((128, 1), li_, np.float32), np.full((128, 1), np.float32(1.0) - li_, np.float32)], 1)
            ins.append({"hT": hT, "wqkv": np.ascontiguousarray(np.stack([wq, wk, wv], 0)), "gqk": np.ascontiguousarray(gqk),
                        "cs": cs, "cmat": cm, "lamp": np.ascontiguousarray(lamp.astype(np.float32)),
                        "subg": np.ascontiguousarray(np.broadcast_to(a_subln[li][None, :], (128, 128)))})
    return ins


IN_SPLITS = (
    ('m_q', 256), ('m_k', 256), ('m_v', 512), ('m_o', 512),
    ('m_if', 4), ('m_ff', 4), ('m_ib', 4), ('m_fb', 4),
    ('r_r', 512), ('r_k', 512), ('r_v', 512),
    ('r_wf', 64), ('r_wb', 64), ('r_af', 64), ('r_ab', 64), ('r_g', 128),
    ('a_q', 512), ('a_k', 512), ('a_v', 512),
    ('g_m', 1024), ('g_r', 1024), ('g_a', 1024),
)


def tri_mask(P, upper, neg=False):
    m = P.sb([128, 128], F32)
    P.memset(m[:], 0.0 if neg else 1.0, [m], eng="pool")
    pat, cm = ([[1, 128]], -1) if upper else ([[-1, 128]], 1)
    P.op("pool", lambda e: e.affine_select(m[:], m[:], pat, ALU.is_ge, -1.0e4 if neg else 0.0, base=0,
                                           channel_multiplier=cm), [m], [m])
    return m


def load_hblk_halo(P, hT, hb, t0, N, lo, hi, eng="sp"):
    a = t0 - 1 if t0 - 1 >= lo else t0
    b = t0 + N + 1 if t0 + N + 1 <= hi else t0 + N
    for kc in range(8):
        for (o, n, ap) in h_pieces(hT, kc, a, b):
            P.dma(eng, hb[:, kc, a - (t0 - 1) + o:a - (t0 - 1) + o + n], ap, writes=[hb],
                  **({"allow_slow_non_contiguous": True} if n == 1 else {}))
    if a == t0:
        P.memset(hb[:, :, 0:1], 0.0, [hb], eng="pool")
    if b == t0 + N:
        P.memset(hb[:, :, N + 1:N + 2], 0.0, [hb], eng="pool")


def chunk_orders():
    f = list(range(SEQT))
    b = [1, 0] + list(range(SEQT - 1, 1, -1))
    return f, b


def build_mlstm(stop=99, P=None, T=None):
    own = P is None
    P = P or Prog()
    hT = P.io(T, "hT", [8, 128, SEQ_ALL], BF16, "ExternalInput")
    wqk = P.io(T, "wqk", [2, D, 64], F32, "ExternalInput")
    wvo = P.io(T, "wvo", [D, 256], F32, "ExternalInput")
    wg = P.io(T, "wg", [D, 4], F32, "ExternalInput")
    cw = P.io(T, "cw", [2, 128, 3, 64], F32, "ExternalInput")
    gb = P.io(T, "gb", [128, 4], F32, "ExternalInput")
    og = P.io(T, "og", [128, 128], F32, "ExternalInput")
    ym = P.io(T, "ym", [128, SEQ_ALL], BF16, "ExternalOutput")

    banks = [P.ps([128, 512], F32, f"bank{i}") for i in range(8)]
    ident = make_ident(P)
    ones = P.sb([128, 128], F32, "ones")
    P.memset(ones[:], 1.0, [ones])
    one1 = P.sb([128, 1], F32, "one1")
    P.memset(one1[:], 1.0, [one1])
    triU = tri_mask(P, True)
    triL = tri_mask(P, False)
    negU = tri_mask(P, True, True)
    negL = tri_mask(P, False, True)

    wst = P.sb([128, 8, 388], F32, "wst")
    for kc in range(8):
        P.dma("sp", wst[:, kc, 0:64], wqk[0, kc * 128:(kc + 1) * 128, :], writes=[wst])
        P.dma("sp", wst[:, kc, 64:128], wqk[1, kc * 128:(kc + 1) * 128, :], writes=[wst])
        P.dma("sp", wst[:, kc, 128:384], wvo[kc * 128:(kc + 1) * 128, :], writes=[wst])
        P.dma("sp", wst[:, kc, 384:388], wg[kc * 128:(kc + 1) * 128, :], writes=[wst])
    cws = P.sb([128, 2, 3, 64], F32, "cws")
    P.dma("act", cws[:, 0], cw[0], writes=[cws])
    P.dma("act", cws[:, 1], cw[1], writes=[cws])
    gbs = P.sb([128, 4], F32, "gbs")
    P.dma("act", gbs[:], gb[:, :], writes=[gbs])
    ogs = P.sb([128, 128], F32, "ogs")
    P.dma("act", ogs[:], og[:, :], writes=[ogs])
    Wqk = P.sb([128, 2, 3, 8, 64], BF16, "Wqk")
    Wvo = P.sb([128, 8, 260], BF16, "Wvo")
    P.cp(Wvo[:], wst[:, :, 128:388], [wst], [Wvo])
    for j in range(2):
        for tap in range(3):
            for kc in range(8):
                P.tt(Wqk[:, j, tap, kc, :], wst[:, kc, j * 64:(j + 1) * 64], cws[:, j, tap, :], ALU.mult, [wst, cws], [Wqk],
                     eng="pool" if kc % 2 else "dve")

    QT = P.sb([64, SEQ_ALL], F32, "QT")
    KT = P.sb([64, SEQ_ALL], F32, "KT")
    VE = P.sb([128, SEQT, 129], F32, "VE")
    P.memset(VE[:, :, 128:129], 1.0, [VE], eng="pool")
    OG = P.sb([128, SEQT, 128], BF16, "OG")
    G = P.sb([128, SEQT, 4], F32, "G")
    hb = [P.sb([128, 8, 514], BF16, f"hb{i}") for i in range(2)]

    blocks = [(0, 256, 0, 256)] + [(256 + 512 * j, 512, 256, SEQ_ALL) for j in range(16)]
    for bi, (t0, N, lo, hi) in enumerate(blocks):
        H = hb[bi % 2]
        load_hblk_halo(P, hT, H, t0, N, lo, hi)
        for j, dst in enumerate((QT, KT)):
            pq = banks[j]
            n = 0
            for tap in range(3):
                for kc in range(8):
                    P.mm(pq[0:64, 0:N], Wqk[:, j, tap, kc, :], H[:, kc, tap:tap + N], n == 0, n == 23, [Wqk, H], [pq])
                    n += 1
            P.act(dst[:, t0:t0 + N], pq[0:64, 0:N], AF.Silu, [pq], [dst], scale=1.0)
        for i in range(N // 128):
            pv = banks[2 + i % 2]
            for kc in range(8):
                P.mm(pv[:, 0:260], H[:, kc, 1 + i * 128:1 + (i + 1) * 128], Wvo[:, kc, :], kc == 0, kc == 7, [H, Wvo], [pv])
            tl = t0 // 128 + i
            P.cp(VE[:, tl, 0:128], pv[:, 0:128], [pv], [VE])
            P.act(OG[:, tl, :], pv[:, 128:256], AF.Sigmoid, [pv], [OG])
            P.tt(G[:, tl, :], pv[:, 256:260], gbs[:], ALU.add, [pv, gbs], [G])
    if stop == 1:
        return P.finish() if own else None
    P.ts(KT[:], KT[:], 0.125, None, ALU.mult, None, [KT], [KT], eng="pool")

    ge = P.sb([128, SEQT, 4], F32, "ge")
    P.act(ge[:], G[:], AF.Exp, [G], [ge], scale=-1.0)
    P.act(ge[:], ge[:], AF.Ln, [ge, one1], [ge], bias=one1[:, 0:1])
    LF = P.sb([128, 2, SEQT], F32, "LF")
    IG = P.sb([128, 2, SEQT], F32, "IG")
    for d in range(2):
        P.ts(LF[:, d, :], ge[:, :, 2 * d + 1], -1.0, None, ALU.mult, None, [ge], [LF])
        P.cp(IG[:, d, :], G[:, :, 2 * d], [G], [IG])
    BC = P.sb([128, 2, SEQT], F32, "BC")
    BT = P.sb([128, 2, SEQT], F32, "BT")
    for d in range(2):
        pb = banks[4 + d]
        P.mm(pb[:, 0:SEQT], (triU if d == 0 else triL)[:], LF[:, d, :], True, True, [triU, triL, LF], [pb])
        P.cp(BC[:, d, :], pb[:, 0:SEQT], [pb], [BC])
        pb2 = banks[6 + d]
        P.mm(pb2[:, 0:SEQT], ones[:], LF[:, d, :], True, True, [ones, LF], [pb2])
        P.cp(BT[:, d, :], pb2[:, 0:SEQT], [pb2], [BT])
    BIAS = P.sb([128, 2, SEQT], F32, "BIAS")
    WS = P.sb([128, 2, SEQT], F32, "WS")
    EB = P.sb([128, 2, SEQT], F32, "EB")
    DEC = P.sb([128, 2, SEQT], F32, "DEC")
    P.tt(BIAS[:], IG[:], BC[:], ALU.subtract, [IG, BC], [BIAS])
    P.tt(WS[:], BIAS[:], BT[:], ALU.add, [BIAS, BT], [WS])
    P.act(WS[:], WS[:], AF.Exp, [WS], [WS])
    P.act(EB[:], BC[:], AF.Exp, [BC], [EB])
    P.act(DEC[:], BT[:], AF.Exp, [BT], [DEC])

    if stop == 2:
        return P.finish() if own else None
    HS = P.sb([128, SEQT, 128], F32, "HS")
    CT = [[P.sb([64, 129], F32, f"CT{d}{i}") for i in range(2)] for d in range(2)]
    for d in range(2):
        P.memset(CT[d][0][:], 0.0, [CT[d][0]])
    lrep = [P.sb([128, 128], F32, f"lrep{i}") for i in range(2)]
    arg = [P.sb([128, 128], F32, f"arg{i}") for i in range(2)]
    ST = [P.sb([128, 128], F32, f"ST{i}") for i in range(2)]
    KW = [P.sb([128, 64], F32, f"KW{i}") for i in range(2)]
    it = [P.sb([128, 129], F32, f"it{i}") for i in range(2)]
    tot = [P.sb([128, 129], F32, f"tot{i}") for i in range(2)]
    sm = [P.sb([128, 2], F32, f"sm{i}") for i in range(2)]
    orders = chunk_orders()
    done = set()
    for step in range(SEQT):
        streams = []
        for d in range(2):
            P.defer = []
            c = orders[d][step]
            cs_ = slice(c * 128, (c + 1) * 128)
            Ccur, Cnew = CT[d][step % 2], CT[d][(step + 1) % 2]
            tri, neg = (triU, negU) if d == 0 else (triL, negL)
            p_brd, p_qk, p_n, p_i, p_kt, p_st = (banks[d * 4 + 0], banks[d * 4 + 1], banks[d * 4 + 2], banks[d * 4 + 3],
                                                 banks[d * 4 + 0], banks[d * 4 + 1])
            L = lrep[d]
            P.ts(L[:], ones[:], LF[:, d, c:c + 1], None, ALU.mult, None, [ones, LF], [L], eng="pool")
            P.mm(p_brd[:, 0:128], L[:], tri[:], True, True, [L, tri], [p_brd])
            A = arg[d]
            P.tt(A[:], p_brd[:, 0:128], neg[:], ALU.add, [p_brd, neg], [A])
            P.act(A[:], A[:], AF.Exp, [A, BIAS], [A], bias=BIAS[:, d, c:c + 1])
            P.mm(p_qk[:, 0:128], KT[:, cs_], QT[:, cs_], True, True, [KT, QT], [p_qk])
            S = ST[d]
            P.tt(S[:], p_qk[:, 0:128], A[:], ALU.mult, [p_qk, A], [S])
            P.mm(p_n[:, 0:129], S[:], VE[:, c, :], True, True, [S, VE], [p_n])
            P.mm(p_i[:, 0:129], QT[:, cs_], Ccur[:], True, True, [QT, Ccur], [p_i])
            I = it[d]
            P.act(I[:], p_i[:, 0:129], AF.Identity, [p_i, EB], [I], scale=EB[:, d, c:c + 1])
            T = tot[d]
            P.tt(T[:], p_n[:, 0:129], I[:], ALU.add, [p_n, I], [T])
            s_ = sm[d]
            P.act(s_[:, 0:1], T[:, 128:129], AF.Abs, [T], [s_])
            P.ts(s_[:, 0:1], s_[:, 0:1], 1.0, None, ALU.max, None, [s_], [s_])
            P.op("dve", lambda e, s_=s_: e.reciprocal(s_[:, 1:2], s_[:, 0:1]), [s_], [s_])
            if c in done:
                P.stt(HS[:, c, :], T[:, 0:128], s_[:, 1:2], HS[:, c, :], ALU.mult, ALU.add, [T, s_, HS], [HS])
            else:
                P.ts(HS[:, c, :], T[:, 0:128], s_[:, 1:2], None, ALU.mult, None, [T, s_], [HS])
                done.add(c)
            P.tr(p_kt[:, 0:64], KT[:, cs_], ident[0:64, 0:64], [KT, ident], [p_kt])
            kw = KW[d]
            P.ts(kw[:], p_kt[:, 0:64], WS[:, d, c:c + 1], None, ALU.mult, None, [p_kt, WS], [kw])
            P.mm(p_st[0:64, 0:129], kw[:], VE[:, c, :], True, True, [kw, VE], [p_st])
            P.stt(Cnew[:], Ccur[:], DEC[0:64, d, c:c + 1], p_st[0:64, 0:129], ALU.mult, ALU.add, [Ccur, DEC, p_st], [Cnew])
            streams.append(P.defer)
            P.defer = None
        for k in range(max(len(x_) for x_ in streams)):
            for x_ in streams:
                if k < len(x_):
                    x_[k]()

    if stop == 3:
        return P.finish() if own else None
    NT_ = 4
    yo = [ST[0], ST[1], arg[0], arg[1]]
    junk = [lrep[0], lrep[1], it[0], it[1]]
    yob = [P.sb([128, 128], BF16, f"yob{i}") for i in range(2)]
    st = [sm[0], sm[1], P.sb([128, 4], F32, "st2"), P.sb([128, 4], F32, "st3")]
    stv = [sm[0], sm[1]]

    def post_tile(c):
        pi = c % NT_
        Y, s_, J = yo[pi], st[pi], junk[pi]
        if pi < 2:
            s_ = None
        sA = (lambda a, b: J[:, 128 - 4 + a:128 - 4 + b]) if False else None
        P.tt(J[:, 0:128], HS[:, c, :], HS[:, c, :], ALU.mult, [HS], [J], eng="pool")
        q = qs_[pi]
        P.op("dve", lambda e, q=q, J=J: e.tensor_reduce(q[:, 0:1], J[:, 0:128], AX.X, ALU.add), [J], [q])
        P.ts(q[:, 1:2], q[:, 0:1], 1.0 / 128, EPS, ALU.mult, ALU.add, [q], [q])
        P.act(q[:, 2:3], q[:, 1:2], AF.Sqrt, [q], [q])
        P.op("dve", lambda e, q=q: e.reciprocal(q[:, 3:4], q[:, 2:3]), [q], [q])
        P.stt(Y[:], HS[:, c, :], q[:, 3:4], ogs[:], ALU.mult, ALU.mult, [HS, q, ogs], [Y])
        P.tt(Y[:], Y[:], OG[:, c, :], ALU.mult, [Y, OG], [Y])
        P.tr(banks[pi][:, 0:128], Y[:], ident[:], [Y, ident], [banks[pi]])
        Yb = yob4[pi]
        P.cp(Yb[:], banks[pi][:, 0:128], [banks[pi]], [Yb], eng="act")
        P.dma("sp", ycols(ym, c * 128), Yb[:], reads=[Yb])

    qs_ = [P.sb([128, 4], F32, f"qs{i}") for i in range(NT_)]
    yob4 = yob + [P.sb([128, 128], BF16, f"yob{i + 2}") for i in range(2)]
    for c0 in range(0, SEQT, NT_):
        streams = []
        for c in range(c0, min(SEQT, c0 + NT_)):
            P.defer = []
            post_tile(c)
            streams.append(P.defer)
            P.defer = None
        for k in range(max(len(x_) for x_ in streams)):
            for x_ in streams:
                if k < len(x_):
                    x_[k]()
    return P.finish() if own else None


def col_offsets():
    off, o = {}, 0
    for name, wdt in IN_SPLITS:
        off[name] = o
        o += wdt
    return off


def mlstm_inputs(hTs, li, w_in, m_conv, m_gate_bias, m_out_norm):
    off = col_offsets()
    ins = []
    for b in range(2):
        hT = np.ascontiguousarray(hTs[b].reshape(8, 128, SEQ_ALL))
        for h in range(4):
            W = w_in[li]
            wq = W[:, off['m_q'] + 64 * h: off['m_q'] + 64 * (h + 1)]
            wk = W[:, off['m_k'] + 64 * h: off['m_k'] + 64 * (h + 1)]
            wvo = np.concatenate([W[:, off['m_v'] + 128 * h: off['m_v'] + 128 * (h + 1)],
                                  W[:, off['m_o'] + 128 * h: off['m_o'] + 128 * (h + 1)]], 1)
            wg = np.stack([W[:, off['m_if'] + h], W[:, off['m_ff'] + h], W[:, off['m_ib'] + h], W[:, off['m_fb'] + h]], 1)
            cq = m_conv[li][:, 64 * h:64 * (h + 1)]
            ck = m_conv[li][:, 256 + 64 * h:256 + 64 * (h + 1)]
            cw = np.stack([np.broadcast_to(cq[None], (128, 3, 64)), np.broadcast_to(ck[None], (128, 3, 64))], 0)
            gbv = m_gate_bias[li][:, h]
            ins.append({"hT": hT, "wqk": np.ascontiguousarray(np.stack([wq, wk], 0)), "wvo": np.ascontiguousarray(wvo),
                        "wg": np.ascontiguousarray(wg), "cw": np.ascontiguousarray(cw.astype(np.float32)),
                        "gb": np.ascontiguousarray(np.broadcast_to(gbv[None, :], (128, 4)).astype(np.float32)),
                        "og": np.ascontiguousarray(np.broadcast_to(m_out_norm[li][None, 128 * h:128 * (h + 1)], (128, 128)))})
    return ins


R_GN_EPS = 64e-5
RWKV_INTERLEAVE = True
W_SCALE = -0.6065306597126334


def aff_mask(P, pat, cm, op, val=1.0, base=0):
    m = P.sb([128, 128], F32)
    P.memset(m[:], val, [m], eng="pool")
    P.op("pool", lambda e: e.affine_select(m[:], m[:], pat, op, 0.0, base=base, channel_multiplier=cm), [m], [m])
    return m


def build_rwkv_v1(P=None, T=None):
    own = P is None
    P = P or Prog()
    hT = P.io(T, "hT", [8, 128, SEQ_ALL], BF16, "ExternalInput")
    wrkv = P.io(T, "wrkv", [D, 384], F32, "ExternalInput")
    crkv = P.io(T, "crkv", [128, 3, 384], F32, "ExternalInput")
    wl = P.io(T, "wl", [D, 384], F32, "ExternalInput")
    w2a2 = P.io(T, "w2a2", [2, 128, 128], F32, "ExternalInput")
    bias01 = P.io(T, "bias01", [1, 2, 256], F32, "ExternalInput")
    g2 = P.io(T, "g2", [128, 128], F32, "ExternalInput")
    vecs = P.io(T, "vecs", [128, 5, 128], F32, "ExternalInput")
    yr = P.io(T, "yr", [128, SEQ_ALL], BF16, "ExternalOutput")

    bk = [P.ps([128, 512], F32, f"bank{i}") for i in range(8)]
    ident = make_ident(P)
    ones = P.sb([128, 128], F32, "ones")
    P.memset(ones[:], 1.0, [ones])
    mI = [aff_mask(P, [[1, 128]], -1, ALU.is_ge), aff_mask(P, [[-1, 128]], 1, ALU.is_ge)]
    mS = [aff_mask(P, [[1, 128]], -1, ALU.is_gt), aff_mask(P, [[-1, 128]], 1, ALU.is_gt)]
    cI = [aff_mask(P, [[1, 128]], -1, ALU.is_ge, W_SCALE), aff_mask(P, [[-1, 128]], 1, ALU.is_ge, W_SCALE)]
    cS = [aff_mask(P, [[1, 128]], -1, ALU.is_gt, W_SCALE), aff_mask(P, [[-1, 128]], 1, ALU.is_gt, W_SCALE)]
    mSI = []
    for d in range(2):
        m = P.sb([128, 256], F32)
        P.cp(m[:, 0:128], mS[d][:], [mS[d]], [m])
        P.cp(m[:, 128:256], mI[d][:], [mI[d]], [m])
        mSI.append(m)

    wst = P.sb([128, 8, 768], F32, "wst")
    for kc in range(8):
        P.dma("sp", wst[:, kc, 0:384], wrkv[kc * 128:(kc + 1) * 128, :], writes=[wst])
        P.dma("act", wst[:, kc, 384:768], wl[kc * 128:(kc + 1) * 128, :], writes=[wst])
    cws = P.sb([128, 3, 384], F32, "cws")
    P.dma("sp", cws[:], crkv[:, :, :], writes=[cws])
    Wc = P.sb([128, 3, 8, 384], BF16, "Wc")
    for tap in range(3):
        for kc in range(8):
            P.tt(Wc[:, tap, kc, :], wst[:, kc, 0:384], cws[:, tap, :], ALU.mult, [wst, cws], [Wc])
    Wl = P.sb([128, 8, 384], BF16, "Wl")
    P.cp(Wl[:], wst[:, :, 384:768], [wst], [Wl])
    W2 = P.sb([128, 2, 128], F32, "W2")
    for d in range(2):
        P.dma("act", W2[:, d, :], w2a2[d], writes=[W2])
    B01 = P.sb([1, 2, 256], F32, "B01")
    P.dma("act", B01[:], bias01[:, :, :], writes=[B01])
    G2 = P.sb([128, 128], F32, "G2")
    P.dma("act", G2[:], g2[:, :], writes=[G2])
    VEC = P.sb([128, 5, 128], F32, "VEC")
    P.dma("act", VEC[:], vecs[:, :, :], writes=[VEC])
    epsg = P.sb([128, 1], F32, "epsg")
    P.memset(epsg[:], R_GN_EPS, [epsg])

    YF = P.sb([128, SEQT, 128], F32, "YF")
    hb = [P.sb([128, 8, 130], BF16, f"hb{i}") for i in range(2)]
    STB = P.sb([128, 128], F32, "STB")

    def TL(shape, name):
        return P.sb(shape, F32, name)

    rkv = TL([128, 384], "rkv")
    pl = TL([128, 128], "pl")
    sgg = TL([128, 128], "sgg")
    sig = TL([128, 128], "sig")
    av = TL([128, 128], "av")
    t0_ = TL([128, 128], "t0_")
    ss = TL([128, 8], "ss")
    kkn = TL([128, 128], "kkn")
    bh = TL([128, 128], "bh")
    t2 = TL([128, 128], "t2")
    key = TL([128, 128], "key")
    eG = TL([128, 256], "eG")
    enG = TL([128, 128], "enG")
    eR = TL([128, 128], "eR")
    AR = TL([128, 256], "AR")
    BtT = TL([128, 128], "BtT")
    KtT = TL([128, 128], "KtT")
    Bb = TL([128, 128], "Bb")
    Kb = TL([128, 128], "Kb")
    Mm = [TL([128, 256], f"Mm{i}") for i in range(2)]
    Ak = [TL([128, 256], f"Ak{i}") for i in range(2)]
    Lm = [TL([128, 128], f"Lm{i}") for i in range(2)]
    TtA = [TL([128, 128], f"TtA{i}") for i in range(2)]
    TA = [TL([128, 128], f"TA{i}") for i in range(2)]
    Pp = [[TL([128, 128], f"Pp{i}{j}") for j in range(2)] for i in range(2)]
    Qq = [[TL([128, 128], f"Qq{i}{j}") for j in range(2)] for i in range(2)]
    X = TL([128, 128], "X")
    U = TL([128, 128], "U")
    yb = TL([128, 128], "yb")
    gn = TL([128, 16], "gn")
    yn = TL([128, 128], "yn")
    rk = TL([128, 128], "rk")
    yo = [TL([128, 128], f"yo{i}") for i in range(2)]
    yob = [P.sb([128, 128], BF16, f"yob{i}") for i in range(2)]

    orders = chunk_orders()
    for d in range(2):
        P.memset(STB[:], 0.0, [STB])
        for step in range(SEQT):
            c = orders[d][step]
            t0 = c * 128
            lo, hi = (0, NCTX) if c < CTXT else (NCTX, SEQ_ALL)
            H = hb[step % 2]
            load_hblk_halo(P, hT, H, t0, 128, lo, hi)
            n = 0
            for tap in range(3):
                for kc in range(8):
                    P.mm(bk[0][:, 0:384], H[:, kc, tap:tap + 128], Wc[:, tap, kc, :], n == 0, n == 23, [H, Wc], [bk[0]])
                    n += 1
            P.cp(rkv[:], bk[0][:, 0:384], [bk[0]], [rkv], eng="act")
            for kc in range(8):
                P.mm(bk[1][:, 0:128], Wl[:, kc, d * 128:(d + 1) * 128], H[:, kc, 1:129], kc == 0, kc == 7, [Wl, H], [bk[1]])
            if d == 1:
                for kc in range(8):
                    P.mm(bk[1][:, 128:256], Wl[:, kc, 256:384], H[:, kc, 1:129], kc == 0, kc == 7, [Wl, H], [bk[1]])
            P.act(pl[0:64, :], bk[1][0:64, 0:128], AF.Tanh, [bk[1]], [pl])
            P.cp(pl[64:128, :], bk[1][64:128, 0:128], [bk[1]], [pl])
            if d == 1:
                P.act(sgg[:], bk[1][:, 128:256], AF.Sigmoid, [bk[1]], [sgg])
            P.mm(bk[1][:, 256:384], pl[0:64, :], W2[0:64, d, :], True, False, [pl, W2], [bk[1]])
            P.mm(bk[1][:, 256:384], ones[0:1, :], B01[0:1, d, 0:128], False, True, [ones, B01], [bk[1]])
            P.mm(bk[1][:, 384:512], pl[64:128, :], W2[64:128, d, :], True, False, [pl, W2], [bk[1]])
            P.mm(bk[1][:, 384:512], ones[0:1, :], B01[0:1, d, 128:256], False, True, [ones, B01], [bk[1]])
            P.act(sig[:], bk[1][:, 256:384], AF.Sigmoid, [bk[1]], [sig])
            P.act(av[:], bk[1][:, 384:512], AF.Sigmoid, [bk[1]], [av])
            r_, k_, v_ = rkv[:, 0:128], rkv[:, 128:256], rkv[:, 256:384]
            P.tt(t0_[:], k_, VEC[:, 0, :], ALU.mult, [rkv, VEC], [t0_])
            P.tt(t2[:], t0_[:], t0_[:], ALU.mult, [t0_], [t2])
            for hh in range(2):
                P.op("dve", lambda e, hh=hh: e.tensor_reduce(ss[:, hh:hh + 1], t2[:, hh * 64:(hh + 1) * 64], AX.X, ALU.add),
                     [t2], [ss])
            P.act(ss[:, 2:4], ss[:, 0:2], AF.Sqrt, [ss], [ss])
            P.ts(ss[:, 2:4], ss[:, 2:4], 1e-12, None, ALU.max, None, [ss], [ss])
            P.op("dve", lambda e: e.reciprocal(ss[:, 4:6], ss[:, 2:4]), [ss], [ss])
            for hh in range(2):
                hs = slice(hh * 64, (hh + 1) * 64)
                P.ts(kkn[:, hs], t0_[:, hs], ss[:, 4 + hh:5 + hh], -1.0, ALU.mult, ALU.mult, [t0_, ss], [kkn])
            P.stt(bh[:], kkn[:], -1.0, av[:], ALU.mult, ALU.mult, [kkn, av], [bh])
            P.stt(t2[:], av[:], -1.0, VEC[:, 1, :], ALU.add, ALU.mult, [av, VEC], [t2])
            P.stt(key[:], t2[:], 1.0, k_, ALU.add, ALU.mult, [t2, rkv], [key])
            P.tr(bk[2][:, 0:128], r_, ident[:], [rkv, ident], [bk[2]])
            P.tr(bk[2][:, 128:256], kkn[:], ident[:], [kkn, ident], [bk[2]])
            P.tr(bk[2][:, 256:384], bh[:], ident[:], [bh, ident], [bk[2]])
            P.tr(bk[2][:, 384:512], key[:], ident[:], [key, ident], [bk[2]])
            P.mm(bk[3][:, 0:128], sig[:], cS[d][:], True, True, [sig, cS[d]], [bk[3]])
            P.mm(bk[3][:, 128:256], sig[:], cI[d][:], True, True, [sig, cI[d]], [bk[3]])
            P.mm(bk[3][:, 256:384], cS[1 - d][:], sig[:], True, True, [sig, cS[1 - d]], [bk[3]])
            P.act(eG[:], bk[3][:, 0:256], AF.Exp, [bk[3]], [eG])
            P.act(enG[:], bk[3][:, 128:256], AF.Exp, [bk[3]], [enG], scale=-1.0)
            P.act(eR[:], bk[3][:, 256:384], AF.Exp, [bk[3]], [eR])
            P.tt(AR[:, 0:128], bk[2][:, 128:256], eG[:, 0:128], ALU.mult, [bk[2], eG], [AR])
            P.tt(AR[:, 128:256], bk[2][:, 0:128], eG[:, 128:256], ALU.mult, [bk[2], eG], [AR])
            P.tt(BtT[:], bk[2][:, 256:384], enG[:], ALU.mult, [bk[2], enG], [BtT])
            P.tt(KtT[:], bk[2][:, 384:512], enG[:], ALU.mult, [bk[2], enG], [KtT])
            P.tt(Bb[:], bh[:], eR[:], ALU.mult, [bh, eR], [Bb])
            P.tt(Kb[:], key[:], eR[:], ALU.mult, [key, eR], [Kb])
            for hh in range(2):
                hp_ = slice(hh * 64, (hh + 1) * 64)
                P.mm(bk[4][:, 0:256], BtT[hp_, :], AR[hp_, :], True, True, [BtT, AR], [bk[4]])
                P.mm(bk[5][:, 0:256], KtT[hp_, :], AR[hp_, :], True, True, [KtT, AR], [bk[5]])
                P.mm(bk[4][:, 256:384], AR[hp_, 0:128], BtT[hp_, :], True, True, [AR, BtT], [bk[4]])
                P.tt(Mm[hh][:], bk[4][:, 0:256], mSI[d][:], ALU.mult, [bk[4], mSI[d]], [Mm[hh]])
                P.tt(Ak[hh][:], bk[5][:, 0:256], mSI[d][:], ALU.mult, [bk[5], mSI[d]], [Ak[hh]])
                P.tt(Lm[hh][:], bk[4][:, 256:384], mS[1 - d][:], ALU.mult, [bk[4], mS[1 - d]], [Lm[hh]])
                P.tt(TtA[hh][:], Mm[hh][:, 0:128], ident[:], ALU.add, [Mm[hh], ident], [TtA[hh]], eng="pool")
                P.tt(TA[hh][:], Lm[hh][:], ident[:], ALU.add, [Lm[hh], ident], [TA[hh]], eng="pool")
                Pc, Qc = Mm[hh], Lm[hh]
                pc_ap, qc_ap = Mm[hh][:, 0:128], Lm[hh][:]
                for lvl in range(6):
                    last = lvl == 5
                    Pn, Qn = Pp[hh][lvl % 2], Qq[hh][lvl % 2]
                    P.mm(bk[6][:, 0:128], qc_ap, pc_ap, True, True, [Pc, Qc], [bk[6]])
                    if not last:
                        P.mm(bk[6][:, 128:256], pc_ap, qc_ap, True, True, [Pc, Qc], [bk[6]])
                    P.cp(Pn[:], bk[6][:, 0:128], [bk[6]], [Pn])
                    if not last:
                        P.cp(Qn[:], bk[6][:, 128:256], [bk[6]], [Qn], eng="act")
                    P.mm(bk[6][:, 256:384], TA[hh][:], Pn[:], True, True, [TA[hh], Pn], [bk[6]])
                    if not last:
                        P.mm(bk[6][:, 384:512], Pn[:], TA[hh][:], True, True, [TA[hh], Pn], [bk[6]])
                    P.tt(TtA[hh][:], TtA[hh][:], bk[6][:, 256:384], ALU.add, [TtA[hh], bk[6]], [TtA[hh]])
                    if not last:
                        P.tt(TA[hh][:], TA[hh][:], bk[6][:, 384:512], ALU.add, [TA[hh], bk[6]], [TA[hh]])
                    Pc, Qc = Pn, Qn
                    pc_ap, qc_ap = Pn[:], Qn[:]
            P.mm(bk[7][:, 0:128], AR[:, 0:128], STB[:], True, False, [AR, STB], [bk[7]])
            for hh in range(2):
                hs = slice(hh * 64, (hh + 1) * 64)
                P.mm(bk[7][:, hs], Ak[hh][:, 0:128], rkv[:, 256 + hh * 64:256 + (hh + 1) * 64], False, hh == 1,
                     [Ak[hh], rkv], [bk[7]])
            P.cp(X[:], bk[7][:, 0:128], [bk[7]], [X])
            for hh in range(2):
                hs = slice(hh * 64, (hh + 1) * 64)
                P.mm(bk[7][:, 128 + hh * 64:128 + (hh + 1) * 64], TtA[hh][:], X[:, hs], True, True, [TtA[hh], X], [bk[7]])
            P.cp(U[:], bk[7][:, 128:256], [bk[7]], [U])
            P.mm(bk[7][:, 256:384], AR[:, 128:256], STB[:], True, False, [AR, STB], [bk[7]])
            for hh in range(2):
                ys = slice(256 + hh * 64, 256 + (hh + 1) * 64)
                hs = slice(hh * 64, (hh + 1) * 64)
                P.mm(bk[7][:, ys], Mm[hh][:, 128:256], U[:, hs], False, False, [Mm[hh], U], [bk[7]])
                P.mm(bk[7][:, ys], Ak[hh][:, 128:256], rkv[:, 256 + hh * 64:256 + (hh + 1) * 64], False, hh == 1,
                     [Ak[hh], rkv], [bk[7]])
            P.mm(bk[7][:, 384:512], Bb[:], U[:], True, False, [Bb, U], [bk[7]])
            P.mm(bk[7][:, 384:512], Kb[:], v_, False, True, [Kb, rkv], [bk[7]])
            dcol = 255 if d == 0 else 128
            if d == 0:
                P.cp(YF[:, c, :], bk[7][:, 256:384], [bk[7]], [YF], eng="act")
            else:
                P.tt(yb[:], bk[7][:, 256:384], YF[:, c, :], ALU.add, [bk[7], YF], [yb])
            for hh in range(2):
                hs = slice(hh * 64, (hh + 1) * 64)
                P.stt(STB[hs, hs], STB[hs, hs], eG[hs, dcol:dcol + 1], bk[7][hs, 384 + hh * 64:384 + (hh + 1) * 64],
                      ALU.mult, ALU.add, [STB, eG, bk[7]], [STB])
            if d == 1:
                P.mm(bk[0][:, 384:512], sgg[:], G2[:], True, True, [sgg, G2], [bk[0]])
                P.tt(t2[:], yb[:], yb[:], ALU.mult, [yb], [t2])
                for hh in range(2):
                    hs = slice(hh * 64, (hh + 1) * 64)
                    P.op("dve", lambda e, hh=hh, hs=hs: e.tensor_reduce(gn[:, hh:hh + 1], yb[:, hs], AX.X, ALU.add), [yb], [gn])
                    P.op("dve", lambda e, hh=hh, hs=hs: e.tensor_reduce(gn[:, 2 + hh:3 + hh], t2[:, hs], AX.X, ALU.add), [t2], [gn])
                P.ts(gn[:, 4:8], gn[:, 0:4], 1.0 / 64, None, ALU.mult, None, [gn], [gn])
                P.tt(gn[:, 8:10], gn[:, 4:6], gn[:, 4:6], ALU.mult, [gn], [gn])
                P.tt(gn[:, 10:12], gn[:, 6:8], gn[:, 8:10], ALU.subtract, [gn], [gn])
                P.act(gn[:, 12:14], gn[:, 10:12], AF.Sqrt, [gn, epsg], [gn], bias=epsg[:, 0:1])
                P.op("dve", lambda e: e.reciprocal(gn[:, 14:16], gn[:, 12:14]), [gn], [gn])
                for hh in range(2):
                    hs = slice(hh * 64, (hh + 1) * 64)
                    P.ts(yn[:, hs], yb[:, hs], gn[:, 4 + hh:5 + hh], gn[:, 14 + hh:15 + hh], ALU.subtract, ALU.mult,
                         [yb, gn], [yn])
                P.tt(yn[:], yn[:], VEC[:, 3, :], ALU.mult, [yn, VEC], [yn])
                P.tt(yn[:], yn[:], VEC[:, 4, :], ALU.add, [yn, VEC], [yn])
                P.tt(rk[:], r_, k_, ALU.mult, [rkv], [rk])
                P.tt(rk[:], rk[:], VEC[:, 2, :], ALU.mult, [rk, VEC], [rk])
                for hh in range(2):
                    hs = slice(hh * 64, (hh + 1) * 64)
                    P.op("dve", lambda e, hh=hh, hs=hs: e.tensor_reduce(ss[:, 6 + hh:7 + hh], rk[:, hs], AX.X, ALU.add), [rk], [ss])
                    P.stt(yn[:, hs], rkv[:, 256 + hh * 64:256 + (hh + 1) * 64], ss[:, 6 + hh:7 + hh], yn[:, hs], ALU.mult, ALU.add,
                          [rkv, ss, yn], [yn])
                Y = yo[step % 2]
                P.tt(Y[:], yn[:], bk[0][:, 384:512], ALU.mult, [yn, bk[0]], [Y])
                P.tr(bk[3][:, 384:512], Y[:], ident[:], [Y, ident], [bk[3]])
                Yb = yob[step % 2]
                P.cp(Yb[:], bk[3][:, 384:512], [bk[3]], [Yb], eng="act")
                P.dma("sp", ycols(yr, t0), Yb[:], reads=[Yb])
    return P.finish() if own else None


def build_rwkv(P=None, T=None):
    own = P is None
    P = P or Prog()
    hT = P.io(T, "hT", [8, 128, SEQ_ALL], BF16, "ExternalInput")
    wrkv = P.io(T, "wrkv", [D, 384], F32, "ExternalInput")
    crkv = P.io(T, "crkv", [128, 3, 384], F32, "ExternalInput")
    wl = P.io(T, "wl", [D, 384], F32, "ExternalInput")
    w2a2 = P.io(T, "w2a2", [2, 128, 128], F32, "ExternalInput")
    bias01 = P.io(T, "bias01", [1, 2, 256], F32, "ExternalInput")
    g2 = P.io(T, "g2", [128, 128], F32, "ExternalInput")
    vecs = P.io(T, "vecs", [128, 5, 128], F32, "ExternalInput")
    yr = P.io(T, "yr", [128, SEQ_ALL], BF16, "ExternalOutput")

    bk = [P.ps([128, 512], F32, f"bank{i}") for i in range(8)]
    ident = make_ident(P)
    ones = P.sb([128, 128], F32, "ones")
    P.memset(ones[:], 1.0, [ones])
    mI = [aff_mask(P, [[1, 128]], -1, ALU.is_ge), aff_mask(P, [[-1, 128]], 1, ALU.is_ge)]
    mS = [aff_mask(P, [[1, 128]], -1, ALU.is_gt), aff_mask(P, [[-1, 128]], 1, ALU.is_gt)]
    cI = [aff_mask(P, [[1, 128]], -1, ALU.is_ge, W_SCALE), aff_mask(P, [[-1, 128]], 1, ALU.is_ge, W_SCALE)]
    cS = [aff_mask(P, [[1, 128]], -1, ALU.is_gt, W_SCALE), aff_mask(P, [[-1, 128]], 1, ALU.is_gt, W_SCALE)]
    mSI2 = P.sb([128, 2, 2, 256], F32, "mSI2")
    mL4 = P.sb([128, 4, 128], F32, "mL4")
    I4 = P.sb([128, 4, 128], F32, "I4")
    for d in range(2):
        for hh in range(2):
            P.cp(mSI2[:, d, hh, 0:128], mS[d][:], [mS[d]], [mSI2])
            P.cp(mSI2[:, d, hh, 128:256], mI[d][:], [mI[d]], [mSI2])
            P.cp(mL4[:, d * 2 + hh, :], mS[1 - d][:], [mS[1 - d]], [mL4])
            P.cp(I4[:, d * 2 + hh, :], ident[:], [ident], [I4])

    wst = P.sb([128, 8, 384], F32, "wst")
    for kc in range(8):
        P.dma("sp", wst[:, kc, :], wrkv[kc * 128:(kc + 1) * 128, :], writes=[wst])
    cws = P.sb([128, 3, 384], F32, "cws")
    P.dma("sp", cws[:], crkv[:, :, :], writes=[cws])
    Wc = P.sb([128, 3, 8, 384], BF16, "Wc")
    for tap in range(3):
        for kc in range(8):
            P.tt(Wc[:, tap, kc, :], wst[:, kc, :], cws[:, tap, :], ALU.mult, [wst, cws], [Wc])
    Wl = P.sb([128, 8, 384], BF16, "Wl")
    for kc in range(8):
        P.dma("act", wst[:, kc, :], wl[kc * 128:(kc + 1) * 128, :], writes=[wst])
    P.cp(Wl[:], wst[:], [wst], [Wl])
    W2 = P.sb([128, 2, 128], F32, "W2")
    for d in range(2):
        P.dma("act", W2[:, d, :], w2a2[d], writes=[W2])
    B01 = P.sb([1, 2, 256], F32, "B01")
    P.dma("act", B01[:], bias01[:, :, :], writes=[B01])
    G2 = P.sb([128, 128], F32, "G2")
    P.dma("act", G2[:], g2[:, :], writes=[G2])
    VEC = P.sb([128, 5, 128], F32, "VEC")
    P.dma("act", VEC[:], vecs[:, :, :], writes=[VEC])
    VK2 = P.sb([128, 2, 2, 128], F32, "VK2")
    for j in range(2):
        for d in range(2):
            P.cp(VK2[:, j, d, :], VEC[:, j, :], [VEC], [VK2])
    epsg = P.sb([128, 1], F32, "epsg")
    P.memset(epsg[:], R_GN_EPS, [epsg])

    YD = [P.sb([128, SEQT, 128], BF16, f"YD{d}") for d in range(2)]
    hb = [P.sb([128, 8, 130], BF16, f"hb{i}") for i in range(4 if RWKV_INTERLEAVE else 2)]
    STB = [P.sb([128, 128], F32, f"STB{d}") for d in range(2)]
    for d in range(2):
        P.memset(STB[d][:], 0.0, [STB[d]])

    def TL(shape, name):
        return P.sb(shape, F32, name)

    NB_ = 2 if RWKV_INTERLEAVE else 1
    rkv_ = [TL([128, 2, 384], f"rkv{i}") for i in range(NB_)]
    pl = TL([128, 2, 128], "pl")
    sig = TL([128, 2, 128], "sig")
    av = TL([128, 2, 128], "av")
    t0_ = TL([128, 2, 128], "t0_")
    t2 = TL([128, 2, 128], "t2")
    ss = TL([128, 16], "ss")
    kkn = TL([128, 2, 128], "kkn")
    bh = TL([128, 2, 128], "bh")
    key = TL([128, 2, 128], "key")
    eG_ = [TL([128, 2, 256], f"eG{i}") for i in range(NB_)]
    enG = TL([128, 2, 128], "enG")
    eR = TL([128, 2, 128], "eR")
    AR_ = [TL([128, 2, 256], f"AR{i}") for i in range(NB_)]
    BtT = TL([128, 2, 128], "BtT")
    KtT = TL([128, 2, 128], "KtT")
    Bb_ = [TL([128, 2, 128], f"Bb{i}") for i in range(NB_)]
    Kb_ = [TL([128, 2, 128], f"Kb{i}") for i in range(NB_)]
    MA_ = [TL([128, 2, 2, 256], f"MA{i}") for i in range(NB_)]
    AK_ = [TL([128, 2, 2, 256], f"AK{i}") for i in range(NB_)]
    L4_ = [TL([128, 4, 128], f"L4{i}") for i in range(NB_)]
    P4 = [TL([128, 4, 128], f"P4{i}") for i in range(2)]
    Q4 = [TL([128, 4, 128], f"Q4{i}") for i in range(2)]
    TtA = TL([128, 4, 128], "TtA")
    TA = TL([128, 4, 128], "TA")
    X2 = TL([128, 2, 128], "X2")
    U2 = TL([128, 2, 128], "U2")

    orders = chunk_orders()

    def pre(step):
        par = (step % 2) if RWKV_INTERLEAVE else 0
        rkv, eG, AR, Bb, Kb, MA, AK, L4 = rkv_[par], eG_[par], AR_[par], Bb_[par], Kb_[par], MA_[par], AK_[par], L4_[par]
        cc = [orders[0][step], orders[1][step]]
        for d in range(2):
            c = cc[d]
            lo, hi = (0, NCTX) if c < CTXT else (NCTX, SEQ_ALL)
            H = hb[(par * 2 + d) % len(hb)]
            load_hblk_halo(P, hT, H, c * 128, 128, lo, hi, eng="sp" if d == 0 else "act")
            n = 0
            for tap in range(3):
                for kc in range(8):
                    P.mm(bk[d][:, 0:384], H[:, kc, tap:tap + 128], Wc[:, tap, kc, :], n == 0, n == 23, [H, Wc], [bk[d]])
                    n += 1
            for kc in range(8):
                P.mm(bk[d][:, 384:512], Wl[:, kc, d * 128:(d + 1) * 128], H[:, kc, 1:129], kc == 0, kc == 7, [Wl, H], [bk[d]])
            P.cp(rkv[:, d, :], bk[d][:, 0:384], [bk[d]], [rkv], eng="act")
            P.act(pl[0:64, d, :], bk[d][0:64, 384:512], AF.Tanh, [bk[d]], [pl])
            P.cp(pl[64:128, d, :], bk[d][64:128, 384:512], [bk[d]], [pl])
        for d in range(2):
            P.mm(bk[2][:, d * 256:d * 256 + 128], pl[0:64, d, :], W2[0:64, d, :], True, False, [pl, W2], [bk[2]])
            P.mm(bk[2][:, d * 256:d * 256 + 128], ones[0:1, :], B01[0:1, d, 0:128], False, True, [ones, B01], [bk[2]])
            P.mm(bk[2][:, d * 256 + 128:d * 256 + 256], pl[64:128, d, :], W2[64:128, d, :], True, False, [pl, W2], [bk[2]])
            P.mm(bk[2][:, d * 256 + 128:d * 256 + 256], ones[0:1, :], B01[0:1, d, 128:256], False, True, [ones, B01], [bk[2]])
        b2v = bk[2][:, :].rearrange("p (d j c) -> p d j c", d=2, j=2)
        P.act(sig[:], b2v[:, :, 0, :], AF.Sigmoid, [bk[2]], [sig])
        P.act(av[:], b2v[:, :, 1, :], AF.Sigmoid, [bk[2]], [av])
        k2 = rkv[:, :, 128:256]
        P.tt(t0_[:], k2, VK2[:, 0], ALU.mult, [rkv, VK2], [t0_])
        P.tt(t2[:], t0_[:], t0_[:], ALU.mult, [t0_], [t2])
        P.op("dve", lambda e: e.tensor_reduce(ss[:, 0:4], t2[:, :, :].rearrange("p d (h k) -> p (d h) k", h=2), AX.X, ALU.add),
             [t2], [ss])
        P.act(ss[:, 4:8], ss[:, 0:4], AF.Sqrt, [ss], [ss])
        P.ts(ss[:, 4:8], ss[:, 4:8], 1e-12, None, ALU.max, None, [ss], [ss])
        P.op("dve", lambda e: e.reciprocal(ss[:, 8:12], ss[:, 4:8]), [ss], [ss])
        for d in range(2):
            for hh in range(2):
                hs = slice(hh * 64, (hh + 1) * 64)
                q = d * 2 + hh
                P.ts(kkn[:, d, hs], t0_[:, d, hs], ss[:, 8 + q:9 + q], -1.0, ALU.mult, ALU.mult, [t0_, ss], [kkn])
        P.stt(bh[:], kkn[:], -1.0, av[:], ALU.mult, ALU.mult, [kkn, av], [bh])
        P.stt(t2[:], av[:], -1.0, VK2[:, 1], ALU.add, ALU.mult, [av, VK2], [t2])
        P.stt(key[:], t2[:], 1.0, k2, ALU.add, ALU.mult, [t2, rkv], [key])
        for d in range(2):
            tb = bk[3] if d == 0 else bk[7]
            P.tr(tb[:, 0:128], rkv[:, d, 0:128], ident[:], [rkv, ident], [tb])
            P.tr(tb[:, 128:256], kkn[:, d, :], ident[:], [kkn, ident], [tb])
            P.tr(tb[:, 256:384], bh[:, d, :], ident[:], [bh, ident], [tb])
            P.tr(tb[:, 384:512], key[:, d, :], ident[:], [key, ident], [tb])
            gb_ = bk[d]
            P.mm(gb_[:, 0:128], sig[:, d, :], cS[d][:], True, True, [sig, cS[d]], [gb_])
            P.mm(gb_[:, 128:256], sig[:, d, :], cI[d][:], True, True, [sig, cI[d]], [gb_])
            P.mm(gb_[:, 256:384], cS[1 - d][:], sig[:, d, :], True, True, [sig, cS[1 - d]], [gb_])
        for d in range(2):
            tb, gb_ = (bk[3] if d == 0 else bk[7]), bk[d]
            P.act(eG[:, d, :], gb_[:, 0:256], AF.Exp, [gb_], [eG])
            P.act(enG[:, d, :], gb_[:, 128:256], AF.Exp, [gb_], [enG], scale=-1.0)
            P.act(eR[:, d, :], gb_[:, 256:384], AF.Exp, [gb_], [eR])
            P.tt(AR[:, d, 0:128], tb[:, 128:256], eG[:, d, 0:128], ALU.mult, [tb, eG], [AR])
            P.tt(AR[:, d, 128:256], tb[:, 0:128], eG[:, d, 128:256], ALU.mult, [tb, eG], [AR])
            P.tt(BtT[:, d, :], tb[:, 256:384], enG[:, d, :], ALU.mult, [tb, enG], [BtT])
            P.tt(KtT[:, d, :], tb[:, 384:512], enG[:, d, :], ALU.mult, [tb, enG], [KtT])
        P.tt(Bb[:], bh[:], eR[:], ALU.mult, [bh, eR], [Bb], eng="pool")
        P.tt(Kb[:], key[:], eR[:], ALU.mult, [key, eR], [Kb], eng="pool")
        for d in range(2):
            for hh in range(2):
                hp_ = slice(hh * 64, (hh + 1) * 64)
                q = d * 2 + hh
                mb = bk[d]
                P.mm(mb[:, hh * 256:(hh + 1) * 256], BtT[hp_, d, :], AR[hp_, d, :], True, True, [BtT, AR], [mb])
                ab = bk[2] if d == 0 else bk[7]
                P.mm(ab[:, hh * 256:(hh + 1) * 256], KtT[hp_, d, :], AR[hp_, d, :], True, True, [KtT, AR], [ab])
                P.mm(bk[3][:, q * 128:(q + 1) * 128], AR[hp_, d, 0:128], BtT[hp_, d, :], True, True, [AR, BtT], [bk[3]])
        for d in range(2):
            mb, ab = bk[d], (bk[2] if d == 0 else bk[7])
            P.tt(MA[:, d], mb[:, :].rearrange("p (h c) -> p h c", h=2), mSI2[:, d], ALU.mult, [mb, mSI2], [MA])
            P.tt(AK[:, d], ab[:, :].rearrange("p (h c) -> p h c", h=2), mSI2[:, d], ALU.mult, [ab, mSI2], [AK])
        P.tt(L4[:], bk[3][:, :].rearrange("p (q c) -> p q c", q=4), mL4[:], ALU.mult, [bk[3], mL4], [L4])

    def inv_chain(step, fill):
        par = (step % 2) if RWKV_INTERLEAVE else 0
        rkv, eG, AR, Bb, Kb, MA, AK, L4 = rkv_[par], eG_[par], AR_[par], Bb_[par], Kb_[par], MA_[par], AK_[par], L4_[par]
        cc = [orders[0][step], orders[1][step]]
        npts = 16
        per = (len(fill) + npts - 1) // npts if fill else 0
        pos = [0]

        def filler():
            if not RWKV_INTERLEAVE:
                return
            for th in fill[pos[0]:pos[0] + per]:
                th()
            pos[0] += per

        M4 = MA[:, :, :, 0:128].rearrange("p d h c -> p (d h) c")
        P.tt(TtA[:], M4, I4[:], ALU.add, [MA, I4], [TtA], eng="pool")
        P.tt(TA[:], L4[:], I4[:], ALU.add, [L4, I4], [TA], eng="pool")
        Pc_buf, Qc_buf = MA, L4
        pc = lambda q: MA[:, q // 2, q % 2, 0:128]
        qc = lambda q: L4[:, q, :]
        bP, bQ, bT, bTT = bk[4], bk[5], bk[6], bk[4]
        for lvl in range(6):
            last = lvl == 5
            Pn, Qn = P4[lvl % 2], Q4[lvl % 2]
            for q in range(4):
                P.mm(bP[:, q * 128:(q + 1) * 128], qc(q), pc(q), True, True, [Pc_buf, Qc_buf], [bP])
            if not last:
                for q in range(4):
                    P.mm(bQ[:, q * 128:(q + 1) * 128], pc(q), qc(q), True, True, [Pc_buf, Qc_buf], [bQ])
            P.cp(Pn[:], bP[:, :].rearrange("p (q c) -> p q c", q=4), [bP], [Pn])
            if not last:
                P.cp(Qn[:], bQ[:, :].rearrange("p (q c) -> p q c", q=4), [bQ], [Qn], eng="act")
            filler()
            for q in range(4):
                P.mm(bT[:, q * 128:(q + 1) * 128], TA[:, q, :], Pn[:, q, :], True, True, [TA, Pn], [bT])
            if not last:
                for q in range(4):
                    P.mm(bTT[:, q * 128:(q + 1) * 128], Pn[:, q, :], TA[:, q, :], True, True, [TA, Pn], [bTT])
            P.tt(TtA[:], TtA[:], bT[:, :].rearrange("p (q c) -> p q c", q=4), ALU.add, [TtA, bT], [TtA])
            if not last:
                P.tt(TA[:], TA[:], bTT[:, :].rearrange("p (q c) -> p q c", q=4), ALU.add, [TA, bTT], [TA])
            filler()
            Pc_buf, Qc_buf = Pn, Qn
            pc = lambda q, Pn=Pn: Pn[:, q, :]
            qc = lambda q, Qn=Qn: Qn[:, q, :]
        bXU, bYS = bk[5], bk[6]
        for d in range(2):
            P.mm(bXU[:, d * 128:(d + 1) * 128], AR[:, d, 0:128], STB[d][:], True, False, [AR, STB[d]], [bXU])
            for hh in range(2):
                o = d * 128 + hh * 64
                P.mm(bXU[:, o:o + 64], AK[:, d, hh, 0:128], rkv[:, d, 256 + hh * 64:256 + (hh + 1) * 64], False, hh == 1,
                     [AK, rkv], [bXU])
        P.cp(X2[:], bXU[:, 0:256].rearrange("p (d c) -> p d c", d=2), [bXU], [X2])
        filler()
        for d in range(2):
            for hh in range(2):
                o = 256 + d * 128 + hh * 64
                P.mm(bXU[:, o:o + 64], TtA[:, d * 2 + hh, :], X2[:, d, hh * 64:(hh + 1) * 64], True, True, [TtA, X2], [bXU])
        P.cp(U2[:], bXU[:, 256:512].rearrange("p (d c) -> p d c", d=2), [bXU], [U2])
        filler()
        for d in range(2):
            P.mm(bYS[:, d * 128:(d + 1) * 128], AR[:, d, 128:256], STB[d][:], True, False, [AR, STB[d]], [bYS])
            for hh in range(2):
                o = d * 128 + hh * 64
                hs = slice(hh * 64, (hh + 1) * 64)
                P.mm(bYS[:, o:o + 64], MA[:, d, hh, 128:256], U2[:, d, hs], False, False, [MA, U2], [bYS])
                P.mm(bYS[:, o:o + 64], AK[:, d, hh, 128:256], rkv[:, d, 256 + hh * 64:256 + (hh + 1) * 64], False, hh == 1,
                     [AK, rkv], [bYS])
        for d in range(2):
            P.mm(bYS[:, 256 + d * 128:256 + (d + 1) * 128], Bb[:, d, :], U2[:, d, :], True, False, [Bb, U2], [bYS])
            P.mm(bYS[:, 256 + d * 128:256 + (d + 1) * 128], Kb[:, d, :], rkv[:, d, 256:384], False, True, [Kb, rkv], [bYS])
        for d in range(2):
            P.cp(YD[d][:, cc[d], :], bYS[:, d * 128:(d + 1) * 128], [bYS], [YD[d]], eng="act")
            dcol = 255 if d == 0 else 128
            for hh in range(2):
                hs = slice(hh * 64, (hh + 1) * 64)
                o = 256 + d * 128 + hh * 64
                P.stt(STB[d][hs, hs], STB[d][hs, hs], eG[hs, d, dcol:dcol + 1], bYS[hs, o:o + 64], ALU.mult, ALU.add,
                      [STB[d], eG, bYS], [STB[d]])
        filler()
        for th in fill[pos[0]:]:
            th()

    pre(0)
    for step in range(SEQT):
        fill = []
        if RWKV_INTERLEAVE and step + 1 < SEQT:
            P.defer = []
            pre(step + 1)
            fill, P.defer = P.defer, None
        inv_chain(step, fill)
        if not RWKV_INTERLEAVE and step + 1 < SEQT:
            pre(step + 1)

    NT_ = 4
    rk2 = [TL([128, 384], f"rk2{i}") for i in range(NT_)]
    sgg = [TL([128, 128], f"sgg{i}") for i in range(NT_)]
    ybs = [TL([128, 128], f"ybs{i}") for i in range(NT_)]
    tq = [TL([128, 128], f"tq{i}") for i in range(NT_)]
    gn = [TL([128, 16], f"gn{i}") for i in range(NT_)]
    yn = [TL([128, 128], f"yn{i}") for i in range(NT_)]
    rk = [TL([128, 128], f"rk{i}") for i in range(NT_)]
    yo = [TL([128, 128], f"yo{i}") for i in range(NT_)]
    yob = [P.sb([128, 128], BF16, f"yob{i}") for i in range(NT_)]
    hbo = [P.sb([128, 8, 130], BF16, f"hbo{i}") for i in range(NT_)]

    def out_tile(c):
        pi = c % NT_
        lo, hi = (0, NCTX) if c < CTXT else (NCTX, SEQ_ALL)
        H = hbo[pi]
        load_hblk_halo(P, hT, H, c * 128, 128, lo, hi)
        b0, b1 = bk[pi * 2], bk[pi * 2 + 1]
        n = 0
        for tap in range(3):
            for kc in range(8):
                P.mm(b0[:, 0:384], H[:, kc, tap:tap + 128], Wc[:, tap, kc, :], n == 0, n == 23, [H, Wc], [b0])
                n += 1
        for kc in range(8):
            P.mm(b1[:, 0:128], Wl[:, kc, 256:384], H[:, kc, 1:129], kc == 0, kc == 7, [Wl, H], [b1])
        R_ = rk2[pi]
        P.cp(R_[:], b0[:, 0:384], [b0], [R_], eng="act")
        P.act(sgg[pi][:], b1[:, 0:128], AF.Sigmoid, [b1], [sgg[pi]])
        P.mm(b1[:, 128:256], sgg[pi][:], G2[:], True, True, [sgg[pi], G2], [b1])
        yb, t2_, g_, y_ = ybs[pi], tq[pi], gn[pi], yn[pi]
        P.tt(yb[:], YD[0][:, c, :], YD[1][:, c, :], ALU.add, [YD[0], YD[1]], [yb])
        P.tt(t2_[:], yb[:], yb[:], ALU.mult, [yb], [t2_], eng="pool")
        P.op("dve", lambda e, g_=g_, yb=yb: e.tensor_reduce(g_[:, 0:2], yb[:, :].rearrange("p (h k) -> p h k", h=2), AX.X, ALU.add),
             [yb], [g_])
        P.op("dve", lambda e, g_=g_, t2_=t2_: e.tensor_reduce(g_[:, 2:4], t2_[:, :].rearrange("p (h k) -> p h k", h=2), AX.X, ALU.add),
             [t2_], [g_])
        P.ts(g_[:, 4:8], g_[:, 0:4], 1.0 / 64, None, ALU.mult, None, [g_], [g_])
        P.tt(g_[:, 8:10], g_[:, 4:6], g_[:, 4:6], ALU.mult, [g_], [g_])
        P.tt(g_[:, 10:12], g_[:, 6:8], g_[:, 8:10], ALU.subtract, [g_], [g_])
        P.act(g_[:, 12:14], g_[:, 10:12], AF.Sqrt, [g_, epsg], [g_], bias=epsg[:, 0:1])
        P.op("dve", lambda e, g_=g_: e.reciprocal(g_[:, 14:16], g_[:, 12:14]), [g_], [g_])
        for hh in range(2):
            hs = slice(hh * 64, (hh + 1) * 64)
            P.ts(y_[:, hs], yb[:, hs], g_[:, 4 + hh:5 + hh], g_[:, 14 + hh:15 + hh], ALU.subtract, ALU.mult, [yb, g_], [y_])
        P.tt(y_[:], y_[:], VEC[:, 3, :], ALU.mult, [y_, VEC], [y_])
        P.tt(y_[:], y_[:], VEC[:, 4, :], ALU.add, [y_, VEC], [y_])
        rk_ = rk[pi]
        P.tt(rk_[:], R_[:, 0:128], R_[:, 128:256], ALU.mult, [R_], [rk_], eng="pool")
        P.tt(rk_[:], rk_[:], VEC[:, 2, :], ALU.mult, [rk_, VEC], [rk_], eng="pool")
        P.op("dve", lambda e, g_=g_, rk_=rk_: e.tensor_reduce(g_[:, 0:2], rk_[:, :].rearrange("p (h k) -> p h k", h=2), AX.X, ALU.add),
             [rk_], [g_])
        for hh in range(2):
            hs = slice(hh * 64, (hh + 1) * 64)
            P.stt(y_[:, hs], R_[:, 256 + hh * 64:256 + (hh + 1) * 64], g_[:, hh:hh + 1], y_[:, hs], ALU.mult, ALU.add,
                  [R_, g_, y_], [y_])
        Y = yo[pi]
        P.tt(Y[:], y_[:], b1[:, 128:256], ALU.mult, [y_, b1], [Y])
        P.tr(b1[:, 256:384], Y[:], ident[:], [Y, ident], [b1])
        Yb = yob[pi]
        P.cp(Yb[:], b1[:, 256:384], [b1], [Yb], eng="act")
        P.dma("sp", ycols(yr, c * 128), Yb[:], reads=[Yb])

    for c0 in range(0, SEQT, NT_):
        streams = []
        for c in range(c0, min(SEQT, c0 + NT_)):
            P.defer = []
            out_tile(c)
            streams.append(P.defer)
            P.defer = None
        for k in range(max(len(st_) for st_ in streams)):
            for st_ in streams:
                if k < len(st_):
                    st_[k]()
    return P.finish() if own else None


def rwkv_inputs(hTs, li, w_in, r_conv, r_w0, r_w2, r_a0, r_a2, r_g2, r_kk, r_ka, r_rk, r_ln_w, r_ln_b):
    off = col_offsets()
    ins = []
    W = w_in[li]
    for b in range(2):
        hT = np.ascontiguousarray(hTs[b].reshape(8, 128, SEQ_ALL))
        for hp in range(4):
            cs_ = slice(128 * hp, 128 * (hp + 1))
            wrkv = np.concatenate([W[:, off[n] + 128 * hp: off[n] + 128 * (hp + 1)] for n in ('r_r', 'r_k', 'r_v')], 1)
            conv = np.concatenate([r_conv[li][:, j * 512 + 128 * hp: j * 512 + 128 * (hp + 1)] for j in range(3)], 1)
            wl = np.concatenate([W[:, off['r_wf']:off['r_wf'] + 64], W[:, off['r_af']:off['r_af'] + 64],
                                 W[:, off['r_wb']:off['r_wb'] + 64], W[:, off['r_ab']:off['r_ab'] + 64],
                                 W[:, off['r_g']:off['r_g'] + 128]], 1)
            w2a2 = np.stack([np.concatenate([r_w2[li, d][:, cs_], r_a2[li, d][:, cs_]], 0) for d in range(2)], 0)
            bias01 = np.stack([np.concatenate([r_w0[li, d][cs_], r_a0[li, d][cs_]], 0) for d in range(2)], 0)[None]
            vecs = np.stack([np.broadcast_to(v[li][None, cs_], (128, 128)) for v in (r_kk, r_ka, r_rk, r_ln_w, r_ln_b)], 1)
            ins.append({"hT": hT, "wrkv": np.ascontiguousarray(wrkv),
                        "crkv": np.ascontiguousarray(np.broadcast_to(conv[None], (128, 3, 384)).astype(np.float32)),
                        "wl": np.ascontiguousarray(wl), "w2a2": np.ascontiguousarray(w2a2.astype(np.float32)),
                        "bias01": np.ascontiguousarray(bias01.astype(np.float32)),
                        "g2": np.ascontiguousarray(r_g2[li][:, cs_]), "vecs": np.ascontiguousarray(vecs.astype(np.float32))})
    return ins


def build_merge(P=None, T=None, fused=False):
    own = P is None
    P = P or Prog()
    x = P.io(T, "x", [NT, 128, D], F32, "ExternalInput")
    hT = P.io(T, "hT", [8, 128, TOK], BF16, "ExternalInput")
    if fused:
        yall = T["yall"]
        sel = T["sel"]
    else:
        yT = P.io(T, "yT", [12, 128, TOK], BF16, "ExternalInput")
    wg = P.io(T, "wg", [D, 3 * D], F32, "ExternalInput")
    wb = P.io(T, "wb", [3, 512, D], F32, "ExternalInput")
    wo = P.io(T, "wo", [D, D], F32, "ExternalInput")
    gateb = P.io(T, "gateb", [2, 128, D], F32, "ExternalInput")
    xo = P.io(T, "xo", [NT, 128, D], F32, "ExternalOutput")

    Wg = P.sb([128, 8, 3 * D], BF16, "Wg")
    Pb = P.sb([128, 12, D], BF16, "Pb")
    Wo = P.sb([128, 8, D], BF16, "Wo")
    stage = [P.sb([128, 1024], F32, f"stg{i}") for i in range(2)]
    n = 0
    for kc in range(8):
        for q in range(3):
            load_cast(P, Wg, Wg[:, kc, q * 1024:(q + 1) * 1024], wg[kc * 128:(kc + 1) * 128, q * 1024:(q + 1) * 1024],
                      stage, None, n, 1024)
            n += 1
    for br in range(3):
        for c in range(4):
            load_cast(P, Pb, Pb[:, br * 4 + c, :], wb[br, c * 128:(c + 1) * 128, :], stage, None, n, 1024)
            n += 1
    for kc in range(8):
        load_cast(P, Wo, Wo[:, kc, :], wo[kc * 128:(kc + 1) * 128, :], stage, None, n, 1024)
        n += 1
    gates = P.sb([128, 2, D], F32, "gates")
    for s in range(2):
        P.dma("act", gates[:, s, :], gateb[s], writes=[gates])

    X = P.sb([128, 3, D], F32, "X")
    hb = P.sb([128, 8, 384], BF16, "hb")
    yb = P.sb([128, 12, 384], BF16, "yb")
    if fused:
        yc = [P.sb([128, 12, 384], BF16, f"yc{i}") for i in range(4)]
        sels = P.sb([128, 4], F32, "sels")
        P.dma("act", sels[:], sel[:, :], writes=[sels])

    zT = P.sb([128, 8, 384], BF16, "zT")
    NS_ = 3
    sg = [P.sb([128, 384], F32, f"sg{i}") for i in range(NS_)]
    za = [P.sb([128, 384], F32, f"za{i}") for i in range(NS_)]
    tm = [P.sb([128, 384], F32, f"tm{i}") for i in range(NS_)]
    tmp = [P.sb([128, 512], F32, f"tmp{i}") for i in range(2)]
    pg = [P.ps([128, 512], F32, f"pg{i}") for i in range(NS_)]
    pp = [P.ps([128, 512], F32, f"pp{i}") for i in range(NS_)]
    py = [P.ps([128, 512], F32, f"py{i}") for i in range(2)]

    for bi, (t0, nb) in enumerate(BLOCKS):
        s = 0 if bi == 0 else 1
        N = nb * 128
        for i in range(nb):
            P.dma("sp", X[:, i, :], x[t0 + i], writes=[X])
        for kc in range(8):
            P.dma("act", hb[:, kc, 0:N], hT[kc, :, t0 * 128:t0 * 128 + N], writes=[hb])
        if fused:
            for cand in range(4):
                for c in range(12):
                    j, br = c % 4, c // 4
                    P.dma("sp" if c % 2 else "act", yc[cand][:, c, 0:N],
                          yall[br, cand, j * 128:(j + 1) * 128, t0 * 128:t0 * 128 + N], writes=[yc[cand]])
            P.ts(yb[:, :, 0:N], yc[0][:, :, 0:N], sels[:, 0:1], None, ALU.mult, None, [yc[0], sels], [yb])
            for cand in range(1, 4):
                P.stt(yb[:, :, 0:N], yc[cand][:, :, 0:N], sels[:, cand:cand + 1], yb[:, :, 0:N], ALU.mult, ALU.add,
                      [yc[cand], sels, yb], [yb])
        else:
            for c in range(12):
                P.dma("sp" if c % 2 else "act", yb[:, c, 0:N], yT[c, :, t0 * 128:t0 * 128 + N], writes=[yb])
        def zchain(dc, si):
            G, Q, S, T_, Z = pg[si], pp[si], sg[si], tm[si], za[si]
            for br in range(3):
                for kc in range(8):
                    P.mm(G[:, 0:N], Wg[:, kc, br * D + dc * 128: br * D + (dc + 1) * 128], hb[:, kc, 0:N], kc == 0, kc == 7,
                         [Wg, hb], [G])
                for c in range(4):
                    P.mm(Q[:, 0:N], Pb[:, br * 4 + c, dc * 128:(dc + 1) * 128], yb[:, br * 4 + c, 0:N], c == 0, c == 3,
                         [Pb, yb], [Q])
                P.act(S[:, 0:N], G[:, 0:N], AF.Sigmoid, [G], [S])
                if br == 0:
                    P.tt(Z[:, 0:N], S[:, 0:N], Q[:, 0:N], ALU.mult, [S, Q], [Z])
                else:
                    P.tt(T_[:, 0:N], S[:, 0:N], Q[:, 0:N], ALU.mult, [S, Q], [T_])
                    if br == 1:
                        P.tt(Z[:, 0:N], Z[:, 0:N], T_[:, 0:N], ALU.add, [Z, T_], [Z], eng="pool")
                    else:
                        P.tt(zT[:, dc, 0:N], Z[:, 0:N], T_[:, 0:N], ALU.add, [Z, T_], [zT])

        for d0 in range(0, 8, NS_):
            streams = []
            for si, dc in enumerate(range(d0, min(8, d0 + NS_))):
                P.defer = []
                zchain(dc, si)
                streams.append(P.defer)
                P.defer = None
            for k_ in range(max(len(x_) for x_ in streams)):
                for x_ in streams:
                    if k_ < len(x_):
                        x_[k_]()
        for i in range(nb):
            for h in range(2):
                Y = py[(i * 2 + h) % 2]
                for dc in range(8):
                    P.mm(Y[:, :], zT[:, dc, i * 128:(i + 1) * 128], Wo[:, dc, h * 512:(h + 1) * 512], dc == 0, dc == 7,
                         [zT, Wo], [Y])
                T2 = tmp[(i * 2 + h) % 2]
                P.tt(T2[:], Y[:], gates[:, s, h * 512:(h + 1) * 512], ALU.mult, [Y, gates], [T2])
                P.tt(X[:, i, h * 512:(h + 1) * 512], X[:, i, h * 512:(h + 1) * 512], T2[:], ALU.add, [X, T2], [X], eng="pool")
            P.dma("sp", xo[t0 + i], X[:, i, :], reads=[X])
    return P.finish() if own else None


def featT_shard(y_b, nchunk):
    pad = np.zeros((4 * TOK, y_b.shape[1]), y_b.dtype)
    pad[:y_b.shape[0]] = y_b
    out = []
    for i in range(4):
        blk = pad[i * TOK:(i + 1) * TOK]
        out.append(np.ascontiguousarray(blk.T.reshape(nchunk, 128, TOK)))
    return out


def merge_inputs(xs, hTs, ymT, yrT, yaT, mods_l, li, w_in, w_branch, w_o):
    off = col_offsets()
    m = mods_l.reshape(3, 9, D)
    wg = np.ascontiguousarray(w_in[li][:, off['g_m']:off['g_m'] + 3 * D])
    ins = []
    for b in range(2):
        xsh = tok_shard(xs[b])
        hsh = featT_shard(np.ascontiguousarray(hTs[b].T), 8)
        yfull = np.concatenate([ymT[b], yrT[b], yaT[b]], axis=0)
        ypad = np.zeros((1536, 4 * TOK), yfull.dtype)
        ypad[:, :SEQ_ALL] = yfull
        for i in range(4):
            gb = np.stack([np.broadcast_to(m[2 if i == 0 else b, 5], (128, D)), np.broadcast_to(m[b, 5], (128, D))], 0)
            ins.append({"x": xsh[i], "hT": hsh[i],
                        "yT": np.ascontiguousarray(ypad[:, i * TOK:(i + 1) * TOK].reshape(12, 128, TOK)),
                        "wg": wg, "wb": w_branch[li], "wo": w_o[li], "gateb": np.ascontiguousarray(gb)})
    return ins


def emit_mod_fused(P, T):
    cT = T["cT"]
    ones = P.sb([128, 128], F32, "ones")
    P.memset(ones[:], 1.0, [ones])
    cs = P.sb([128, 8, 2], F32, "cs")
    sg = P.sb([128, 8, 2], F32, "sg")
    P.dma("sp", cs[:], cT[:, :, :], writes=[cs])
    P.act(sg[:], cs[:], AF.Sigmoid, [cs], [sg])
    P.tt(cs[:], cs[:], sg[:], ALU.mult, [cs, sg], [cs])
    crep = [P.sb([128, 8, 128], F32, f"crep{i}") for i in range(2)]
    for st in range(2):
        for kc in range(8):
            P.ts(crep[st][:, kc, :], ones[:], cs[:, kc, st:st + 1], None, ALU.mult, None, [ones, cs], [crep[st]])
    Wk = [P.sb([128, 8, 1024], F32, f"Wk{i}") for i in range(2)]
    pm = [P.ps([128, 512], F32, f"pm{i}") for i in range(2)]
    pg = [P.ps([128, 512], F32, f"pgm{i}") for i in range(2)]
    n = 0
    for l in range(2):
        bpps = P.sb([128, 72], F32, f"bpps{l}")
        P.dma("act", bpps[:], T[f"bpp{l}"][:, :], writes=[bpps])
        bgbs = P.sb([128, 3, 1024], F32, f"bgbs{l}")
        for gi in range(3):
            P.dma("act", bgbs[:, gi, :], T[f"bgb{l}"][gi], writes=[bgbs])
        ngs = P.sb([128, 24], F32, f"ngs{l}")
        P.dma("act", ngs[:], T[f"ng{l}"][:, :], writes=[ngs])
        MODT = P.sb([128, 2, 72], F32, f"MODT{l}")
        for k in range(9):
            W = Wk[n % 2]
            for kc in range(8):
                P.dma("sp", W[:, kc, :], T[f"wada{l}"][kc * 128:(kc + 1) * 128, k * 1024:(k + 1) * 1024], writes=[W])
            ps = pm[n % 2]
            for c in range(8):
                for kc in range(8):
                    P.mm(ps[:, 2 * c:2 * c + 2], W[:, kc, c * 128:(c + 1) * 128], cs[:, kc, :], kc == 0, kc == 7, [W, cs], [ps])
            for st in range(2):
                P.tt(MODT[:, st, k * 8:(k + 1) * 8], ps[:, st:16:2], bpps[:, k * 8:(k + 1) * 8], ALU.add, [ps, bpps], [MODT])
            if k in (2, 5, 8):
                gi = (2, 5, 8).index(k)
                GB = P.sb([128, 2, 1024], F32, f"GB{l}{gi}")
                for st in range(2):
                    for h in range(2):
                        pq = pg[(st * 2 + h) % 2]
                        for kc in range(8):
                            P.mm(pq[:, :], crep[st][:, kc, :], W[:, kc, h * 512:(h + 1) * 512], kc == 0, kc == 7, [crep[st], W], [pq])
                        P.tt(GB[:, st, h * 512:(h + 1) * 512], pq[:, :], bgbs[:, gi, h * 512:(h + 1) * 512], ALU.add, [pq, bgbs], [GB])
                    P.dma("act", T[f"gb{l}_{gi}"][st], GB[:, st, :], reads=[GB])
            n += 1
        for which, ks in ((0, (0, 1, None, 3, 4, None)), (1, (6, 7, None, None, None, None))):
            PP = P.sb([128, 96], F32, f"PP{l}{which}")
            P.memset(PP[:], 0.0, [PP])
            for st in range(2):
                for j, k in enumerate(ks):
                    dst = PP[:, st * 48 + j * 8: st * 48 + (j + 1) * 8]
                    if k is not None:
                        P.cp(dst, MODT[:, st, k * 8:(k + 1) * 8], [MODT], [PP])
                    elif j == 2:
                        gidx = 0 if which == 0 else 2
                        P.cp(dst, ngs[:, gidx * 8:(gidx + 1) * 8], [ngs], [PP])
                    elif j == 5 and which == 0:
                        P.cp(dst, ngs[:, 8:16], [ngs], [PP])
            P.dma("sp", T[f"pp{l}_{which}"][:, :], PP[:], reads=[PP])


def build_fused(upto=99, dbg=None):
    P = Prog()
    E = lambda name, shape, dt=F32: P.dram(name, shape, dt, "ExternalInput")
    x0 = E("x0", [NT, 128, D])
    out = P.dram("out", [NT, 128, D], F32, "ExternalOutput")
    sel = E("sel", [128, 4])
    Tm = {"cT": E("cT", [128, 8, 2])}
    pp, gb = {}, {}
    for l in range(2):
        Tm[f"wada{l}"] = E(f"wada{l}", [D, 9 * D])
        Tm[f"bpp{l}"] = E(f"bpp{l}", [128, 72])
        Tm[f"bgb{l}"] = E(f"bgb{l}", [3, 128, D])
        Tm[f"ng{l}"] = E(f"ng{l}", [128, 24])
        for w in range(2):
            pp[l, w] = Tm[f"pp{l}_{w}"] = P.idram([128, 96], F32, f"pp{l}_{w}")
        for gi in range(3):
            gb[l, gi] = Tm[f"gb{l}_{gi}"] = P.idram([2, 128, D], F32, f"gb{l}_{gi}")
    a_cs = E("a_cs", [2, 128, 8192])
    a_cmat = E("a_cmat", [2, 128, 128])
    emit_mod_fused(P, Tm)
    P.end_stage()
    groups = [[0, 1, 2, 3], [4, 5, 6, 7]]
    dummy = Buf(None, "coll")

    def done(src3):
        t = P.sb([128, D], F32, "dbgt")
        for i in range(NT):
            P.dma("sp", t[:], src3[i], writes=[t])
            P.dma("sp", out[i], t[:], reads=[t])
        return P.finish()

    xcur = x0
    for l in range(2):
        x1 = P.idram([NT, 128, D], F32, f"x1_{l}")
        hsrc = P.idram([8, 128, TOK], BF16, f"hsrc{l}")
        h3 = hsrc
        build_ffn(True, P, {"x": xcur, "wgu": E(f"w1gu{l}", [D, 2 * DFF]), "wd": E(f"w1d{l}", [DFF, D]),
                            "pp": pp[l, 0], "gateb": gb[l, 0], "xo": x1, "ho": h3})
        P.end_stage()
        if upto == 10 * l + 1:
            return done(x1)
        hall = P.idram([8, 4 * 128, TOK], BF16, f"hall{l}")
        hall.gath = True
        for kc in range(8):
            P.coll("AllGather", hsrc[kc], hall[kc], groups, writes=[dummy])
        P.end_stage()
        if upto == 10 * l + 5:
            return done(x1)
        ysrc = P.idram([3, 4, 128, TOK], BF16, f"ysrc{l}")
        zt = P.sb([128, 4 * TOK - SEQ_ALL], BF16, "zt")
        P.memset(zt[:], 0.0, [zt])
        for br in range(3):
            P.dma("act", ysrc[br, 3, :, SEQ_ALL - 3 * TOK:TOK], zt[:], reads=[zt])
        yb_ = []
        for br in range(3):
            yb_.append(Buf(ysrc[br], f"yout{br}"))
            yb_[-1].gath = True
        build_attn(False, P, {"hT": hall, "wqkv": E(f"a_wqkv{l}", [3, D, 128]), "gqk": E(f"a_gqk{l}", [128, 2]),
                              "cs": a_cs, "cmat": a_cmat, "lamp": E(f"a_lamp{l}", [128, 258]),
                              "subg": E(f"a_subg{l}", [128, 128]), "ya": yb_[2]})
        P.end_stage()
        build_mlstm(99, P, {"hT": hall, "wqk": E(f"m_wqk{l}", [2, D, 64]), "wvo": E(f"m_wvo{l}", [D, 256]),
                            "wg": E(f"m_wg{l}", [D, 4]), "cw": E(f"m_cw{l}", [2, 128, 3, 64]), "gb": E(f"m_gb{l}", [128, 4]),
                            "og": E(f"m_og{l}", [128, 128]), "ym": yb_[0]})
        P.end_stage()
        build_rwkv(P, {"hT": hall, "wrkv": E(f"r_wrkv{l}", [D, 384]), "crkv": E(f"r_crkv{l}", [128, 3, 384]),
                       "wl": E(f"r_wl{l}", [D, 384]), "w2a2": E(f"r_w2a2{l}", [2, 128, 128]),
                       "bias01": E(f"r_bias01{l}", [1, 2, 256]), "g2": E(f"r_g2{l}", [128, 128]),
                       "vecs": E(f"r_vecs{l}", [128, 5, 128]), "yr": yb_[1]})
        P.end_stage()
        yall = P.idram([3, 4, 4 * 128, TOK], BF16, f"yall{l}")
        for br in range(3):
            for q in range(4):
                P.coll("AllGather", ysrc[br, q], yall[br, q], groups, writes=[dummy])
        P.end_stage()
        x2 = P.idram([NT, 128, D], F32, f"x2_{l}")
        build_merge(P, {"x": x1, "hT": h3, "yall": yall, "sel": sel, "wg": E(f"g_wg{l}", [D, 3 * D]),
                        "wb": E(f"g_wb{l}", [3, 512, D]), "wo": E(f"g_wo{l}", [D, D]), "gateb": gb[l, 1], "xo": x2}, fused=True)
        P.end_stage()
        if upto == 10 * l + 2:
            return done(x2)
        x3 = out if l == 1 else P.idram([NT, 128, D], F32, f"x3_{l}")
        build_ffn(False, P, {"x": x2, "wgu": E(f"w2gu{l}", [D, 2 * DFF]), "wd": E(f"w2d{l}", [DFF, D]),
                             "pp": pp[l, 1], "gateb": gb[l, 2], "xo": x3})
        P.end_stage()
        if upto == 10 * l + 3 and l == 0:
            return done(x3)
        xcur = x3
    return P.finish()


def fused_inputs(x, c, ctx, c_ctx, w_ada, b_ada, norm_g, ffn1_w_gu, ffn1_w_down, ffn2_w_gu, ffn2_w_down,
                 w_in, m_conv, m_gate_bias, m_out_norm, r_conv, r_w0, r_w2, r_a0, r_a2, r_g2, r_kk, r_ka,
                 r_rk, r_ln_w, r_ln_b, a_qk_norm, a_lambda, a_subln, w_branch, w_o):
    C = np.ascontiguousarray
    xs = [np.concatenate([ctx[b], x[b]], 0) for b in range(2)]
    dummy_h = [np.zeros((D, SEQ_ALL), NPBF) for _ in range(2)]
    off = col_offsets()
    per = [dict() for _ in range(8)]
    shards = [tok_shard(xs[b]) for b in range(2)]
    cs_tab, cm = rope_tables(), attn_consts()
    for core in range(8):
        b, i = core // 4, core % 4
        d = per[core]
        d["x0"] = shards[b][i]
        sel = np.zeros((128, 4), np.float32)
        sel[:, i] = 1.0
        d["sel"] = sel
        cA = c_ctx if i == 0 else c[b]
        d["cT"] = C(np.stack([cA, c[b]], 0).reshape(2, 8, 128).transpose(2, 1, 0).astype(np.float32))
        d["a_cs"], d["a_cmat"] = cs_tab, cm
    for l in range(2):
        bpp = C(b_ada[l].reshape(9, 8, 128).transpose(2, 0, 1).reshape(128, 72))
        bgb = C(np.stack([np.broadcast_to(b_ada[l].reshape(9, D)[k][None], (128, D)) for k in (2, 5, 8)], 0))
        ng = C(norm_g[l].reshape(3, 8, 128).transpose(2, 0, 1).reshape(128, 24))
        ai = attn_inputs(dummy_h, l, w_in, a_qk_norm, a_lambda, a_subln)
        mi = mlstm_inputs(dummy_h, l, w_in, m_conv, m_gate_bias, m_out_norm)
        ri = rwkv_inputs(dummy_h, l, w_in, r_conv, r_w0, r_w2, r_a0, r_a2, r_g2, r_kk, r_ka, r_rk, r_ln_w, r_ln_b)
        wg = C(w_in[l][:, off['g_m']:off['g_m'] + 3 * D])
        for core in range(8):
            d = per[core]
            d[f"wada{l}"] = C(w_ada[l]); d[f"bpp{l}"] = bpp; d[f"bgb{l}"] = bgb; d[f"ng{l}"] = ng
            d[f"w1gu{l}"] = C(ffn1_w_gu[l]); d[f"w1d{l}"] = C(ffn1_w_down[l])
            d[f"w2gu{l}"] = C(ffn2_w_gu[l]); d[f"w2d{l}"] = C(ffn2_w_down[l])
            for k in ("wqkv", "gqk", "lamp", "subg"):
                d[f"a_{k}{l}"] = ai[core][k]
            for k in ("wqk", "wvo", "wg", "cw", "gb", "og"):
                d[f"m_{k}{l}"] = mi[core][k]
            for k in ("wrkv", "crkv", "wl", "w2a2", "bias01", "g2", "vecs"):
                d[f"r_{k}{l}"] = ri[core][k]
            d[f"g_wg{l}"] = wg; d[f"g_wb{l}"] = C(w_branch[l]); d[f"g_wo{l}"] = C(w_o[l])
    return per


def kernel(**inputs):
    f = {k: np.asarray(v, dtype=np.float32) for k, v in inputs.items()}
    nc = _prog("fused", build_fused)
    res = _run(nc, fused_inputs(**f))
    xs = tok_unshard(res, "out")
    return np.stack([xs[b][NCTX:] for b in range(2)], 0).astype(np.float32)


_PROGS = {}


def _prog(name, fn):
    if name not in _PROGS:
        _PROGS[name] = fn()
    return _PROGS[name]


def _run(nc, ins):
    return run_bass_kernel_spmd(nc, ins, core_ids=list(range(8))).results


def kernel_unfused(x, c, ctx, c_ctx, w_ada, b_ada, norm_g, ffn1_w_gu, ffn1_w_down, ffn2_w_gu, ffn2_w_down,
           w_in, m_conv, m_gate_bias, m_out_norm, r_conv, r_w0, r_w2, r_a0, r_a2, r_g2, r_kk, r_ka,
           r_rk, r_ln_w, r_ln_b, a_qk_norm, a_lambda, a_subln, w_branch, w_o):
    f = lambda a: np.asarray(a, dtype=np.float32)
    (x, c, ctx, c_ctx, w_ada, b_ada, norm_g, ffn1_w_gu, ffn1_w_down, ffn2_w_gu, ffn2_w_down, w_in, m_conv, m_gate_bias,
     m_out_norm, r_conv, r_w0, r_w2, r_a0, r_a2, r_g2, r_kk, r_ka, r_rk, r_ln_w, r_ln_b, a_qk_norm, a_lambda, a_subln,
     w_branch, w_o) = map(f, (x, c, ctx, c_ctx, w_ada, b_ada, norm_g, ffn1_w_gu, ffn1_w_down, ffn2_w_gu, ffn2_w_down, w_in,
                              m_conv, m_gate_bias, m_out_norm, r_conv, r_w0, r_w2, r_a0, r_a2, r_g2, r_kk, r_ka, r_rk,
                              r_ln_w, r_ln_b, a_qk_norm, a_lambda, a_subln, w_branch, w_o))
    mods = run_mod(c, c_ctx, w_ada, b_ada)
    xs = [np.concatenate([ctx[b], x[b]], 0) for b in range(2)]
    for li in range(2):
        res = _run(_prog("ffn_h", lambda: build_ffn(True)),
                   ffn_inputs(xs, mods[li], li, 1, norm_g, np.ascontiguousarray(ffn1_w_gu[li]), np.ascontiguousarray(ffn1_w_down[li]), True))
        xs = tok_unshard(res, "xo")
        hTs = hT_unshard(res, "ho")
        ra = _run(_prog("attn", build_attn), attn_inputs(hTs, li, w_in, a_qk_norm, a_lambda, a_subln))
        rm = _run(_prog("mlstm", build_mlstm), mlstm_inputs(hTs, li, w_in, m_conv, m_gate_bias, m_out_norm))
        rr = _run(_prog("rwkv", build_rwkv), rwkv_inputs(hTs, li, w_in, r_conv, r_w0, r_w2, r_a0, r_a2, r_g2, r_kk, r_ka,
                                                          r_rk, r_ln_w, r_ln_b))
        yas = [np.concatenate([ra[b * 4 + h]["ya"] for h in range(4)], axis=0) for b in range(2)]
        yms = [np.concatenate([rm[b * 4 + h]["ym"] for h in range(4)], axis=0) for b in range(2)]
        yrs = [np.concatenate([rr[b * 4 + h]["yr"] for h in range(4)], axis=0) for b in range(2)]
        res = _run(_prog("merge", build_merge), merge_inputs(xs, hTs, yms, yrs, yas, mods[li], li, w_in, w_branch, w_o))
        xs = tok_unshard(res, "xo")
        res = _run(_prog("ffn", lambda: build_ffn(False)),
                   ffn_inputs(xs, mods[li], li, 2, norm_g, np.ascontiguousarray(ffn2_w_gu[li]), np.ascontiguousarray(ffn2_w_down[li]), False))
        xs = tok_unshard(res, "xo")
    return np.stack([xs[b][NCTX:] for b in range(2)], 0).astype(np.float32)
```
